# Optimizing a Trainium2 kernel written in Bass

```python
import math
import jax, jax.numpy as jnp
from jax import lax
import numpy as np

D_MODEL = 1024
BATCH = 2
SEQ = 8192
DEPTH = 2

CTX_LEN = 256
GRID_W = 64

HEAD_DIM = 64
ATTN_WIDTH = D_MODEL // 2
N_Q_HEADS = ATTN_WIDTH // HEAD_DIM
N_KV_HEADS = N_Q_HEADS // 4
WINDOW = 128
BLOCK = 128
ROPE_BASE = 10000.0
NEG_INF = -1e30

S5_WIDTH = D_MODEL // 4
S5_GROUP = 16
S5_GROUPS = S5_WIDTH // S5_GROUP
S5_STATE = 64

CONV_WIDTH = D_MODEL - ATTN_WIDTH - S5_WIDTH
CONV_K = 31

MIX_WIDTH = ATTN_WIDTH + S5_WIDTH + CONV_WIDTH
Q_END = ATTN_WIDTH
K_END = Q_END + N_KV_HEADS * HEAD_DIM
V_END = K_END + N_KV_HEADS * HEAD_DIM
U_END = V_END + S5_WIDTH
IN_WIDTH = U_END + 2 * CONV_WIDTH

N_EXPERTS = 16
EXPERT_FF = 2 * D_MODEL
CAPACITY_FACTOR = 2

DEEPNORM_ALPHA = (2.0 * DEPTH) ** 0.25
DEEPNORM_BETA = (8.0 * DEPTH) ** -0.25
LN_EPS = 1e-5

kernel_name = 'hybrid_s5_conformer_swa_ec_dit_block'


def layer_norm(x, g=None, b=None):
    x32 = x.astype(jnp.float32)
    mu = jnp.mean(x32, axis=-1, keepdims=True)
    var = jnp.mean(jnp.square(x32 - mu), axis=-1, keepdims=True)
    y = (x32 - mu) * lax.rsqrt(var + LN_EPS)
    if g is not None:
        y = y * g.astype(jnp.float32) + b.astype(jnp.float32)
    return y.astype(x.dtype)


def modulate(h, shift, scale):
    return h * (1 + scale) + shift


def rope_1d(x, pos):
    f = x.shape[-1] // 2
    inv_freq = ROPE_BASE ** (-jnp.arange(f, dtype=jnp.float32) / f)
    ang = pos.astype(jnp.float32)[:, None] * inv_freq
    cos = jnp.cos(ang)[None, :, None, :]
    sin = jnp.sin(ang)[None, :, None, :]
    x1 = x[..., :f].astype(jnp.float32)
    x2 = x[..., f:].astype(jnp.float32)
    return jnp.concatenate([x1 * cos - x2 * sin, x2 * cos + x1 * sin], axis=-1).astype(x.dtype)


def axial_rope(x, pos_row, pos_col):
    half = x.shape[-1] // 2
    return jnp.concatenate([rope_1d(x[..., :half], pos_row), rope_1d(x[..., half:], pos_col)], axis=-1)


def window_attention(q, k, v, k_ctx, v_ctx, sink):
    bsz, seq, n_q, dh = q.shape
    g = n_q // N_KV_HEADS
    nb = seq // BLOCK
    lc = k_ctx.shape[1]
    qb = q.reshape(bsz, nb, BLOCK, N_KV_HEADS, g, dh)

    def band(t):
        tp = jnp.pad(t, ((0, 0), (BLOCK, BLOCK), (0, 0), (0, 0))).reshape(bsz, nb + 2, BLOCK, N_KV_HEADS, dh)
        return jnp.concatenate([tp[:, :-2], tp[:, 1:-1], tp[:, 2:]], axis=2)

    k_win, v_win = band(k), band(v)
    scale = dh ** -0.5
    s_win = jnp.einsum('bnqhgd,bnkhd->bnhgqk', qb, k_win, preferred_element_type=jnp.float32) * scale
    qi = jnp.arange(BLOCK)
    kj = jnp.arange(3 * BLOCK)
    rel = (kj[None, :] - BLOCK) - qi[:, None]
    kpos = jnp.arange(nb)[:, None] * BLOCK - BLOCK + kj[None, :]
    mask = (jnp.abs(rel) <= WINDOW)[None] & ((kpos >= 0) & (kpos < seq))[:, None, :]
    s_win = jnp.where(mask[None, :, None, None], s_win, NEG_INF)
    s_ctx = jnp.einsum('bnqhgd,bkhd->bnhgqk', qb, k_ctx, preferred_element_type=jnp.float32) * scale
    s_sink = jnp.broadcast_to(sink.reshape(N_KV_HEADS, g, 1, 1).astype(jnp.float32), s_ctx.shape[:-1] + (1,))
    p = jax.nn.softmax(jnp.concatenate([s_win, s_ctx, s_sink], axis=-1), axis=-1)
    p_win = p[..., :3 * BLOCK].astype(v.dtype)
    p_ctx = p[..., 3 * BLOCK:3 * BLOCK + lc].astype(v.dtype)
    o = (jnp.einsum('bnhgqk,bnkhd->bnqhgd', p_win, v_win)
         + jnp.einsum('bnhgqk,bkhd->bnqhgd', p_ctx, v_ctx))
    return o.reshape(bsz, seq, n_q * dh)


def context_attention(q, k, v, sink):
    bsz, lc, n_q, dh = q.shape
    g = n_q // N_KV_HEADS
    qg = q.reshape(bsz, lc, N_KV_HEADS, g, dh)
    s = jnp.einsum('bqhgd,bkhd->bhgqk', qg, k, preferred_element_type=jnp.float32) * dh ** -0.5
    s_sink = jnp.broadcast_to(sink.reshape(N_KV_HEADS, g, 1, 1).astype(jnp.float32), s.shape[:-1] + (1,))
    p = jax.nn.softmax(jnp.concatenate([s, s_sink], axis=-1), axis=-1)[..., :lc].astype(v.dtype)
    return jnp.einsum('bhgqk,bkhd->bqhgd', p, v).reshape(bsz, lc, n_q * dh)


def _ssm_combine(e1, e2):
    a1, b1 = e1
    a2, b2 = e2
    return a1 * a2, a2 * b1 + b2


def s5_discretize(lam_re, lam_im, log_dt, b_re, b_im):
    lam = lax.complex(lam_re.astype(jnp.float32), lam_im.astype(jnp.float32))
    dt = jnp.exp(log_dt.astype(jnp.float32))[:, None]
    lbar = jnp.exp(lam * dt)
    bmat = lax.complex(b_re.astype(jnp.float32), b_im.astype(jnp.float32))
    bbar = ((lbar - 1) / lam)[..., None] * bmat
    return lbar, bbar


def s5_scan(u_c, lbar, bbar, h0, reverse):
    bu = jnp.einsum('blgc,gpc->blgp', u_c, bbar)
    if h0 is not None:
        bu = bu.at[:, -1 if reverse else 0].add(lbar * h0)
    a = jnp.broadcast_to(lbar, bu.shape)
    _, h = lax.associative_scan(_ssm_combine, (a, bu), reverse=reverse, axis=1)
    return h


def s5_mixer(u, lam_re, lam_im, log_dt, b_re, b_im, c_re, c_im, d_skip, h0, with_output):
    bsz, seq, _ = u.shape
    u_g = u.astype(jnp.float32).reshape(bsz, seq, S5_GROUPS, S5_GROUP)
    u_c = u_g.astype(jnp.complex64)
    y = d_skip.astype(jnp.float32).reshape(S5_GROUPS, S5_GROUP) * u_g if with_output else None
    finals = []
    for direction, reverse in enumerate((False, True)):
        lbar, bbar = s5_discretize(lam_re[direction], lam_im[direction], log_dt[direction],
                                   b_re[direction], b_im[direction])
        h = s5_scan(u_c, lbar, bbar, None if h0 is None else h0[direction], reverse)
        finals.append(h[:, 0] if reverse else h[:, -1])
        if with_output:
            cmat = lax.complex(c_re[direction].astype(jnp.float32), c_im[direction].astype(jnp.float32))
            y = y + jnp.real(jnp.einsum('blgp,gcp->blgc', h, cmat))
    if with_output:
        y = y.reshape(bsz, seq, S5_WIDTH).astype(u.dtype)
    return y, (finals[0], finals[1])


def s5_glu(y, w_glu, b_glu):
    z = jax.nn.gelu(y)
    return z * jax.nn.sigmoid(z @ w_glu + b_glu)


def conformer_conv(p, w_dw, b_dw, ln_g, ln_b, w_pw, b_pw):
    a, g = jnp.split(p, 2, axis=-1)
    h = a * jax.nn.sigmoid(g)
    h = lax.conv_general_dilated(h, w_dw, window_strides=(1,), padding=[(CONV_K // 2, CONV_K // 2)],
                                 dimension_numbers=('NWC', 'WIO', 'NWC'),
                                 feature_group_count=CONV_WIDTH) + b_dw
    h = jax.nn.silu(layer_norm(h, ln_g, ln_b))
    return h @ w_pw + b_pw


def expert_choice(h, w_router, w_gate, w_up, w_down):
    bsz, n, d = h.shape
    cap = CAPACITY_FACTOR * n // N_EXPERTS
    aff = jax.nn.softmax(jnp.einsum('bnd,de->ben', h, w_router, preferred_element_type=jnp.float32), axis=1)
    gates, idx = lax.top_k(aff, cap)
    xs = jax.vmap(lambda hb, ib: hb[ib])(h, idx)
    hid = (jax.nn.silu(jnp.einsum('becd,edf->becf', xs, w_gate))
           * jnp.einsum('becd,edf->becf', xs, w_up))
    y = jnp.einsum('becf,efd->becd', hid, w_down) * gates[..., None].astype(h.dtype)
    return jax.vmap(lambda ib, yb: jnp.zeros((n, d), yb.dtype).at[ib.reshape(-1)].add(yb.reshape(-1, d)))(idx, y)


def setup_inputs(seed: int = 0) -> dict:
    key = jax.random.key(seed)
    ks = iter(jax.random.split(key, 48))
    f32 = jnp.float32
    D = D_MODEL

    def nrm(shape, scale):
        return scale * jax.random.normal(next(ks), shape, f32)

    x = nrm((BATCH, SEQ, D), 1.0)
    c = nrm((BATCH, D), 1.0)
    ctx = nrm((BATCH, CTX_LEN, D), 1.0)
    c_ctx = nrm((D,), 1.0)
    w_mod = nrm((DEPTH, D, 6 * D), 0.5 * D ** -0.5)
    b_mod = nrm((DEPTH, 6 * D), 0.02)
    w_in = nrm((DEPTH, D, IN_WIDTH), D ** -0.5)
    w_in = w_in.at[:, :, K_END:V_END].multiply(DEEPNORM_BETA)
    b_in = nrm((DEPTH, IN_WIDTH), 0.02)
    attn_sink = nrm((DEPTH, N_Q_HEADS), 0.5)
    n_idx = jnp.arange(S5_STATE, dtype=f32)
    s5_lam_re = -0.5 + nrm((DEPTH, 2, S5_GROUPS, S5_STATE), 0.01)
    s5_lam_im = math.pi * n_idx + nrm((DEPTH, 2, S5_GROUPS, S5_STATE), 0.01)
    s5_log_dt = jax.random.uniform(next(ks), (DEPTH, 2, S5_GROUPS), f32, math.log(1e-3), math.log(1e-1))
    s5_b_re = nrm((DEPTH, 2, S5_GROUPS, S5_STATE, S5_GROUP), (2 * S5_GROUP) ** -0.5)
    s5_b_im = nrm((DEPTH, 2, S5_GROUPS, S5_STATE, S5_GROUP), (2 * S5_GROUP) ** -0.5)
    s5_c_re = nrm((DEPTH, 2, S5_GROUPS, S5_GROUP, S5_STATE), S5_STATE ** -0.5)
    s5_c_im = nrm((DEPTH, 2, S5_GROUPS, S5_GROUP, S5_STATE), S5_STATE ** -0.5)
    s5_d = nrm((DEPTH, S5_WIDTH), 1.0)
    s5_w_glu = nrm((DEPTH, S5_WIDTH, S5_WIDTH), S5_WIDTH ** -0.5)
    s5_b_glu = nrm((DEPTH, S5_WIDTH), 0.02)
    conv_w_dw = nrm((DEPTH, CONV_K, 1, CONV_WIDTH), CONV_K ** -0.5)
    conv_b_dw = nrm((DEPTH, CONV_WIDTH), 0.02)
    conv_ln_g = 1.0 + nrm((DEPTH, CONV_WIDTH), 0.05)
    conv_ln_b = nrm((DEPTH, CONV_WIDTH), 0.02)
    conv_w_pw = nrm((DEPTH, CONV_WIDTH, CONV_WIDTH), CONV_WIDTH ** -0.5)
    conv_b_pw = nrm((DEPTH, CONV_WIDTH), 0.02)
    w_out = nrm((DEPTH, MIX_WIDTH, D), DEEPNORM_BETA * MIX_WIDTH ** -0.5)
    b_out = nrm((DEPTH, D), 0.02)
    ln1_g = 1.0 + nrm((DEPTH, D), 0.05)
    ln1_b = nrm((DEPTH, D), 0.02)
    w_router = nrm((DEPTH, D, N_EXPERTS), D ** -0.5)
    exp_w_gate = nrm((DEPTH, N_EXPERTS, D, EXPERT_FF), D ** -0.5)
    exp_w_up = nrm((DEPTH, N_EXPERTS, D, EXPERT_FF), D ** -0.5)
    exp_w_down = nrm((DEPTH, N_EXPERTS, EXPERT_FF, D), DEEPNORM_BETA * EXPERT_FF ** -0.5)
    ln2_g = 1.0 + nrm((DEPTH, D), 0.05)
    ln2_b = nrm((DEPTH, D), 0.02)
    return {'x': x, 'c': c, 'ctx': ctx, 'c_ctx': c_ctx, 'w_mod': w_mod, 'b_mod': b_mod,
            'w_in': w_in, 'b_in': b_in, 'attn_sink': attn_sink,
            's5_lam_re': s5_lam_re, 's5_lam_im': s5_lam_im, 's5_log_dt': s5_log_dt,
            's5_b_re': s5_b_re, 's5_b_im': s5_b_im, 's5_c_re': s5_c_re, 's5_c_im': s5_c_im,
            's5_d': s5_d, 's5_w_glu': s5_w_glu, 's5_b_glu': s5_b_glu,
            'conv_w_dw': conv_w_dw, 'conv_b_dw': conv_b_dw, 'conv_ln_g': conv_ln_g, 'conv_ln_b': conv_ln_b,
            'conv_w_pw': conv_w_pw, 'conv_b_pw': conv_b_pw, 'w_out': w_out, 'b_out': b_out,
            'ln1_g': ln1_g, 'ln1_b': ln1_b, 'w_router': w_router, 'exp_w_gate': exp_w_gate,
            'exp_w_up': exp_w_up, 'exp_w_down': exp_w_down, 'ln2_g': ln2_g, 'ln2_b': ln2_b}


def reference(x, c, ctx, c_ctx, w_mod, b_mod, w_in, b_in, attn_sink,
              s5_lam_re, s5_lam_im, s5_log_dt, s5_b_re, s5_b_im, s5_c_re, s5_c_im,
              s5_d, s5_w_glu, s5_b_glu, conv_w_dw, conv_b_dw, conv_ln_g, conv_ln_b,
              conv_w_pw, conv_b_pw, w_out, b_out, ln1_g, ln1_b, w_router, exp_w_gate,
              exp_w_up, exp_w_down, ln2_g, ln2_b):
    bsz, seq, _ = x.shape
    lc = ctx.shape[1]
    ROWS = seq // GRID_W
    pos_row = jnp.repeat(jnp.arange(ROWS, dtype=jnp.int32), GRID_W)
    pos_col = jnp.tile(jnp.arange(GRID_W, dtype=jnp.int32), ROWS)
    alpha = DEEPNORM_ALPHA
    silu_c = jax.nn.silu(c)
    silu_cc = jax.nn.silu(c_ctx)
    xc = ctx
    for l in range(DEPTH):
        last = l == DEPTH - 1
        mod_x = jnp.split((silu_c @ w_mod[l] + b_mod[l])[:, None, :], 6, axis=-1)
        mod_c = jnp.split((silu_cc @ w_mod[l] + b_mod[l])[None, None, :], 6, axis=-1)

        hx = modulate(layer_norm(x), mod_x[0], mod_x[1])
        hc = modulate(layer_norm(xc), mod_c[0], mod_c[1])
        px = hx @ w_in[l] + b_in[l]
        col0, col1 = (Q_END, U_END) if last else (0, IN_WIDTH)
        pc = hc @ w_in[l][:, col0:col1] + b_in[l][col0:col1]

        k_c = pc[..., Q_END - col0:K_END - col0].reshape(bsz, lc, N_KV_HEADS, HEAD_DIM)
        v_c = pc[..., K_END - col0:V_END - col0].reshape(bsz, lc, N_KV_HEADS, HEAD_DIM)
        q = axial_rope(px[..., :Q_END].reshape(bsz, seq, N_Q_HEADS, HEAD_DIM), pos_row, pos_col)
        k = axial_rope(px[..., Q_END:K_END].reshape(bsz, seq, N_KV_HEADS, HEAD_DIM), pos_row, pos_col)
        v = px[..., K_END:V_END].reshape(bsz, seq, N_KV_HEADS, HEAD_DIM)
        attn_x = window_attention(q, k, v, k_c, v_c, attn_sink[l])

        s5_params = (s5_lam_re[l], s5_lam_im[l], s5_log_dt[l], s5_b_re[l], s5_b_im[l],
                     s5_c_re[l], s5_c_im[l], s5_d[l])
        y_c_s5, h_ctx = s5_mixer(pc[..., V_END - col0:U_END - col0], *s5_params, None, not last)
        y_x_s5, _ = s5_mixer(px[..., V_END:U_END], *s5_params, h_ctx, True)
        s5_x = s5_glu(y_x_s5, s5_w_glu[l], s5_b_glu[l])

        conv_params = (conv_w_dw[l], conv_b_dw[l], conv_ln_g[l], conv_ln_b[l], conv_w_pw[l], conv_b_pw[l])
        conv_x = conformer_conv(px[..., U_END:], *conv_params)

        y_mix = jnp.concatenate([attn_x, s5_x, conv_x], axis=-1) @ w_out[l] + b_out[l]
        x_mid = layer_norm(alpha * x + mod_x[2] * y_mix, ln1_g[l], ln1_b[l])

        if not last:
            q_c = pc[..., :Q_END].reshape(bsz, lc, N_Q_HEADS, HEAD_DIM)
            attn_c = context_attention(q_c, k_c, v_c, attn_sink[l])
            s5_c = s5_glu(y_c_s5, s5_w_glu[l], s5_b_glu[l])
            conv_c = conformer_conv(pc[..., U_END:], *conv_params)
            yc_mix = jnp.concatenate([attn_c, s5_c, conv_c], axis=-1) @ w_out[l] + b_out[l]
            xc = layer_norm(alpha * xc + mod_c[2] * yc_mix, ln1_g[l], ln1_b[l])

        h2 = modulate(layer_norm(x_mid), mod_x[3], mod_x[4])
        moe_x = expert_choice(h2, w_router[l], exp_w_gate[l], exp_w_up[l], exp_w_down[l])
        x = layer_norm(alpha * x_mid + mod_x[5] * moe_x, ln2_g[l], ln2_b[l])
        if not last:
            hc2 = modulate(layer_norm(xc), mod_c[3], mod_c[4])
            moe_c = expert_choice(hc2, w_router[l], exp_w_gate[l], exp_w_up[l], exp_w_down[l])
            xc = layer_norm(alpha * xc + mod_c[5] * moe_c, ln2_g[l], ln2_b[l])
    return x
```

```python
import os
import numpy as np
import concourse.bass as bass
import concourse.mybir as mybir
from concourse.bass_utils import run_bass_kernel_spmd

F32 = mybir.dt.float32
BF16 = mybir.dt.bfloat16
I32 = mybir.dt.int32
AF = mybir.ActivationFunctionType
ALU = mybir.AluOpType
AX = mybir.AxisListType

ENGS = ("pe", "act", "dve", "pool", "sp")
NDMA = 24


class Prog:
    def __init__(self, nc):
        self.nc = nc
        self.q = {e: [] for e in ENGS}
        self.cnt = {e: 0 for e in ENGS}
        self.seen = {e: {} for e in ENGS}
        self.last_w = {}
        self.readers = {}
        self.dma_i = {e: 0 for e in ENGS}
        self.dma_slot_val = {(e, i): 0 for e in ENGS for i in range(NDMA)}
        self.sb_off = 16640
        self.sb_marks = []
        self.sb_max = 0
        self.uid = 0

    def sb(self, name, shape, dtype=F32):
        self.uid += 1
        name = "%s_%d" % (name, self.uid)
        esz = mybir.dt.size(dtype)
        n = 1
        for s in shape[1:]:
            n *= s
        nbytes = (n * esz + 63) // 64 * 64
        t = self.nc.alloc_sbuf_tensor_at(name, list(shape), dtype, offset=self.sb_off)
        self.last_off = self.sb_off
        self.sb_off += nbytes
        self.sb_max = max(self.sb_max, self.sb_off)
        assert self.sb_off <= 229376, (name, self.sb_off)
        return t

    def sb_at(self, name, shape, dtype, offset):
        self.uid += 1
        return self.nc.alloc_sbuf_tensor_at("%s_%d" % (name, self.uid), list(shape), dtype, offset=offset)

    def mark(self):
        self.sb_marks.append(self.sb_off)

    def release(self):
        self.barrier()
        self.sb_off = self.sb_marks.pop()

    def _deps(self, reads, writes):
        deps = {}

        def add(k, v):
            if v > deps.get(k, 0):
                deps[k] = v
        for r in reads:
            w = self.last_w.get(r)
            if w:
                add(*w)
        for r in writes:
            w = self.last_w.get(r)
            if w:
                add(*w)
            for k, v in self.readers.get(r, {}).items():
                add(k, v)
        return deps

    def _commit(self, me, reads, writes):
        k, v = me
        for r in reads:
            self.readers.setdefault(r, {})[k] = v
        for r in writes:
            self.last_w[r] = me
            self.readers[r] = {}

    def op(self, eng, fn, reads=(), writes=()):
        deps = self._deps(reads, writes)
        waits = []
        for k, v in deps.items():
            if eng == "pe" and k == "pe":
                continue
            if self.seen[eng].get(k, 0) >= v:
                continue
            self.seen[eng][k] = v
            waits.append((k, v))
        self.cnt[eng] += 1
        me = (eng, self.cnt[eng])
        self.q[eng].append((waits, fn, eng, 1))
        self._commit(me, reads, writes)
        return me

    def dma(self, fn, reads=(), writes=(), eng="sp"):
        slot = (eng, self.dma_i[eng] % NDMA)
        self.dma_i[eng] += 1
        key = ("dma", slot)
        deps = self._deps(reads, writes)
        prev = self.dma_slot_val[slot]
        if prev:
            deps[key] = max(deps.get(key, 0), prev)
        waits = []
        for k, v in deps.items():
            if self.seen[eng].get(k, 0) >= v:
                continue
            self.seen[eng][k] = v
            waits.append((k, v))
        val = prev + 16
        self.dma_slot_val[slot] = val
        me = (key, val)
        self.q[eng].append((waits, fn, key, 16))
        self._commit(me, reads, writes)
        return me

    def barrier(self):
        targets = [(e, self.cnt[e]) for e in ENGS if self.cnt[e]]
        targets += [(("dma", s), v) for s, v in self.dma_slot_val.items() if v]
        for eng in ENGS:
            waits = []
            for k, v in targets:
                if self.seen[eng].get(k, 0) >= v:
                    continue
                self.seen[eng][k] = v
                waits.append((k, v))
            if waits:
                self.q[eng].append((waits, None, None, 0))

    def emit(self):
        nc = self.nc
        from contextlib import ExitStack
        with ExitStack() as st:
            sems = {}
            for e in ENGS:
                sems[e] = st.enter_context(nc.semaphore("s_" + e))
            for (e, i), v in self.dma_slot_val.items():
                if v:
                    sems[("dma", (e, i))] = st.enter_context(nc.semaphore("s_dma_%s%d" % (e, i)))
            block = st.enter_context(nc.Block())
            handles = {"pe": block.tensor, "act": block.scalar, "dve": block.vector,
                       "pool": block.gpsimd, "sp": block.sync}
            for e in ENGS:
                items = self.q[e]

                def body(h, items=items):
                    for waits, fn, incsem, incv in items:
                        for k, v in waits:
                            h.wait_ge(sems[k], v)
                        if fn is not None:
                            fn(h).then_inc(sems[incsem], incv)
                handles[e](body)


T = 8192
DM = 1024
LC = 256
NEXP = 16
FF = 2048
DEPTH = 2
ALPHA = (2.0 * DEPTH) ** 0.25
EPS = 1e-5
NCH = 19
CH_Q, CH_QR, CH_K, CH_KR, CH_V, CH_U, CH_A, CH_G = 0, 4, 8, 10, 12, 13, 15, 17


class Builder:
    def __init__(self, stop=None, dbg=False, nlayers=DEPTH):
        nc = bass.Bass("TRN2", target_bir_lowering=False)
        self.nc = nc
        self.P = Prog(nc)
        self.stop = stop
        self.dbg = dbg
        self.nlayers = nlayers
        self.dbg_outs = {}

        def di(name, shape, dt=F32):
            return nc.dram_tensor(name, list(shape), dt, kind="ExternalInput").ap()
        self.x_in = di("x", [T, DM])
        self.ctx_in = di("ctx", [LC, DM])
        self.ccol = di("ccol", [128, 8, 2])
        self.w_mod = di("w_mod", [DEPTH, DM, 6 * DM])
        self.b_mod = di("b_mod", [DEPTH, 6 * DM])
        self.w_in = di("w_in", [DEPTH, DM, 1536])
        self.b_in = di("b_in", [DEPTH, 1536])
        self.cosT = di("cosT", [128, T])
        self.sinT = di("sinT", [128, T])
        self.attn_sink = di("attn_sink", [DEPTH, 8])
        self.conv_w_col = di("conv_w_col", [DEPTH, 128, 2, 31])
        self.conv_vec_col = di("conv_vec_col", [DEPTH, 128, 2, 4])
        self.conv_w_pw = di("conv_w_pw", [DEPTH, 256, 256])
        self.w_out = di("w_out", [DEPTH, DM, DM])
        self.b_out = di("b_out", [DEPTH, DM])
        self.ln1_g = di("ln1_g", [DEPTH, DM])
        self.ln1_b = di("ln1_b", [DEPTH, DM])
        self.ln2_g = di("ln2_g", [DEPTH, DM])
        self.ln2_b = di("ln2_b", [DEPTH, DM])
        self.w_router = di("w_router", [DEPTH, DM, NEXP])
        self.s5nat = di("s5nat", [DEPTH, 128, 1072])
        self.s5_dcol = di("s5_dcol", [DEPTH, 128, 16])
        self.s5_w_glu = di("s5_w_glu", [DEPTH, 256, 256])
        self.s5_bglu_col = di("s5_bglu_col", [DEPTH, 128, 2])
        self.w_gate = di("exp_w_gate", [DEPTH, NEXP, DM, FF])
        self.w_up = di("exp_w_up", [DEPTH, NEXP, DM, FF])
        self.w_down = di("exp_w_down", [DEPTH, NEXP, FF, DM])
        self.out = nc.dram_tensor("out", [T, DM], F32, kind="ExternalOutput").ap()

        def dscr(name, shape, dt=F32):
            return nc.dram_tensor(name, list(shape), dt, kind="Internal").ap()
        self.dscr = dscr
        self.modrows = dscr("modrows", [DEPTH, 2, 6 * DM])
        self.qT_d = dscr("qT_d", [512, T], BF16)
        self.kT_d = dscr("kT_d", [256, T], BF16)
        self.v_d = dscr("v_d", [T, 130], BF16)
        self.uT_d = dscr("uT_d", [256, T], BF16)
        self.hT_d = dscr("hT_d", [256, T], BF16)
        self.qcT_d = dscr("qcT_d", [512, LC], BF16)
        self.kcT_d = dscr("kcT_d", [256, LC], BF16)
        self.vc_d = dscr("vc_d", [LC, 130], BF16)
        self.ucT_d = dscr("ucT_d", [256, LC], BF16)
        self.hcT_d = dscr("hcT_d", [256, LC], BF16)
        self.catT_d = dscr("catT_d", [DM, T], BF16)
        self.catcT_d = dscr("catcT_d", [DM, LC], BF16)
        self.xmid_d = dscr("xmid_d", [T, DM], F32)
        self.xcmid_d = dscr("xcmid_d", [LC, DM], F32)
        self.h2_d = dscr("h2_d", [T, DM], BF16)
        self.h2c_d = dscr("h2c_d", [LC, DM], BF16)
        self.moe_d = dscr("moe_d", [T, DM], F32)
        self.moec_d = dscr("moec_d", [LC, DM], F32)
        self.x1_d = dscr("x1_d", [T, DM], F32)
        self.ssh_d = dscr("ssh_d", [2, 8, 2, 128, T // 8], BF16)
        self.xc1_d = dscr("xc1_d", [LC, DM], F32)

        self.ps = [nc.alloc_psum_tensor("ps%d" % i, [128, 512], F32) for i in range(8)]
        self.PS = ["ps%d" % i for i in range(8)]
        self.consts()

    def dbg_out(self, name, shape, dt=F32):
        t = self.nc.dram_tensor("dbg_" + name, list(shape), dt, kind="ExternalOutput").ap()
        self.dbg_outs[name] = t
        return t

    def consts(self):
        P = self.P
        self.identf = P.sb("identf", [128, 128], F32)
        self.ident = P.sb("ident", [128, 128], BF16)
        self.ones_bf = P.sb("ones", [128, 128], BF16)
        P.op("pool", lambda e: e.memset(self.identf[:], 1.0), writes=["identf"])
        P.op("pool", lambda e: e.affine_select(out=self.identf[:], in_=self.identf[:], pattern=[[-1, 128]],
                                               compare_op=ALU.is_equal, fill=0.0, base=0, channel_multiplier=1),
             reads=["identf"], writes=["identf"])
        P.op("dve", lambda e: e.tensor_copy(out=self.ident[:], in_=self.identf[:]), reads=["identf"], writes=["ident"])
        P.op("pool", lambda e: e.memset(self.ones_bf[:], 1.0), writes=["ones"])

    def mm(self, out, lhsT, rhs, start, stop, reads, writes):
        self.P.op("pe", lambda e: e.matmul(out=out, lhsT=lhsT, rhs=rhs, start=start, stop=stop), reads, writes)

    def phase_mod(self, l):
        P = self.P
        P.mark()
        sc = P.sb("sc", [128, 8, 2], F32)
        scb = P.sb("scb", [128, 8, 2], BF16)
        P.dma(lambda e: e.dma_start(out=sc[:], in_=self.ccol), writes=["sc"])
        P.op("act", lambda e: e.activation(out=scb[:], in_=sc[:], func=AF.Silu), reads=["sc"], writes=["scb"])
        bm = P.sb("bm", [2, 6 * DM], F32)
        P.dma(lambda e: e.dma_start(out=bm[:], in_=self.b_mod[l].partition_broadcast(2)), writes=["bm"])
        mods = P.sb("mods", [2, 6 * DM], F32)
        wts = [P.sb("wmt%d" % i, [128, 8, 512], BF16) for i in range(2)]
        for ch in range(12):
            wt = wts[ch % 2]
            wn = "wmt%d" % (ch % 2)
            src = self.w_mod[l][:, ch * 512:(ch + 1) * 512].rearrange("(k p) n -> p k n", p=128)
            P.dma(lambda e, wt=wt, src=src: e.dma_start(out=wt[:], in_=src), writes=[wn], eng="pool")
            pb = ch % 2
            for k in range(8):
                self.mm(self.ps[pb][0:2, :], scb[:, k, :], wt[:, k, :], k == 0, k == 7, [wn, "scb"], [self.PS[pb]])
            P.op("dve", lambda e, ch=ch, pb=pb: e.tensor_tensor(out=mods[:, ch * 512:(ch + 1) * 512], in0=self.ps[pb][0:2, :],
                                                               in1=bm[:, ch * 512:(ch + 1) * 512], op=ALU.add),
                 reads=[self.PS[pb], "bm"], writes=["mods"])
        P.dma(lambda e: e.dma_start(out=self.modrows[l], in_=mods[:]), reads=["mods"], writes=["modrows"])
        P.release()

    def load_w_in(self, l):
        P = self.P
        self.w9 = P.sb("w9", [128, 9, 1536], BF16)
        w9 = self.w9
        P.op("pool", lambda e: e.memset(w9[:, 8, :], 0.0), writes=["w9"])
        for k0 in range(0, 8, 2):
            src = self.w_in[l][k0 * 128:(k0 + 2) * 128, :].rearrange("(k p) n -> p k n", p=128)
            P.dma(lambda e, k0=k0, src=src: e.dma_start(out=w9[:, k0:k0 + 2, :], in_=src), writes=["w9"], eng="pool")
        P.dma(lambda e: e.dma_start(out=w9[0:1, 8, :], in_=self.b_in[l:l + 1, :]), writes=["w9"], eng="pool")
        self.wext = P.sb("wext", [128, 9, NCH * 128], BF16)
        self.bext = P.sb("bext", [128, NCH], F32)

    def build_wext(self, l, var):
        P = self.P
        w9, wext = self.w9, self.wext
        P.mark()
        P.op("act", lambda e: e.activation(out=wext[:, :, 0:512], in_=w9[:, :, 0:512], func=AF.Copy), reads=["w9"], writes=["wext"])
        P.op("dve", lambda e: e.tensor_copy(out=wext[:, :, 12 * 128:19 * 128], in_=w9[:, :, 640:1536]), reads=["w9"], writes=["wext"])
        kd = wext[:, :, 1024:1280].rearrange("p k (h d f) -> p k h d f", h=2, d=2)
        ks = w9[:, :, 512:640].rearrange("p k (h f) -> p k h f", h=2)
        for d in range(2):
            P.op("pool", lambda e, d=d: e.tensor_copy(out=kd[:, :, :, d, :], in_=ks), reads=["w9"], writes=["wext"])
        for (dst0, src0, n) in ((512, 0, 512), (1280, 1024, 256)):
            dv = wext[:, :, dst0:dst0 + n].rearrange("p k (a s f) -> p k a s f", s=2, f=16)
            sv = wext[:, :, src0:src0 + n].rearrange("p k (a s f) -> p k a s f", s=2, f=16)
            P.op("act", lambda e, dv=dv, sv=sv: e.mul(out=dv[:, :, :, 0, :], in_=sv[:, :, :, 1, :], mul=-1.0),
                 reads=["wext"], writes=["wext"])
            P.op("dve", lambda e, dv=dv, sv=sv: e.tensor_copy(out=dv[:, :, :, 1, :], in_=sv[:, :, :, 0, :]),
                 reads=["wext"], writes=["wext"])
        sh = P.sb("sh", [128, 8], F32)
        scl = P.sb("scl", [128, 8], F32)
        P.dma(lambda e: e.dma_start(out=sh[:], in_=self.modrows[l, var, 0:1024].rearrange("(k p) -> p k", p=128), allow_slow_non_contiguous=True),
              reads=["modrows"], writes=["sh"])
        P.dma(lambda e: e.dma_start(out=scl[:], in_=self.modrows[l, var, 1024:2048].rearrange("(k p) -> p k", p=128), allow_slow_non_contiguous=True),
              reads=["modrows"], writes=["scl"])
        sha = P.sb("sha", [128, 9], BF16)
        P.op("pool", lambda e: e.memset(sha[:], 0.0), writes=["sha"])
        P.op("pool", lambda e: e.memset(sha[0:1, 8:9], 1.0), reads=["sha"], writes=["sha"])
        P.op("dve", lambda e: e.tensor_copy(out=sha[:, 0:8], in_=sh[:]), reads=["sh", "sha"], writes=["sha"])
        P.op("dve", lambda e: e.tensor_scalar_add(out=scl[:], in0=scl[:], scalar1=1.0), reads=["scl"], writes=["scl"])
        for c in range(NCH):
            for k in range(9):
                self.mm(self.ps[7][:, c:c + 1], wext[:, k, c * 128:(c + 1) * 128], sha[:, k:k + 1], k == 0, k == 8,
                        ["wext", "sha"], [self.PS[7]])
        P.op("dve", lambda e: e.tensor_copy(out=self.bext[:], in_=self.ps[7][:, 0:NCH]), reads=[self.PS[7]], writes=["bext"])
        for k in range(8):
            if k % 2 == 0:
                P.op("dve", lambda e, k=k: e.tensor_scalar(out=wext[:, k, :], in0=wext[:, k, :], scalar1=scl[:, k:k + 1], scalar2=None, op0=ALU.mult),
                     reads=["wext", "scl", self.PS[7]], writes=["wext"])
            else:
                P.op("act", lambda e, k=k: e.activation(out=wext[:, k, :], in_=wext[:, k, :], func=AF.Copy, scale=scl[:, k:k + 1]),
                     reads=["wext", "scl", self.PS[7]], writes=["wext"])
        P.release()

    def ln_tile_T(self, src_rows, src_res, xnT, xnT_res, col0, bufi):
        self.ln_tile_compute(src_rows, src_res, bufi)
        self.ln_tile_transpose(xnT, xnT_res, col0, bufi)

    def ln_tile_compute(self, src_rows, src_res, bufi):
        P = self.P
        xt, xn, st, mv, rs = self.ln_bufs[bufi]
        tg = "ln%d" % bufi
        P.dma(lambda e: e.dma_start(out=xt[:], in_=src_rows), reads=[src_res], writes=[tg + "xt"])
        for hh in range(2):
            P.op("dve", lambda e, hh=hh: e.bn_stats(out=st[:, hh, :], in_=xt[:, hh * 512:(hh + 1) * 512]), reads=[tg + "xt"], writes=[tg + "st"])
        P.op("dve", lambda e: e.bn_aggr(out=mv[:], in_=st[:].rearrange("p a b -> p (a b)")), reads=[tg + "st"], writes=[tg + "mv"])
        eps_t = self.eps_t
        P.op("act", lambda e: e.activation(out=rs[:], in_=mv[:, 1:2], func=AF.Sqrt, bias=eps_t[:, 0:1], scale=1.0), reads=[tg + "mv", "eps"], writes=[tg + "rs"])
        P.op("dve", lambda e: e.reciprocal(out=rs[:], in_=rs[:]), reads=[tg + "rs"], writes=[tg + "rs"])
        P.op("dve", lambda e: e.tensor_scalar(out=xn[:], in0=xt[:], scalar1=mv[:, 0:1], scalar2=rs[:, 0:1], op0=ALU.subtract, op1=ALU.mult),
             reads=[tg + "xt", tg + "mv", tg + "rs"], writes=[tg + "xn"])

    def ln_tile_transpose(self, xnT, xnT_res, col0, bufi):
        P = self.P
        xt, xn, st, mv, rs = self.ln_bufs[bufi]
        tg = "ln%d" % bufi
        pb = 6 + bufi % 2
        psT = self.ps[pb][:, :].bitcast(BF16)
        for k in range(8):
            P.op("pe", lambda e, k=k: e.transpose(out=psT[:, k * 128:(k + 1) * 128], in_=xn[:, k * 128:(k + 1) * 128], identity=self.ident[:]),
                 reads=[tg + "xn", "ident"], writes=[self.PS[pb]])
        P.op("act", lambda e: e.activation(out=xnT[:, :, col0:col0 + 128], in_=psT.rearrange("p (k t) -> p k t", k=8), func=AF.Copy),
             reads=[self.PS[pb]], writes=[xnT_res])

    def alloc_ln_bufs(self):
        P = self.P
        self.eps_t = eps_t = P.sb("eps", [128, 1], F32)
        P.op("pool", lambda e: e.memset(eps_t[:], EPS), writes=["eps"])
        self.ln_bufs = []
        for i in range(4):
            self.ln_bufs.append((P.sb("xt%d" % i, [128, DM], F32), P.sb("xn%d" % i, [128, DM], BF16), P.sb("st%d" % i, [128, 2, 6], F32),
                                 P.sb("mv%d" % i, [128, 2], F32), P.sb("rs%d" % i, [128, 1], F32)))

    def inproj_ln(self, src, src_res, t0, N, xnT, xres):
        for i in range(N // 128):
            self.ln_tile_T(src[t0 + i * 128:t0 + (i + 1) * 128, :], src_res, xnT, xres, i * 128, i % 4)

    def inproj_proj(self, l, t0, N, is_ctx, need_full, xnT, xres, blk):
        P = self.P
        nt = N // 128
        wext, bext = self.wext, self.bext
        bank, obc, tfc = self._bank, self._obc, self._tfc

        def proj(c):
            pb = bank[0]
            bank[0] = (pb + 1) % 5
            for k in range(8):
                self.mm(self.ps[pb][:, 0:N], wext[:, k, c * 128:(c + 1) * 128], xnT[:, k, 0:N], k == 0, k == 7, ["wext", xres], [self.PS[pb]])
            return pb

        def nob():
            i = obc[0] % len(self.ev_bf)
            obc[0] += 1
            return self.ev_bf[i], "evbf%d" % i

        def ntf():
            i = tfc[0] % len(self.ev_f)
            tfc[0] += 1
            return self.ev_f[i], "evf%d" % i
        cs, sn = self.cs_ts[blk % 2], self.sn_ts[blk % 2]
        csn, snn = "cs%d" % (blk % 2), "sn%d" % (blk % 2)
        if not is_ctx:
            P.dma(lambda e: e.dma_start(out=cs[:, 0:N], in_=self.cosT[:, t0:t0 + N]), writes=[csn])
            P.dma(lambda e: e.dma_start(out=sn[:, 0:N], in_=self.sinT[:, t0:t0 + N]), writes=[snn])
        qk = []
        if need_full:
            qk += [(c, CH_QR + c, (self.qcT_d if is_ctx else self.qT_d)[c * 128:(c + 1) * 128, t0:t0 + N]) for c in range(4)]
        qk += [(CH_K + c, CH_KR + c, (self.kcT_d if is_ctx else self.kT_d)[c * 128:(c + 1) * 128, t0:t0 + N]) for c in range(2)]
        for j, (c, cr, dst) in enumerate(qk):
            ob, on = nob()
            if is_ctx:
                pb = proj(c)
                P.op("act", lambda e, c=c, ob=ob, pb=pb: e.activation(out=ob[:, 0:N], in_=self.ps[pb][:, 0:N], func=AF.Identity, bias=bext[:, c:c + 1], scale=1.0),
                     reads=[self.PS[pb], "bext"], writes=[on])
            else:
                p0 = proj(c)
                p1 = proj(cr)
                t1, t1n = ntf()
                t2, t2n = ntf()
                P.op("dve", lambda e, c=c, p0=p0, t1=t1: e.scalar_tensor_tensor(out=t1[:, 0:N], in0=self.ps[p0][:, 0:N], scalar=bext[:, c:c + 1], in1=cs[:, 0:N], op0=ALU.add, op1=ALU.mult),
                     reads=[self.PS[p0], "bext", csn], writes=[t1n])
                P.op("dve", lambda e, cr=cr, p1=p1, t2=t2: e.scalar_tensor_tensor(out=t2[:, 0:N], in0=self.ps[p1][:, 0:N], scalar=bext[:, cr:cr + 1], in1=sn[:, 0:N], op0=ALU.add, op1=ALU.mult),
                     reads=[self.PS[p1], "bext", snn], writes=[t2n])
                P.op("pool", lambda e, ob=ob, t1=t1, t2=t2: e.tensor_tensor(out=ob[:, 0:N], in0=t1[:, 0:N], in1=t2[:, 0:N], op=ALU.add), reads=[t1n, t2n], writes=[on])
            P.dma(lambda e, dst=dst, ob=ob: e.dma_start(out=dst, in_=ob[:, 0:N]), reads=[on])
            if self._hook is not None:
                self._hook(j)
        pv = proj(CH_V)
        vb, vbn = nob()
        P.op("act", lambda e: e.activation(out=vb[:, 0:N], in_=self.ps[pv][:, 0:N], func=AF.Identity, bias=bext[:, CH_V:CH_V + 1], scale=1.0),
             reads=[self.PS[pv], "bext"], writes=[vbn])
        pt = 5
        psT = self.ps[pt][:, :].bitcast(BF16)
        for i in range(nt):
            P.op("pe", lambda e, i=i: e.transpose(out=psT[:, i * 128:(i + 1) * 128], in_=vb[:, i * 128:(i + 1) * 128], identity=self.ident[:]),
                 reads=[vbn, "ident"], writes=[self.PS[pt]])
        va = self.v_augs[blk % 2]
        van = "vaug%d" % (blk % 2)
        P.op("dve", lambda e: e.tensor_copy(out=va[:, 0:nt, :, 0:64], in_=psT[:, 0:nt * 128].rearrange("p (i h f) -> p i h f", i=nt, h=2)),
             reads=[self.PS[pt]], writes=[van])
        vdst = (self.vc_d if is_ctx else self.v_d)[t0:t0 + N, :].rearrange("(i p) f -> p i f", p=128)
        P.dma(lambda e: e.dma_start(out=vdst, in_=va[:, 0:nt, :, :].rearrange("p i h f -> p i (h f)")), reads=[van])
        if self._hook is not None:
            self._hook(6)
        for c in range(2):
            pu = proj(CH_U + c)
            ob, on = nob()
            P.op("act", lambda e, c=c, ob=ob, pu=pu: e.activation(out=ob[:, 0:N], in_=self.ps[pu][:, 0:N], func=AF.Identity, bias=bext[:, CH_U + c:CH_U + c + 1], scale=1.0),
                 reads=[self.PS[pu], "bext"], writes=[on])
            dst = (self.ucT_d if is_ctx else self.uT_d)[c * 128:(c + 1) * 128, t0:t0 + N]
            P.dma(lambda e, dst=dst, ob=ob: e.dma_start(out=dst, in_=ob[:, 0:N]), reads=[on])
            if self._hook is not None:
                self._hook(7 + c)
        if need_full:
            for c in range(2):
                pa = proj(CH_A + c)
                pg = proj(CH_G + c)
                sg, sgn = ntf()
                ob, on = nob()
                P.op("act", lambda e, c=c, pg=pg, sg=sg: e.activation(out=sg[:, 0:N], in_=self.ps[pg][:, 0:N], func=AF.Sigmoid, bias=bext[:, CH_G + c:CH_G + c + 1], scale=1.0),
                     reads=[self.PS[pg], "bext"], writes=[sgn])
                P.op("dve", lambda e, c=c, pa=pa, sg=sg, ob=ob: e.scalar_tensor_tensor(out=ob[:, 0:N], in0=self.ps[pa][:, 0:N], scalar=bext[:, CH_A + c:CH_A + c + 1], in1=sg[:, 0:N], op0=ALU.add, op1=ALU.mult),
                     reads=[self.PS[pa], "bext", sgn], writes=[on])
                dst = (self.hcT_d if is_ctx else self.hT_d)[c * 128:(c + 1) * 128, t0:t0 + N]
                P.dma(lambda e, dst=dst, ob=ob: e.dma_start(out=dst, in_=ob[:, 0:N]), reads=[on])

    def phase_inproj(self, l, is_ctx, need_full):
        P = self.P
        P.mark()
        self.alloc_ln_bufs()
        xnTs = [P.sb("xnT%d" % i, [128, 8, 512], BF16) for i in range(2)]
        self.cs_ts = [P.sb("cs%d" % i, [128, 512], F32) for i in range(2)]
        self.sn_ts = [P.sb("sn%d" % i, [128, 512], F32) for i in range(2)]
        self.ev_f = [P.sb("evf%d" % i, [128, 512], F32) for i in range(6)]
        self.ev_bf = [P.sb("evbf%d" % i, [128, 512], BF16) for i in range(8)]
        self.v_augs = [P.sb("vaug%d" % i, [128, 4, 2, 65], BF16) for i in range(2)]
        for i, va in enumerate(self.v_augs):
            P.op("pool", lambda e, va=va: e.memset(va[:], 1.0), writes=["vaug%d" % i])
        self._bank, self._obc, self._tfc = [0], [0], [0]
        self._hook = None
        if is_ctx:
            self.inproj_ln(self.xc_src, "xc", 0, LC, xnTs[0], "xnT0")
            self.inproj_proj(l, 0, LC, True, need_full, xnTs[0], "xnT0", 0)
        else:
            nb_ = int(os.environ.get('DBG_NBLK', T // 512))
            self.inproj_ln(self.x_src, "xsrc", 0, 512, xnTs[0], "xnT0")
            for b in range(nb_):
                if b + 1 < nb_:
                    def hook(step, b=b):
                        t1_ = (b + 1) * 512
                        xn_, xr_ = xnTs[(b + 1) % 2], "xnT%d" % ((b + 1) % 2)
                        if step in (0, 2, 4, 6):
                            i = step // 2
                            self.ln_tile_compute(self.x_src[t1_ + i * 128:t1_ + (i + 1) * 128, :], "xsrc", i)
                        if step in (2, 4, 6, 8):
                            i = step // 2 - 1
                            self.ln_tile_transpose(xn_, xr_, i * 128, i)
                    self._hook = hook
                else:
                    self._hook = None
                self.inproj_proj(l, b * 512, 512, False, True, xnTs[b % 2], "xnT%d" % (b % 2), b)
            self._hook = None
        P.release()

    def attn_consts(self, l):
        P = self.P
        mf = P.sb("mf", [128, 128], F32)
        self.mlo = P.sb("mlo", [128, 128], BF16)
        self.mhi = P.sb("mhi", [128, 128], BF16)
        for (m, name, sign) in ((self.mlo, "mlo", 1), (self.mhi, "mhi", -1)):
            P.op("pool", lambda e: e.memset(mf[:], 1.0), writes=["mf"])
            P.op("pool", lambda e, sign=sign: e.affine_select(out=mf[:], in_=mf[:], pattern=[[-sign, 128]], compare_op=ALU.is_ge, fill=0.0,
                                                               base=0, channel_multiplier=sign), reads=["mf"], writes=["mf"])
            P.op("dve", lambda e, m=m: e.tensor_copy(out=m[:], in_=mf[:]), reads=["mf"], writes=[name])
        self.esink = es = P.sb("esink", [128, 8], F32)
        P.dma(lambda e: e.dma_start(out=es[:], in_=self.attn_sink[l].partition_broadcast(128)), writes=["esink"])
        P.op("act", lambda e: e.activation(out=es[:], in_=es[:], func=AF.Exp), reads=["esink"], writes=["esink"])

    def attn_qtile(self, qblk, qres, qc0, keys, stage, stage_res, sti):
        P = self.P
        nk = len(keys)
        if int(os.environ.get("DBG_Q", 99)) < 0:
            return
        at = self.at_tile
        esink = self.esink
        for h in range(2):
            for ki, (kf, kres, vf, vres, mask) in enumerate(keys):
                for hh in range(4):
                    c, s = 2 * h + hh // 2, hh % 2
                    self.mm(self.ps[ki][:, hh * 128:(hh + 1) * 128], kf(h, s), qblk[:, c, qc0:qc0 + 128],
                            True, True, [kres, qres], [self.PS[ki]])
            pT = self.pT
            dq = int(os.environ.get("DBG_Q", 99))
            if dq == 0:
                continue
            for ki, (kf, kres, vf, vres, mask) in enumerate(keys):
                P.op("act", lambda e, ki=ki: e.activation(out=pT[:, ki, :], in_=self.ps[ki][:, :], func=AF.Exp, scale=0.125),
                     reads=[self.PS[ki]], writes=["pT%d" % ki])
                if mask is not None:
                    P.op("pool", lambda e, ki=ki, mask=mask: e.tensor_tensor(out=pT[:, ki, :].rearrange("p (a i) -> p a i", a=4), in0=pT[:, ki, :].rearrange("p (a i) -> p a i", a=4),
                                                                          in1=mask[:, :].unsqueeze(1).to_broadcast([128, 4, 128]), op=ALU.mult),
                         reads=["pT%d" % ki, "mlo", "mhi"], writes=["pT%d" % ki])
            if dq == 1:
                continue
            for hh in range(4):
                for ki, (kf, kres, vf, vres, mask) in enumerate(keys):
                    self.mm(self.ps[5][:, hh * 65:(hh + 1) * 65], pT[:, ki, hh * 128:(hh + 1) * 128], vf(h), ki == 0, ki == nk - 1,
                            ["pT%d" % ki, vres], [self.PS[5]])
            if dq == 2:
                continue
            den = self.den
            ov = self.ps[5][:, 0:260].rearrange("p (a f) -> p a f", a=4)
            P.op("dve", lambda e, h=h: e.tensor_tensor(out=den[:], in0=ov[:, :, 64], in1=esink[:, 4 * h:4 * h + 4], op=ALU.add),
                 reads=[self.PS[5], "esink"], writes=["den"])
            P.op("dve", lambda e: e.reciprocal(out=den[:], in_=den[:]), reads=["den"], writes=["den"])
            P.op("dve", lambda e, h=h: e.tensor_tensor(out=at[:, 256 * h:256 * (h + 1)].rearrange("p (a f) -> p a f", a=4), in0=ov[:, :, 0:64],
                                                      in1=den[:, :].unsqueeze(2).to_broadcast([128, 4, 64]), op=ALU.mult),
                 reads=[self.PS[5], "den"], writes=["at"])
        if int(os.environ.get("DBG_Q", 99)) <= 3:
            return
        psT = self.ps[6][:, :].bitcast(BF16)
        for c in range(4):
            P.op("pe", lambda e, c=c: e.transpose(out=psT[:, c * 128:(c + 1) * 128], in_=at[:, c * 128:(c + 1) * 128], identity=self.ident[:]),
                 reads=["at", "ident"], writes=[self.PS[6]])
        P.op("act", lambda e: e.activation(out=stage[:, :, sti * 128:(sti + 1) * 128], in_=psT[:, 0:512].rearrange("p (c t) -> p c t", c=4), func=AF.Copy),
             reads=[self.PS[6]], writes=[stage_res])

    def phase_attn(self, l, do_ctx):
        P = self.P
        P.mark()
        self.attn_consts(l)
        dbgA = int(os.environ.get("DBG_A", 99))
        if dbgA == 0:
            P.release()
            return
        self.pT = P.sb("pT", [128, 5, 512], BF16)
        self.den = P.sb("den", [128, 4], F32)
        self.at_tile = P.sb("at", [128, 512], BF16)
        kc = P.sb("kc", [128, 2, 2, LC], BF16)
        vc = P.sb("vc", [128, 2, 130], BF16)
        for s_ in range(2):
            P.dma(lambda e, s_=s_: e.dma_start(out=kc[:, s_, :, :], in_=self.kcT_d.rearrange("(h p) t -> p h t", p=128)), writes=["kc"])
            P.op("pool", lambda e, s_=s_: e.memset(kc[64 * (1 - s_):64 * (2 - s_), s_, :, :], 0.0), reads=["kc"], writes=["kc"])
        P.dma(lambda e: e.dma_start(out=vc[:], in_=self.vc_d.rearrange("(i p) f -> p i f", p=128)), writes=["vc"])
        ckeys = [(lambda h, s_, i=i: kc[:, s_, h, i * 128:(i + 1) * 128], "kc", lambda h, i=i: vc[:, i, h * 65:(h + 1) * 65], "vc", None) for i in range(2)]
        stages = [P.sb("ast%d" % i, [128, 4, 512], BF16) for i in range(2)]
        if dbgA == 1:
            P.release()
            return
        if do_ctx:
            qb = P.sb("qcb", [128, 4, LC], BF16)
            P.dma(lambda e, qb=qb: e.dma_start(out=qb[:], in_=self.qcT_d.rearrange("(c p) t -> p c t", p=128)), writes=["qcb"])
            for n in range(2):
                self.attn_qtile(qb, "qcb", n * 128, ckeys, stages[0], "ast0", n)
            P.dma(lambda e: e.dma_start(out=self.catcT_d[0:512, :].rearrange("(c p) t -> p c t", p=128), in_=stages[0][:, :, 0:LC]), reads=["ast0"])
        if dbgA == 2:
            P.release()
            return
        kx = P.sb("kx", [128, 2, 2, T], BF16)
        vx = P.sb("vx", [128, T // 128, 130], BF16)
        for s_ in range(2):
            for h in range(2):
                P.dma(lambda e, h=h, s_=s_: e.dma_start(out=kx[:, s_, h, :], in_=self.kT_d[h * 128:(h + 1) * 128, :]), writes=["kx"])
                P.op("pool", lambda e, h=h, s_=s_: e.memset(kx[64 * (1 - s_):64 * (2 - s_), s_, h, :], 0.0), reads=["kx"], writes=["kx"])
        for i0 in range(0, T // 128, 8):
            P.dma(lambda e, i0=i0: e.dma_start(out=vx[:, i0:i0 + 8, :], in_=self.v_d[i0 * 128:(i0 + 8) * 128, :].rearrange("(i p) f -> p i f", p=128)), writes=["vx"])
        qbs = [P.sb("qb%d" % i, [128, 4, 512], BF16) for i in range(2)]
        for b in range(int(os.environ.get('DBG_NBLK', T // 512))):
            qb, qn = qbs[b % 2], "qb%d" % (b % 2)
            P.dma(lambda e, qb=qb, b=b: e.dma_start(out=qb[:], in_=self.qT_d[:, b * 512:(b + 1) * 512].rearrange("(c p) t -> p c t", p=128)), writes=[qn])
            st, sn = stages[b % 2], "ast%d" % (b % 2)
            for i in range(4):
                n = 4 * b + i
                keys = []
                for (kt, mask) in ((n - 1, self.mlo), (n, None), (n + 1, self.mhi)):
                    if 0 <= kt < T // 128:
                        keys.append((lambda h, s_, kt=kt: kx[:, s_, h, kt * 128:(kt + 1) * 128], "kx", lambda h, kt=kt: vx[:, kt, h * 65:(h + 1) * 65], "vx", mask))
                self.attn_qtile(qb, qn, i * 128, keys + ckeys, st, sn, i)
            P.dma(lambda e, st=st, b=b: e.dma_start(out=self.catT_d[0:512, b * 512:(b + 1) * 512].rearrange("(c p) t -> p c t", p=128), in_=st[:]), reads=[sn])
        P.release()

    def phase_conv(self, l, is_ctx):
        P = self.P
        P.mark()
        N = LC if is_ctx else T
        src = self.hcT_d if is_ctx else self.hT_d
        dstT = self.catcT_d if is_ctx else self.catT_d
        wcol = P.sb("cw", [128, 2, 31], F32)
        bcol = P.sb("cb", [128, 2, 4], F32)
        P.dma(lambda e: e.dma_start(out=wcol[:], in_=self.conv_w_col[l]), writes=["cw"])
        P.dma(lambda e: e.dma_start(out=bcol[:], in_=self.conv_vec_col[l]), writes=["cb"])
        wpw = P.sb("wpw", [128, 2, 256], BF16)
        P.dma(lambda e: e.dma_start(out=wpw[:], in_=self.conv_w_pw[l].rearrange("(k p) n -> p k n", p=128)), writes=["wpw"], eng="pool")
        diag = P.sb("diag", [128, 2, 31, 128], BF16)
        for cc in range(2):
            for k in range(31):
                if k % 2 == 0:
                    P.op("dve", lambda e, cc=cc, k=k: e.tensor_scalar(out=diag[:, cc, k, :], in0=self.identf[:], scalar1=wcol[:, cc, k:k + 1], scalar2=None, op0=ALU.mult),
                         reads=["identf", "cw"], writes=["diag"])
                else:
                    P.op("act", lambda e, cc=cc, k=k: e.activation(out=diag[:, cc, k, :], in_=self.identf[:], func=AF.Copy, scale=wcol[:, cc, k:k + 1]),
                         reads=["identf", "cw"], writes=["diag"])
        avg = P.sb("avg", [128, 128], BF16)
        P.op("pool", lambda e: e.memset(avg[:], 1.0 / 256.0), writes=["avg"])
        hp = P.sb("hp", [128, 2, N + 30], BF16)
        P.op("pool", lambda e: e.memset(hp[:, :, 0:15], 0.0), writes=["hp"])
        P.op("pool", lambda e: e.memset(hp[:, :, N + 15:N + 30], 0.0), reads=["hp"], writes=["hp"])
        for cc in range(2):
            P.dma(lambda e, cc=cc: e.dma_start(out=hp[:, cc, 15:15 + N], in_=src[cc * 128:(cc + 1) * 128, :]), reads=["hp"], writes=["hp"])
        BW = min(512, N)
        hcs = [P.sb("hcv%d" % i, [128, 2, BW], F32) for i in range(2)]
        hcbs = [P.sb("hcb%d" % i, [128, 2, BW], BF16) for i in range(2)]
        sqs = [P.sb("sq%d" % i, [128, 2, BW], BF16) for i in range(2)]
        m2 = P.sb("m2", [128, BW], F32)
        rstd = P.sb("rstd", [128, BW], F32)
        tts = [P.sb("tt%d" % i, [128, BW], F32) for i in range(2)]
        hn = P.sb("hn", [128, 2, BW], BF16)
        ob = [P.sb("cob%d" % i, [128, BW], BF16) for i in range(2)]
        eps_t = P.sb("ceps", [128, 1], F32)
        P.op("pool", lambda e: e.memset(eps_t[:], EPS), writes=["ceps"])
        nb = N // BW
        if not is_ctx:
            nb = int(os.environ.get("DBG_NBLK", nb))

        def SA(b):
            t0 = b * BW
            j = b % 2
            hc, hcb, sq = hcs[j], hcbs[j], sqs[j]
            for cc in range(2):
                pb = 2 * j + cc
                for k in range(31):
                    self.mm(self.ps[pb][:, 0:BW], diag[:, cc, k, :], hp[:, cc, t0 + k:t0 + k + BW], k == 0, k == 30, ["diag", "hp"], [self.PS[pb]])
                P.op("act", lambda e, cc=cc, pb=pb: e.activation(out=hc[:, cc, :], in_=self.ps[pb][:, 0:BW], func=AF.Identity, bias=bcol[:, cc, 0:1], scale=1.0),
                     reads=[self.PS[pb], "cb"], writes=["hcv%d" % j])
                P.op("act", lambda e, cc=cc, pb=pb: e.activation(out=sq[:, cc, :], in_=self.ps[pb][:, 0:BW], func=AF.Square, bias=bcol[:, cc, 0:1], scale=1.0),
                     reads=[self.PS[pb], "cb"], writes=["sq%d" % j])
                P.op("pool", lambda e, cc=cc: e.tensor_copy(out=hcb[:, cc, :], in_=hc[:, cc, :]), reads=["hcv%d" % j], writes=["hcb%d" % j])

        def SB(b):
            t0 = b * BW
            j = b % 2
            hc, hcb, sq = hcs[j], hcbs[j], sqs[j]
            for cc in range(2):
                self.mm(self.ps[4][:, 0:BW], avg[:], hcb[:, cc, :], cc == 0, cc == 1, ["avg", "hcb%d" % j], [self.PS[4]])
            for cc in range(2):
                self.mm(self.ps[5][:, 0:BW], avg[:], sq[:, cc, :], cc == 0, cc == 1, ["avg", "sq%d" % j], [self.PS[5]])
            P.op("act", lambda e: e.activation(out=m2[:], in_=self.ps[4][:, 0:BW], func=AF.Square), reads=[self.PS[4]], writes=["m2"])
            P.op("dve", lambda e: e.tensor_tensor(out=rstd[:], in0=self.ps[5][:, 0:BW], in1=m2[:], op=ALU.subtract), reads=[self.PS[5], "m2"], writes=["rstd"])
            P.op("act", lambda e: e.activation(out=rstd[:], in_=rstd[:], func=AF.Sqrt, bias=eps_t[:, 0:1], scale=1.0), reads=["rstd", "ceps"], writes=["rstd"])
            P.op("dve", lambda e: e.reciprocal(out=rstd[:], in_=rstd[:]), reads=["rstd"], writes=["rstd"])
            for cc in range(2):
                tt = tts[cc]
                P.op("dve", lambda e, cc=cc, tt=tt: e.tensor_tensor(out=tt[:], in0=hc[:, cc, :], in1=self.ps[4][:, 0:BW], op=ALU.subtract), reads=["hcv%d" % j, self.PS[4]], writes=["tt%d" % cc])
                P.op("dve", lambda e, tt=tt: e.tensor_tensor(out=tt[:], in0=tt[:], in1=rstd[:], op=ALU.mult), reads=["tt%d" % cc, "rstd"], writes=["tt%d" % cc])
                P.op("act", lambda e, cc=cc, tt=tt: e.activation(out=hn[:, cc, :], in_=tt[:], func=AF.Silu, bias=bcol[:, cc, 2:3], scale=bcol[:, cc, 1:2]),
                     reads=["tt%d" % cc, "cb"], writes=["hn"])
            for co in range(2):
                for cc in range(2):
                    self.mm(self.ps[6 + co][:, 0:BW], wpw[:, cc, co * 128:(co + 1) * 128], hn[:, cc, :], cc == 0, cc == 1, ["wpw", "hn"], [self.PS[6 + co]])
                o = ob[co]
                P.op("act", lambda e, co=co, o=o: e.activation(out=o[:], in_=self.ps[6 + co][:, 0:BW], func=AF.Identity, bias=bcol[:, co, 3:4], scale=1.0),
                     reads=[self.PS[6 + co], "cb"], writes=["cob%d" % co])
                P.dma(lambda e, co=co, o=o: e.dma_start(out=dstT[768 + co * 128:768 + (co + 1) * 128, t0:t0 + BW], in_=o[:]), reads=["cob%d" % co])
        if nb:
            SA(0)
        for b in range(nb):
            if b + 1 < nb:
                SA(b + 1)
            SB(b)
        P.release()

    def row_tile(self, name, src_row):
        t = self.P.sb(name, [128, DM], F32)
        self.P.dma(lambda e: e.dma_start(out=t[:], in_=src_row.partition_broadcast(128)), reads=["modrows"], writes=[name])
        return t

    def ln_rows(self, xt, xres, outt, ores, tg):
        P = self.P
        st, mv, rs = self.lnr_bufs[tg]
        n = "lnr%d" % tg
        for hh in range(2):
            P.op("dve", lambda e, hh=hh: e.bn_stats(out=st[:, hh, :], in_=xt[:, hh * 512:(hh + 1) * 512]), reads=[xres], writes=[n + "st"])
        P.op("dve", lambda e: e.bn_aggr(out=mv[:], in_=st[:].rearrange("p a b -> p (a b)")), reads=[n + "st"], writes=[n + "mv"])
        eps_t = self.eps2
        P.op("act", lambda e: e.activation(out=rs[:], in_=mv[:, 1:2], func=AF.Sqrt, bias=eps_t[:, 0:1], scale=1.0), reads=[n + "mv", "eps2"], writes=[n + "rs"])
        P.op("dve", lambda e: e.reciprocal(out=rs[:], in_=rs[:]), reads=[n + "rs"], writes=[n + "rs"])
        P.op("dve", lambda e: e.tensor_scalar(out=outt[:], in0=xt[:], scalar1=mv[:, 0:1], scalar2=rs[:, 0:1], op0=ALU.subtract, op1=ALU.mult),
             reads=[xres, n + "mv", n + "rs"], writes=[ores])

    def alloc_lnr(self):
        P = self.P
        self.eps2 = e2 = P.sb("eps2", [128, 1], F32)
        P.op("pool", lambda e: e.memset(e2[:], EPS), writes=["eps2"])
        self.lnr_bufs = [(P.sb("lst%d" % i, [128, 2, 6], F32), P.sb("lmv%d" % i, [128, 2], F32), P.sb("lrs%d" % i, [128, 1], F32)) for i in range(2)]

    def phase_outproj(self, l, var):
        P = self.P
        P.mark()
        N = LC if var else T
        catT = self.catcT_d if var else self.catT_d
        xsrc = self.xc_src if var else self.x_src
        xmid_d = self.xcmid_d if var else self.xmid_d
        h2_d = self.h2c_d if var else self.h2_d
        logits = self.logits_c if var else self.logits_x
        mr = self.modrows[l, var]
        g1 = self.row_tile("g1", mr[2 * DM:3 * DM])
        sc2 = self.row_tile("sc2", mr[4 * DM:5 * DM])
        sh2 = self.row_tile("sh2", mr[3 * DM:4 * DM])
        bo = self.row_tile("bo", self.b_out[l])
        lg = self.row_tile("lg", self.ln1_g[l])
        lb = self.row_tile("lb", self.ln1_b[l])
        P.op("pool", lambda e: e.tensor_tensor(out=bo[:], in0=bo[:], in1=g1[:], op=ALU.mult), reads=["bo", "g1"], writes=["bo"])
        P.op("pool", lambda e: e.tensor_scalar_add(out=sc2[:], in0=sc2[:], scalar1=1.0), reads=["sc2"], writes=["sc2"])
        wo = P.sb("wo", [128, 8, DM], BF16)
        for k0 in range(0, 8, 2):
            P.dma(lambda e, k0=k0: e.dma_start(out=wo[:, k0:k0 + 2, :], in_=self.w_out[l][k0 * 128:(k0 + 2) * 128, :].rearrange("(k p) n -> p k n", p=128)), writes=["wo"], eng="pool")
        for k in range(8):
            eng = ("dve", "pool")[k % 2]
            P.op(eng, lambda e, k=k: e.tensor_tensor(out=wo[:, k, :], in0=wo[:, k, :], in1=g1[:], op=ALU.mult), reads=["wo", "g1"], writes=["wo"])
        wr = P.sb("wr", [128, 8, NEXP], BF16)
        P.dma(lambda e: e.dma_start(out=wr[:], in_=self.w_router[l].rearrange("(k p) n -> p k n", p=128)), writes=["wr"], eng="pool")
        eps_t = P.sb("oeps", [128, 1], F32)
        P.op("pool", lambda e: e.memset(eps_t[:], EPS), writes=["oeps"])
        BW = min(512, N)
        tpb = BW // 128
        D = 6
        cbs = [P.sb("catb%d" % i, [128, 8, BW], BF16) for i in range(3)]
        xts = [P.sb("oxt%d" % i, [128, DM], F32) for i in range(D)]
        rts = [P.sb("ort%d" % i, [128, DM], F32) for i in range(D)]
        xms = [P.sb("oxm%d" % i, [128, DM], F32) for i in range(D)]
        hfs = [P.sb("ohf%d" % i, [128, DM], F32) for i in range(D)]
        hbs = [P.sb("ohb%d" % i, [128, DM], BF16) for i in range(D)]
        sts = [P.sb("ost%d" % i, [128, 2, 2, 6], F32) for i in range(D)]
        mvs = [P.sb("omv%d" % i, [128, 2, 4], F32) for i in range(D)]
        h2Ts = [P.sb("h2T%d" % i, [128, 8, 128], BF16) for i in range(2)]
        nt_ = N // 128
        if not var:
            nt_ = int(os.environ.get("DBG_NBLK", nt_ // tpb)) * tpb

        def ln_stats(src, sres, d, w):
            st, mv = sts[d], mvs[d]
            sn, mn = "ost%d_%d" % (d, w), "omv%d_%d" % (d, w)
            for hh in range(2):
                P.op("dve", lambda e, hh=hh: e.bn_stats(out=st[:, w, hh, :], in_=src[:, hh * 512:(hh + 1) * 512]), reads=[sres], writes=[sn])
            P.op("dve", lambda e: e.bn_aggr(out=mv[:, w, 0:2], in_=st[:, w, :, :].rearrange("p a b -> p (a b)")), reads=[sn], writes=[mn])
            P.op("act", lambda e: e.activation(out=mv[:, w, 2:3], in_=mv[:, w, 1:2], func=AF.Sqrt, bias=eps_t[:, 0:1], scale=1.0), reads=[mn, "oeps"], writes=[mn])
            P.op("dve", lambda e: e.reciprocal(out=mv[:, w, 2:3], in_=mv[:, w, 2:3]), reads=[mn], writes=[mn])
            P.op("dve", lambda e: e.scalar_tensor_tensor(out=mv[:, w, 3:4], in0=mv[:, w, 0:1], scalar=-1.0, in1=mv[:, w, 2:3], op0=ALU.mult, op1=ALU.mult), reads=[mn], writes=[mn])
            return mn

        def S0(n):
            d = n % D
            if n % tpb == 0:
                b_ = n // tpb
                cb = cbs[b_ % 3]
                P.dma(lambda e: e.dma_start(out=cb[:], in_=catT[:, b_ * BW:(b_ + 1) * BW].rearrange("(k p) t -> p k t", p=128)), writes=["catb%d" % (b_ % 3)])
            xt = xts[d]
            P.dma(lambda e: e.dma_start(out=xt[:], in_=xsrc[n * 128:(n + 1) * 128, :]), writes=["oxt%d" % d])

        def S1(n):
            d = n % D
            b_, i = n // tpb, n % tpb
            cb, cn = cbs[b_ % 3], "catb%d" % (b_ % 3)
            xt = xts[d]
            for hf in range(2):
                pb = 2 * (n % 2) + hf
                for k in range(8):
                    self.mm(self.ps[pb][:, :], cb[:, k, i * 128:(i + 1) * 128], wo[:, k, hf * 512:(hf + 1) * 512], k == 0, k == 7, [cn, "wo"], [self.PS[pb]])
            P.op("act", lambda e: e.activation(out=xt[:], in_=xt[:], func=AF.Copy, scale=ALPHA), reads=["oxt%d" % d], writes=["oxt%d" % d])
            P.op("pool", lambda e: e.tensor_tensor(out=xt[:], in0=xt[:], in1=bo[:], op=ALU.add), reads=["oxt%d" % d, "bo"], writes=["oxt%d" % d])

        def S2(n):
            d = n % D
            xt, rt = xts[d], rts[d]
            for hf in range(2):
                pb = 2 * (n % 2) + hf
                P.op("dve", lambda e, hf=hf, pb=pb: e.tensor_tensor(out=rt[:, hf * 512:(hf + 1) * 512], in0=self.ps[pb][:, :], in1=xt[:, hf * 512:(hf + 1) * 512], op=ALU.add),
                     reads=[self.PS[pb], "oxt%d" % d], writes=["ort%d" % d])
            ln_stats(rt, "ort%d" % d, d, 0)

        def S3(n):
            d = n % D
            rt, xm, mv = rts[d], xms[d], mvs[d]
            P.op("act", lambda e: e.activation(out=xm[:], in_=rt[:], func=AF.Identity, scale=mv[:, 0, 2:3], bias=mv[:, 0, 3:4]), reads=["ort%d" % d, "omv%d_0" % d], writes=["oxm%d" % d])
            P.op("dve", lambda e: e.tensor_tensor(out=xm[:], in0=xm[:], in1=lg[:], op=ALU.mult), reads=["oxm%d" % d, "lg"], writes=["oxm%d" % d])
            P.op("pool", lambda e: e.tensor_tensor(out=xm[:], in0=xm[:], in1=lb[:], op=ALU.add), reads=["oxm%d" % d, "lb"], writes=["oxm%d" % d])
            P.dma(lambda e: e.dma_start(out=xmid_d[n * 128:(n + 1) * 128, :], in_=xm[:]), reads=["oxm%d" % d])

        def S4(n):
            d = n % D
            ln_stats(xms[d], "oxm%d" % d, d, 1)

        def S5(n):
            d = n % D
            xm, hf_, hb, mv = xms[d], hfs[d], hbs[d], mvs[d]
            P.op("act", lambda e: e.activation(out=hf_[:], in_=xm[:], func=AF.Identity, scale=mv[:, 1, 2:3], bias=mv[:, 1, 3:4]), reads=["oxm%d" % d, "omv%d_1" % d], writes=["ohf%d" % d])
            P.op("dve", lambda e: e.tensor_tensor(out=hf_[:], in0=hf_[:], in1=sc2[:], op=ALU.mult), reads=["ohf%d" % d, "sc2"], writes=["ohf%d" % d])
            P.op("pool", lambda e: e.tensor_tensor(out=hb[:], in0=hf_[:], in1=sh2[:], op=ALU.add), reads=["ohf%d" % d, "sh2"], writes=["ohb%d" % d])
            P.dma(lambda e: e.dma_start(out=h2_d[n * 128:(n + 1) * 128, :], in_=hb[:]), reads=["ohb%d" % d])

        def S6(n):
            d = n % D
            hb = hbs[d]
            j = n % 2
            h2T = h2Ts[j]
            psT = self.ps[4 + j][:, :].bitcast(BF16)
            for k in range(8):
                P.op("pe", lambda e, k=k: e.transpose(out=psT[:, k * 128:(k + 1) * 128], in_=hb[:, k * 128:(k + 1) * 128], identity=self.ident[:]),
                     reads=["ohb%d" % d, "ident"], writes=[self.PS[4 + j]])
            P.op("act", lambda e: e.activation(out=h2T[:], in_=psT.rearrange("p (k t) -> p k t", k=8), func=AF.Copy), reads=[self.PS[4 + j]], writes=["h2T%d" % j])
            for k in range(8):
                self.mm(self.ps[6 + j][:, 0:NEXP], h2T[:, k, :], wr[:, k, :], k == 0, k == 7, ["h2T%d" % j, "wr"], [self.PS[6 + j]])
            P.op("act", lambda e: e.activation(out=logits[:, n, :], in_=self.ps[6 + j][:, 0:NEXP], func=AF.Copy), reads=[self.PS[6 + j]], writes=["logits%d" % var])
        stages = [S0, S1, S2, S3, S4, S5, S6]
        for step in range(nt_ + len(stages) - 1):
            for si, st_ in enumerate(stages):
                n = step - si
                if 0 <= n < nt_:
                    st_(n)
        P.release()

    def phase_moe(self, l, do_ctx):
        P = self.P
        P.mark()
        NT = T // 128
        NTA = NT + 2
        CAPX, CAPC = 2 * T // NEXP, 2 * LC // NEXP
        nexp = int(os.environ.get("DBG_NEXP", NEXP))
        NS = NEXP * NT
        pos = P.sb("pos", [128, NEXP, NT], F32)
        k128 = P.sb("k128", [128, 9], F32)
        affh = P.sb("affh", [128, NTA, NEXP], BF16)
        affl = P.sb("affl", [128, NTA, NEXP], BF16)
        iq = P.sb("iq", [128, 128], F32)
        ip = P.sb("ip", [128, 1], F32)
        tix = P.sb("tix", [128, NT], BF16)
        posc = P.sb("posc", [128, NEXP, 2], F32)
        Bc = P.sb("Bc", [128, NEXP, 2], BF16)
        P.mark()
        aff = P.sb("aff", [128, NTA, NEXP], F32)
        P.op("dve", lambda e: e.tensor_copy(out=aff[:, 0:NT, :], in_=self.logits_x[:]), reads=["logits0"], writes=["aff"])
        P.op("dve", lambda e: e.tensor_copy(out=aff[:, NT:NTA, :], in_=self.logits_c[:]), reads=["logits1", "aff"], writes=["aff"])
        mx = P.sb("mx", [128, NTA], F32)
        P.op("dve", lambda e: e.tensor_reduce(out=mx[:], in_=aff[:], axis=AX.X, op=ALU.max), reads=["aff"], writes=["mx"])
        P.op("dve", lambda e: e.tensor_tensor(out=aff[:], in0=aff[:], in1=mx[:, :].unsqueeze(2).to_broadcast([128, NTA, NEXP]), op=ALU.subtract), reads=["aff", "mx"], writes=["aff"])
        P.op("act", lambda e: e.activation(out=aff[:], in_=aff[:], func=AF.Exp), reads=["aff"], writes=["aff"])
        P.op("dve", lambda e: e.tensor_reduce(out=mx[:], in_=aff[:], axis=AX.X, op=ALU.add), reads=["aff"], writes=["mx"])
        P.op("dve", lambda e: e.reciprocal(out=mx[:], in_=mx[:]), reads=["mx"], writes=["mx"])
        P.op("dve", lambda e: e.tensor_tensor(out=aff[:], in0=aff[:], in1=mx[:, :].unsqueeze(2).to_broadcast([128, NTA, NEXP]), op=ALU.mult), reads=["aff", "mx"], writes=["aff"])
        lo = P.sb("lo", [128, 2, NEXP], F32)
        hi = P.sb("hi", [128, 2, NEXP], F32)
        mid = P.sb("mid", [128, 2, NEXP], F32)
        capt = P.sb("capt", [128, 2, NEXP], F32)
        tmp = P.sb("btmp", [128, 2, NEXP], F32)
        pred = P.sb("pred", [128, 2, NEXP], F32)
        cnt = P.sb("cnt", [128, 2, NEXP], BF16)
        cntf = P.sb("cntf", [128, 2, NEXP], F32)
        cmp_ = P.sb("cmp", [128, NTA, NEXP], BF16)
        P.op("pool", lambda e: e.memset(lo[:], 0.0), writes=["lo"])
        P.op("pool", lambda e: e.memset(hi[:], 1.0), writes=["hi"])
        P.op("pool", lambda e: e.memset(mid[:], 0.5), writes=["mid"])
        P.op("pool", lambda e: e.memset(capt[:, 0, :], float(CAPX)), writes=["capt"])
        P.op("pool", lambda e: e.memset(capt[:, 1, :], float(CAPC)), reads=["capt"], writes=["capt"])
        groups = ((0, 0, NT), (1, NT, NTA))

        def compare(thr, thr_res):
            for (g, a, b) in groups:
                P.op("dve", lambda e, g=g, a=a, b=b: e.tensor_tensor(out=cmp_[:, a:b, :], in0=aff[:, a:b, :], in1=thr[:, g, :].unsqueeze(1).to_broadcast([128, b - a, NEXP]), op=ALU.is_ge),
                     reads=["aff", thr_res], writes=["cmp"])
        for it in range(int(os.environ.get("DBG_NBIS", 34))):
            compare(mid, "mid")
            for (g, a, b) in groups:
                P.op("dve", lambda e, g=g, a=a, b=b: e.tensor_reduce(out=cntf[:, g, :], in_=cmp_[:, a:b, :].rearrange("p t e -> p e t"), axis=AX.X, op=ALU.add),
                     reads=["cmp"], writes=["cntf"])
            P.op("dve", lambda e: e.tensor_copy(out=cnt[:], in_=cntf[:]), reads=["cntf"], writes=["cnt"])
            self.mm(self.ps[0][:, 0:2 * NEXP], self.ones_bf[:], cnt[:].rearrange("p g e -> p (g e)"), True, True, ["cnt", "ones"], [self.PS[0]])
            P.op("dve", lambda e: e.tensor_tensor(out=pred[:].rearrange("p g e -> p (g e)"), in0=self.ps[0][:, 0:2 * NEXP], in1=capt[:].rearrange("p g e -> p (g e)"), op=ALU.is_ge),
                 reads=[self.PS[0], "capt"], writes=["pred"])
            P.op("dve", lambda e: e.tensor_tensor(out=tmp[:], in0=pred[:], in1=mid[:], op=ALU.mult), reads=["pred", "mid"], writes=["btmp"])
            P.op("dve", lambda e: e.tensor_tensor(out=lo[:], in0=lo[:], in1=tmp[:], op=ALU.max), reads=["lo", "btmp"], writes=["lo"])
            P.op("dve", lambda e: e.scalar_tensor_tensor(out=tmp[:], in0=pred[:], scalar=4.0, in1=mid[:], op0=ALU.mult, op1=ALU.add), reads=["pred", "mid", "lo"], writes=["btmp"])
            P.op("dve", lambda e: e.tensor_tensor(out=hi[:], in0=hi[:], in1=tmp[:], op=ALU.min), reads=["hi", "btmp"], writes=["hi"])
            P.op("dve", lambda e: e.tensor_tensor(out=tmp[:], in0=lo[:], in1=hi[:], op=ALU.add), reads=["lo", "hi"], writes=["btmp"])
            P.op("dve", lambda e: e.tensor_scalar(out=mid[:], in0=tmp[:], scalar1=0.5, scalar2=None, op0=ALU.mult), reads=["btmp"], writes=["mid"])
        compare(lo, "lo")
        NS = NEXP * NT
        mA = P.sb("mA", [128, NEXP, NT], F32)
        mB = P.sb("mB", [128, NEXP, NT], F32)
        msk = P.sb("msk", [128, NEXP, NT], F32)
        P.op("dve", lambda e: e.tensor_copy(out=msk[:], in_=cmp_[:, 0:NT, :].rearrange("p t e -> p e t")), reads=["cmp"], writes=["msk"])
        P.op("pool", lambda e: e.tensor_copy(out=mA[:], in_=msk[:]), reads=["msk"], writes=["mA"])
        cur, nxt, cn, nn = mA, mB, "mA", "mB"
        sft = 1
        while sft < NT:
            P.op("dve", lambda e, cur=cur, nxt=nxt, sft=sft: e.tensor_tensor(out=nxt[:, :, sft:], in0=cur[:, :, sft:], in1=cur[:, :, 0:NT - sft], op=ALU.add), reads=[cn], writes=[nn])
            P.op("pool", lambda e, cur=cur, nxt=nxt, sft=sft: e.tensor_copy(out=nxt[:, :, 0:sft], in_=cur[:, :, 0:sft]), reads=[cn], writes=[nn])
            cur, nxt, cn, nn = nxt, cur, nn, cn
            sft *= 2
        inc, incn = cur, cn
        rc = P.sb("rc", [128, NEXP], BF16)
        P.op("dve", lambda e: e.tensor_copy(out=rc[:], in_=inc[:, :, NT - 1]), reads=[incn], writes=["rc"])
        tri = P.sb("tri", [128, 128], BF16)
        trif = P.sb("trif", [128, 128], F32)
        P.op("pool", lambda e: e.memset(trif[:], 1.0), writes=["trif"])
        P.op("pool", lambda e: e.affine_select(out=trif[:], in_=trif[:], pattern=[[1, 128]], compare_op=ALU.is_gt, fill=0.0, base=0, channel_multiplier=-1), reads=["trif"], writes=["trif"])
        P.op("dve", lambda e: e.tensor_copy(out=tri[:], in_=trif[:]), reads=["trif"], writes=["tri"])
        self.mm(self.ps[1][:, 0:NEXP], tri[:], rc[:], True, True, ["tri", "rc"], [self.PS[1]])
        P.op("dve", lambda e: e.tensor_tensor(out=pos[:], in0=inc[:], in1=msk[:], op=ALU.subtract), reads=[incn, "msk"], writes=["pos"])
        rb = P.sb("rb", [128, NEXP], F32)
        P.op("dve", lambda e: e.tensor_copy(out=rb[:], in_=self.ps[1][:, 0:NEXP]), reads=[self.PS[1]], writes=["rb"])
        P.op("dve", lambda e: e.tensor_tensor(out=pos[:], in0=pos[:], in1=rb[:, :].unsqueeze(2).to_broadcast([128, NEXP, NT]), op=ALU.add), reads=["pos", "rb"], writes=["pos"])
        P.op("dve", lambda e: e.scalar_tensor_tensor(out=pos[:], in0=msk[:], scalar=-8192.0, in1=pos[:], op0=ALU.mult, op1=ALU.add), reads=["pos", "msk"], writes=["pos"])
        P.op("dve", lambda e: e.tensor_scalar_add(out=pos[:], in0=pos[:], scalar1=8192.0), reads=["pos"], writes=["pos"])
        P.op("pool", lambda e: e.iota(k128[:], pattern=[[128, 9]], base=0, channel_multiplier=0, allow_small_or_imprecise_dtypes=True), writes=["k128"])
        afft = P.sb("afft", [128, NTA, NEXP], F32)
        P.op("dve", lambda e: e.tensor_copy(out=affh[:], in_=aff[:]), reads=["aff"], writes=["affh"])
        P.op("dve", lambda e: e.tensor_tensor(out=afft[:], in0=aff[:], in1=affh[:], op=ALU.subtract), reads=["aff", "affh"], writes=["afft"])
        P.op("dve", lambda e: e.tensor_copy(out=affl[:], in_=afft[:]), reads=["afft"], writes=["affl"])
        P.op("pool", lambda e: e.iota(iq[:], pattern=[[1, 128]], base=0, channel_multiplier=0, allow_small_or_imprecise_dtypes=True), writes=["iq"])
        P.op("pool", lambda e: e.iota(ip[:], pattern=[[1, 1]], base=0, channel_multiplier=1, allow_small_or_imprecise_dtypes=True), writes=["ip"])
        P.op("pool", lambda e: e.iota(tix[:], pattern=[[1, NT]], base=0, channel_multiplier=0, allow_small_or_imprecise_dtypes=True), writes=["tix"])
        if do_ctx:
            mc = P.sb("mc", [128, NEXP, 2], F32)
            P.op("dve", lambda e: e.tensor_copy(out=mc[:], in_=cmp_[:, NT:NTA, :].rearrange("p t e -> p e t")), reads=["cmp"], writes=["mc"])
            rcc = P.sb("rcc", [128, NEXP], BF16)
            P.op("dve", lambda e: e.tensor_tensor(out=rcc[:], in0=mc[:, :, 0], in1=mc[:, :, 1], op=ALU.add), reads=["mc"], writes=["rcc"])
            self.mm(self.ps[1][:, NEXP:2 * NEXP], tri[:], rcc[:], True, True, ["tri", "rcc"], [self.PS[1]])
            P.op("dve", lambda e: e.tensor_copy(out=posc[:, :, 0], in_=self.ps[1][:, NEXP:2 * NEXP]), reads=[self.PS[1]], writes=["posc"])
            P.op("dve", lambda e: e.tensor_tensor(out=posc[:, :, 1], in0=posc[:, :, 0], in1=mc[:, :, 0], op=ALU.add), reads=["posc", "mc"], writes=["posc"])
            P.op("dve", lambda e: e.scalar_tensor_tensor(out=posc[:], in0=mc[:], scalar=-8192.0, in1=posc[:], op0=ALU.mult, op1=ALU.add), reads=["posc", "mc"], writes=["posc"])
            P.op("dve", lambda e: e.tensor_scalar_add(out=posc[:], in0=posc[:], scalar1=8192.0), reads=["posc"], writes=["posc"])
            P.op("dve", lambda e: e.tensor_single_scalar(out=Bc[:], in_=posc[:], scalar=float(CAPC), op=ALU.is_lt), reads=["posc"], writes=["Bc"])
        P.release()
        P.mark()
        zt = P.sb("zt", [128, 1024], F32)
        P.op("pool", lambda e: e.memset(zt[:], 0.0), writes=["zt"])
        for i in range(T // 128):
            P.dma(lambda e, i=i: e.dma_start(out=self.moe_d[i * 128:(i + 1) * 128, :], in_=zt[:]), reads=["zt"], writes=["moe_z%d" % i])
        if do_ctx:
            for i in range(LC // 128):
                P.dma(lambda e, i=i: e.dma_start(out=self.moec_d[i * 128:(i + 1) * 128, :], in_=zt[:]), reads=["zt"], writes=["moe_zc%d" % i])
        P.release()
        xss = [P.sb("xs%d" % i, [128, 9, DM], BF16) for i in range(2)]
        ge9e = P.sb("ge9e", [128, NT, 9], BF16)
        B8e = P.sb("B8e", [128, NT, 8], BF16)
        hi8e = P.sb("hi8e", [128, NT], F32)
        lo7e = P.sb("lo7e", [128, NT], F32)
        xsT = P.sb("xsT", [128, 8, 1056], BF16)
        hidT = P.sb("hidT", [128, 16, 1056], BF16)
        wds = [P.sb("wd%d" % i, [128, 16, DM], BF16) for i in range(2)]
        wgs = [P.sb("wg%d" % i, [128, 8, 256], BF16) for i in range(2)]
        wus = [P.sb("wu%d" % i, [128, 8, 256], BF16) for i in range(2)]
        Ats = [P.sb("At%d" % i, [128, 128], BF16) for i in range(2)]
        Abs_ = [P.sb("Ab0", [128, 16, 128], BF16)] * 2
        Rs = [P.sb("R%d" % i, [128, NT, 32], BF16) for i in range(2)]
        Rc = P.sb("Rc", [128, 2, 4], BF16)
        idxf = P.sb("idxf", [128, 9], F32)
        idxi = [P.sb("idxi%d" % i, [128, 9], I32) for i in range(2)]
        gts = [P.sb("gts%d" % i, [128, 9], F32) for i in range(2)]
        sg = P.sb("sgm", [128, 1056], F32)
        ysts = [P.sb("yst%d" % i, [128, DM], F32) for i in range(2)]
        NSL = 1056 if do_ctx else 1024
        cwc = [0]
        prev_sc = ["moe_z"]
        def partA(ex):
            pe_ = ex % 2
            R, Rn = Rs[pe_], "R%d" % pe_
            ii, iin = idxi[pe_], "idxi%d" % pe_
            gt, gtn = gts[pe_], "gts%d" % pe_
            xs, xsn = xss[pe_], "xs%d" % pe_
            b8 = B8e[:]
            pe3 = pos[:, ex, :]
            P.op("dve", lambda e, pe3=pe3: e.tensor_tensor(out=ge9e[:], in0=pe3.unsqueeze(2).to_broadcast([128, NT, 9]), in1=k128[:, :].unsqueeze(1).to_broadcast([128, NT, 9]), op=ALU.is_ge), reads=["pos", "k128"], writes=["ge9e"])
            P.op("dve", lambda e: e.tensor_tensor(out=B8e[:], in0=ge9e[:, :, 0:8], in1=ge9e[:, :, 1:9], op=ALU.subtract), reads=["ge9e"], writes=["B8"])
            P.op("dve", lambda e: e.tensor_reduce(out=hi8e[:], in_=ge9e[:, :, 1:9], axis=AX.X, op=ALU.add), reads=["ge9e"], writes=["hi8e"])
            P.op("dve", lambda e, pe3=pe3: e.scalar_tensor_tensor(out=lo7e[:], in0=hi8e[:], scalar=-128.0, in1=pe3, op0=ALU.mult, op1=ALU.add), reads=["hi8e", "pos"], writes=["lo7"])
            P.op("dve", lambda e, R=R, b8=b8: e.tensor_tensor(out=R[:, :, 0:8], in0=b8, in1=tix[:, :].unsqueeze(2).to_broadcast([128, NT, 8]), op=ALU.mult), reads=["B8", "tix"], writes=[Rn])
            P.op("dve", lambda e, R=R, b8=b8: e.tensor_scalar(out=R[:, :, 8:16], in0=b8, scalar1=ip[:, 0:1], scalar2=None, op0=ALU.mult), reads=["B8", "ip"], writes=[Rn])
            P.op("dve", lambda e, R=R, b8=b8, ex=ex: e.tensor_tensor(out=R[:, :, 16:24], in0=b8, in1=affh[:, 0:NT, ex].unsqueeze(2).to_broadcast([128, NT, 8]), op=ALU.mult), reads=["B8", "affh"], writes=[Rn])
            P.op("dve", lambda e, R=R, b8=b8, ex=ex: e.tensor_tensor(out=R[:, :, 24:32], in0=b8, in1=affl[:, 0:NT, ex].unsqueeze(2).to_broadcast([128, NT, 8]), op=ALU.mult), reads=["B8", "affl"], writes=[Rn])
            for t0 in range(0, NT, 16):
                Ab, Abn = Abs_[0], "Ab0"
                P.op("dve", lambda e, Ab=Ab, t0=t0: e.tensor_tensor(out=Ab[:], in0=iq[:, :].unsqueeze(1).to_broadcast([128, 16, 128]),
                                                                   in1=lo7e[:, t0:t0 + 16].unsqueeze(2).to_broadcast([128, 16, 128]), op=ALU.is_equal), reads=["iq", "lo7"], writes=[Abn])
                for t in range(t0, t0 + 16):
                    self.mm(self.ps[2][:, 0:32], Ab[:, t - t0, :], R[:, t, :], t == 0, t == NT - 1, [Abn, Rn], [self.PS[2]])
            if do_ctx:
                P.op("dve", lambda e, ex=ex: e.tensor_scalar(out=Rc[:, :, 0], in0=Bc[:, ex, :], scalar1=float(NT), scalar2=None, op0=ALU.mult) if False else
                     e.tensor_copy(out=Rc[:, :, 0], in_=Bc[:, ex, :]), reads=["Bc"], writes=["Rc"])
                P.op("dve", lambda e, ex=ex: e.tensor_scalar(out=Rc[:, :, 1], in0=Bc[:, ex, :], scalar1=ip[:, 0:1], scalar2=None, op0=ALU.mult), reads=["Bc", "ip", "Rc"], writes=["Rc"])
                P.op("dve", lambda e, ex=ex: e.tensor_tensor(out=Rc[:, :, 2], in0=Bc[:, ex, :], in1=affh[:, NT:NTA, ex], op=ALU.mult), reads=["Bc", "affh", "Rc"], writes=["Rc"])
                P.op("dve", lambda e, ex=ex: e.tensor_tensor(out=Rc[:, :, 3], in0=Bc[:, ex, :], in1=affl[:, NT:NTA, ex], op=ALU.mult), reads=["Bc", "affl", "Rc"], writes=["Rc"])
                P.op("pool", lambda e: e.memset(Rc[:, 0, 0:1], 0.0), reads=["Rc"], writes=["Rc"])
                for t in range(2):
                    At, An = Ats[t % 2], "At%d" % (t % 2)
                    P.op("dve", lambda e, At=At, ex=ex, t=t: e.tensor_scalar(out=At[:], in0=iq[:], scalar1=posc[:, ex, t:t + 1], scalar2=None, op0=ALU.is_equal), reads=["iq", "posc"], writes=[An])
                    self.mm(self.ps[2][:, 32:36], At[:], Rc[:, t, :], t == 0, t == 1, [An, "Rc"], [self.PS[2]])
            P.op("dve", lambda e: e.scalar_tensor_tensor(out=idxf[:, 0:8], in0=self.ps[2][:, 0:8], scalar=128.0, in1=self.ps[2][:, 8:16], op0=ALU.mult, op1=ALU.add) if False else
                 e.tensor_copy(out=idxf[:, 0:8], in_=self.ps[2][:, 8:16]), reads=[self.PS[2]], writes=["idxf"])
            P.op("dve", lambda e: e.scalar_tensor_tensor(out=idxf[:, 0:8], in0=self.ps[2][:, 0:8], scalar=128.0, in1=idxf[:, 0:8], op0=ALU.mult, op1=ALU.add), reads=[self.PS[2], "idxf"], writes=["idxf"])
            P.op("dve", lambda e, gt=gt: e.tensor_copy(out=gt[:, 0:8], in_=self.ps[2][:, 24:32]), reads=[self.PS[2]], writes=[gtn])
            P.op("dve", lambda e, gt=gt: e.tensor_tensor(out=gt[:, 0:8], in0=self.ps[2][:, 16:24], in1=gt[:, 0:8], op=ALU.add), reads=[self.PS[2], gtn], writes=[gtn])
            if do_ctx:
                P.op("dve", lambda e: e.tensor_copy(out=idxf[:, 8:9], in_=self.ps[2][:, 33:34]), reads=[self.PS[2], "idxf"], writes=["idxf"])
                P.op("dve", lambda e: e.scalar_tensor_tensor(out=idxf[:, 8:9], in0=self.ps[2][:, 32:33], scalar=128.0, in1=idxf[:, 8:9], op0=ALU.mult, op1=ALU.add), reads=[self.PS[2], "idxf"], writes=["idxf"])
                P.op("dve", lambda e, gt=gt: e.tensor_copy(out=gt[:, 8:9], in_=self.ps[2][:, 35:36]), reads=[self.PS[2], gtn], writes=[gtn])
                P.op("dve", lambda e, gt=gt: e.tensor_tensor(out=gt[:, 8:9], in0=self.ps[2][:, 34:35], in1=gt[:, 8:9], op=ALU.add), reads=[self.PS[2], gtn], writes=[gtn])
            P.op("dve", lambda e, ii=ii: e.tensor_copy(out=ii[:], in_=idxf[:]), reads=["idxf"], writes=[iin])
            for k in range(8):
                P.dma(lambda e, k=k, ii=ii: e.indirect_dma_start(out=xs[:, k, :], out_offset=None, in_=self.h2_d[:, :],
                                                                in_offset=bass.IndirectOffsetOnAxis(ap=ii[:, k:k + 1], axis=0)),
                      reads=[iin, "h2"], writes=[xsn], eng="pool")
            if do_ctx:
                P.dma(lambda e, ii=ii: e.indirect_dma_start(out=xs[0:CAPC, 8, :], out_offset=None, in_=self.h2c_d[:, :],
                                                           in_offset=bass.IndirectOffsetOnAxis(ap=ii[0:CAPC, 8:9], axis=0)),
                      reads=[iin, "h2"], writes=[xsn], eng="pool")

        def partB(ex):
            pe_ = ex % 2
            ii, iin = idxi[pe_], "idxi%d" % pe_
            gt, gtn = gts[pe_], "gts%d" % pe_
            xs, xsn = xss[pe_], "xs%d" % pe_
            nst = 9 if do_ctx else 8
            for j in range(nst):
                rows = 128 if j < 8 else CAPC
                pb = 6 + j % 2
                psT = self.ps[pb][:, :].bitcast(BF16)
                for dk in range(8):
                    P.op("pe", lambda e, j=j, dk=dk, psT=psT, rows=rows: e.transpose(out=psT[:, dk * 128:dk * 128 + rows], in_=xs[0:rows, j, dk * 128:(dk + 1) * 128], identity=self.ident[0:rows, 0:rows]),
                         reads=[xsn, "ident"], writes=[self.PS[pb]])
                P.op("act", lambda e, j=j, psT=psT, rows=rows: e.activation(out=xsT[:, :, j * 128:j * 128 + rows], in_=psT.rearrange("p (k t) -> p k t", k=8)[:, :, 0:rows], func=AF.Copy),
                     reads=[self.PS[pb]], writes=["xsT"])

        def partC(ex):
            pe_ = ex % 2
            ii, iin = idxi[pe_], "idxi%d" % pe_
            gt, gtn = gts[pe_], "gts%d" % pe_
            xs, xsn = xss[pe_], "xs%d" % pe_
            nst = 9 if do_ctx else 8
            wd_unused = None
            wd, wdn = wds[pe_], "wd%d" % pe_
            for f0 in range(0, 16, 4):
                P.dma(lambda e, f0=f0, wd=wd, ex=ex: e.dma_start(out=wd[:, f0:f0 + 4, :], in_=self.w_down[l, ex, f0 * 128:(f0 + 4) * 128, :].rearrange("(k p) n -> p k n", p=128)), writes=[wdn], eng="pool")
            segs = [(0, 512), (512, 512)] + ([(1024, CAPC)] if do_ctx else [])
            for fc in range(8):
                wg, wgn = wgs[cwc[0] % 2], "wg%d" % (cwc[0] % 2)
                wu, wun = wus[cwc[0] % 2], "wu%d" % (cwc[0] % 2)
                cwc[0] += 1
                P.dma(lambda e, wg=wg, ex=ex, fc=fc: e.dma_start(out=wg[:], in_=self.w_gate[l, ex, :, fc * 256:(fc + 1) * 256].rearrange("(k p) n -> p k n", p=128)), writes=[wgn], eng="pool")
                P.dma(lambda e, wu=wu, ex=ex, fc=fc: e.dma_start(out=wu[:], in_=self.w_up[l, ex, :, fc * 256:(fc + 1) * 256].rearrange("(k p) n -> p k n", p=128)), writes=[wun], eng="pool")
                for fi in range(2):
                    ft = fc * 2 + fi
                    for si, (s0, sn_) in enumerate(segs):
                        for k in range(8):
                            self.mm(self.ps[si][:, 0:sn_], wg[:, k, fi * 128:(fi + 1) * 128], xsT[:, k, s0:s0 + sn_], k == 0, k == 7, [wgn, "xsT"], [self.PS[si]])
                        for k in range(8):
                            self.mm(self.ps[3 + si][:, 0:sn_], wu[:, k, fi * 128:(fi + 1) * 128], xsT[:, k, s0:s0 + sn_], k == 0, k == 7, [wun, "xsT"], [self.PS[3 + si]])
                    for si, (s0, sn_) in enumerate(segs):
                        P.op("act", lambda e, si=si, s0=s0, sn_=sn_: e.activation(out=sg[:, s0:s0 + sn_], in_=self.ps[si][:, 0:sn_], func=AF.Silu), reads=[self.PS[si]], writes=["sgm%d" % si])
                        P.op("dve", lambda e, si=si, s0=s0, sn_=sn_, ft=ft: e.tensor_tensor(out=hidT[:, ft, s0:s0 + sn_], in0=self.ps[3 + si][:, 0:sn_], in1=sg[:, s0:s0 + sn_], op=ALU.mult),
                             reads=[self.PS[3 + si], "sgm%d" % si], writes=["hidT"])
            cur_sc = []
            for j in range(nst):
                rows = 128 if j < 8 else CAPC
                ys, ysn = ysts[j % 2], "yst%d" % (j % 2)
                for hf in range(2):
                    pb = 6 + hf
                    for ft in range(16):
                        self.mm(self.ps[pb][0:rows, :], hidT[:, ft, j * 128:j * 128 + rows], wd[:, ft, hf * 512:(hf + 1) * 512], ft == 0, ft == 15, ["hidT", wdn], [self.PS[pb]])
                    P.op("act", lambda e, ys=ys, hf=hf, pb=pb, rows=rows, gt=gt, j=j: e.activation(out=ys[0:rows, hf * 512:(hf + 1) * 512], in_=self.ps[pb][0:rows, :], func=AF.Copy, scale=gt[0:rows, j:j + 1]),
                         reads=[self.PS[pb], gtn], writes=[ysn])
                dst = self.moe_d if j < 8 else self.moec_d
                scn = "sc%d_%d" % (ex, j)
                P.dma(lambda e, ys=ys, rows=rows, ii=ii, j=j, dst=dst: e.indirect_dma_start(out=dst[:, :], out_offset=bass.IndirectOffsetOnAxis(ap=ii[0:rows, j:j + 1], axis=0),
                                                                                           in_=ys[0:rows, :], in_offset=None, compute_op=ALU.add),
                      reads=[ysn, iin] + list(prev_sc), writes=[scn], eng="pool")
                cur_sc.append(scn)
            prev_sc[:] = cur_sc

        partA(0)
        for ex in range(nexp):
            partB(ex)
            if ex + 1 < nexp:
                partA(ex + 1)
            partC(ex)
        P.release()

    def phase_ln2(self, l, var, dst):
        P = self.P
        P.mark()
        N = LC if var else T
        xmid_d = self.xcmid_d if var else self.xmid_d
        moe_d = self.moec_d if var else self.moe_d
        mr = self.modrows[l, var]
        g2 = self.row_tile("g2", mr[5 * DM:6 * DM])
        lg = self.row_tile("lg2", self.ln2_g[l])
        lb = self.row_tile("lb2", self.ln2_b[l])
        eps_t = P.sb("feps", [128, 1], F32)
        P.op("pool", lambda e: e.memset(eps_t[:], EPS), writes=["feps"])
        D = 5
        xts = [P.sb("fxt%d" % i, [128, DM], F32) for i in range(D)]
        mts = [P.sb("fmt%d" % i, [128, DM], F32) for i in range(D)]
        ots = [P.sb("fot%d" % i, [128, DM], F32) for i in range(D)]
        sts = [P.sb("fst%d" % i, [128, 2, 6], F32) for i in range(D)]
        mvs = [P.sb("fmv%d" % i, [128, 4], F32) for i in range(D)]
        nt = N // 128
        if not var:
            nt = int(os.environ.get("DBG_NBLK", nt // 4)) * 4

        def S0(n):
            d = n % D
            xt, mt = xts[d], mts[d]
            P.dma(lambda e: e.dma_start(out=xt[:], in_=xmid_d[n * 128:(n + 1) * 128, :]), writes=["fxt%d" % d])
            P.dma(lambda e: e.dma_start(out=mt[:], in_=moe_d[n * 128:(n + 1) * 128, :]), writes=["fmt%d" % d], eng="act")

        def S1(n):
            d = n % D
            xt, mt, st, mv = xts[d], mts[d], sts[d], mvs[d]
            P.op("pool", lambda e: e.tensor_tensor(out=mt[:], in0=mt[:], in1=g2[:], op=ALU.mult), reads=["fmt%d" % d, "g2"], writes=["fmt%d" % d])
            P.op("act", lambda e: e.activation(out=xt[:], in_=xt[:], func=AF.Copy, scale=ALPHA), reads=["fxt%d" % d], writes=["fxt%d" % d])
            P.op("dve", lambda e: e.tensor_tensor(out=mt[:], in0=mt[:], in1=xt[:], op=ALU.add), reads=["fmt%d" % d, "fxt%d" % d], writes=["fmt%d" % d])
            sn, mn = "fst%d" % d, "fmv%d" % d
            for hh in range(2):
                P.op("dve", lambda e, hh=hh: e.bn_stats(out=st[:, hh, :], in_=mt[:, hh * 512:(hh + 1) * 512]), reads=["fmt%d" % d], writes=[sn])
            P.op("dve", lambda e: e.bn_aggr(out=mv[:, 0:2], in_=st[:].rearrange("p a b -> p (a b)")), reads=[sn], writes=[mn])
            P.op("act", lambda e: e.activation(out=mv[:, 2:3], in_=mv[:, 1:2], func=AF.Sqrt, bias=eps_t[:, 0:1], scale=1.0), reads=[mn, "feps"], writes=[mn])
            P.op("dve", lambda e: e.reciprocal(out=mv[:, 2:3], in_=mv[:, 2:3]), reads=[mn], writes=[mn])
            P.op("dve", lambda e: e.scalar_tensor_tensor(out=mv[:, 3:4], in0=mv[:, 0:1], scalar=-1.0, in1=mv[:, 2:3], op0=ALU.mult, op1=ALU.mult), reads=[mn], writes=[mn])

        def S2(n):
            d = n % D
            mt, ot, mv = mts[d], ots[d], mvs[d]
            P.op("act", lambda e: e.activation(out=ot[:], in_=mt[:], func=AF.Identity, scale=mv[:, 2:3], bias=mv[:, 3:4]), reads=["fmt%d" % d, "fmv%d" % d], writes=["fot%d" % d])
            P.op("dve", lambda e: e.tensor_tensor(out=ot[:], in0=ot[:], in1=lg[:], op=ALU.mult), reads=["fot%d" % d, "lg2"], writes=["fot%d" % d])
            P.op("pool", lambda e: e.tensor_tensor(out=ot[:], in0=ot[:], in1=lb[:], op=ALU.add), reads=["fot%d" % d, "lb2"], writes=["fot%d" % d])
            P.dma(lambda e: e.dma_start(out=dst[n * 128:(n + 1) * 128, :], in_=ot[:]), reads=["fot%d" % d])
        stages = [S0, S1, S2]
        for step in range(nt + len(stages) - 1):
            for si, st_ in enumerate(stages):
                n = step - si
                if 0 <= n < nt:
                    st_(n)
        P.release()

    def s5_consts(self):
        P = self.P
        sel = self.sel
        P.op("pool", lambda e: e.memset(sel[:], 0.0), writes=["sel"])
        for a in range(8):
            for b in range(8):
                eng = ("dve", "pool", "act")[(a * 8 + b) % 3]
                if eng == "act":
                    P.op("act", lambda e, a=a, b=b: e.activation(out=sel[:, a, b, 16 * b:16 * b + 16], in_=self.identf[:, 16 * a:16 * a + 16], func=AF.Copy), reads=["identf", "sel"], writes=["sel"])
                else:
                    P.op(eng, lambda e, a=a, b=b: e.tensor_copy(out=sel[:, a, b, 16 * b:16 * b + 16], in_=self.identf[:, 16 * a:16 * a + 16]), reads=["identf", "sel"], writes=["sel"])
        qi = P.sb("qi", [128, 1], I32)
        P.op("pool", lambda e: e.iota(qi[:], pattern=[[1, 1]], base=0, channel_multiplier=1), writes=["qi"])
        P.op("dve", lambda e: e.tensor_single_scalar(out=qi[:], in_=qi[:], scalar=4, op=ALU.arith_shift_right), reads=["qi"], writes=["qi"])
        qf = P.sb("qf", [128, 1], F32)
        P.op("dve", lambda e: e.tensor_copy(out=qf[:], in_=qi[:]), reads=["qi"], writes=["qf"])
        iv = P.sb("iv", [128, 8, 16], F32)
        P.op("pool", lambda e: e.iota(iv[:], pattern=[[1, 8], [0, 16]], base=0, channel_multiplier=0, allow_small_or_imprecise_dtypes=True), writes=["iv"])
        self.mskf = P.sb("mskf", [128, 128], F32)
        self.mskb = P.sb("mskb", [128, 128], F32)
        P.op("dve", lambda e: e.tensor_scalar(out=self.mskf[:], in0=iv[:].rearrange("p a b -> p (a b)"), scalar1=qf[:, 0:1], scalar2=None, op0=ALU.is_ge), reads=["iv", "qf"], writes=["mskf"])
        P.op("dve", lambda e: e.tensor_scalar(out=self.mskb[:], in0=iv[:].rearrange("p a b -> p (a b)"), scalar1=qf[:, 0:1], scalar2=None, op0=ALU.is_le), reads=["iv", "qf"], writes=["mskb"])

    def s5_setup(self, l):
        P = self.P
        S = ["S5S"]

        def so(eng, fn):
            P.op(eng, fn, reads=S, writes=S)

        def tt(o, a, b, op):
            so("dve", lambda e: e.tensor_tensor(out=o, in0=a, in1=b, op=op))

        def ts(o, a, c1, op0):
            so("dve", lambda e: e.tensor_scalar(out=o, in0=a, scalar1=c1, scalar2=None, op0=op0))

        def cp(o, a):
            so("dve", lambda e: e.tensor_copy(out=o, in_=a))

        def cmul(outr, outi, ar, ai, br, bi, t1, t2):
            tt(t1, ar, br, ALU.mult)
            tt(t2, ai, bi, ALU.mult)
            tt(outr, t1, t2, ALU.subtract)
            tt(t1, ar, bi, ALU.mult)
            tt(t2, ai, br, ALU.mult)
            tt(outi, t1, t2, ALU.add)
        sp = P.sb("s5par", [128, 1072], F32)
        P.dma(lambda e: e.dma_start(out=sp[:], in_=self.s5nat[l]), writes=S)
        lre, lim, ldt = sp[:, 0:16], sp[:, 16:32], sp[:, 32:48]
        Bre = sp[:, 48:304].rearrange("p (s c) -> p s c", s=16)
        Bim = sp[:, 304:560].rearrange("p (s c) -> p s c", s=16)
        Cre = sp[:, 560:816].rearrange("p (s c) -> p s c", s=16)
        Cim = sp[:, 816:1072].rearrange("p (s c) -> p s c", s=16)
        sm = P.sb("s5sm", [128, 16, 16], F32)
        R = lambda i: sm[:, i, :]
        dt, xr, th, cs_, sn_, t1, t2, am1, den, zr, zi = [R(i) for i in range(11)]
        so("act", lambda e: e.activation(out=dt, in_=ldt, func=AF.Exp))
        tt(xr, lre, dt, ALU.mult)
        tt(th, lim, dt, ALU.mult)
        hp_ = P.sb("halfpi", [128, 1], F32)
        so("pool", lambda e: e.memset(hp_[:], float(np.pi / 2)))
        so("act", lambda e: e.activation(out=sn_, in_=th, func=AF.Sin, scale=1.0 / 16))
        so("act", lambda e: e.activation(out=cs_, in_=th, func=AF.Sin, scale=1.0 / 16, bias=hp_[:, 0:1]))
        for _ in range(4):
            tt(t1, cs_, cs_, ALU.mult)
            tt(t2, sn_, sn_, ALU.mult)
            tt(sn_, sn_, cs_, ALU.mult)
            ts(sn_, sn_, 2.0, ALU.mult)
            tt(cs_, t1, t2, ALU.subtract)
        ekr = P.sb("ekr", [128, 16, 9], F32)
        eki = P.sb("eki", [128, 16, 9], F32)
        so("pool", lambda e: e.memset(ekr[:, :, 0], 1.0))
        so("pool", lambda e: e.memset(eki[:, :, 0], 0.0))
        cp(ekr[:, :, 1], cs_)
        cp(eki[:, :, 1], sn_)
        for k in range(1, 8):
            cmul(ekr[:, :, k + 1], eki[:, :, k + 1], ekr[:, :, k], eki[:, :, k], cs_, sn_, t1, t2)
        kv = P.sb("kv", [128, 16], F32)
        so("pool", lambda e: e.iota(kv[:], pattern=[[1, 16]], base=-7, channel_multiplier=0, allow_small_or_imprecise_dtypes=True))
        mag = P.sb("mag", [128, 16, 16], F32)
        apr = P.sb("apr", [128, 16, 16], F32)
        api = P.sb("api", [128, 16, 16], F32)
        tt(mag[:], xr.unsqueeze(2).to_broadcast([128, 16, 16]), kv[:, :].unsqueeze(1).to_broadcast([128, 16, 16]), ALU.mult)
        so("act", lambda e: e.activation(out=mag[:], in_=mag[:], func=AF.Exp))
        tt(apr[:, :, 7:16], mag[:, :, 7:16], ekr[:], ALU.mult)
        tt(api[:, :, 7:16], mag[:, :, 7:16], eki[:], ALU.mult)
        tt(apr[:, :, 0:7], mag[:, :, 0:7], ekr[:, :, 7:0:-1], ALU.mult)
        tt(api[:, :, 0:7], mag[:, :, 0:7], eki[:, :, 7:0:-1], ALU.mult)
        ts(api[:, :, 0:7], api[:, :, 0:7], -1.0, ALU.mult)
        ts(am1, apr[:, :, 8], -1.0, ALU.add)
        tt(t1, lre, lre, ALU.mult)
        tt(t2, lim, lim, ALU.mult)
        tt(den, t1, t2, ALU.add)
        so("dve", lambda e: e.reciprocal(out=den, in_=den))
        tt(t1, am1, lre, ALU.mult)
        tt(t2, api[:, :, 8], lim, ALU.mult)
        tt(zr, t1, t2, ALU.add)
        tt(zr, zr, den, ALU.mult)
        tt(t1, api[:, :, 8], lre, ALU.mult)
        tt(t2, am1, lim, ALU.mult)
        tt(zi, t1, t2, ALU.subtract)
        tt(zi, zi, den, ALU.mult)
        bbr = P.sb("bbr", [128, 16, 16], F32)
        bbi = P.sb("bbi", [128, 16, 16], F32)
        w1 = P.sb("w1", [128, 16, 8, 16], F32)
        w2 = P.sb("w2", [128, 16, 8, 16], F32)
        bc = lambda a: a.unsqueeze(2).to_broadcast([128, 16, 16])
        cmul(bbr[:], bbi[:], bc(zr), bc(zi), Bre, Bim, w1[:, :, 0, :], w2[:, :, 0, :])
        prod_r = P.sb("prodr", [128, 16, 8, 16], F32)
        prod_i = P.sb("prodi", [128, 16, 8, 16], F32)
        xa_r = P.sb("xar", [128, 16, 8, 16], F32)
        xa_i = P.sb("xai", [128, 16, 8, 16], F32)
        zA = P.sb("zA", [128, 128], F32)
        zB = P.sb("zB", [128, 128], F32)
        kacc = P.sb("kacc", [128, 128], F32)
        ktmp = P.sb("ktmp", [128, 128], F32)
        dcol = P.sb("dcol", [128, 16], F32)
        P.dma(lambda e: e.dma_start(out=dcol[:], in_=self.s5_dcol[l]), reads=S, writes=S)
        b4 = lambda a: a.unsqueeze(3).to_broadcast([128, 16, 8, 16])
        c4 = lambda a: a.unsqueeze(2).to_broadcast([128, 16, 8, 16])
        sl_neg, sl_pos, sl_7m, sl_p1, sl_8m = slice(7, None, -1), slice(7, 15), slice(14, 6, -1), slice(8, 16), slice(15, 7, -1)

        def prod(outr, outi, sl, mr, mi):
            cmul(outr, outi, b4(apr[:, :, sl]), b4(api[:, :, sl]), c4(mr), c4(mi), w1[:], w2[:])

        def gap(t, d, g):
            return t[:, d * 8 + g // 2, :, :].rearrange("p a b -> p (a b)")

        def zpad(dst, src, par, scale):
            so("pool", lambda e: e.memset(dst[64 * (1 - par):64 * (2 - par), :], 0.0))
            so("dve", lambda e: e.tensor_scalar(out=dst[64 * par:64 * par + 64, :], in0=src[64 * par:64 * par + 64, :], scalar1=scale, scalar2=None, op0=ALU.mult))
        PS0, PS1 = [self.PS[0]] + S, [self.PS[1]] + S
        for d in range(2):
            prod(prod_r[:], prod_i[:], sl_7m if d == 0 else sl_pos, bbr[:], bbi[:])
            for g in range(16):
                for ri, src in enumerate((prod_r, prod_i)):
                    zpad(zA, gap(src, d, g), g % 2, 1.0)
                    P.op("pe", lambda e: e.transpose(out=self.ps[0][:, 0:128], in_=zA[:], identity=self.identf[:]), reads=PS0, writes=PS0)
                    P.op("act", lambda e, d=d, g=g, ri=ri: e.activation(out=self.FT[:, d, g, ri, :], in_=self.ps[0][:, 0:128], func=AF.Copy), reads=PS0, writes=PS0)
            prod(prod_r[:], prod_i[:], sl_p1 if d == 0 else sl_8m, Cre, Cim)
            for g in range(16):
                for ri, (src, sc_) in enumerate(((prod_r, 1.0), (prod_i, -1.0))):
                    zpad(self.EZ[:, d, g, ri, :], gap(src, d, g), g % 2, sc_)
            prod(xa_r[:], xa_i[:], sl_neg if d == 0 else sl_pos, bbr[:], bbi[:])
            prod(prod_r[:], prod_i[:], sl_pos if d == 0 else sl_neg, Cre, Cim)
            for g in range(16):
                zpad(zA, gap(xa_r, d, g), g % 2, 1.0)
                zpad(zB, gap(xa_i, d, g), g % 2, -1.0)
                P.op("pe", lambda e, d=d, g=g: e.matmul(out=self.ps[1][:, 0:128], lhsT=zA[:], rhs=gap(prod_r, d, g), start=True, stop=False), reads=PS1, writes=PS1)
                P.op("pe", lambda e, d=d, g=g: e.matmul(out=self.ps[1][:, 0:128], lhsT=zB[:], rhs=gap(prod_i, d, g), start=False, stop=True), reads=PS1, writes=PS1)
                msk = self.mskf if d == 0 else self.mskb
                P.op("dve", lambda e, msk=msk: e.tensor_tensor(out=ktmp[:], in0=self.ps[1][:, 0:128], in1=msk[:], op=ALU.mult), reads=PS1 + ["mskf", "mskb"], writes=PS1)
                if d == 0:
                    so("dve", lambda e, g=g: e.scalar_tensor_tensor(out=self.Kf32[:, g, :], in0=self.identf[:], scalar=dcol[:, g:g + 1], in1=ktmp[:], op0=ALU.mult, op1=ALU.add))
                else:
                    tt(kacc[:], ktmp[:], self.Kf32[:, g, :], ALU.add)
                    cp(self.KtotT[:, g, :], kacc[:])
        g1r, g1i, phc, phs = self.g1r, self.g1i, self.phc, self.phs
        cp(g1r[:, :, 0], ekr[:, :, 8])
        cp(g1i[:, :, 0], eki[:, :, 8])
        for m in range(7):
            cmul(g1r[:, :, m + 1], g1i[:, :, m + 1], g1r[:, :, m], g1i[:, :, m], g1r[:, :, m], g1i[:, :, m], t1, t2)
        so("pool", lambda e: e.memset(phc[:, :, 0:1], 1.0))
        so("pool", lambda e: e.memset(phs[:, :, 0:1], 0.0))
        wv1 = w1[:].rearrange("p s a b -> p s (a b)")
        wv2 = w2[:].rearrange("p s a b -> p s (a b)")
        for m in range(7):
            n = 1 << m
            cmul(phc[:, :, n:2 * n], phs[:, :, n:2 * n], phc[:, :, 0:n], phs[:, :, 0:n],
                 g1r[:, :, m:m + 1].to_broadcast([128, 16, n]), g1i[:, :, m:m + 1].to_broadcast([128, 16, n]), wv1[:, :, 0:n], wv2[:, :, 0:n])
        cp(self.rho[:], mag[:, :, 15:16].to_broadcast([128, 16, 128]))

    def s5_run(self, N, uT_src, use_h0, with_output, out_dst, store_final, tag):
        P = self.P
        P.mark()
        TT = 8 * N
        W = min(512, N)
        nh = N // W
        L = min(128, N)
        nseg = N // L
        ut = P.sb("s5ut", [128, 2, TT], BF16)
        ut_off = P.last_off
        U = P.sb("s5U", [128, 16, N], BF16)
        U_off = P.last_off
        for hc in range(2):
            P.dma(lambda e, hc=hc: e.dma_start(out=ut[:, hc, :], in_=uT_src[hc * 128:(hc + 1) * 128, :]), writes=["s5ut"])
        sel = self.sel
        cnt = 0
        for g in range(16):
            uv = ut[:, g // 8, :].rearrange("p (j i) -> p i j", i=8)
            for h in range(nh):
                pb = cnt % 2
                cnt += 1
                for i0 in range(8):
                    self.mm(self.ps[pb][:, 0:W], sel[:, g % 8, i0, :], uv[:, i0, h * W:(h + 1) * W], i0 == 0, i0 == 7, ["sel", "s5ut"], [self.PS[pb]])
                if pb == 0:
                    P.op("act", lambda e, g=g, h=h, pb=pb: e.activation(out=U[:, g, h * W:(h + 1) * W], in_=self.ps[pb][:, 0:W], func=AF.Copy), reads=[self.PS[pb]], writes=["s5U"])
                else:
                    P.op("dve", lambda e, g=g, h=h, pb=pb: e.tensor_copy(out=U[:, g, h * W:(h + 1) * W], in_=self.ps[pb][:, 0:W]), reads=[self.PS[pb]], writes=["s5U"])
        P.barrier()
        P.mark()
        sets = []
        for si_ in range(2):
            bufs = []
            for bi in range(8):
                if si_ == 0:
                    bufs.append(P.sb("s5w%d_%d" % (si_, bi), [128, N], F32))
                else:
                    bufs.append(P.sb_at("s5w%d_%d" % (si_, bi), [128, N], F32, ut_off + bi * N * 4))
            sets.append(bufs)
        shbs = [P.sb("s5shb%d" % i, [128, 2, N], BF16) for i in range(2)]
        inis = [P.sb("s5ini%d" % i, [128, 4], F32) for i in range(2)]
        hfin, g1r, g1i, hnew, rho = self.hfin, self.g1r, self.g1i, self.hnew, self.rho
        FT, phc, phs = self.FT, self.phc, self.phs
        bankc = [0]

        def SA(s_):
            d, gp = s_ // 8, s_ % 8
            k_ = s_ % 2
            Vr, Vi = sets[k_][0], sets[k_][1]
            for ri, V in enumerate((Vr, Vi)):
                vn = "s5V%d_%d" % (ri, k_)
                for h in range(nh):
                    pb = 2 + bankc[0] % 4
                    bankc[0] += 1
                    self.mm(self.ps[pb][:, 0:W], FT[:, d, 2 * gp, ri, :], U[:, 2 * gp, h * W:(h + 1) * W], True, False, ["FT", "s5U"], [self.PS[pb]])
                    self.mm(self.ps[pb][:, 0:W], FT[:, d, 2 * gp + 1, ri, :], U[:, 2 * gp + 1, h * W:(h + 1) * W], False, True, ["FT", "s5U"], [self.PS[pb]])
                    if d == 0:
                        ov = V[:, h * W:(h + 1) * W]
                    else:
                        ov = V[:, N - 1 - h * W:(N - 1 - (h + 1) * W if (h + 1) * W < N else None):-1]
                    P.op("act", lambda e, ov=ov, pb=pb: e.activation(out=ov, in_=self.ps[pb][:, 0:W], func=AF.Copy), reads=[self.PS[pb]], writes=[vn])

        def SB(s_):
            d, gp = s_ // 8, s_ % 8
            k_ = s_ % 2
            Vr, Vi, Wr, Wi, Sr, Si, ta, tc_ = sets[k_]
            shb, ini = shbs[k_], inis[k_]
            nm = lambda x: "s5%s_%d" % (x, k_)
            c3 = phc[:, s_, 0:L].unsqueeze(1).to_broadcast([128, nseg, L])
            s3 = phs[:, s_, 0:L].unsqueeze(1).to_broadcast([128, nseg, L])
            v3 = lambda t: t[:, :].rearrange("p (a b) -> p a b", b=L)
            P.op("dve", lambda e: e.tensor_tensor(out=v3(ta), in0=v3(Vr), in1=c3, op=ALU.mult), reads=[nm("V0"), "ph"], writes=[nm("ta")])
            P.op("dve", lambda e: e.tensor_tensor(out=v3(Wr), in0=v3(Vi), in1=s3, op=ALU.mult), reads=[nm("V1"), "ph"], writes=[nm("Wr")])
            P.op("dve", lambda e: e.tensor_tensor(out=Wr[:], in0=Wr[:], in1=ta[:], op=ALU.add), reads=[nm("Wr"), nm("ta")], writes=[nm("Wr")])
            P.op("pool", lambda e: e.tensor_tensor(out=v3(tc_), in0=v3(Vi), in1=c3, op=ALU.mult), reads=[nm("V1"), "ph"], writes=[nm("tc")])
            P.op("pool", lambda e: e.tensor_tensor(out=v3(Wi), in0=v3(Vr), in1=s3, op=ALU.mult), reads=[nm("V0"), "ph"], writes=[nm("Wi")])
            P.op("pool", lambda e: e.tensor_tensor(out=Wi[:], in0=tc_[:], in1=Wi[:], op=ALU.subtract), reads=[nm("Wi"), nm("tc")], writes=[nm("Wi")])
            for sg_ in range(nseg):
                a, b = sg_ * L, (sg_ + 1) * L
                qr = None
                if sg_ == 0:
                    if use_h0:
                        qr, qi = g1r[:, s_, 0:1], g1i[:, s_, 0:1]
                        lr, li = hfin[:, s_, 0:1], hfin[:, s_, 1:2]
                else:
                    m7 = {128: 7, 32: 5}[L]
                    qr, qi = g1r[:, s_, m7:m7 + 1], g1i[:, s_, m7:m7 + 1]
                    lr, li = Sr[:, a - 1:a], Si[:, a - 1:a]
                rds = [nm("Sr"), nm("Si"), "hfin", "g1", nm("ini")]
                if qr is None:
                    P.op("pool", lambda e: e.memset(ini[:], 0.0), reads=[nm("ini")], writes=[nm("ini")])
                else:
                    P.op("dve", lambda e, li=li, qi=qi: e.tensor_tensor(out=ini[:, 2:3], in0=li, in1=qi, op=ALU.mult), reads=rds, writes=[nm("ini")])
                    P.op("dve", lambda e, lr=lr, qr=qr: e.scalar_tensor_tensor(out=ini[:, 0:1], in0=lr, scalar=qr, in1=ini[:, 2:3], op0=ALU.mult, op1=ALU.subtract), reads=rds, writes=[nm("ini")])
                    P.op("dve", lambda e, li=li, qr=qr: e.tensor_tensor(out=ini[:, 3:4], in0=li, in1=qr, op=ALU.mult), reads=rds, writes=[nm("ini")])
                    P.op("dve", lambda e, lr=lr, qi=qi: e.scalar_tensor_tensor(out=ini[:, 1:2], in0=lr, scalar=qi, in1=ini[:, 3:4], op0=ALU.mult, op1=ALU.add), reads=rds, writes=[nm("ini")])
                P.op("dve", lambda e, a=a, b=b: e.tensor_tensor_scan(out=Sr[:, a:b], data0=rho[:, s_, 0:L], data1=Wr[:, a:b], initial=ini[:, 0:1], op0=ALU.mult, op1=ALU.add),
                     reads=[nm("Wr"), "rhot", nm("ini")], writes=[nm("Sr")])
                P.op("dve", lambda e, a=a, b=b: e.tensor_tensor_scan(out=Si[:, a:b], data0=rho[:, s_, 0:L], data1=Wi[:, a:b], initial=ini[:, 1:2], op0=ALU.mult, op1=ALU.add),
                     reads=[nm("Wi"), "rhot", nm("ini")], writes=[nm("Si")])
            P.op("dve", lambda e: e.tensor_tensor(out=v3(ta), in0=v3(Sr), in1=c3, op=ALU.mult), reads=[nm("Sr"), "ph"], writes=[nm("ta")])
            P.op("dve", lambda e: e.tensor_tensor(out=v3(Vr), in0=v3(Si), in1=s3, op=ALU.mult), reads=[nm("Si"), "ph"], writes=[nm("V0")])
            P.op("dve", lambda e: e.tensor_tensor(out=Wr[:], in0=ta[:], in1=Vr[:], op=ALU.subtract), reads=[nm("ta"), nm("V0")], writes=[nm("Wr")])
            P.op("pool", lambda e: e.tensor_tensor(out=v3(tc_), in0=v3(Si), in1=c3, op=ALU.mult), reads=[nm("Si"), "ph"], writes=[nm("tc")])
            P.op("pool", lambda e: e.tensor_tensor(out=v3(Vi), in0=v3(Sr), in1=s3, op=ALU.mult), reads=[nm("Sr"), "ph"], writes=[nm("V1")])
            P.op("pool", lambda e: e.tensor_tensor(out=Wi[:], in0=tc_[:], in1=Vi[:], op=ALU.add), reads=[nm("tc"), nm("V1")], writes=[nm("Wi")])
            for ri, H in enumerate((Wr, Wi)):
                hn = nm(("Wr", "Wi")[ri])
                if d == 0:
                    P.op("act", lambda e, ri=ri, H=H: e.activation(out=shb[:, ri, 1:N], in_=H[:, 0:N - 1], func=AF.Copy), reads=[hn], writes=[nm("shb")])
                    edge = shb[:, ri, 0:1]
                else:
                    P.op("act", lambda e, ri=ri, H=H: e.activation(out=shb[:, ri, 0:N - 1], in_=H[:, N - 2::-1], func=AF.Copy), reads=[hn], writes=[nm("shb")])
                    edge = shb[:, ri, N - 1:N]
                if use_h0:
                    P.op("dve", lambda e, edge=edge, ri=ri: e.tensor_copy(out=edge, in_=hfin[:, s_, ri:ri + 1]), reads=["hfin", nm("shb")], writes=[nm("shb")])
                else:
                    P.op("pool", lambda e, edge=edge: e.memset(edge, 0.0), reads=[nm("shb")], writes=[nm("shb")])
            if with_output:
                P.dma(lambda e: e.dma_start(out=self.ssh_d[d, gp, :, :, 0:N].rearrange("r p n -> p r n"), in_=shb[:]), reads=[nm("shb")], writes=["ssh_d"])
            if store_final:
                for ri, H in enumerate((Wr, Wi)):
                    P.op("dve", lambda e, ri=ri, H=H: e.tensor_copy(out=hnew[:, s_, ri:ri + 1], in_=H[:, N - 1:N]), reads=[nm(("Wr", "Wi")[ri]), nm("shb")], writes=["hnew"])
        SA(0)
        for s_ in range(16):
            if s_ + 1 < 16:
                SA(s_ + 1)
            SB(s_)
        if store_final:
            P.op("dve", lambda e: e.tensor_copy(out=hfin[:], in_=hnew[:]), reads=["hnew", "s5shb_0", "s5shb_1", "s5ini_0", "s5ini_1"], writes=["hfin"])
        P.release()
        if with_output:
            Yb = P.sb_at("s5Y", [128, 16, N], BF16, ut_off)
            ssbs = [P.sb("s5ssb%d" % i, [128, 2, 2, N], BF16) for i in range(2)]
            for g in range(16):
                gp = g // 2
                ssb, ssn = ssbs[gp % 2], "s5ssb%d" % (gp % 2)
                if g % 2 == 0:
                    for d in range(2):
                        P.dma(lambda e, d=d, gp=gp, ssb=ssb: e.dma_start(out=ssb[:, d, :, :], in_=self.ssh_d[d, gp, :, :, 0:N].rearrange("r p n -> p r n")), reads=["ssh_d"], writes=[ssn])
                for h in range(nh):
                    pb = 4 + (g * nh + h) % 2
                    sl = slice(h * W, (h + 1) * W)
                    self.mm(self.ps[pb][:, 0:W], self.KtotT[:, g, :], U[:, g, sl], True, False, ["S5S", "s5U"], [self.PS[pb]])
                    for d in range(2):
                        for ri in range(2):
                            self.mm(self.ps[pb][:, 0:W], self.EZ[:, d, g, ri, :], ssb[:, d, ri, sl], False, d == 1 and ri == 1, ["S5S", ssn], [self.PS[pb]])
                    if pb == 4:
                        P.op("act", lambda e, g=g, sl=sl, pb=pb: e.activation(out=Yb[:, g, sl], in_=self.ps[pb][:, 0:W], func=AF.Copy), reads=[self.PS[pb]], writes=["s5ut"])
                    else:
                        P.op("dve", lambda e, g=g, sl=sl, pb=pb: e.tensor_copy(out=Yb[:, g, sl], in_=self.ps[pb][:, 0:W]), reads=[self.PS[pb]], writes=["s5ut"])
            if self.dbg and tag == "x" and self.stop == "s5":
                oy = self.dbg_out("Yb", [128, 16, N], BF16)
                P.dma(lambda e: e.dma_start(out=oy, in_=Yb[:]), reads=["s5ut"])
                ou = self.dbg_out("U", [128, 16, N], BF16)
                P.dma(lambda e: e.dma_start(out=ou, in_=U[:]), reads=["s5U"])
                self.dump("ssh", self.ssh_d, [2, 8, 2, 128, T // 8], BF16)
                for nm, t_, shp, dt_ in (("FT", self.FT, [128, 2, 16, 2, 128], BF16), ("EZ", self.EZ, [128, 2, 16, 2, 128], BF16), ("KtotT", self.KtotT, [128, 16, 128], BF16),
                                         ("phc", self.phc, [128, 16, 128], F32), ("phs", self.phs, [128, 16, 128], F32), ("rho", self.rho, [128, 16, 128], F32),
                                         ("g1r", self.g1r, [128, 16, 8], F32), ("g1i", self.g1i, [128, 16, 8], F32), ("hfin", self.hfin, [128, 16, 2], F32)):
                    od = self.dbg_out(nm, shp, dt_)
                    P.dma(lambda e, od=od, t_=t_: e.dma_start(out=od, in_=t_[:]), reads=["S5S", "hfin"])
            zT = P.sb_at("s5zT", [128, 2, TT], BF16, U_off)
            y2 = P.sb("s5y2", [128, W], F32)
            sgm = P.sb("s5sg", [128, W], F32)
            cnt = 0
            for hc in range(2):
                zv = zT[:, hc, :].rearrange("p (j i) -> p i j", i=8)
                for i0 in range(8):
                    for h in range(nh):
                        pb = 6 + cnt % 2
                        cnt += 1
                        sl = slice(h * W, (h + 1) * W)
                        for g8 in range(8):
                            self.mm(self.ps[pb][:, 0:W], sel[:, i0, g8, :], Yb[:, hc * 8 + g8, sl], g8 == 0, g8 == 7, ["sel", "s5ut"], [self.PS[pb]])
                        P.op("act", lambda e, pb=pb: e.activation(out=y2[:], in_=self.ps[pb][:, 0:W], func=AF.Square), reads=[self.PS[pb]], writes=["s5y2"])
                        P.op("dve", lambda e: e.tensor_scalar(out=y2[:], in0=y2[:], scalar1=0.044715, scalar2=1.0, op0=ALU.mult, op1=ALU.add), reads=["s5y2"], writes=["s5y2"])
                        P.op("dve", lambda e, pb=pb: e.tensor_tensor(out=y2[:], in0=self.ps[pb][:, 0:W], in1=y2[:], op=ALU.mult), reads=[self.PS[pb], "s5y2"], writes=["s5y2"])
                        P.op("act", lambda e: e.activation(out=sgm[:], in_=y2[:], func=AF.Sigmoid, scale=1.5957691216057308), reads=["s5y2"], writes=["s5sg"])
                        P.op("dve", lambda e, pb=pb, zv=zv, i0=i0, sl=sl: e.tensor_tensor(out=zv[:, i0, sl], in0=self.ps[pb][:, 0:W], in1=sgm[:], op=ALU.mult), reads=[self.PS[pb], "s5sg"], writes=["s5U"])
            BW = min(512, TT)
            gt = P.sb("s5gt", [128, BW], F32)
            obs = [P.sb("s5ob%d" % i, [128, BW], BF16) for i in range(2)]
            for b in range(TT // BW):
                sl = slice(b * BW, (b + 1) * BW)
                for ho in range(2):
                    pb = 2 + ho
                    for hc in range(2):
                        self.mm(self.ps[pb][:, 0:BW], self.wglu[:, hc, ho * 128:(ho + 1) * 128], zT[:, hc, sl], hc == 0, hc == 1, ["wglu", "s5U"], [self.PS[pb]])
                    P.op("act", lambda e, pb=pb, ho=ho: e.activation(out=gt[:], in_=self.ps[pb][:, 0:BW], func=AF.Sigmoid, bias=self.bglu[:, ho:ho + 1], scale=1.0), reads=[self.PS[pb], "wglu"], writes=["s5gt"])
                    ob = obs[ho]
                    P.op("dve", lambda e, ob=ob, ho=ho, sl=sl: e.tensor_tensor(out=ob[:], in0=zT[:, ho, sl], in1=gt[:], op=ALU.mult), reads=["s5U", "s5gt"], writes=["s5ob%d" % ho])
                    P.dma(lambda e, ob=ob, ho=ho, sl=sl: e.dma_start(out=out_dst[ho * 128:(ho + 1) * 128, sl], in_=ob[:]), reads=["s5ob%d" % ho])
        P.release()

    def phase_s5(self, l, last):
        P = self.P
        P.mark()
        self.sel = P.sb("sel", [128, 8, 8, 128], BF16)
        self.KtotT = P.sb("KtotT", [128, 16, 128], BF16)
        self.FT = P.sb("FT", [128, 2, 16, 2, 128], BF16)
        self.EZ = P.sb("EZ", [128, 2, 16, 2, 128], BF16)
        self.phc = P.sb("phc", [128, 16, 128], F32)
        self.phs = P.sb("phs", [128, 16, 128], F32)
        self.rho = P.sb("rhot", [128, 16, 128], F32)
        self.g1r = P.sb("g1r", [128, 16, 8], F32)
        self.g1i = P.sb("g1i", [128, 16, 8], F32)
        self.hfin = P.sb("hfin", [128, 16, 2], F32)
        self.hnew = P.sb("hnew", [128, 16, 2], F32)
        self.wglu = P.sb("wglu", [128, 2, 256], BF16)
        self.bglu = P.sb("bglu", [128, 2], F32)
        wglu, bglu = self.wglu, self.bglu
        P.dma(lambda e: e.dma_start(out=wglu[:], in_=self.s5_w_glu[l].rearrange("(k p) n -> p k n", p=128)), writes=["wglu"], eng="pool")
        P.dma(lambda e: e.dma_start(out=bglu[:], in_=self.s5_bglu_col[l]), writes=["wglu"])
        P.mark()
        self.Kf32 = P.sb("Kf32", [128, 16, 128], F32)
        self.s5_consts()
        self.s5_setup(l)
        P.op("pool", lambda e: e.memset(self.hnew[:], 0.0), reads=["S5S"], writes=["S5S", "ph", "rhot", "g1", "FT", "hnew", "sel"])
        P.release()
        self.s5_run(LC // 8, self.ucT_d, False, not last, self.catcT_d[512:768, :], True, "c")
        self.s5_run(T // 8, self.uT_d, True, True, self.catT_d[512:768, :], False, "x")
        P.release()

    def dump(self, name, src, shape, dt):
        o = self.dbg_out(name, shape, dt)
        self.P.dma(lambda e: e.dma_start(out=o, in_=src))

    def build(self):
        P = self.P
        self.x_src, self.xc_src = self.x_in, self.ctx_in
        self.logits_x = P.sb("logits_x", [128, T // 128, NEXP], F32)
        self.logits_c = P.sb("logits_c", [128, LC // 128, NEXP], F32)
        P.op("pool", lambda e: e.memset(self.logits_x[:], 0.0), writes=["logits0"])
        P.op("pool", lambda e: e.memset(self.logits_c[:], 0.0), writes=["logits1"])
        for l in range(self.nlayers):
            last = l == DEPTH - 1
            self.phase_mod(l)
            if self.stop == "mod":
                self.dump("modrows", self.modrows[l], [2, 6 * DM], F32)
                break
            P.mark()
            self.load_w_in(l)
            self.build_wext(l, 1)
            self.phase_inproj(l, True, not last)
            self.build_wext(l, 0)
            self.phase_inproj(l, False, True)
            P.release()
            if self.stop == "inproj":
                self.dump("qT", self.qT_d, [512, T], BF16)
                self.dump("kT", self.kT_d, [256, T], BF16)
                self.dump("v", self.v_d, [T, 130], BF16)
                self.dump("uT", self.uT_d, [256, T], BF16)
                self.dump("hT", self.hT_d, [256, T], BF16)
                self.dump("kcT", self.kcT_d, [256, LC], BF16)
                self.dump("ucT", self.ucT_d, [256, LC], BF16)
                break
            if os.environ.get("DBG_SKIPATTN") is None:
                self.phase_attn(l, not last)
            if self.stop == "attn":
                self.dump("catT", self.catT_d, [DM, T], BF16)
                self.dump("catcT", self.catcT_d, [DM, LC], BF16)
                break
            if os.environ.get("DBG_SKIPS5") is None:
                self.phase_s5(l, last)
            if self.stop == "s5":
                self.dump("catT", self.catT_d, [DM, T], BF16)
                self.dump("catcT", self.catcT_d, [DM, LC], BF16)
                break
            if not last:
                self.phase_conv(l, True)
            self.phase_conv(l, False)
            if self.stop == "conv":
                self.dump("catT", self.catT_d, [DM, T], BF16)
                self.dump("catcT", self.catcT_d, [DM, LC], BF16)
                break
            if not last:
                self.phase_outproj(l, 1)
            self.phase_outproj(l, 0)
            if self.stop == "outproj":
                self.dump("catT", self.catT_d, [DM, T], BF16)
                self.dump("catcT", self.catcT_d, [DM, LC], BF16)
                self.dump("xmid", self.xmid_d, [T, DM], F32)
                self.dump("xcmid", self.xcmid_d, [LC, DM], F32)
                self.dump("h2", self.h2_d, [T, DM], BF16)
                lo = self.dbg_out("logits", [128, T // 128, NEXP], F32)
                P.dma(lambda e: e.dma_start(out=lo, in_=self.logits_x[:]), reads=["logits0"])
                break
            P.barrier()
            self.phase_moe(l, not last)
            if self.stop == "moe":
                self.dump("h2", self.h2_d, [T, DM], BF16)
                self.dump("h2c", self.h2c_d, [LC, DM], BF16)
                self.dump("moe", self.moe_d, [T, DM], F32)
                self.dump("moec", self.moec_d, [LC, DM], F32)
                lo = self.dbg_out("logits", [128, T // 128, NEXP], F32)
                P.dma(lambda e: e.dma_start(out=lo, in_=self.logits_x[:]), reads=["logits0"])
                lo2 = self.dbg_out("logitsc", [128, LC // 128, NEXP], F32)
                P.dma(lambda e: e.dma_start(out=lo2, in_=self.logits_c[:]), reads=["logits1"])
                break
            if not last:
                self.phase_ln2(l, 1, self.xc1_d)
            self.phase_ln2(l, 0, self.out if last else self.x1_d)
            self.x_src, self.xc_src = self.x1_d, self.xc1_d
            if self.stop == "ln2":
                self.dump("x1", self.x1_d, [T, DM], F32)
                self.dump("xc1", self.xc1_d, [LC, DM], F32)
                break
        P.barrier()
        P.emit()
        return self.nc


def rope_tables():
    t = np.arange(T)
    pos_row = (t // 64).astype(np.float32)
    pos_col = (t % 64).astype(np.float32)
    inv_freq = (10000.0 ** (-np.arange(16, dtype=np.float32) / 16)).astype(np.float32)
    ang = np.zeros((64, T), np.float32)
    for j in range(64):
        pos = pos_row if j < 32 else pos_col
        ang[j] = pos * inv_freq[j % 16]
    cos = np.cos(ang).astype(np.float32)
    sin = np.sin(ang).astype(np.float32)
    return np.ascontiguousarray(np.concatenate([cos, cos], 0)), np.ascontiguousarray(np.concatenate([sin, sin], 0))


def make_in_maps(inputs, ncores=2):
    f = lambda a: np.ascontiguousarray(np.asarray(a, dtype=np.float32))
    cosT, sinT = rope_tables()
    maps = []
    for b in range(ncores):
        ccol = np.stack([f(inputs["c"])[b].reshape(8, 128).T, f(inputs["c_ctx"]).reshape(8, 128).T], axis=-1)
        m = {"x": f(inputs["x"])[b], "ctx": f(inputs["ctx"])[b], "ccol": np.ascontiguousarray(ccol),
             "w_mod": f(inputs["w_mod"]), "b_mod": f(inputs["b_mod"]), "w_in": f(inputs["w_in"]), "b_in": f(inputs["b_in"]),
             "cosT": cosT, "sinT": sinT, "attn_sink": f(inputs["attn_sink"]),
             "conv_w_col": np.ascontiguousarray(f(inputs["conv_w_dw"])[:, :, 0, :].reshape(DEPTH, 31, 2, 128).transpose(0, 3, 2, 1)),
             "conv_vec_col": np.ascontiguousarray(np.stack([f(inputs[k]).reshape(DEPTH, 2, 128).transpose(0, 2, 1)
                                                            for k in ("conv_b_dw", "conv_ln_g", "conv_ln_b", "conv_b_pw")], axis=-1)),
             "conv_w_pw": f(inputs["conv_w_pw"])}
        def nat(a):
            a = f(a)
            return a
        lam = lambda k: f(inputs[k]).reshape(DEPTH, 2, 8, 2, 64).transpose(0, 3, 4, 1, 2).reshape(DEPTH, 128, 16)
        ldt = np.broadcast_to(f(inputs["s5_log_dt"]).reshape(DEPTH, 2, 8, 2, 1).transpose(0, 3, 4, 1, 2), (DEPTH, 2, 64, 2, 8)).reshape(DEPTH, 128, 16)
        bm_ = lambda k: f(inputs[k]).reshape(DEPTH, 2, 8, 2, 64, 16).transpose(0, 3, 4, 1, 2, 5).reshape(DEPTH, 128, 256)
        cm_ = lambda k: f(inputs[k]).reshape(DEPTH, 2, 8, 2, 16, 64).transpose(0, 3, 5, 1, 2, 4).reshape(DEPTH, 128, 256)
        m["s5nat"] = np.ascontiguousarray(np.concatenate([lam("s5_lam_re"), lam("s5_lam_im"), ldt, bm_("s5_b_re"), bm_("s5_b_im"), cm_("s5_c_re"), cm_("s5_c_im")], axis=2))
        m["s5_dcol"] = np.ascontiguousarray(np.tile(f(inputs["s5_d"]).reshape(DEPTH, 16, 16).transpose(0, 2, 1), (1, 8, 1)))
        m["s5_w_glu"] = f(inputs["s5_w_glu"])
        m["s5_bglu_col"] = np.ascontiguousarray(f(inputs["s5_b_glu"]).reshape(DEPTH, 2, 128).transpose(0, 2, 1))
        for k in ("w_out", "b_out", "ln1_g", "ln1_b", "ln2_g", "ln2_b", "w_router", "exp_w_gate", "exp_w_up", "exp_w_down"):
            m[k] = f(inputs[k])
        maps.append(m)
    return maps


def kernel(**inputs):
    B = Builder()
    nc = B.build()
    maps = make_in_maps(inputs)
    res = run_bass_kernel_spmd(nc, maps, core_ids=[0, 1])
    return np.stack([r["out"] for r in res.results], 0).astype(np.float32)
```

```python
import os
import numpy as np
import concourse.bass as bass
import concourse.mybir as mybir
from concourse.bass_utils import run_bass_kernel_spmd

F32 = mybir.dt.float32
BF16 = mybir.dt.bfloat16
I32 = mybir.dt.int32
AF = mybir.ActivationFunctionType
ALU = mybir.AluOpType
AX = mybir.AxisListType

ENGS = ("pe", "act", "dve", "pool", "sp")
NDMA = 24


class Prog:
    def __init__(self, nc):
        self.nc = nc
        self.q = {e: [] for e in ENGS}
        self.cnt = {e: 0 for e in ENGS}
        self.seen = {e: {} for e in ENGS}
        self.last_w = {}
        self.readers = {}
        self.dma_i = {e: 0 for e in ENGS}
        self.dma_slot_val = {(e, i): 0 for e in ENGS for i in range(NDMA)}
        self.sb_off = 16640
        self.sb_marks = []
        self.sb_max = 0
        self.uid = 0

    def sb(self, name, shape, dtype=F32):
        self.uid += 1
        name = "%s_%d" % (name, self.uid)
        esz = mybir.dt.size(dtype)
        n = 1
        for s in shape[1:]:
            n *= s
        nbytes = (n * esz + 63) // 64 * 64
        t = self.nc.alloc_sbuf_tensor_at(name, list(shape), dtype, offset=self.sb_off)
        self.last_off = self.sb_off
        self.sb_off += nbytes
        self.sb_max = max(self.sb_max, self.sb_off)
        assert self.sb_off <= 229376, (name, self.sb_off)
        return t

    def sb_at(self, name, shape, dtype, offset):
        self.uid += 1
        return self.nc.alloc_sbuf_tensor_at("%s_%d" % (name, self.uid), list(shape), dtype, offset=offset)

    def mark(self):
        self.sb_marks.append(self.sb_off)

    def release(self):
        self.barrier()
        self.sb_off = self.sb_marks.pop()

    def _deps(self, reads, writes):
        deps = {}

        def add(k, v):
            if v > deps.get(k, 0):
                deps[k] = v
        for r in reads:
            w = self.last_w.get(r)
            if w:
                add(*w)
        for r in writes:
            w = self.last_w.get(r)
            if w:
                add(*w)
            for k, v in self.readers.get(r, {}).items():
                add(k, v)
        return deps

    def _commit(self, me, reads, writes):
        k, v = me
        for r in reads:
            self.readers.setdefault(r, {})[k] = v
        for r in writes:
            self.last_w[r] = me
            self.readers[r] = {}

    def op(self, eng, fn, reads=(), writes=()):
        deps = self._deps(reads, writes)
        waits = []
        for k, v in deps.items():
            if eng == "pe" and k == "pe":
                continue
            if self.seen[eng].get(k, 0) >= v:
                continue
            self.seen[eng][k] = v
            waits.append((k, v))
        self.cnt[eng] += 1
        me = (eng, self.cnt[eng])
        self.q[eng].append((waits, fn, eng, 1))
        self._commit(me, reads, writes)
        return me

    def dma(self, fn, reads=(), writes=(), eng="sp"):
        slot = (eng, self.dma_i[eng] % NDMA)
        self.dma_i[eng] += 1
        key = ("dma", slot)
        deps = self._deps(reads, writes)
        prev = self.dma_slot_val[slot]
        if prev:
            deps[key] = max(deps.get(key, 0), prev)
        waits = []
        for k, v in deps.items():
            if self.seen[eng].get(k, 0) >= v:
                continue
            self.seen[eng][k] = v
            waits.append((k, v))
        val = prev + 16
        self.dma_slot_val[slot] = val
        me = (key, val)
        self.q[eng].append((waits, fn, key, 16))
        self._commit(me, reads, writes)
        return me

    def barrier(self):
        targets = [(e, self.cnt[e]) for e in ENGS if self.cnt[e]]
        targets += [(("dma", s), v) for s, v in self.dma_slot_val.items() if v]
        for eng in ENGS:
            waits = []
            for k, v in targets:
                if self.seen[eng].get(k, 0) >= v:
                    continue
                self.seen[eng][k] = v
                waits.append((k, v))
            if waits:
                self.q[eng].append((waits, None, None, 0))

    def emit(self):
        nc = self.nc
        from contextlib import ExitStack
        with ExitStack() as st:
            sems = {}
            for e in ENGS:
                sems[e] = st.enter_context(nc.semaphore("s_" + e))
            for (e, i), v in self.dma_slot_val.items():
                if v:
                    sems[("dma", (e, i))] = st.enter_context(nc.semaphore("s_dma_%s%d" % (e, i)))
            block = st.enter_context(nc.Block())
            handles = {"pe": block.tensor, "act": block.scalar, "dve": block.vector,
                       "pool": block.gpsimd, "sp": block.sync}
            for e in ENGS:
                items = self.q[e]

                def body(h, items=items):
                    for waits, fn, incsem, incv in items:
                        for k, v in waits:
                            h.wait_ge(sems[k], v)
                        if fn is not None:
                            fn(h).then_inc(sems[incsem], incv)
                handles[e](body)


T = 8192
DM = 1024
LC = 256
NEXP = 16
FF = 2048
DEPTH = 2
ALPHA = (2.0 * DEPTH) ** 0.25
EPS = 1e-5
NCH = 19
CH_Q, CH_QR, CH_K, CH_KR, CH_V, CH_U, CH_A, CH_G = 0, 4, 8, 10, 12, 13, 15, 17


class Builder:
    def __init__(self, stop=None, dbg=False, nlayers=DEPTH):
        nc = bass.Bass("TRN2", target_bir_lowering=False)
        self.nc = nc
        self.P = Prog(nc)
        self.stop = stop
        self.dbg = dbg
        self.nlayers = nlayers
        self.dbg_outs = {}

        def di(name, shape, dt=F32):
            return nc.dram_tensor(name, list(shape), dt, kind="ExternalInput").ap()
        self.x_in = di("x", [T, DM])
        self.ctx_in = di("ctx", [LC, DM])
        self.ccol = di("ccol", [128, 8, 2])
        self.w_mod = di("w_mod", [DEPTH, DM, 6 * DM])
        self.b_mod = di("b_mod", [DEPTH, 6 * DM])
        self.w_in = di("w_in", [DEPTH, DM, 1536])
        self.b_in = di("b_in", [DEPTH, 1536])
        self.cosT = di("cosT", [128, T])
        self.sinT = di("sinT", [128, T])
        self.attn_sink = di("attn_sink", [DEPTH, 8])
        self.conv_w_col = di("conv_w_col", [DEPTH, 128, 2, 31])
        self.conv_vec_col = di("conv_vec_col", [DEPTH, 128, 2, 4])
        self.conv_w_pw = di("conv_w_pw", [DEPTH, 256, 256])
        self.w_out = di("w_out", [DEPTH, DM, DM])
        self.b_out = di("b_out", [DEPTH, DM])
        self.ln1_g = di("ln1_g", [DEPTH, DM])
        self.ln1_b = di("ln1_b", [DEPTH, DM])
        self.ln2_g = di("ln2_g", [DEPTH, DM])
        self.ln2_b = di("ln2_b", [DEPTH, DM])
        self.w_router = di("w_router", [DEPTH, DM, NEXP])
        self.s5nat = di("s5nat", [DEPTH, 128, 1072])
        self.s5_dcol = di("s5_dcol", [DEPTH, 128, 16])
        self.s5_w_glu = di("s5_w_glu", [DEPTH, 256, 256])
        self.s5_bglu_col = di("s5_bglu_col", [DEPTH, 128, 2])
        self.w_gate = di("exp_w_gate", [DEPTH, NEXP, DM, FF])
        self.w_up = di("exp_w_up", [DEPTH, NEXP, DM, FF])
        self.w_down = di("exp_w_down", [DEPTH, NEXP, FF, DM])
        self.out = nc.dram_tensor("out", [T, DM], F32, kind="ExternalOutput").ap()

        def dscr(name, shape, dt=F32):
            return nc.dram_tensor(name, list(shape), dt, kind="Internal").ap()
        self.dscr = dscr
        self.modrows = dscr("modrows", [DEPTH, 2, 6 * DM])
        self.qT_d = dscr("qT_d", [512, T], BF16)
        self.kT_d = dscr("kT_d", [256, T], BF16)
        self.v_d = dscr("v_d", [T, 130], BF16)
        self.uT_d = dscr("uT_d", [256, T], BF16)
        self.hT_d = dscr("hT_d", [256, T], BF16)
        self.qcT_d = dscr("qcT_d", [512, LC], BF16)
        self.kcT_d = dscr("kcT_d", [256, LC], BF16)
        self.vc_d = dscr("vc_d", [LC, 130], BF16)
        self.ucT_d = dscr("ucT_d", [256, LC], BF16)
        self.hcT_d = dscr("hcT_d", [256, LC], BF16)
        self.catT_d = dscr("catT_d", [DM, T], BF16)
        self.catcT_d = dscr("catcT_d", [DM, LC], BF16)
        self.xmid_d = dscr("xmid_d", [T, DM], F32)
        self.xcmid_d = dscr("xcmid_d", [LC, DM], F32)
        self.h2_d = dscr("h2_d", [T, DM], BF16)
        self.h2c_d = dscr("h2c_d", [LC, DM], BF16)
        self.moe_d = dscr("moe_d", [T, DM], F32)
        self.moec_d = dscr("moec_d", [LC, DM], F32)
        self.x1_d = dscr("x1_d", [T, DM], F32)
        self.ssh_d = dscr("ssh_d", [2, 8, 2, 128, T // 8], BF16)
        self.xc1_d = dscr("xc1_d", [LC, DM], F32)

        self.ps = [nc.alloc_psum_tensor("ps%d" % i, [128, 512], F32) for i in range(8)]
        self.PS = ["ps%d" % i for i in range(8)]
        self.consts()

    def dbg_out(self, name, shape, dt=F32):
        t = self.nc.dram_tensor("dbg_" + name, list(shape), dt, kind="ExternalOutput").ap()
        self.dbg_outs[name] = t
        return t

    def consts(self):
        P = self.P
        self.identf = P.sb("identf", [128, 128], F32)
        self.ident = P.sb("ident", [128, 128], BF16)
        self.ones_bf = P.sb("ones", [128, 128], BF16)
        P.op("pool", lambda e: e.memset(self.identf[:], 1.0), writes=["identf"])
        P.op("pool", lambda e: e.affine_select(out=self.identf[:], in_=self.identf[:], pattern=[[-1, 128]],
                                               compare_op=ALU.is_equal, fill=0.0, base=0, channel_multiplier=1),
             reads=["identf"], writes=["identf"])
        P.op("dve", lambda e: e.tensor_copy(out=self.ident[:], in_=self.identf[:]), reads=["identf"], writes=["ident"])
        P.op("pool", lambda e: e.memset(self.ones_bf[:], 1.0), writes=["ones"])

    def mm(self, out, lhsT, rhs, start, stop, reads, writes):
        self.P.op("pe", lambda e: e.matmul(out=out, lhsT=lhsT, rhs=rhs, start=start, stop=stop), reads, writes)

    def phase_mod(self, l):
        P = self.P
        P.mark()
        sc = P.sb("sc", [128, 8, 2], F32)
        scb = P.sb("scb", [128, 8, 2], BF16)
        P.dma(lambda e: e.dma_start(out=sc[:], in_=self.ccol), writes=["sc"])
        P.op("act", lambda e: e.activation(out=scb[:], in_=sc[:], func=AF.Silu), reads=["sc"], writes=["scb"])
        bm = P.sb("bm", [2, 6 * DM], F32)
        P.dma(lambda e: e.dma_start(out=bm[:], in_=self.b_mod[l].partition_broadcast(2)), writes=["bm"])
        mods = P.sb("mods", [2, 6 * DM], F32)
        wts = [P.sb("wmt%d" % i, [128, 8, 512], BF16) for i in range(2)]
        for ch in range(12):
            wt = wts[ch % 2]
            wn = "wmt%d" % (ch % 2)
            src = self.w_mod[l][:, ch * 512:(ch + 1) * 512].rearrange("(k p) n -> p k n", p=128)
            P.dma(lambda e, wt=wt, src=src: e.dma_start(out=wt[:], in_=src), writes=[wn], eng="pool")
            pb = ch % 2
            for k in range(8):
                self.mm(self.ps[pb][0:2, :], scb[:, k, :], wt[:, k, :], k == 0, k == 7, [wn, "scb"], [self.PS[pb]])
            P.op("dve", lambda e, ch=ch, pb=pb: e.tensor_tensor(out=mods[:, ch * 512:(ch + 1) * 512], in0=self.ps[pb][0:2, :],
                                                               in1=bm[:, ch * 512:(ch + 1) * 512], op=ALU.add),
                 reads=[self.PS[pb], "bm"], writes=["mods"])
        P.dma(lambda e: e.dma_start(out=self.modrows[l], in_=mods[:]), reads=["mods"], writes=["modrows"])
        P.release()

    def load_w_in(self, l):
        P = self.P
        self.w9 = P.sb("w9", [128, 9, 1536], BF16)
        w9 = self.w9
        P.op("pool", lambda e: e.memset(w9[:, 8, :], 0.0), writes=["w9"])
        for k0 in range(0, 8, 2):
            src = self.w_in[l][k0 * 128:(k0 + 2) * 128, :].rearrange("(k p) n -> p k n", p=128)
            P.dma(lambda e, k0=k0, src=src: e.dma_start(out=w9[:, k0:k0 + 2, :], in_=src), writes=["w9"], eng="pool")
        P.dma(lambda e: e.dma_start(out=w9[0:1, 8, :], in_=self.b_in[l:l + 1, :]), writes=["w9"], eng="pool")
        self.wext = P.sb("wext", [128, 9, NCH * 128], BF16)
        self.bext = P.sb("bext", [128, NCH], F32)

    def build_wext(self, l, var):
        P = self.P
        w9, wext = self.w9, self.wext
        P.mark()
        P.op("act", lambda e: e.activation(out=wext[:, :, 0:512], in_=w9[:, :, 0:512], func=AF.Copy), reads=["w9"], writes=["wext"])
        P.op("dve", lambda e: e.tensor_copy(out=wext[:, :, 12 * 128:19 * 128], in_=w9[:, :, 640:1536]), reads=["w9"], writes=["wext"])
        kd = wext[:, :, 1024:1280].rearrange("p k (h d f) -> p k h d f", h=2, d=2)
        ks = w9[:, :, 512:640].rearrange("p k (h f) -> p k h f", h=2)
        for d in range(2):
            P.op("pool", lambda e, d=d: e.tensor_copy(out=kd[:, :, :, d, :], in_=ks), reads=["w9"], writes=["wext"])
        for (dst0, src0, n) in ((512, 0, 512), (1280, 1024, 256)):
            dv = wext[:, :, dst0:dst0 + n].rearrange("p k (a s f) -> p k a s f", s=2, f=16)
            sv = wext[:, :, src0:src0 + n].rearrange("p k (a s f) -> p k a s f", s=2, f=16)
            P.op("act", lambda e, dv=dv, sv=sv: e.mul(out=dv[:, :, :, 0, :], in_=sv[:, :, :, 1, :], mul=-1.0),
                 reads=["wext"], writes=["wext"])
            P.op("dve", lambda e, dv=dv, sv=sv: e.tensor_copy(out=dv[:, :, :, 1, :], in_=sv[:, :, :, 0, :]),
                 reads=["wext"], writes=["wext"])
        sh = P.sb("sh", [128, 8], F32)
        scl = P.sb("scl", [128, 8], F32)
        P.dma(lambda e: e.dma_start(out=sh[:], in_=self.modrows[l, var, 0:1024].rearrange("(k p) -> p k", p=128), allow_slow_non_contiguous=True),
              reads=["modrows"], writes=["sh"])
        P.dma(lambda e: e.dma_start(out=scl[:], in_=self.modrows[l, var, 1024:2048].rearrange("(k p) -> p k", p=128), allow_slow_non_contiguous=True),
              reads=["modrows"], writes=["scl"])
        sha = P.sb("sha", [128, 9], BF16)
        P.op("pool", lambda e: e.memset(sha[:], 0.0), writes=["sha"])
        P.op("pool", lambda e: e.memset(sha[0:1, 8:9], 1.0), reads=["sha"], writes=["sha"])
        P.op("dve", lambda e: e.tensor_copy(out=sha[:, 0:8], in_=sh[:]), reads=["sh", "sha"], writes=["sha"])
        P.op("dve", lambda e: e.tensor_scalar_add(out=scl[:], in0=scl[:], scalar1=1.0), reads=["scl"], writes=["scl"])
        for c in range(NCH):
            for k in range(9):
                self.mm(self.ps[7][:, c:c + 1], wext[:, k, c * 128:(c + 1) * 128], sha[:, k:k + 1], k == 0, k == 8,
                        ["wext", "sha"], [self.PS[7]])
        P.op("dve", lambda e: e.tensor_copy(out=self.bext[:], in_=self.ps[7][:, 0:NCH]), reads=[self.PS[7]], writes=["bext"])
        for k in range(8):
            if k % 2 == 0:
                P.op("dve", lambda e, k=k: e.tensor_scalar(out=wext[:, k, :], in0=wext[:, k, :], scalar1=scl[:, k:k + 1], scalar2=None, op0=ALU.mult),
                     reads=["wext", "scl", self.PS[7]], writes=["wext"])
            else:
                P.op("act", lambda e, k=k: e.activation(out=wext[:, k, :], in_=wext[:, k, :], func=AF.Copy, scale=scl[:, k:k + 1]),
                     reads=["wext", "scl", self.PS[7]], writes=["wext"])
        P.release()

    def ln_tile_T(self, src_rows, src_res, xnT, xnT_res, col0, bufi):
        P = self.P
        xt, xn, st, mv, rs = self.ln_bufs[bufi]
        tg = "ln%d" % bufi
        P.dma(lambda e: e.dma_start(out=xt[:], in_=src_rows), reads=[src_res], writes=[tg + "xt"])
        for hh in range(2):
            P.op("dve", lambda e, hh=hh: e.bn_stats(out=st[:, hh, :], in_=xt[:, hh * 512:(hh + 1) * 512]), reads=[tg + "xt"], writes=[tg + "st"])
        P.op("dve", lambda e: e.bn_aggr(out=mv[:], in_=st[:].rearrange("p a b -> p (a b)")), reads=[tg + "st"], writes=[tg + "mv"])
        eps_t = self.eps_t
        P.op("act", lambda e: e.activation(out=rs[:], in_=mv[:, 1:2], func=AF.Sqrt, bias=eps_t[:, 0:1], scale=1.0), reads=[tg + "mv", "eps"], writes=[tg + "rs"])
        P.op("dve", lambda e: e.reciprocal(out=rs[:], in_=rs[:]), reads=[tg + "rs"], writes=[tg + "rs"])
        P.op("dve", lambda e: e.tensor_scalar(out=xn[:], in0=xt[:], scalar1=mv[:, 0:1], scalar2=rs[:, 0:1], op0=ALU.subtract, op1=ALU.mult),
             reads=[tg + "xt", tg + "mv", tg + "rs"], writes=[tg + "xn"])
        pb = 6 + bufi
        psT = self.ps[pb][:, :].bitcast(BF16)
        for k in range(8):
            P.op("pe", lambda e, k=k: e.transpose(out=psT[:, k * 128:(k + 1) * 128], in_=xn[:, k * 128:(k + 1) * 128], identity=self.ident[:]),
                 reads=[tg + "xn", "ident"], writes=[self.PS[pb]])
        P.op("act", lambda e: e.activation(out=xnT[:, :, col0:col0 + 128], in_=psT.rearrange("p (k t) -> p k t", k=8), func=AF.Copy),
             reads=[self.PS[pb]], writes=[xnT_res])

    def alloc_ln_bufs(self):
        P = self.P
        self.eps_t = eps_t = P.sb("eps", [128, 1], F32)
        P.op("pool", lambda e: e.memset(eps_t[:], EPS), writes=["eps"])
        self.ln_bufs = []
        for i in range(2):
            self.ln_bufs.append((P.sb("xt%d" % i, [128, DM], F32), P.sb("xn%d" % i, [128, DM], BF16), P.sb("st%d" % i, [128, 2, 6], F32),
                                 P.sb("mv%d" % i, [128, 2], F32), P.sb("rs%d" % i, [128, 1], F32)))

    def inproj_ln(self, src, src_res, t0, N, xnT, xres):
        for i in range(N // 128):
            self.ln_tile_T(src[t0 + i * 128:t0 + (i + 1) * 128, :], src_res, xnT, xres, i * 128, i % 2)

    def inproj_proj(self, l, t0, N, is_ctx, need_full, xnT, xres, blk):
        P = self.P
        nt = N // 128
        wext, bext = self.wext, self.bext
        bank, obc, tfc = self._bank, self._obc, self._tfc

        def proj(c):
            pb = bank[0]
            bank[0] = (pb + 1) % 5
            for k in range(8):
                self.mm(self.ps[pb][:, 0:N], wext[:, k, c * 128:(c + 1) * 128], xnT[:, k, 0:N], k == 0, k == 7, ["wext", xres], [self.PS[pb]])
            return pb

        def nob():
            i = obc[0] % len(self.ev_bf)
            obc[0] += 1
            return self.ev_bf[i], "evbf%d" % i

        def ntf():
            i = tfc[0] % len(self.ev_f)
            tfc[0] += 1
            return self.ev_f[i], "evf%d" % i
        cs, sn = self.cs_ts[blk % 2], self.sn_ts[blk % 2]
        csn, snn = "cs%d" % (blk % 2), "sn%d" % (blk % 2)
        if not is_ctx:
            P.dma(lambda e: e.dma_start(out=cs[:, 0:N], in_=self.cosT[:, t0:t0 + N]), writes=[csn])
            P.dma(lambda e: e.dma_start(out=sn[:, 0:N], in_=self.sinT[:, t0:t0 + N]), writes=[snn])
        qk = []
        if need_full:
            qk += [(c, CH_QR + c, (self.qcT_d if is_ctx else self.qT_d)[c * 128:(c + 1) * 128, t0:t0 + N]) for c in range(4)]
        qk += [(CH_K + c, CH_KR + c, (self.kcT_d if is_ctx else self.kT_d)[c * 128:(c + 1) * 128, t0:t0 + N]) for c in range(2)]
        for j, (c, cr, dst) in enumerate(qk):
            ob, on = nob()
            if is_ctx:
                pb = proj(c)
                P.op("act", lambda e, c=c, ob=ob, pb=pb: e.activation(out=ob[:, 0:N], in_=self.ps[pb][:, 0:N], func=AF.Identity, bias=bext[:, c:c + 1], scale=1.0),
                     reads=[self.PS[pb], "bext"], writes=[on])
            else:
                p0 = proj(c)
                p1 = proj(cr)
                t1, t1n = ntf()
                t2, t2n = ntf()
                P.op("dve", lambda e, c=c, p0=p0, t1=t1: e.scalar_tensor_tensor(out=t1[:, 0:N], in0=self.ps[p0][:, 0:N], scalar=bext[:, c:c + 1], in1=cs[:, 0:N], op0=ALU.add, op1=ALU.mult),
                     reads=[self.PS[p0], "bext", csn], writes=[t1n])
                P.op("dve", lambda e, cr=cr, p1=p1, t2=t2: e.scalar_tensor_tensor(out=t2[:, 0:N], in0=self.ps[p1][:, 0:N], scalar=bext[:, cr:cr + 1], in1=sn[:, 0:N], op0=ALU.add, op1=ALU.mult),
                     reads=[self.PS[p1], "bext", snn], writes=[t2n])
                P.op("pool", lambda e, ob=ob, t1=t1, t2=t2: e.tensor_tensor(out=ob[:, 0:N], in0=t1[:, 0:N], in1=t2[:, 0:N], op=ALU.add), reads=[t1n, t2n], writes=[on])
            P.dma(lambda e, dst=dst, ob=ob: e.dma_start(out=dst, in_=ob[:, 0:N]), reads=[on])
        pv = proj(CH_V)
        vb, vbn = nob()
        P.op("act", lambda e: e.activation(out=vb[:, 0:N], in_=self.ps[pv][:, 0:N], func=AF.Identity, bias=bext[:, CH_V:CH_V + 1], scale=1.0),
             reads=[self.PS[pv], "bext"], writes=[vbn])
        pt = 5
        psT = self.ps[pt][:, :].bitcast(BF16)
        for i in range(nt):
            P.op("pe", lambda e, i=i: e.transpose(out=psT[:, i * 128:(i + 1) * 128], in_=vb[:, i * 128:(i + 1) * 128], identity=self.ident[:]),
                 reads=[vbn, "ident"], writes=[self.PS[pt]])
        va = self.v_augs[blk % 2]
        van = "vaug%d" % (blk % 2)
        P.op("dve", lambda e: e.tensor_copy(out=va[:, 0:nt, :, 0:64], in_=psT[:, 0:nt * 128].rearrange("p (i h f) -> p i h f", i=nt, h=2)),
             reads=[self.PS[pt]], writes=[van])
        vdst = (self.vc_d if is_ctx else self.v_d)[t0:t0 + N, :].rearrange("(i p) f -> p i f", p=128)
        P.dma(lambda e: e.dma_start(out=vdst, in_=va[:, 0:nt, :, :].rearrange("p i h f -> p i (h f)")), reads=[van])
        for c in range(2):
            pu = proj(CH_U + c)
            ob, on = nob()
            P.op("act", lambda e, c=c, ob=ob, pu=pu: e.activation(out=ob[:, 0:N], in_=self.ps[pu][:, 0:N], func=AF.Identity, bias=bext[:, CH_U + c:CH_U + c + 1], scale=1.0),
                 reads=[self.PS[pu], "bext"], writes=[on])
            dst = (self.ucT_d if is_ctx else self.uT_d)[c * 128:(c + 1) * 128, t0:t0 + N]
            P.dma(lambda e, dst=dst, ob=ob: e.dma_start(out=dst, in_=ob[:, 0:N]), reads=[on])
        if need_full:
            for c in range(2):
                pa = proj(CH_A + c)
                pg = proj(CH_G + c)
                sg, sgn = ntf()
                ob, on = nob()
                P.op("act", lambda e, c=c, pg=pg, sg=sg: e.activation(out=sg[:, 0:N], in_=self.ps[pg][:, 0:N], func=AF.Sigmoid, bias=bext[:, CH_G + c:CH_G + c + 1], scale=1.0),
                     reads=[self.PS[pg], "bext"], writes=[sgn])
                P.op("dve", lambda e, c=c, pa=pa, sg=sg, ob=ob: e.scalar_tensor_tensor(out=ob[:, 0:N], in0=self.ps[pa][:, 0:N], scalar=bext[:, CH_A + c:CH_A + c + 1], in1=sg[:, 0:N], op0=ALU.add, op1=ALU.mult),
                     reads=[self.PS[pa], "bext", sgn], writes=[on])
                dst = (self.hcT_d if is_ctx else self.hT_d)[c * 128:(c + 1) * 128, t0:t0 + N]
                P.dma(lambda e, dst=dst, ob=ob: e.dma_start(out=dst, in_=ob[:, 0:N]), reads=[on])

    def phase_inproj(self, l, is_ctx, need_full):
        P = self.P
        P.mark()
        self.alloc_ln_bufs()
        xnTs = [P.sb("xnT%d" % i, [128, 8, 512], BF16) for i in range(2)]
        self.cs_ts = [P.sb("cs%d" % i, [128, 512], F32) for i in range(2)]
        self.sn_ts = [P.sb("sn%d" % i, [128, 512], F32) for i in range(2)]
        self.ev_f = [P.sb("evf%d" % i, [128, 512], F32) for i in range(6)]
        self.ev_bf = [P.sb("evbf%d" % i, [128, 512], BF16) for i in range(8)]
        self.v_augs = [P.sb("vaug%d" % i, [128, 4, 2, 65], BF16) for i in range(2)]
        for i, va in enumerate(self.v_augs):
            P.op("pool", lambda e, va=va: e.memset(va[:], 1.0), writes=["vaug%d" % i])
        self._bank, self._obc, self._tfc = [0], [0], [0]
        if is_ctx:
            self.inproj_ln(self.xc_src, "xc", 0, LC, xnTs[0], "xnT0")
            self.inproj_proj(l, 0, LC, True, need_full, xnTs[0], "xnT0", 0)
        else:
            nb_ = int(os.environ.get('DBG_NBLK', T // 512))
            self.inproj_ln(self.x_src, "xsrc", 0, 512, xnTs[0], "xnT0")
            for b in range(nb_):
                if b + 1 < nb_:
                    self.inproj_ln(self.x_src, "xsrc", (b + 1) * 512, 512, xnTs[(b + 1) % 2], "xnT%d" % ((b + 1) % 2))
                self.inproj_proj(l, b * 512, 512, False, True, xnTs[b % 2], "xnT%d" % (b % 2), b)
        P.release()

    def attn_consts(self, l):
        P = self.P
        mf = P.sb("mf", [128, 128], F32)
        self.mlo = P.sb("mlo", [128, 128], BF16)
        self.mhi = P.sb("mhi", [128, 128], BF16)
        for (m, name, sign) in ((self.mlo, "mlo", 1), (self.mhi, "mhi", -1)):
            P.op("pool", lambda e: e.memset(mf[:], 1.0), writes=["mf"])
            P.op("pool", lambda e, sign=sign: e.affine_select(out=mf[:], in_=mf[:], pattern=[[-sign, 128]], compare_op=ALU.is_ge, fill=0.0,
                                                               base=0, channel_multiplier=sign), reads=["mf"], writes=["mf"])
            P.op("dve", lambda e, m=m: e.tensor_copy(out=m[:], in_=mf[:]), reads=["mf"], writes=[name])
        self.esink = es = P.sb("esink", [128, 8], F32)
        P.dma(lambda e: e.dma_start(out=es[:], in_=self.attn_sink[l].partition_broadcast(128)), writes=["esink"])
        P.op("act", lambda e: e.activation(out=es[:], in_=es[:], func=AF.Exp), reads=["esink"], writes=["esink"])

    def attn_qtile(self, qblk, qres, qc0, keys, stage, stage_res, sti):
        P = self.P
        nk = len(keys)
        if int(os.environ.get("DBG_Q", 99)) < 0:
            return
        at = self.at_tile
        esink = self.esink
        for h in range(2):
            for ki, (kf, kres, vf, vres, mask) in enumerate(keys):
                for hh in range(4):
                    c, s = 2 * h + hh // 2, hh % 2
                    self.mm(self.ps[ki][:, hh * 128:(hh + 1) * 128], kf(h, s), qblk[:, c, qc0:qc0 + 128],
                            True, True, [kres, qres], [self.PS[ki]])
            pT = self.pT
            dq = int(os.environ.get("DBG_Q", 99))
            if dq == 0:
                continue
            for ki, (kf, kres, vf, vres, mask) in enumerate(keys):
                P.op("act", lambda e, ki=ki: e.activation(out=pT[:, ki, :], in_=self.ps[ki][:, :], func=AF.Exp, scale=0.125),
                     reads=[self.PS[ki]], writes=["pT%d" % ki])
                if mask is not None:
                    P.op("pool", lambda e, ki=ki, mask=mask: e.tensor_tensor(out=pT[:, ki, :].rearrange("p (a i) -> p a i", a=4), in0=pT[:, ki, :].rearrange("p (a i) -> p a i", a=4),
                                                                          in1=mask[:, :].unsqueeze(1).to_broadcast([128, 4, 128]), op=ALU.mult),
                         reads=["pT%d" % ki, "mlo", "mhi"], writes=["pT%d" % ki])
            if dq == 1:
                continue
            for hh in range(4):
                for ki, (kf, kres, vf, vres, mask) in enumerate(keys):
                    self.mm(self.ps[5][:, hh * 65:(hh + 1) * 65], pT[:, ki, hh * 128:(hh + 1) * 128], vf(h), ki == 0, ki == nk - 1,
                            ["pT%d" % ki, vres], [self.PS[5]])
            if dq == 2:
                continue
            den = self.den
            ov = self.ps[5][:, 0:260].rearrange("p (a f) -> p a f", a=4)
            P.op("dve", lambda e, h=h: e.tensor_tensor(out=den[:], in0=ov[:, :, 64], in1=esink[:, 4 * h:4 * h + 4], op=ALU.add),
                 reads=[self.PS[5], "esink"], writes=["den"])
            P.op("dve", lambda e: e.reciprocal(out=den[:], in_=den[:]), reads=["den"], writes=["den"])
            P.op("dve", lambda e, h=h: e.tensor_tensor(out=at[:, 256 * h:256 * (h + 1)].rearrange("p (a f) -> p a f", a=4), in0=ov[:, :, 0:64],
                                                      in1=den[:, :].unsqueeze(2).to_broadcast([128, 4, 64]), op=ALU.mult),
                 reads=[self.PS[5], "den"], writes=["at"])
        if int(os.environ.get("DBG_Q", 99)) <= 3:
            return
        psT = self.ps[6][:, :].bitcast(BF16)
        for c in range(4):
            P.op("pe", lambda e, c=c: e.transpose(out=psT[:, c * 128:(c + 1) * 128], in_=at[:, c * 128:(c + 1) * 128], identity=self.ident[:]),
                 reads=["at", "ident"], writes=[self.PS[6]])
        P.op("act", lambda e: e.activation(out=stage[:, :, sti * 128:(sti + 1) * 128], in_=psT[:, 0:512].rearrange("p (c t) -> p c t", c=4), func=AF.Copy),
             reads=[self.PS[6]], writes=[stage_res])

    def phase_attn(self, l, do_ctx):
        P = self.P
        P.mark()
        self.attn_consts(l)
        dbgA = int(os.environ.get("DBG_A", 99))
        if dbgA == 0:
            P.release()
            return
        self.pT = P.sb("pT", [128, 5, 512], BF16)
        self.den = P.sb("den", [128, 4], F32)
        self.at_tile = P.sb("at", [128, 512], BF16)
        kc = P.sb("kc", [128, 2, 2, LC], BF16)
        vc = P.sb("vc", [128, 2, 130], BF16)
        for s_ in range(2):
            P.dma(lambda e, s_=s_: e.dma_start(out=kc[:, s_, :, :], in_=self.kcT_d.rearrange("(h p) t -> p h t", p=128)), writes=["kc"])
            P.op("pool", lambda e, s_=s_: e.memset(kc[64 * (1 - s_):64 * (2 - s_), s_, :, :], 0.0), reads=["kc"], writes=["kc"])
        P.dma(lambda e: e.dma_start(out=vc[:], in_=self.vc_d.rearrange("(i p) f -> p i f", p=128)), writes=["vc"])
        ckeys = [(lambda h, s_, i=i: kc[:, s_, h, i * 128:(i + 1) * 128], "kc", lambda h, i=i: vc[:, i, h * 65:(h + 1) * 65], "vc", None) for i in range(2)]
        stages = [P.sb("ast%d" % i, [128, 4, 512], BF16) for i in range(2)]
        if dbgA == 1:
            P.release()
            return
        if do_ctx:
            qb = P.sb("qcb", [128, 4, LC], BF16)
            P.dma(lambda e, qb=qb: e.dma_start(out=qb[:], in_=self.qcT_d.rearrange("(c p) t -> p c t", p=128)), writes=["qcb"])
            for n in range(2):
                self.attn_qtile(qb, "qcb", n * 128, ckeys, stages[0], "ast0", n)
            P.dma(lambda e: e.dma_start(out=self.catcT_d[0:512, :].rearrange("(c p) t -> p c t", p=128), in_=stages[0][:, :, 0:LC]), reads=["ast0"])
        if dbgA == 2:
            P.release()
            return
        kx = P.sb("kx", [128, 2, 2, T], BF16)
        vx = P.sb("vx", [128, T // 128, 130], BF16)
        for s_ in range(2):
            for h in range(2):
                P.dma(lambda e, h=h, s_=s_: e.dma_start(out=kx[:, s_, h, :], in_=self.kT_d[h * 128:(h + 1) * 128, :]), writes=["kx"])
                P.op("pool", lambda e, h=h, s_=s_: e.memset(kx[64 * (1 - s_):64 * (2 - s_), s_, h, :], 0.0), reads=["kx"], writes=["kx"])
        for i0 in range(0, T // 128, 8):
            P.dma(lambda e, i0=i0: e.dma_start(out=vx[:, i0:i0 + 8, :], in_=self.v_d[i0 * 128:(i0 + 8) * 128, :].rearrange("(i p) f -> p i f", p=128)), writes=["vx"])
        qbs = [P.sb("qb%d" % i, [128, 4, 512], BF16) for i in range(2)]
        for b in range(int(os.environ.get('DBG_NBLK', T // 512))):
            qb, qn = qbs[b % 2], "qb%d" % (b % 2)
            P.dma(lambda e, qb=qb, b=b: e.dma_start(out=qb[:], in_=self.qT_d[:, b * 512:(b + 1) * 512].rearrange("(c p) t -> p c t", p=128)), writes=[qn])
            st, sn = stages[b % 2], "ast%d" % (b % 2)
            for i in range(4):
                n = 4 * b + i
                keys = []
                for (kt, mask) in ((n - 1, self.mlo), (n, None), (n + 1, self.mhi)):
                    if 0 <= kt < T // 128:
                        keys.append((lambda h, s_, kt=kt: kx[:, s_, h, kt * 128:(kt + 1) * 128], "kx", lambda h, kt=kt: vx[:, kt, h * 65:(h + 1) * 65], "vx", mask))
                self.attn_qtile(qb, qn, i * 128, keys + ckeys, st, sn, i)
            P.dma(lambda e, st=st, b=b: e.dma_start(out=self.catT_d[0:512, b * 512:(b + 1) * 512].rearrange("(c p) t -> p c t", p=128), in_=st[:]), reads=[sn])
        P.release()

    def phase_conv(self, l, is_ctx):
        P = self.P
        P.mark()
        N = LC if is_ctx else T
        src = self.hcT_d if is_ctx else self.hT_d
        dstT = self.catcT_d if is_ctx else self.catT_d
        wcol = P.sb("cw", [128, 2, 31], F32)
        bcol = P.sb("cb", [128, 2, 4], F32)
        P.dma(lambda e: e.dma_start(out=wcol[:], in_=self.conv_w_col[l]), writes=["cw"])
        P.dma(lambda e: e.dma_start(out=bcol[:], in_=self.conv_vec_col[l]), writes=["cb"])
        wpw = P.sb("wpw", [128, 2, 256], BF16)
        P.dma(lambda e: e.dma_start(out=wpw[:], in_=self.conv_w_pw[l].rearrange("(k p) n -> p k n", p=128)), writes=["wpw"], eng="pool")
        diag = P.sb("diag", [128, 2, 31, 128], BF16)
        for cc in range(2):
            for k in range(31):
                if k % 2 == 0:
                    P.op("dve", lambda e, cc=cc, k=k: e.tensor_scalar(out=diag[:, cc, k, :], in0=self.identf[:], scalar1=wcol[:, cc, k:k + 1], scalar2=None, op0=ALU.mult),
                         reads=["identf", "cw"], writes=["diag"])
                else:
                    P.op("act", lambda e, cc=cc, k=k: e.activation(out=diag[:, cc, k, :], in_=self.identf[:], func=AF.Copy, scale=wcol[:, cc, k:k + 1]),
                         reads=["identf", "cw"], writes=["diag"])
        avg = P.sb("avg", [128, 128], BF16)
        P.op("pool", lambda e: e.memset(avg[:], 1.0 / 256.0), writes=["avg"])
        hp = P.sb("hp", [128, 2, N + 30], BF16)
        P.op("pool", lambda e: e.memset(hp[:, :, 0:15], 0.0), writes=["hp"])
        P.op("pool", lambda e: e.memset(hp[:, :, N + 15:N + 30], 0.0), reads=["hp"], writes=["hp"])
        for cc in range(2):
            P.dma(lambda e, cc=cc: e.dma_start(out=hp[:, cc, 15:15 + N], in_=src[cc * 128:(cc + 1) * 128, :]), reads=["hp"], writes=["hp"])
        BW = min(512, N)
        hcs = [P.sb("hcv%d" % i, [128, 2, BW], F32) for i in range(2)]
        hcbs = [P.sb("hcb%d" % i, [128, 2, BW], BF16) for i in range(2)]
        sqs = [P.sb("sq%d" % i, [128, 2, BW], BF16) for i in range(2)]
        m2 = P.sb("m2", [128, BW], F32)
        rstd = P.sb("rstd", [128, BW], F32)
        tts = [P.sb("tt%d" % i, [128, BW], F32) for i in range(2)]
        hn = P.sb("hn", [128, 2, BW], BF16)
        ob = [P.sb("cob%d" % i, [128, BW], BF16) for i in range(2)]
        eps_t = P.sb("ceps", [128, 1], F32)
        P.op("pool", lambda e: e.memset(eps_t[:], EPS), writes=["ceps"])
        nb = N // BW
        if not is_ctx:
            nb = int(os.environ.get("DBG_NBLK", nb))

        def SA(b):
            t0 = b * BW
            j = b % 2
            hc, hcb, sq = hcs[j], hcbs[j], sqs[j]
            for cc in range(2):
                pb = 2 * j + cc
                for k in range(31):
                    self.mm(self.ps[pb][:, 0:BW], diag[:, cc, k, :], hp[:, cc, t0 + k:t0 + k + BW], k == 0, k == 30, ["diag", "hp"], [self.PS[pb]])
                P.op("act", lambda e, cc=cc, pb=pb: e.activation(out=hc[:, cc, :], in_=self.ps[pb][:, 0:BW], func=AF.Identity, bias=bcol[:, cc, 0:1], scale=1.0),
                     reads=[self.PS[pb], "cb"], writes=["hcv%d" % j])
                P.op("act", lambda e, cc=cc, pb=pb: e.activation(out=sq[:, cc, :], in_=self.ps[pb][:, 0:BW], func=AF.Square, bias=bcol[:, cc, 0:1], scale=1.0),
                     reads=[self.PS[pb], "cb"], writes=["sq%d" % j])
                P.op("pool", lambda e, cc=cc: e.tensor_copy(out=hcb[:, cc, :], in_=hc[:, cc, :]), reads=["hcv%d" % j], writes=["hcb%d" % j])

        def SB(b):
            t0 = b * BW
            j = b % 2
            hc, hcb, sq = hcs[j], hcbs[j], sqs[j]
            for cc in range(2):
                self.mm(self.ps[4][:, 0:BW], avg[:], hcb[:, cc, :], cc == 0, cc == 1, ["avg", "hcb%d" % j], [self.PS[4]])
            for cc in range(2):
                self.mm(self.ps[5][:, 0:BW], avg[:], sq[:, cc, :], cc == 0, cc == 1, ["avg", "sq%d" % j], [self.PS[5]])
            P.op("act", lambda e: e.activation(out=m2[:], in_=self.ps[4][:, 0:BW], func=AF.Square), reads=[self.PS[4]], writes=["m2"])
            P.op("dve", lambda e: e.tensor_tensor(out=rstd[:], in0=self.ps[5][:, 0:BW], in1=m2[:], op=ALU.subtract), reads=[self.PS[5], "m2"], writes=["rstd"])
            P.op("act", lambda e: e.activation(out=rstd[:], in_=rstd[:], func=AF.Sqrt, bias=eps_t[:, 0:1], scale=1.0), reads=["rstd", "ceps"], writes=["rstd"])
            P.op("dve", lambda e: e.reciprocal(out=rstd[:], in_=rstd[:]), reads=["rstd"], writes=["rstd"])
            for cc in range(2):
                tt = tts[cc]
                P.op("dve", lambda e, cc=cc, tt=tt: e.tensor_tensor(out=tt[:], in0=hc[:, cc, :], in1=self.ps[4][:, 0:BW], op=ALU.subtract), reads=["hcv%d" % j, self.PS[4]], writes=["tt%d" % cc])
                P.op("dve", lambda e, tt=tt: e.tensor_tensor(out=tt[:], in0=tt[:], in1=rstd[:], op=ALU.mult), reads=["tt%d" % cc, "rstd"], writes=["tt%d" % cc])
                P.op("act", lambda e, cc=cc, tt=tt: e.activation(out=hn[:, cc, :], in_=tt[:], func=AF.Silu, bias=bcol[:, cc, 2:3], scale=bcol[:, cc, 1:2]),
                     reads=["tt%d" % cc, "cb"], writes=["hn"])
            for co in range(2):
                for cc in range(2):
                    self.mm(self.ps[6 + co][:, 0:BW], wpw[:, cc, co * 128:(co + 1) * 128], hn[:, cc, :], cc == 0, cc == 1, ["wpw", "hn"], [self.PS[6 + co]])
                o = ob[co]
                P.op("act", lambda e, co=co, o=o: e.activation(out=o[:], in_=self.ps[6 + co][:, 0:BW], func=AF.Identity, bias=bcol[:, co, 3:4], scale=1.0),
                     reads=[self.PS[6 + co], "cb"], writes=["cob%d" % co])
                P.dma(lambda e, co=co, o=o: e.dma_start(out=dstT[768 + co * 128:768 + (co + 1) * 128, t0:t0 + BW], in_=o[:]), reads=["cob%d" % co])
        if nb:
            SA(0)
        for b in range(nb):
            if b + 1 < nb:
                SA(b + 1)
            SB(b)
        P.release()

    def row_tile(self, name, src_row):
        t = self.P.sb(name, [128, DM], F32)
        self.P.dma(lambda e: e.dma_start(out=t[:], in_=src_row.partition_broadcast(128)), reads=["modrows"], writes=[name])
        return t

    def ln_rows(self, xt, xres, outt, ores, tg):
        P = self.P
        st, mv, rs = self.lnr_bufs[tg]
        n = "lnr%d" % tg
        for hh in range(2):
            P.op("dve", lambda e, hh=hh: e.bn_stats(out=st[:, hh, :], in_=xt[:, hh * 512:(hh + 1) * 512]), reads=[xres], writes=[n + "st"])
        P.op("dve", lambda e: e.bn_aggr(out=mv[:], in_=st[:].rearrange("p a b -> p (a b)")), reads=[n + "st"], writes=[n + "mv"])
        eps_t = self.eps2
        P.op("act", lambda e: e.activation(out=rs[:], in_=mv[:, 1:2], func=AF.Sqrt, bias=eps_t[:, 0:1], scale=1.0), reads=[n + "mv", "eps2"], writes=[n + "rs"])
        P.op("dve", lambda e: e.reciprocal(out=rs[:], in_=rs[:]), reads=[n + "rs"], writes=[n + "rs"])
        P.op("dve", lambda e: e.tensor_scalar(out=outt[:], in0=xt[:], scalar1=mv[:, 0:1], scalar2=rs[:, 0:1], op0=ALU.subtract, op1=ALU.mult),
             reads=[xres, n + "mv", n + "rs"], writes=[ores])

    def alloc_lnr(self):
        P = self.P
        self.eps2 = e2 = P.sb("eps2", [128, 1], F32)
        P.op("pool", lambda e: e.memset(e2[:], EPS), writes=["eps2"])
        self.lnr_bufs = [(P.sb("lst%d" % i, [128, 2, 6], F32), P.sb("lmv%d" % i, [128, 2], F32), P.sb("lrs%d" % i, [128, 1], F32)) for i in range(2)]

    def phase_outproj(self, l, var):
        P = self.P
        P.mark()
        N = LC if var else T
        catT = self.catcT_d if var else self.catT_d
        xsrc = self.xc_src if var else self.x_src
        xmid_d = self.xcmid_d if var else self.xmid_d
        h2_d = self.h2c_d if var else self.h2_d
        logits = self.logits_c if var else self.logits_x
        mr = self.modrows[l, var]
        g1 = self.row_tile("g1", mr[2 * DM:3 * DM])
        sc2 = self.row_tile("sc2", mr[4 * DM:5 * DM])
        sh2 = self.row_tile("sh2", mr[3 * DM:4 * DM])
        bo = self.row_tile("bo", self.b_out[l])
        lg = self.row_tile("lg", self.ln1_g[l])
        lb = self.row_tile("lb", self.ln1_b[l])
        P.op("pool", lambda e: e.tensor_tensor(out=bo[:], in0=bo[:], in1=g1[:], op=ALU.mult), reads=["bo", "g1"], writes=["bo"])
        P.op("pool", lambda e: e.tensor_scalar_add(out=sc2[:], in0=sc2[:], scalar1=1.0), reads=["sc2"], writes=["sc2"])
        wo = P.sb("wo", [128, 8, DM], BF16)
        for k0 in range(0, 8, 2):
            P.dma(lambda e, k0=k0: e.dma_start(out=wo[:, k0:k0 + 2, :], in_=self.w_out[l][k0 * 128:(k0 + 2) * 128, :].rearrange("(k p) n -> p k n", p=128)), writes=["wo"], eng="pool")
        for k in range(8):
            eng = ("dve", "pool")[k % 2]
            P.op(eng, lambda e, k=k: e.tensor_tensor(out=wo[:, k, :], in0=wo[:, k, :], in1=g1[:], op=ALU.mult), reads=["wo", "g1"], writes=["wo"])
        wr = P.sb("wr", [128, 8, NEXP], BF16)
        P.dma(lambda e: e.dma_start(out=wr[:], in_=self.w_router[l].rearrange("(k p) n -> p k n", p=128)), writes=["wr"], eng="pool")
        eps_t = P.sb("oeps", [128, 1], F32)
        P.op("pool", lambda e: e.memset(eps_t[:], EPS), writes=["oeps"])
        BW = min(512, N)
        tpb = BW // 128
        D = 6
        cbs = [P.sb("catb%d" % i, [128, 8, BW], BF16) for i in range(3)]
        xts = [P.sb("oxt%d" % i, [128, DM], F32) for i in range(D)]
        rts = [P.sb("ort%d" % i, [128, DM], F32) for i in range(D)]
        xms = [P.sb("oxm%d" % i, [128, DM], F32) for i in range(D)]
        hfs = [P.sb("ohf%d" % i, [128, DM], F32) for i in range(D)]
        hbs = [P.sb("ohb%d" % i, [128, DM], BF16) for i in range(D)]
        sts = [P.sb("ost%d" % i, [128, 2, 2, 6], F32) for i in range(D)]
        mvs = [P.sb("omv%d" % i, [128, 2, 4], F32) for i in range(D)]
        h2Ts = [P.sb("h2T%d" % i, [128, 8, 128], BF16) for i in range(2)]
        nt_ = N // 128
        if not var:
            nt_ = int(os.environ.get("DBG_NBLK", nt_ // tpb)) * tpb

        def ln_stats(src, sres, d, w):
            st, mv = sts[d], mvs[d]
            sn, mn = "ost%d_%d" % (d, w), "omv%d_%d" % (d, w)
            for hh in range(2):
                P.op("dve", lambda e, hh=hh: e.bn_stats(out=st[:, w, hh, :], in_=src[:, hh * 512:(hh + 1) * 512]), reads=[sres], writes=[sn])
            P.op("dve", lambda e: e.bn_aggr(out=mv[:, w, 0:2], in_=st[:, w, :, :].rearrange("p a b -> p (a b)")), reads=[sn], writes=[mn])
            P.op("act", lambda e: e.activation(out=mv[:, w, 2:3], in_=mv[:, w, 1:2], func=AF.Sqrt, bias=eps_t[:, 0:1], scale=1.0), reads=[mn, "oeps"], writes=[mn])
            P.op("dve", lambda e: e.reciprocal(out=mv[:, w, 2:3], in_=mv[:, w, 2:3]), reads=[mn], writes=[mn])
            P.op("dve", lambda e: e.scalar_tensor_tensor(out=mv[:, w, 3:4], in0=mv[:, w, 0:1], scalar=-1.0, in1=mv[:, w, 2:3], op0=ALU.mult, op1=ALU.mult), reads=[mn], writes=[mn])
            return mn

        def S0(n):
            d = n % D
            if n % tpb == 0:
                b_ = n // tpb
                cb = cbs[b_ % 3]
                P.dma(lambda e: e.dma_start(out=cb[:], in_=catT[:, b_ * BW:(b_ + 1) * BW].rearrange("(k p) t -> p k t", p=128)), writes=["catb%d" % (b_ % 3)])
            xt = xts[d]
            P.dma(lambda e: e.dma_start(out=xt[:], in_=xsrc[n * 128:(n + 1) * 128, :]), writes=["oxt%d" % d])

        def S1(n):
            d = n % D
            b_, i = n // tpb, n % tpb
            cb, cn = cbs[b_ % 3], "catb%d" % (b_ % 3)
            xt = xts[d]
            for hf in range(2):
                pb = 2 * (n % 2) + hf
                for k in range(8):
                    self.mm(self.ps[pb][:, :], cb[:, k, i * 128:(i + 1) * 128], wo[:, k, hf * 512:(hf + 1) * 512], k == 0, k == 7, [cn, "wo"], [self.PS[pb]])
            P.op("act", lambda e: e.activation(out=xt[:], in_=xt[:], func=AF.Copy, scale=ALPHA), reads=["oxt%d" % d], writes=["oxt%d" % d])
            P.op("pool", lambda e: e.tensor_tensor(out=xt[:], in0=xt[:], in1=bo[:], op=ALU.add), reads=["oxt%d" % d, "bo"], writes=["oxt%d" % d])

        def S2(n):
            d = n % D
            xt, rt = xts[d], rts[d]
            for hf in range(2):
                pb = 2 * (n % 2) + hf
                P.op("dve", lambda e, hf=hf, pb=pb: e.tensor_tensor(out=rt[:, hf * 512:(hf + 1) * 512], in0=self.ps[pb][:, :], in1=xt[:, hf * 512:(hf + 1) * 512], op=ALU.add),
                     reads=[self.PS[pb], "oxt%d" % d], writes=["ort%d" % d])
            ln_stats(rt, "ort%d" % d, d, 0)

        def S3(n):
            d = n % D
            rt, xm, mv = rts[d], xms[d], mvs[d]
            P.op("act", lambda e: e.activation(out=xm[:], in_=rt[:], func=AF.Identity, scale=mv[:, 0, 2:3], bias=mv[:, 0, 3:4]), reads=["ort%d" % d, "omv%d_0" % d], writes=["oxm%d" % d])
            P.op("dve", lambda e: e.tensor_tensor(out=xm[:], in0=xm[:], in1=lg[:], op=ALU.mult), reads=["oxm%d" % d, "lg"], writes=["oxm%d" % d])
            P.op("pool", lambda e: e.tensor_tensor(out=xm[:], in0=xm[:], in1=lb[:], op=ALU.add), reads=["oxm%d" % d, "lb"], writes=["oxm%d" % d])
            P.dma(lambda e: e.dma_start(out=xmid_d[n * 128:(n + 1) * 128, :], in_=xm[:]), reads=["oxm%d" % d])

        def S4(n):
            d = n % D
            ln_stats(xms[d], "oxm%d" % d, d, 1)

        def S5(n):
            d = n % D
            xm, hf_, hb, mv = xms[d], hfs[d], hbs[d], mvs[d]
            P.op("act", lambda e: e.activation(out=hf_[:], in_=xm[:], func=AF.Identity, scale=mv[:, 1, 2:3], bias=mv[:, 1, 3:4]), reads=["oxm%d" % d, "omv%d_1" % d], writes=["ohf%d" % d])
            P.op("dve", lambda e: e.tensor_tensor(out=hf_[:], in0=hf_[:], in1=sc2[:], op=ALU.mult), reads=["ohf%d" % d, "sc2"], writes=["ohf%d" % d])
            P.op("pool", lambda e: e.tensor_tensor(out=hb[:], in0=hf_[:], in1=sh2[:], op=ALU.add), reads=["ohf%d" % d, "sh2"], writes=["ohb%d" % d])
            P.dma(lambda e: e.dma_start(out=h2_d[n * 128:(n + 1) * 128, :], in_=hb[:]), reads=["ohb%d" % d])

        def S6(n):
            d = n % D
            hb = hbs[d]
            j = n % 2
            h2T = h2Ts[j]
            psT = self.ps[4 + j][:, :].bitcast(BF16)
            for k in range(8):
                P.op("pe", lambda e, k=k: e.transpose(out=psT[:, k * 128:(k + 1) * 128], in_=hb[:, k * 128:(k + 1) * 128], identity=self.ident[:]),
                     reads=["ohb%d" % d, "ident"], writes=[self.PS[4 + j]])
            P.op("act", lambda e: e.activation(out=h2T[:], in_=psT.rearrange("p (k t) -> p k t", k=8), func=AF.Copy), reads=[self.PS[4 + j]], writes=["h2T%d" % j])
            for k in range(8):
                self.mm(self.ps[6 + j][:, 0:NEXP], h2T[:, k, :], wr[:, k, :], k == 0, k == 7, ["h2T%d" % j, "wr"], [self.PS[6 + j]])
            P.op("act", lambda e: e.activation(out=logits[:, n, :], in_=self.ps[6 + j][:, 0:NEXP], func=AF.Copy), reads=[self.PS[6 + j]], writes=["logits%d" % var])
        stages = [S0, S1, S2, S3, S4, S5, S6]
        for step in range(nt_ + len(stages) - 1):
            for si in reversed(range(len(stages))):
                n = step - si
                if 0 <= n < nt_:
                    stages[si](n)
        P.release()

    def phase_moe(self, l, do_ctx):
        P = self.P
        P.mark()
        NT = T // 128
        NTA = NT + 2
        CAPX, CAPC = 2 * T // NEXP, 2 * LC // NEXP
        nexp = int(os.environ.get("DBG_NEXP", NEXP))
        NS = NEXP * NT
        pos = P.sb("pos", [128, NEXP, NT], F32)
        k128 = P.sb("k128", [128, 9], F32)
        affh = P.sb("affh", [128, NTA, NEXP], BF16)
        affl = P.sb("affl", [128, NTA, NEXP], BF16)
        iq = P.sb("iq", [128, 128], F32)
        ip = P.sb("ip", [128, 1], F32)
        tix = P.sb("tix", [128, NT], BF16)
        posc = P.sb("posc", [128, NEXP, 2], F32)
        Bc = P.sb("Bc", [128, NEXP, 2], BF16)
        P.mark()
        aff = P.sb("aff", [128, NTA, NEXP], F32)
        P.op("dve", lambda e: e.tensor_copy(out=aff[:, 0:NT, :], in_=self.logits_x[:]), reads=["logits0"], writes=["aff"])
        P.op("dve", lambda e: e.tensor_copy(out=aff[:, NT:NTA, :], in_=self.logits_c[:]), reads=["logits1", "aff"], writes=["aff"])
        mx = P.sb("mx", [128, NTA], F32)
        P.op("dve", lambda e: e.tensor_reduce(out=mx[:], in_=aff[:], axis=AX.X, op=ALU.max), reads=["aff"], writes=["mx"])
        P.op("dve", lambda e: e.tensor_tensor(out=aff[:], in0=aff[:], in1=mx[:, :].unsqueeze(2).to_broadcast([128, NTA, NEXP]), op=ALU.subtract), reads=["aff", "mx"], writes=["aff"])
        P.op("act", lambda e: e.activation(out=aff[:], in_=aff[:], func=AF.Exp), reads=["aff"], writes=["aff"])
        P.op("dve", lambda e: e.tensor_reduce(out=mx[:], in_=aff[:], axis=AX.X, op=ALU.add), reads=["aff"], writes=["mx"])
        P.op("dve", lambda e: e.reciprocal(out=mx[:], in_=mx[:]), reads=["mx"], writes=["mx"])
        P.op("dve", lambda e: e.tensor_tensor(out=aff[:], in0=aff[:], in1=mx[:, :].unsqueeze(2).to_broadcast([128, NTA, NEXP]), op=ALU.mult), reads=["aff", "mx"], writes=["aff"])
        lo = P.sb("lo", [128, 2, NEXP], F32)
        hi = P.sb("hi", [128, 2, NEXP], F32)
        mid = P.sb("mid", [128, 2, NEXP], F32)
        capt = P.sb("capt", [128, 2, NEXP], F32)
        tmp = P.sb("btmp", [128, 2, NEXP], F32)
        pred = P.sb("pred", [128, 2, NEXP], F32)
        cnt = P.sb("cnt", [128, 2, NEXP], BF16)
        cntf = P.sb("cntf", [128, 2, NEXP], F32)
        cmp_ = P.sb("cmp", [128, NTA, NEXP], BF16)
        P.op("pool", lambda e: e.memset(lo[:], 0.0), writes=["lo"])
        P.op("pool", lambda e: e.memset(hi[:], 1.0), writes=["hi"])
        P.op("pool", lambda e: e.memset(mid[:], 0.5), writes=["mid"])
        P.op("pool", lambda e: e.memset(capt[:, 0, :], float(CAPX)), writes=["capt"])
        P.op("pool", lambda e: e.memset(capt[:, 1, :], float(CAPC)), reads=["capt"], writes=["capt"])
        groups = ((0, 0, NT), (1, NT, NTA))

        def compare(thr, thr_res):
            for (g, a, b) in groups:
                P.op("dve", lambda e, g=g, a=a, b=b: e.tensor_tensor(out=cmp_[:, a:b, :], in0=aff[:, a:b, :], in1=thr[:, g, :].unsqueeze(1).to_broadcast([128, b - a, NEXP]), op=ALU.is_ge),
                     reads=["aff", thr_res], writes=["cmp"])
        for it in range(int(os.environ.get("DBG_NBIS", 34))):
            compare(mid, "mid")
            for (g, a, b) in groups:
                P.op("dve", lambda e, g=g, a=a, b=b: e.tensor_reduce(out=cntf[:, g, :], in_=cmp_[:, a:b, :].rearrange("p t e -> p e t"), axis=AX.X, op=ALU.add),
                     reads=["cmp"], writes=["cntf"])
            P.op("dve", lambda e: e.tensor_copy(out=cnt[:], in_=cntf[:]), reads=["cntf"], writes=["cnt"])
            self.mm(self.ps[0][:, 0:2 * NEXP], self.ones_bf[:], cnt[:].rearrange("p g e -> p (g e)"), True, True, ["cnt", "ones"], [self.PS[0]])
            P.op("dve", lambda e: e.tensor_tensor(out=pred[:].rearrange("p g e -> p (g e)"), in0=self.ps[0][:, 0:2 * NEXP], in1=capt[:].rearrange("p g e -> p (g e)"), op=ALU.is_ge),
                 reads=[self.PS[0], "capt"], writes=["pred"])
            P.op("dve", lambda e: e.tensor_tensor(out=tmp[:], in0=pred[:], in1=mid[:], op=ALU.mult), reads=["pred", "mid"], writes=["btmp"])
            P.op("dve", lambda e: e.tensor_tensor(out=lo[:], in0=lo[:], in1=tmp[:], op=ALU.max), reads=["lo", "btmp"], writes=["lo"])
            P.op("dve", lambda e: e.scalar_tensor_tensor(out=tmp[:], in0=pred[:], scalar=4.0, in1=mid[:], op0=ALU.mult, op1=ALU.add), reads=["pred", "mid", "lo"], writes=["btmp"])
            P.op("dve", lambda e: e.tensor_tensor(out=hi[:], in0=hi[:], in1=tmp[:], op=ALU.min), reads=["hi", "btmp"], writes=["hi"])
            P.op("dve", lambda e: e.tensor_tensor(out=tmp[:], in0=lo[:], in1=hi[:], op=ALU.add), reads=["lo", "hi"], writes=["btmp"])
            P.op("dve", lambda e: e.tensor_scalar(out=mid[:], in0=tmp[:], scalar1=0.5, scalar2=None, op0=ALU.mult), reads=["btmp"], writes=["mid"])
        compare(lo, "lo")
        NS = NEXP * NT
        mA = P.sb("mA", [128, NEXP, NT], F32)
        mB = P.sb("mB", [128, NEXP, NT], F32)
        msk = P.sb("msk", [128, NEXP, NT], F32)
        P.op("dve", lambda e: e.tensor_copy(out=msk[:], in_=cmp_[:, 0:NT, :].rearrange("p t e -> p e t")), reads=["cmp"], writes=["msk"])
        P.op("pool", lambda e: e.tensor_copy(out=mA[:], in_=msk[:]), reads=["msk"], writes=["mA"])
        cur, nxt, cn, nn = mA, mB, "mA", "mB"
        sft = 1
        while sft < NT:
            P.op("dve", lambda e, cur=cur, nxt=nxt, sft=sft: e.tensor_tensor(out=nxt[:, :, sft:], in0=cur[:, :, sft:], in1=cur[:, :, 0:NT - sft], op=ALU.add), reads=[cn], writes=[nn])
            P.op("pool", lambda e, cur=cur, nxt=nxt, sft=sft: e.tensor_copy(out=nxt[:, :, 0:sft], in_=cur[:, :, 0:sft]), reads=[cn], writes=[nn])
            cur, nxt, cn, nn = nxt, cur, nn, cn
            sft *= 2
        inc, incn = cur, cn
        rc = P.sb("rc", [128, NEXP], BF16)
        P.op("dve", lambda e: e.tensor_copy(out=rc[:], in_=inc[:, :, NT - 1]), reads=[incn], writes=["rc"])
        tri = P.sb("tri", [128, 128], BF16)
        trif = P.sb("trif", [128, 128], F32)
        P.op("pool", lambda e: e.memset(trif[:], 1.0), writes=["trif"])
        P.op("pool", lambda e: e.affine_select(out=trif[:], in_=trif[:], pattern=[[1, 128]], compare_op=ALU.is_gt, fill=0.0, base=0, channel_multiplier=-1), reads=["trif"], writes=["trif"])
        P.op("dve", lambda e: e.tensor_copy(out=tri[:], in_=trif[:]), reads=["trif"], writes=["tri"])
        self.mm(self.ps[1][:, 0:NEXP], tri[:], rc[:], True, True, ["tri", "rc"], [self.PS[1]])
        P.op("dve", lambda e: e.tensor_tensor(out=pos[:], in0=inc[:], in1=msk[:], op=ALU.subtract), reads=[incn, "msk"], writes=["pos"])
        rb = P.sb("rb", [128, NEXP], F32)
        P.op("dve", lambda e: e.tensor_copy(out=rb[:], in_=self.ps[1][:, 0:NEXP]), reads=[self.PS[1]], writes=["rb"])
        P.op("dve", lambda e: e.tensor_tensor(out=pos[:], in0=pos[:], in1=rb[:, :].unsqueeze(2).to_broadcast([128, NEXP, NT]), op=ALU.add), reads=["pos", "rb"], writes=["pos"])
        P.op("dve", lambda e: e.scalar_tensor_tensor(out=pos[:], in0=msk[:], scalar=-8192.0, in1=pos[:], op0=ALU.mult, op1=ALU.add), reads=["pos", "msk"], writes=["pos"])
        P.op("dve", lambda e: e.tensor_scalar_add(out=pos[:], in0=pos[:], scalar1=8192.0), reads=["pos"], writes=["pos"])
        P.op("pool", lambda e: e.iota(k128[:], pattern=[[128, 9]], base=0, channel_multiplier=0, allow_small_or_imprecise_dtypes=True), writes=["k128"])
        afft = P.sb("afft", [128, NTA, NEXP], F32)
        P.op("dve", lambda e: e.tensor_copy(out=affh[:], in_=aff[:]), reads=["aff"], writes=["affh"])
        P.op("dve", lambda e: e.tensor_tensor(out=afft[:], in0=aff[:], in1=affh[:], op=ALU.subtract), reads=["aff", "affh"], writes=["afft"])
        P.op("dve", lambda e: e.tensor_copy(out=affl[:], in_=afft[:]), reads=["afft"], writes=["affl"])
        P.op("pool", lambda e: e.iota(iq[:], pattern=[[1, 128]], base=0, channel_multiplier=0, allow_small_or_imprecise_dtypes=True), writes=["iq"])
        P.op("pool", lambda e: e.iota(ip[:], pattern=[[1, 1]], base=0, channel_multiplier=1, allow_small_or_imprecise_dtypes=True), writes=["ip"])
        P.op("pool", lambda e: e.iota(tix[:], pattern=[[1, NT]], base=0, channel_multiplier=0, allow_small_or_imprecise_dtypes=True), writes=["tix"])
        if do_ctx:
            mc = P.sb("mc", [128, NEXP, 2], F32)
            P.op("dve", lambda e: e.tensor_copy(out=mc[:], in_=cmp_[:, NT:NTA, :].rearrange("p t e -> p e t")), reads=["cmp"], writes=["mc"])
            rcc = P.sb("rcc", [128, NEXP], BF16)
            P.op("dve", lambda e: e.tensor_tensor(out=rcc[:], in0=mc[:, :, 0], in1=mc[:, :, 1], op=ALU.add), reads=["mc"], writes=["rcc"])
            self.mm(self.ps[1][:, NEXP:2 * NEXP], tri[:], rcc[:], True, True, ["tri", "rcc"], [self.PS[1]])
            P.op("dve", lambda e: e.tensor_copy(out=posc[:, :, 0], in_=self.ps[1][:, NEXP:2 * NEXP]), reads=[self.PS[1]], writes=["posc"])
            P.op("dve", lambda e: e.tensor_tensor(out=posc[:, :, 1], in0=posc[:, :, 0], in1=mc[:, :, 0], op=ALU.add), reads=["posc", "mc"], writes=["posc"])
            P.op("dve", lambda e: e.scalar_tensor_tensor(out=posc[:], in0=mc[:], scalar=-8192.0, in1=posc[:], op0=ALU.mult, op1=ALU.add), reads=["posc", "mc"], writes=["posc"])
            P.op("dve", lambda e: e.tensor_scalar_add(out=posc[:], in0=posc[:], scalar1=8192.0), reads=["posc"], writes=["posc"])
            P.op("dve", lambda e: e.tensor_single_scalar(out=Bc[:], in_=posc[:], scalar=float(CAPC), op=ALU.is_lt), reads=["posc"], writes=["Bc"])
        P.release()
        P.mark()
        zt = P.sb("zt", [128, 1024], F32)
        P.op("pool", lambda e: e.memset(zt[:], 0.0), writes=["zt"])
        for i in range(T // 128):
            P.dma(lambda e, i=i: e.dma_start(out=self.moe_d[i * 128:(i + 1) * 128, :], in_=zt[:]), reads=["zt"], writes=["moe_z%d" % i])
        if do_ctx:
            for i in range(LC // 128):
                P.dma(lambda e, i=i: e.dma_start(out=self.moec_d[i * 128:(i + 1) * 128, :], in_=zt[:]), reads=["zt"], writes=["moe_zc%d" % i])
        P.release()
        xss = [P.sb("xs%d" % i, [128, 9, DM], BF16) for i in range(2)]
        ge9e = P.sb("ge9e", [128, NT, 9], BF16)
        B8e = P.sb("B8e", [128, NT, 8], BF16)
        hi8e = P.sb("hi8e", [128, NT], F32)
        lo7e = P.sb("lo7e", [128, NT], F32)
        xsT = P.sb("xsT", [128, 8, 1056], BF16)
        hidT = P.sb("hidT", [128, 16, 1056], BF16)
        wds = [P.sb("wd%d" % i, [128, 16, DM], BF16) for i in range(2)]
        wgs = [P.sb("wg%d" % i, [128, 8, 256], BF16) for i in range(2)]
        wus = [P.sb("wu%d" % i, [128, 8, 256], BF16) for i in range(2)]
        Ats = [P.sb("At%d" % i, [128, 128], BF16) for i in range(2)]
        Abs_ = [P.sb("Ab0", [128, 16, 128], BF16)] * 2
        Rs = [P.sb("R%d" % i, [128, NT, 32], BF16) for i in range(2)]
        Rc = P.sb("Rc", [128, 2, 4], BF16)
        idxf = P.sb("idxf", [128, 9], F32)
        idxi = [P.sb("idxi%d" % i, [128, 9], I32) for i in range(2)]
        gts = [P.sb("gts%d" % i, [128, 9], F32) for i in range(2)]
        sg = P.sb("sgm", [128, 1056], F32)
        ysts = [P.sb("yst%d" % i, [128, DM], F32) for i in range(2)]
        NSL = 1056 if do_ctx else 1024
        cwc = [0]
        prev_sc = ["moe_z"]
        def partA(ex):
            pe_ = ex % 2
            R, Rn = Rs[pe_], "R%d" % pe_
            ii, iin = idxi[pe_], "idxi%d" % pe_
            gt, gtn = gts[pe_], "gts%d" % pe_
            xs, xsn = xss[pe_], "xs%d" % pe_
            b8 = B8e[:]
            pe3 = pos[:, ex, :]
            P.op("dve", lambda e, pe3=pe3: e.tensor_tensor(out=ge9e[:], in0=pe3.unsqueeze(2).to_broadcast([128, NT, 9]), in1=k128[:, :].unsqueeze(1).to_broadcast([128, NT, 9]), op=ALU.is_ge), reads=["pos", "k128"], writes=["ge9e"])
            P.op("dve", lambda e: e.tensor_tensor(out=B8e[:], in0=ge9e[:, :, 0:8], in1=ge9e[:, :, 1:9], op=ALU.subtract), reads=["ge9e"], writes=["B8"])
            P.op("dve", lambda e: e.tensor_reduce(out=hi8e[:], in_=ge9e[:, :, 1:9], axis=AX.X, op=ALU.add), reads=["ge9e"], writes=["hi8e"])
            P.op("dve", lambda e, pe3=pe3: e.scalar_tensor_tensor(out=lo7e[:], in0=hi8e[:], scalar=-128.0, in1=pe3, op0=ALU.mult, op1=ALU.add), reads=["hi8e", "pos"], writes=["lo7"])
            P.op("dve", lambda e, R=R, b8=b8: e.tensor_tensor(out=R[:, :, 0:8], in0=b8, in1=tix[:, :].unsqueeze(2).to_broadcast([128, NT, 8]), op=ALU.mult), reads=["B8", "tix"], writes=[Rn])
            P.op("dve", lambda e, R=R, b8=b8: e.tensor_scalar(out=R[:, :, 8:16], in0=b8, scalar1=ip[:, 0:1], scalar2=None, op0=ALU.mult), reads=["B8", "ip"], writes=[Rn])
            P.op("dve", lambda e, R=R, b8=b8, ex=ex: e.tensor_tensor(out=R[:, :, 16:24], in0=b8, in1=affh[:, 0:NT, ex].unsqueeze(2).to_broadcast([128, NT, 8]), op=ALU.mult), reads=["B8", "affh"], writes=[Rn])
            P.op("dve", lambda e, R=R, b8=b8, ex=ex: e.tensor_tensor(out=R[:, :, 24:32], in0=b8, in1=affl[:, 0:NT, ex].unsqueeze(2).to_broadcast([128, NT, 8]), op=ALU.mult), reads=["B8", "affl"], writes=[Rn])
            for t0 in range(0, NT, 16):
                Ab, Abn = Abs_[0], "Ab0"
                P.op("dve", lambda e, Ab=Ab, t0=t0: e.tensor_tensor(out=Ab[:], in0=iq[:, :].unsqueeze(1).to_broadcast([128, 16, 128]),
                                                                   in1=lo7e[:, t0:t0 + 16].unsqueeze(2).to_broadcast([128, 16, 128]), op=ALU.is_equal), reads=["iq", "lo7"], writes=[Abn])
                for t in range(t0, t0 + 16):
                    self.mm(self.ps[2][:, 0:32], Ab[:, t - t0, :], R[:, t, :], t == 0, t == NT - 1, [Abn, Rn], [self.PS[2]])
            if do_ctx:
                P.op("dve", lambda e, ex=ex: e.tensor_scalar(out=Rc[:, :, 0], in0=Bc[:, ex, :], scalar1=float(NT), scalar2=None, op0=ALU.mult) if False else
                     e.tensor_copy(out=Rc[:, :, 0], in_=Bc[:, ex, :]), reads=["Bc"], writes=["Rc"])
                P.op("dve", lambda e, ex=ex: e.tensor_scalar(out=Rc[:, :, 1], in0=Bc[:, ex, :], scalar1=ip[:, 0:1], scalar2=None, op0=ALU.mult), reads=["Bc", "ip", "Rc"], writes=["Rc"])
                P.op("dve", lambda e, ex=ex: e.tensor_tensor(out=Rc[:, :, 2], in0=Bc[:, ex, :], in1=affh[:, NT:NTA, ex], op=ALU.mult), reads=["Bc", "affh", "Rc"], writes=["Rc"])
                P.op("dve", lambda e, ex=ex: e.tensor_tensor(out=Rc[:, :, 3], in0=Bc[:, ex, :], in1=affl[:, NT:NTA, ex], op=ALU.mult), reads=["Bc", "affl", "Rc"], writes=["Rc"])
                P.op("pool", lambda e: e.memset(Rc[:, 0, 0:1], 0.0), reads=["Rc"], writes=["Rc"])
                for t in range(2):
                    At, An = Ats[t % 2], "At%d" % (t % 2)
                    P.op("dve", lambda e, At=At, ex=ex, t=t: e.tensor_scalar(out=At[:], in0=iq[:], scalar1=posc[:, ex, t:t + 1], scalar2=None, op0=ALU.is_equal), reads=["iq", "posc"], writes=[An])
                    self.mm(self.ps[2][:, 32:36], At[:], Rc[:, t, :], t == 0, t == 1, [An, "Rc"], [self.PS[2]])
            P.op("dve", lambda e: e.scalar_tensor_tensor(out=idxf[:, 0:8], in0=self.ps[2][:, 0:8], scalar=128.0, in1=self.ps[2][:, 8:16], op0=ALU.mult, op1=ALU.add) if False else
                 e.tensor_copy(out=idxf[:, 0:8], in_=self.ps[2][:, 8:16]), reads=[self.PS[2]], writes=["idxf"])
            P.op("dve", lambda e: e.scalar_tensor_tensor(out=idxf[:, 0:8], in0=self.ps[2][:, 0:8], scalar=128.0, in1=idxf[:, 0:8], op0=ALU.mult, op1=ALU.add), reads=[self.PS[2], "idxf"], writes=["idxf"])
            P.op("dve", lambda e, gt=gt: e.tensor_copy(out=gt[:, 0:8], in_=self.ps[2][:, 24:32]), reads=[self.PS[2]], writes=[gtn])
            P.op("dve", lambda e, gt=gt: e.tensor_tensor(out=gt[:, 0:8], in0=self.ps[2][:, 16:24], in1=gt[:, 0:8], op=ALU.add), reads=[self.PS[2], gtn], writes=[gtn])
            if do_ctx:
                P.op("dve", lambda e: e.tensor_copy(out=idxf[:, 8:9], in_=self.ps[2][:, 33:34]), reads=[self.PS[2], "idxf"], writes=["idxf"])
                P.op("dve", lambda e: e.scalar_tensor_tensor(out=idxf[:, 8:9], in0=self.ps[2][:, 32:33], scalar=128.0, in1=idxf[:, 8:9], op0=ALU.mult, op1=ALU.add), reads=[self.PS[2], "idxf"], writes=["idxf"])
                P.op("dve", lambda e, gt=gt: e.tensor_copy(out=gt[:, 8:9], in_=self.ps[2][:, 35:36]), reads=[self.PS[2], gtn], writes=[gtn])
                P.op("dve", lambda e, gt=gt: e.tensor_tensor(out=gt[:, 8:9], in0=self.ps[2][:, 34:35], in1=gt[:, 8:9], op=ALU.add), reads=[self.PS[2], gtn], writes=[gtn])
            P.op("dve", lambda e, ii=ii: e.tensor_copy(out=ii[:], in_=idxf[:]), reads=["idxf"], writes=[iin])
            for k in range(8):
                P.dma(lambda e, k=k, ii=ii: e.indirect_dma_start(out=xs[:, k, :], out_offset=None, in_=self.h2_d[:, :],
                                                                in_offset=bass.IndirectOffsetOnAxis(ap=ii[:, k:k + 1], axis=0)),
                      reads=[iin, "h2"], writes=[xsn], eng="pool")
            if do_ctx:
                P.dma(lambda e, ii=ii: e.indirect_dma_start(out=xs[0:CAPC, 8, :], out_offset=None, in_=self.h2c_d[:, :],
                                                           in_offset=bass.IndirectOffsetOnAxis(ap=ii[0:CAPC, 8:9], axis=0)),
                      reads=[iin, "h2"], writes=[xsn], eng="pool")

        def partB(ex):
            pe_ = ex % 2
            ii, iin = idxi[pe_], "idxi%d" % pe_
            gt, gtn = gts[pe_], "gts%d" % pe_
            xs, xsn = xss[pe_], "xs%d" % pe_
            nst = 9 if do_ctx else 8
            for j in range(nst):
                rows = 128 if j < 8 else CAPC
                pb = 6 + j % 2
                psT = self.ps[pb][:, :].bitcast(BF16)
                for dk in range(8):
                    P.op("pe", lambda e, j=j, dk=dk, psT=psT, rows=rows: e.transpose(out=psT[:, dk * 128:dk * 128 + rows], in_=xs[0:rows, j, dk * 128:(dk + 1) * 128], identity=self.ident[0:rows, 0:rows]),
                         reads=[xsn, "ident"], writes=[self.PS[pb]])
                P.op("act", lambda e, j=j, psT=psT, rows=rows: e.activation(out=xsT[:, :, j * 128:j * 128 + rows], in_=psT.rearrange("p (k t) -> p k t", k=8)[:, :, 0:rows], func=AF.Copy),
                     reads=[self.PS[pb]], writes=["xsT"])

        def partC(ex):
            pe_ = ex % 2
            ii, iin = idxi[pe_], "idxi%d" % pe_
            gt, gtn = gts[pe_], "gts%d" % pe_
            xs, xsn = xss[pe_], "xs%d" % pe_
            nst = 9 if do_ctx else 8
            wd_unused = None
            wd, wdn = wds[pe_], "wd%d" % pe_
            for f0 in range(0, 16, 4):
                P.dma(lambda e, f0=f0, wd=wd, ex=ex: e.dma_start(out=wd[:, f0:f0 + 4, :], in_=self.w_down[l, ex, f0 * 128:(f0 + 4) * 128, :].rearrange("(k p) n -> p k n", p=128)), writes=[wdn], eng="pool")
            segs = [(0, 512), (512, 512)] + ([(1024, CAPC)] if do_ctx else [])
            for fc in range(8):
                wg, wgn = wgs[cwc[0] % 2], "wg%d" % (cwc[0] % 2)
                wu, wun = wus[cwc[0] % 2], "wu%d" % (cwc[0] % 2)
                cwc[0] += 1
                P.dma(lambda e, wg=wg, ex=ex, fc=fc: e.dma_start(out=wg[:], in_=self.w_gate[l, ex, :, fc * 256:(fc + 1) * 256].rearrange("(k p) n -> p k n", p=128)), writes=[wgn], eng="pool")
                P.dma(lambda e, wu=wu, ex=ex, fc=fc: e.dma_start(out=wu[:], in_=self.w_up[l, ex, :, fc * 256:(fc + 1) * 256].rearrange("(k p) n -> p k n", p=128)), writes=[wun], eng="pool")
                for fi in range(2):
                    ft = fc * 2 + fi
                    for si, (s0, sn_) in enumerate(segs):
                        for k in range(8):
                            self.mm(self.ps[si][:, 0:sn_], wg[:, k, fi * 128:(fi + 1) * 128], xsT[:, k, s0:s0 + sn_], k == 0, k == 7, [wgn, "xsT"], [self.PS[si]])
                        for k in range(8):
                            self.mm(self.ps[3 + si][:, 0:sn_], wu[:, k, fi * 128:(fi + 1) * 128], xsT[:, k, s0:s0 + sn_], k == 0, k == 7, [wun, "xsT"], [self.PS[3 + si]])
                    for si, (s0, sn_) in enumerate(segs):
                        P.op("act", lambda e, si=si, s0=s0, sn_=sn_: e.activation(out=sg[:, s0:s0 + sn_], in_=self.ps[si][:, 0:sn_], func=AF.Silu), reads=[self.PS[si]], writes=["sgm%d" % si])
                        P.op("dve", lambda e, si=si, s0=s0, sn_=sn_, ft=ft: e.tensor_tensor(out=hidT[:, ft, s0:s0 + sn_], in0=self.ps[3 + si][:, 0:sn_], in1=sg[:, s0:s0 + sn_], op=ALU.mult),
                             reads=[self.PS[3 + si], "sgm%d" % si], writes=["hidT"])
            cur_sc = []
            for j in range(nst):
                rows = 128 if j < 8 else CAPC
                ys, ysn = ysts[j % 2], "yst%d" % (j % 2)
                for hf in range(2):
                    pb = 6 + hf
                    for ft in range(16):
                        self.mm(self.ps[pb][0:rows, :], hidT[:, ft, j * 128:j * 128 + rows], wd[:, ft, hf * 512:(hf + 1) * 512], ft == 0, ft == 15, ["hidT", wdn], [self.PS[pb]])
                    P.op("act", lambda e, ys=ys, hf=hf, pb=pb, rows=rows, gt=gt, j=j: e.activation(out=ys[0:rows, hf * 512:(hf + 1) * 512], in_=self.ps[pb][0:rows, :], func=AF.Copy, scale=gt[0:rows, j:j + 1]),
                         reads=[self.PS[pb], gtn], writes=[ysn])
                dst = self.moe_d if j < 8 else self.moec_d
                scn = "sc%d_%d" % (ex, j)
                P.dma(lambda e, ys=ys, rows=rows, ii=ii, j=j, dst=dst: e.indirect_dma_start(out=dst[:, :], out_offset=bass.IndirectOffsetOnAxis(ap=ii[0:rows, j:j + 1], axis=0),
                                                                                           in_=ys[0:rows, :], in_offset=None, compute_op=ALU.add),
                      reads=[ysn, iin] + list(prev_sc), writes=[scn], eng="pool")
                cur_sc.append(scn)
            prev_sc[:] = cur_sc

        partA(0)
        for ex in range(nexp):
            partB(ex)
            if ex + 1 < nexp:
                partA(ex + 1)
            partC(ex)
        P.release()

    def phase_ln2(self, l, var, dst):
        P = self.P
        P.mark()
        N = LC if var else T
        xmid_d = self.xcmid_d if var else self.xmid_d
        moe_d = self.moec_d if var else self.moe_d
        mr = self.modrows[l, var]
        g2 = self.row_tile("g2", mr[5 * DM:6 * DM])
        lg = self.row_tile("lg2", self.ln2_g[l])
        lb = self.row_tile("lb2", self.ln2_b[l])
        eps_t = P.sb("feps", [128, 1], F32)
        P.op("pool", lambda e: e.memset(eps_t[:], EPS), writes=["feps"])
        D = 7
        xts = [P.sb("fxt%d" % i, [128, DM], F32) for i in range(D)]
        mts = [P.sb("fmt%d" % i, [128, DM], F32) for i in range(D)]
        ots = [P.sb("fot%d" % i, [128, DM], F32) for i in range(D)]
        sts = [P.sb("fst%d" % i, [128, 2, 6], F32) for i in range(D)]
        mvs = [P.sb("fmv%d" % i, [128, 4], F32) for i in range(D)]
        nt = N // 128
        if not var:
            nt = int(os.environ.get("DBG_NBLK", nt // 4)) * 4

        def S0(n):
            d = n % D
            xt, mt = xts[d], mts[d]
            P.dma(lambda e: e.dma_start(out=xt[:], in_=xmid_d[n * 128:(n + 1) * 128, :]), writes=["fxt%d" % d])
            P.dma(lambda e: e.dma_start(out=mt[:], in_=moe_d[n * 128:(n + 1) * 128, :]), writes=["fmt%d" % d], eng="act")

        def S1(n):
            d = n % D
            xt, mt = xts[d], mts[d]
            P.op("pool", lambda e: e.tensor_tensor(out=mt[:], in0=mt[:], in1=g2[:], op=ALU.mult), reads=["fmt%d" % d, "g2"], writes=["fmt%d" % d])
            P.op("act", lambda e: e.activation(out=xt[:], in_=xt[:], func=AF.Copy, scale=ALPHA), reads=["fxt%d" % d], writes=["fxt%d" % d])

        def S2(n):
            d = n % D
            xt, mt, st, mv = xts[d], mts[d], sts[d], mvs[d]
            P.op("dve", lambda e: e.tensor_tensor(out=mt[:], in0=mt[:], in1=xt[:], op=ALU.add), reads=["fmt%d" % d, "fxt%d" % d], writes=["fmt%d" % d])
            sn, mn = "fst%d" % d, "fmv%d" % d
            for hh in range(2):
                P.op("dve", lambda e, hh=hh: e.bn_stats(out=st[:, hh, :], in_=mt[:, hh * 512:(hh + 1) * 512]), reads=["fmt%d" % d], writes=[sn])
            P.op("dve", lambda e: e.bn_aggr(out=mv[:, 0:2], in_=st[:].rearrange("p a b -> p (a b)")), reads=[sn], writes=[mn])
            P.op("act", lambda e: e.activation(out=mv[:, 2:3], in_=mv[:, 1:2], func=AF.Sqrt, bias=eps_t[:, 0:1], scale=1.0), reads=[mn, "feps"], writes=[mn])
            P.op("dve", lambda e: e.reciprocal(out=mv[:, 2:3], in_=mv[:, 2:3]), reads=[mn], writes=[mn])
            P.op("dve", lambda e: e.scalar_tensor_tensor(out=mv[:, 3:4], in0=mv[:, 0:1], scalar=-1.0, in1=mv[:, 2:3], op0=ALU.mult, op1=ALU.mult), reads=[mn], writes=[mn])

        def S3(n):
            d = n % D
            mt, ot, mv = mts[d], ots[d], mvs[d]
            P.op("act", lambda e: e.activation(out=ot[:], in_=mt[:], func=AF.Identity, scale=mv[:, 2:3], bias=mv[:, 3:4]), reads=["fmt%d" % d, "fmv%d" % d], writes=["fot%d" % d])
            P.op("dve", lambda e: e.tensor_tensor(out=ot[:], in0=ot[:], in1=lg[:], op=ALU.mult), reads=["fot%d" % d, "lg2"], writes=["fot%d" % d])

        def S4(n):
            d = n % D
            ot = ots[d]
            P.op("pool", lambda e: e.tensor_tensor(out=ot[:], in0=ot[:], in1=lb[:], op=ALU.add), reads=["fot%d" % d, "lb2"], writes=["fot%d" % d])
            P.dma(lambda e: e.dma_start(out=dst[n * 128:(n + 1) * 128, :], in_=ot[:]), reads=["fot%d" % d])
        stages = [S0, S1, S2, S3, S4]
        for step in range(nt + len(stages) - 1):
            for si in reversed(range(len(stages))):
                n = step - si
                if 0 <= n < nt:
                    stages[si](n)
        P.release()

    def s5_consts(self):
        P = self.P
        sel = self.sel
        P.op("pool", lambda e: e.memset(sel[:], 0.0), writes=["sel"])
        for a in range(8):
            for b in range(8):
                eng = ("dve", "pool", "act")[(a * 8 + b) % 3]
                if eng == "act":
                    P.op("act", lambda e, a=a, b=b: e.activation(out=sel[:, a, b, 16 * b:16 * b + 16], in_=self.identf[:, 16 * a:16 * a + 16], func=AF.Copy), reads=["identf", "sel"], writes=["sel"])
                else:
                    P.op(eng, lambda e, a=a, b=b: e.tensor_copy(out=sel[:, a, b, 16 * b:16 * b + 16], in_=self.identf[:, 16 * a:16 * a + 16]), reads=["identf", "sel"], writes=["sel"])
        qi = P.sb("qi", [128, 1], I32)
        P.op("pool", lambda e: e.iota(qi[:], pattern=[[1, 1]], base=0, channel_multiplier=1), writes=["qi"])
        P.op("dve", lambda e: e.tensor_single_scalar(out=qi[:], in_=qi[:], scalar=4, op=ALU.arith_shift_right), reads=["qi"], writes=["qi"])
        qf = P.sb("qf", [128, 1], F32)
        P.op("dve", lambda e: e.tensor_copy(out=qf[:], in_=qi[:]), reads=["qi"], writes=["qf"])
        iv = P.sb("iv", [128, 8, 16], F32)
        P.op("pool", lambda e: e.iota(iv[:], pattern=[[1, 8], [0, 16]], base=0, channel_multiplier=0, allow_small_or_imprecise_dtypes=True), writes=["iv"])
        self.mskf = P.sb("mskf", [128, 128], F32)
        self.mskb = P.sb("mskb", [128, 128], F32)
        P.op("dve", lambda e: e.tensor_scalar(out=self.mskf[:], in0=iv[:].rearrange("p a b -> p (a b)"), scalar1=qf[:, 0:1], scalar2=None, op0=ALU.is_ge), reads=["iv", "qf"], writes=["mskf"])
        P.op("dve", lambda e: e.tensor_scalar(out=self.mskb[:], in0=iv[:].rearrange("p a b -> p (a b)"), scalar1=qf[:, 0:1], scalar2=None, op0=ALU.is_le), reads=["iv", "qf"], writes=["mskb"])

    def s5_setup(self, l):
        P = self.P
        S = ["S5S"]

        def so(eng, fn):
            P.op(eng, fn, reads=S, writes=S)

        def tt(o, a, b, op):
            so("dve", lambda e: e.tensor_tensor(out=o, in0=a, in1=b, op=op))

        def ts(o, a, c1, op0):
            so("dve", lambda e: e.tensor_scalar(out=o, in0=a, scalar1=c1, scalar2=None, op0=op0))

        def cp(o, a):
            so("dve", lambda e: e.tensor_copy(out=o, in_=a))

        def cmul(outr, outi, ar, ai, br, bi, t1, t2):
            tt(t1, ar, br, ALU.mult)
            tt(t2, ai, bi, ALU.mult)
            tt(outr, t1, t2, ALU.subtract)
            tt(t1, ar, bi, ALU.mult)
            tt(t2, ai, br, ALU.mult)
            tt(outi, t1, t2, ALU.add)
        sp = P.sb("s5par", [128, 1072], F32)
        P.dma(lambda e: e.dma_start(out=sp[:], in_=self.s5nat[l]), writes=S)
        lre, lim, ldt = sp[:, 0:16], sp[:, 16:32], sp[:, 32:48]
        Bre = sp[:, 48:304].rearrange("p (s c) -> p s c", s=16)
        Bim = sp[:, 304:560].rearrange("p (s c) -> p s c", s=16)
        Cre = sp[:, 560:816].rearrange("p (s c) -> p s c", s=16)
        Cim = sp[:, 816:1072].rearrange("p (s c) -> p s c", s=16)
        sm = P.sb("s5sm", [128, 16, 16], F32)
        R = lambda i: sm[:, i, :]
        dt, xr, th, cs_, sn_, t1, t2, am1, den, zr, zi = [R(i) for i in range(11)]
        so("act", lambda e: e.activation(out=dt, in_=ldt, func=AF.Exp))
        tt(xr, lre, dt, ALU.mult)
        tt(th, lim, dt, ALU.mult)
        hp_ = P.sb("halfpi", [128, 1], F32)
        so("pool", lambda e: e.memset(hp_[:], float(np.pi / 2)))
        so("act", lambda e: e.activation(out=sn_, in_=th, func=AF.Sin, scale=1.0 / 16))
        so("act", lambda e: e.activation(out=cs_, in_=th, func=AF.Sin, scale=1.0 / 16, bias=hp_[:, 0:1]))
        for _ in range(4):
            tt(t1, cs_, cs_, ALU.mult)
            tt(t2, sn_, sn_, ALU.mult)
            tt(sn_, sn_, cs_, ALU.mult)
            ts(sn_, sn_, 2.0, ALU.mult)
            tt(cs_, t1, t2, ALU.subtract)
        ekr = P.sb("ekr", [128, 16, 9], F32)
        eki = P.sb("eki", [128, 16, 9], F32)
        so("pool", lambda e: e.memset(ekr[:, :, 0], 1.0))
        so("pool", lambda e: e.memset(eki[:, :, 0], 0.0))
        cp(ekr[:, :, 1], cs_)
        cp(eki[:, :, 1], sn_)
        for k in range(1, 8):
            cmul(ekr[:, :, k + 1], eki[:, :, k + 1], ekr[:, :, k], eki[:, :, k], cs_, sn_, t1, t2)
        kv = P.sb("kv", [128, 16], F32)
        so("pool", lambda e: e.iota(kv[:], pattern=[[1, 16]], base=-7, channel_multiplier=0, allow_small_or_imprecise_dtypes=True))
        mag = P.sb("mag", [128, 16, 16], F32)
        apr = P.sb("apr", [128, 16, 16], F32)
        api = P.sb("api", [128, 16, 16], F32)
        tt(mag[:], xr.unsqueeze(2).to_broadcast([128, 16, 16]), kv[:, :].unsqueeze(1).to_broadcast([128, 16, 16]), ALU.mult)
        so("act", lambda e: e.activation(out=mag[:], in_=mag[:], func=AF.Exp))
        tt(apr[:, :, 7:16], mag[:, :, 7:16], ekr[:], ALU.mult)
        tt(api[:, :, 7:16], mag[:, :, 7:16], eki[:], ALU.mult)
        tt(apr[:, :, 0:7], mag[:, :, 0:7], ekr[:, :, 7:0:-1], ALU.mult)
        tt(api[:, :, 0:7], mag[:, :, 0:7], eki[:, :, 7:0:-1], ALU.mult)
        ts(api[:, :, 0:7], api[:, :, 0:7], -1.0, ALU.mult)
        ts(am1, apr[:, :, 8], -1.0, ALU.add)
        tt(t1, lre, lre, ALU.mult)
        tt(t2, lim, lim, ALU.mult)
        tt(den, t1, t2, ALU.add)
        so("dve", lambda e: e.reciprocal(out=den, in_=den))
        tt(t1, am1, lre, ALU.mult)
        tt(t2, api[:, :, 8], lim, ALU.mult)
        tt(zr, t1, t2, ALU.add)
        tt(zr, zr, den, ALU.mult)
        tt(t1, api[:, :, 8], lre, ALU.mult)
        tt(t2, am1, lim, ALU.mult)
        tt(zi, t1, t2, ALU.subtract)
        tt(zi, zi, den, ALU.mult)
        bbr = P.sb("bbr", [128, 16, 16], F32)
        bbi = P.sb("bbi", [128, 16, 16], F32)
        w1 = P.sb("w1", [128, 16, 8, 16], F32)
        w2 = P.sb("w2", [128, 16, 8, 16], F32)
        bc = lambda a: a.unsqueeze(2).to_broadcast([128, 16, 16])
        cmul(bbr[:], bbi[:], bc(zr), bc(zi), Bre, Bim, w1[:, :, 0, :], w2[:, :, 0, :])
        prod_r = P.sb("prodr", [128, 16, 8, 16], F32)
        prod_i = P.sb("prodi", [128, 16, 8, 16], F32)
        xa_r = P.sb("xar", [128, 16, 8, 16], F32)
        xa_i = P.sb("xai", [128, 16, 8, 16], F32)
        zA = P.sb("zA", [128, 128], F32)
        zB = P.sb("zB", [128, 128], F32)
        kacc = P.sb("kacc", [128, 128], F32)
        ktmp = P.sb("ktmp", [128, 128], F32)
        dcol = P.sb("dcol", [128, 16], F32)
        P.dma(lambda e: e.dma_start(out=dcol[:], in_=self.s5_dcol[l]), reads=S, writes=S)
        b4 = lambda a: a.unsqueeze(3).to_broadcast([128, 16, 8, 16])
        c4 = lambda a: a.unsqueeze(2).to_broadcast([128, 16, 8, 16])
        sl_neg, sl_pos, sl_7m, sl_p1, sl_8m = slice(7, None, -1), slice(7, 15), slice(14, 6, -1), slice(8, 16), slice(15, 7, -1)

        def prod(outr, outi, sl, mr, mi):
            cmul(outr, outi, b4(apr[:, :, sl]), b4(api[:, :, sl]), c4(mr), c4(mi), w1[:], w2[:])

        def gap(t, d, g):
            return t[:, d * 8 + g // 2, :, :].rearrange("p a b -> p (a b)")

        def zpad(dst, src, par, scale):
            so("pool", lambda e: e.memset(dst[64 * (1 - par):64 * (2 - par), :], 0.0))
            so("dve", lambda e: e.tensor_scalar(out=dst[64 * par:64 * par + 64, :], in0=src[64 * par:64 * par + 64, :], scalar1=scale, scalar2=None, op0=ALU.mult))
        PS0, PS1 = [self.PS[0]] + S, [self.PS[1]] + S
        for d in range(2):
            prod(prod_r[:], prod_i[:], sl_7m if d == 0 else sl_pos, bbr[:], bbi[:])
            for g in range(16):
                for ri, src in enumerate((prod_r, prod_i)):
                    zpad(zA, gap(src, d, g), g % 2, 1.0)
                    P.op("pe", lambda e: e.transpose(out=self.ps[0][:, 0:128], in_=zA[:], identity=self.identf[:]), reads=PS0, writes=PS0)
                    P.op("act", lambda e, d=d, g=g, ri=ri: e.activation(out=self.FT[:, d, g, ri, :], in_=self.ps[0][:, 0:128], func=AF.Copy), reads=PS0, writes=PS0)
            prod(prod_r[:], prod_i[:], sl_p1 if d == 0 else sl_8m, Cre, Cim)
            for g in range(16):
                for ri, (src, sc_) in enumerate(((prod_r, 1.0), (prod_i, -1.0))):
                    zpad(self.EZ[:, d, g, ri, :], gap(src, d, g), g % 2, sc_)
            prod(xa_r[:], xa_i[:], sl_neg if d == 0 else sl_pos, bbr[:], bbi[:])
            prod(prod_r[:], prod_i[:], sl_pos if d == 0 else sl_neg, Cre, Cim)
            for g in range(16):
                zpad(zA, gap(xa_r, d, g), g % 2, 1.0)
                zpad(zB, gap(xa_i, d, g), g % 2, -1.0)
                P.op("pe", lambda e, d=d, g=g: e.matmul(out=self.ps[1][:, 0:128], lhsT=zA[:], rhs=gap(prod_r, d, g), start=True, stop=False), reads=PS1, writes=PS1)
                P.op("pe", lambda e, d=d, g=g: e.matmul(out=self.ps[1][:, 0:128], lhsT=zB[:], rhs=gap(prod_i, d, g), start=False, stop=True), reads=PS1, writes=PS1)
                msk = self.mskf if d == 0 else self.mskb
                P.op("dve", lambda e, msk=msk: e.tensor_tensor(out=ktmp[:], in0=self.ps[1][:, 0:128], in1=msk[:], op=ALU.mult), reads=PS1 + ["mskf", "mskb"], writes=PS1)
                if d == 0:
                    so("dve", lambda e, g=g: e.scalar_tensor_tensor(out=self.Kf32[:, g, :], in0=self.identf[:], scalar=dcol[:, g:g + 1], in1=ktmp[:], op0=ALU.mult, op1=ALU.add))
                else:
                    tt(kacc[:], ktmp[:], self.Kf32[:, g, :], ALU.add)
                    cp(self.KtotT[:, g, :], kacc[:])
        g1r, g1i, phc, phs = self.g1r, self.g1i, self.phc, self.phs
        cp(g1r[:, :, 0], ekr[:, :, 8])
        cp(g1i[:, :, 0], eki[:, :, 8])
        for m in range(7):
            cmul(g1r[:, :, m + 1], g1i[:, :, m + 1], g1r[:, :, m], g1i[:, :, m], g1r[:, :, m], g1i[:, :, m], t1, t2)
        so("pool", lambda e: e.memset(phc[:, :, 0:1], 1.0))
        so("pool", lambda e: e.memset(phs[:, :, 0:1], 0.0))
        wv1 = w1[:].rearrange("p s a b -> p s (a b)")
        wv2 = w2[:].rearrange("p s a b -> p s (a b)")
        for m in range(7):
            n = 1 << m
            cmul(phc[:, :, n:2 * n], phs[:, :, n:2 * n], phc[:, :, 0:n], phs[:, :, 0:n],
                 g1r[:, :, m:m + 1].to_broadcast([128, 16, n]), g1i[:, :, m:m + 1].to_broadcast([128, 16, n]), wv1[:, :, 0:n], wv2[:, :, 0:n])
        cp(self.rho[:], mag[:, :, 15:16].to_broadcast([128, 16, 128]))

    def s5_run(self, N, uT_src, use_h0, with_output, out_dst, store_final, tag):
        P = self.P
        P.mark()
        TT = 8 * N
        W = min(512, N)
        nh = N // W
        L = min(128, N)
        nseg = N // L
        ut = P.sb("s5ut", [128, 2, TT], BF16)
        ut_off = P.last_off
        U = P.sb("s5U", [128, 16, N], BF16)
        U_off = P.last_off
        for hc in range(2):
            P.dma(lambda e, hc=hc: e.dma_start(out=ut[:, hc, :], in_=uT_src[hc * 128:(hc + 1) * 128, :]), writes=["s5ut"])
        sel = self.sel
        cnt = 0
        for g in range(16):
            uv = ut[:, g // 8, :].rearrange("p (j i) -> p i j", i=8)
            for h in range(nh):
                pb = cnt % 2
                cnt += 1
                for i0 in range(8):
                    self.mm(self.ps[pb][:, 0:W], sel[:, g % 8, i0, :], uv[:, i0, h * W:(h + 1) * W], i0 == 0, i0 == 7, ["sel", "s5ut"], [self.PS[pb]])
                if pb == 0:
                    P.op("act", lambda e, g=g, h=h, pb=pb: e.activation(out=U[:, g, h * W:(h + 1) * W], in_=self.ps[pb][:, 0:W], func=AF.Copy), reads=[self.PS[pb]], writes=["s5U"])
                else:
                    P.op("dve", lambda e, g=g, h=h, pb=pb: e.tensor_copy(out=U[:, g, h * W:(h + 1) * W], in_=self.ps[pb][:, 0:W]), reads=[self.PS[pb]], writes=["s5U"])
        P.barrier()
        P.mark()
        sets = []
        for si_ in range(2):
            bufs = []
            for bi in range(8):
                if si_ == 0:
                    bufs.append(P.sb("s5w%d_%d" % (si_, bi), [128, N], F32))
                else:
                    bufs.append(P.sb_at("s5w%d_%d" % (si_, bi), [128, N], F32, ut_off + bi * N * 4))
            sets.append(bufs)
        shbs = [P.sb("s5shb%d" % i, [128, 2, N], BF16) for i in range(2)]
        inis = [P.sb("s5ini%d" % i, [128, 4], F32) for i in range(2)]
        hfin, g1r, g1i, hnew, rho = self.hfin, self.g1r, self.g1i, self.hnew, self.rho
        FT, phc, phs = self.FT, self.phc, self.phs
        bankc = [0]

        def SA(s_):
            d, gp = s_ // 8, s_ % 8
            k_ = s_ % 2
            Vr, Vi = sets[k_][0], sets[k_][1]
            for ri, V in enumerate((Vr, Vi)):
                vn = "s5V%d_%d" % (ri, k_)
                for h in range(nh):
                    pb = 2 + bankc[0] % 4
                    bankc[0] += 1
                    self.mm(self.ps[pb][:, 0:W], FT[:, d, 2 * gp, ri, :], U[:, 2 * gp, h * W:(h + 1) * W], True, False, ["FT", "s5U"], [self.PS[pb]])
                    self.mm(self.ps[pb][:, 0:W], FT[:, d, 2 * gp + 1, ri, :], U[:, 2 * gp + 1, h * W:(h + 1) * W], False, True, ["FT", "s5U"], [self.PS[pb]])
                    if d == 0:
                        ov = V[:, h * W:(h + 1) * W]
                    else:
                        ov = V[:, N - 1 - h * W:(N - 1 - (h + 1) * W if (h + 1) * W < N else None):-1]
                    P.op("act", lambda e, ov=ov, pb=pb: e.activation(out=ov, in_=self.ps[pb][:, 0:W], func=AF.Copy), reads=[self.PS[pb]], writes=[vn])

        def SB(s_):
            d, gp = s_ // 8, s_ % 8
            k_ = s_ % 2
            Vr, Vi, Wr, Wi, Sr, Si, ta, tc_ = sets[k_]
            shb, ini = shbs[k_], inis[k_]
            nm = lambda x: "s5%s_%d" % (x, k_)
            c3 = phc[:, s_, 0:L].unsqueeze(1).to_broadcast([128, nseg, L])
            s3 = phs[:, s_, 0:L].unsqueeze(1).to_broadcast([128, nseg, L])
            v3 = lambda t: t[:, :].rearrange("p (a b) -> p a b", b=L)
            P.op("dve", lambda e: e.tensor_tensor(out=v3(ta), in0=v3(Vr), in1=c3, op=ALU.mult), reads=[nm("V0"), "ph"], writes=[nm("ta")])
            P.op("dve", lambda e: e.tensor_tensor(out=v3(Wr), in0=v3(Vi), in1=s3, op=ALU.mult), reads=[nm("V1"), "ph"], writes=[nm("Wr")])
            P.op("dve", lambda e: e.tensor_tensor(out=Wr[:], in0=Wr[:], in1=ta[:], op=ALU.add), reads=[nm("Wr"), nm("ta")], writes=[nm("Wr")])
            P.op("pool", lambda e: e.tensor_tensor(out=v3(tc_), in0=v3(Vi), in1=c3, op=ALU.mult), reads=[nm("V1"), "ph"], writes=[nm("tc")])
            P.op("pool", lambda e: e.tensor_tensor(out=v3(Wi), in0=v3(Vr), in1=s3, op=ALU.mult), reads=[nm("V0"), "ph"], writes=[nm("Wi")])
            P.op("pool", lambda e: e.tensor_tensor(out=Wi[:], in0=tc_[:], in1=Wi[:], op=ALU.subtract), reads=[nm("Wi"), nm("tc")], writes=[nm("Wi")])
            for sg_ in range(nseg):
                a, b = sg_ * L, (sg_ + 1) * L
                qr = None
                if sg_ == 0:
                    if use_h0:
                        qr, qi = g1r[:, s_, 0:1], g1i[:, s_, 0:1]
                        lr, li = hfin[:, s_, 0:1], hfin[:, s_, 1:2]
                else:
                    m7 = {128: 7, 32: 5}[L]
                    qr, qi = g1r[:, s_, m7:m7 + 1], g1i[:, s_, m7:m7 + 1]
                    lr, li = Sr[:, a - 1:a], Si[:, a - 1:a]
                rds = [nm("Sr"), nm("Si"), "hfin", "g1", nm("ini")]
                if qr is None:
                    P.op("pool", lambda e: e.memset(ini[:], 0.0), reads=[nm("ini")], writes=[nm("ini")])
                else:
                    P.op("dve", lambda e, li=li, qi=qi: e.tensor_tensor(out=ini[:, 2:3], in0=li, in1=qi, op=ALU.mult), reads=rds, writes=[nm("ini")])
                    P.op("dve", lambda e, lr=lr, qr=qr: e.scalar_tensor_tensor(out=ini[:, 0:1], in0=lr, scalar=qr, in1=ini[:, 2:3], op0=ALU.mult, op1=ALU.subtract), reads=rds, writes=[nm("ini")])
                    P.op("dve", lambda e, li=li, qr=qr: e.tensor_tensor(out=ini[:, 3:4], in0=li, in1=qr, op=ALU.mult), reads=rds, writes=[nm("ini")])
                    P.op("dve", lambda e, lr=lr, qi=qi: e.scalar_tensor_tensor(out=ini[:, 1:2], in0=lr, scalar=qi, in1=ini[:, 3:4], op0=ALU.mult, op1=ALU.add), reads=rds, writes=[nm("ini")])
                P.op("dve", lambda e, a=a, b=b: e.tensor_tensor_scan(out=Sr[:, a:b], data0=rho[:, s_, 0:L], data1=Wr[:, a:b], initial=ini[:, 0:1], op0=ALU.mult, op1=ALU.add),
                     reads=[nm("Wr"), "rhot", nm("ini")], writes=[nm("Sr")])
                P.op("dve", lambda e, a=a, b=b: e.tensor_tensor_scan(out=Si[:, a:b], data0=rho[:, s_, 0:L], data1=Wi[:, a:b], initial=ini[:, 1:2], op0=ALU.mult, op1=ALU.add),
                     reads=[nm("Wi"), "rhot", nm("ini")], writes=[nm("Si")])
            P.op("dve", lambda e: e.tensor_tensor(out=v3(ta), in0=v3(Sr), in1=c3, op=ALU.mult), reads=[nm("Sr"), "ph"], writes=[nm("ta")])
            P.op("dve", lambda e: e.tensor_tensor(out=v3(Vr), in0=v3(Si), in1=s3, op=ALU.mult), reads=[nm("Si"), "ph"], writes=[nm("V0")])
            P.op("dve", lambda e: e.tensor_tensor(out=Wr[:], in0=ta[:], in1=Vr[:], op=ALU.subtract), reads=[nm("ta"), nm("V0")], writes=[nm("Wr")])
            P.op("pool", lambda e: e.tensor_tensor(out=v3(tc_), in0=v3(Si), in1=c3, op=ALU.mult), reads=[nm("Si"), "ph"], writes=[nm("tc")])
            P.op("pool", lambda e: e.tensor_tensor(out=v3(Vi), in0=v3(Sr), in1=s3, op=ALU.mult), reads=[nm("Sr"), "ph"], writes=[nm("V1")])
            P.op("pool", lambda e: e.tensor_tensor(out=Wi[:], in0=tc_[:], in1=Vi[:], op=ALU.add), reads=[nm("tc"), nm("V1")], writes=[nm("Wi")])
            for ri, H in enumerate((Wr, Wi)):
                hn = nm(("Wr", "Wi")[ri])
                if d == 0:
                    P.op("act", lambda e, ri=ri, H=H: e.activation(out=shb[:, ri, 1:N], in_=H[:, 0:N - 1], func=AF.Copy), reads=[hn], writes=[nm("shb")])
                    edge = shb[:, ri, 0:1]
                else:
                    P.op("act", lambda e, ri=ri, H=H: e.activation(out=shb[:, ri, 0:N - 1], in_=H[:, N - 2::-1], func=AF.Copy), reads=[hn], writes=[nm("shb")])
                    edge = shb[:, ri, N - 1:N]
                if use_h0:
                    P.op("dve", lambda e, edge=edge, ri=ri: e.tensor_copy(out=edge, in_=hfin[:, s_, ri:ri + 1]), reads=["hfin", nm("shb")], writes=[nm("shb")])
                else:
                    P.op("pool", lambda e, edge=edge: e.memset(edge, 0.0), reads=[nm("shb")], writes=[nm("shb")])
            if with_output:
                P.dma(lambda e: e.dma_start(out=self.ssh_d[d, gp, :, :, 0:N].rearrange("r p n -> p r n"), in_=shb[:]), reads=[nm("shb")], writes=["ssh_d"])
            if store_final:
                for ri, H in enumerate((Wr, Wi)):
                    P.op("dve", lambda e, ri=ri, H=H: e.tensor_copy(out=hnew[:, s_, ri:ri + 1], in_=H[:, N - 1:N]), reads=[nm(("Wr", "Wi")[ri]), nm("shb")], writes=["hnew"])
        SA(0)
        for s_ in range(16):
            if s_ + 1 < 16:
                SA(s_ + 1)
            SB(s_)
        if store_final:
            P.op("dve", lambda e: e.tensor_copy(out=hfin[:], in_=hnew[:]), reads=["hnew", "s5shb_0", "s5shb_1", "s5ini_0", "s5ini_1"], writes=["hfin"])
        P.release()
        if with_output:
            Yb = P.sb_at("s5Y", [128, 16, N], BF16, ut_off)
            ssbs = [P.sb("s5ssb%d" % i, [128, 2, 2, N], BF16) for i in range(2)]
            for g in range(16):
                gp = g // 2
                ssb, ssn = ssbs[gp % 2], "s5ssb%d" % (gp % 2)
                if g % 2 == 0:
                    for d in range(2):
                        P.dma(lambda e, d=d, gp=gp, ssb=ssb: e.dma_start(out=ssb[:, d, :, :], in_=self.ssh_d[d, gp, :, :, 0:N].rearrange("r p n -> p r n")), reads=["ssh_d"], writes=[ssn])
                for h in range(nh):
                    pb = 4 + (g * nh + h) % 2
                    sl = slice(h * W, (h + 1) * W)
                    self.mm(self.ps[pb][:, 0:W], self.KtotT[:, g, :], U[:, g, sl], True, False, ["S5S", "s5U"], [self.PS[pb]])
                    for d in range(2):
                        for ri in range(2):
                            self.mm(self.ps[pb][:, 0:W], self.EZ[:, d, g, ri, :], ssb[:, d, ri, sl], False, d == 1 and ri == 1, ["S5S", ssn], [self.PS[pb]])
                    if pb == 4:
                        P.op("act", lambda e, g=g, sl=sl, pb=pb: e.activation(out=Yb[:, g, sl], in_=self.ps[pb][:, 0:W], func=AF.Copy), reads=[self.PS[pb]], writes=["s5ut"])
                    else:
                        P.op("dve", lambda e, g=g, sl=sl, pb=pb: e.tensor_copy(out=Yb[:, g, sl], in_=self.ps[pb][:, 0:W]), reads=[self.PS[pb]], writes=["s5ut"])
            if self.dbg and tag == "x" and self.stop == "s5":
                oy = self.dbg_out("Yb", [128, 16, N], BF16)
                P.dma(lambda e: e.dma_start(out=oy, in_=Yb[:]), reads=["s5ut"])
                ou = self.dbg_out("U", [128, 16, N], BF16)
                P.dma(lambda e: e.dma_start(out=ou, in_=U[:]), reads=["s5U"])
                self.dump("ssh", self.ssh_d, [2, 8, 2, 128, T // 8], BF16)
                for nm, t_, shp, dt_ in (("FT", self.FT, [128, 2, 16, 2, 128], BF16), ("EZ", self.EZ, [128, 2, 16, 2, 128], BF16), ("KtotT", self.KtotT, [128, 16, 128], BF16),
                                         ("phc", self.phc, [128, 16, 128], F32), ("phs", self.phs, [128, 16, 128], F32), ("rho", self.rho, [128, 16, 128], F32),
                                         ("g1r", self.g1r, [128, 16, 8], F32), ("g1i", self.g1i, [128, 16, 8], F32), ("hfin", self.hfin, [128, 16, 2], F32)):
                    od = self.dbg_out(nm, shp, dt_)
                    P.dma(lambda e, od=od, t_=t_: e.dma_start(out=od, in_=t_[:]), reads=["S5S", "hfin"])
            zT = P.sb_at("s5zT", [128, 2, TT], BF16, U_off)
            y2 = P.sb("s5y2", [128, W], F32)
            sgm = P.sb("s5sg", [128, W], F32)
            cnt = 0
            for hc in range(2):
                zv = zT[:, hc, :].rearrange("p (j i) -> p i j", i=8)
                for i0 in range(8):
                    for h in range(nh):
                        pb = 6 + cnt % 2
                        cnt += 1
                        sl = slice(h * W, (h + 1) * W)
                        for g8 in range(8):
                            self.mm(self.ps[pb][:, 0:W], sel[:, i0, g8, :], Yb[:, hc * 8 + g8, sl], g8 == 0, g8 == 7, ["sel", "s5ut"], [self.PS[pb]])
                        P.op("act", lambda e, pb=pb: e.activation(out=y2[:], in_=self.ps[pb][:, 0:W], func=AF.Square), reads=[self.PS[pb]], writes=["s5y2"])
                        P.op("dve", lambda e: e.tensor_scalar(out=y2[:], in0=y2[:], scalar1=0.044715, scalar2=1.0, op0=ALU.mult, op1=ALU.add), reads=["s5y2"], writes=["s5y2"])
                        P.op("dve", lambda e, pb=pb: e.tensor_tensor(out=y2[:], in0=self.ps[pb][:, 0:W], in1=y2[:], op=ALU.mult), reads=[self.PS[pb], "s5y2"], writes=["s5y2"])
                        P.op("act", lambda e: e.activation(out=sgm[:], in_=y2[:], func=AF.Sigmoid, scale=1.5957691216057308), reads=["s5y2"], writes=["s5sg"])
                        P.op("dve", lambda e, pb=pb, zv=zv, i0=i0, sl=sl: e.tensor_tensor(out=zv[:, i0, sl], in0=self.ps[pb][:, 0:W], in1=sgm[:], op=ALU.mult), reads=[self.PS[pb], "s5sg"], writes=["s5U"])
            BW = min(512, TT)
            gt = P.sb("s5gt", [128, BW], F32)
            obs = [P.sb("s5ob%d" % i, [128, BW], BF16) for i in range(2)]
            for b in range(TT // BW):
                sl = slice(b * BW, (b + 1) * BW)
                for ho in range(2):
                    pb = 2 + ho
                    for hc in range(2):
                        self.mm(self.ps[pb][:, 0:BW], self.wglu[:, hc, ho * 128:(ho + 1) * 128], zT[:, hc, sl], hc == 0, hc == 1, ["wglu", "s5U"], [self.PS[pb]])
                    P.op("act", lambda e, pb=pb, ho=ho: e.activation(out=gt[:], in_=self.ps[pb][:, 0:BW], func=AF.Sigmoid, bias=self.bglu[:, ho:ho + 1], scale=1.0), reads=[self.PS[pb], "wglu"], writes=["s5gt"])
                    ob = obs[ho]
                    P.op("dve", lambda e, ob=ob, ho=ho, sl=sl: e.tensor_tensor(out=ob[:], in0=zT[:, ho, sl], in1=gt[:], op=ALU.mult), reads=["s5U", "s5gt"], writes=["s5ob%d" % ho])
                    P.dma(lambda e, ob=ob, ho=ho, sl=sl: e.dma_start(out=out_dst[ho * 128:(ho + 1) * 128, sl], in_=ob[:]), reads=["s5ob%d" % ho])
        P.release()

    def phase_s5(self, l, last):
        P = self.P
        P.mark()
        self.sel = P.sb("sel", [128, 8, 8, 128], BF16)
        self.KtotT = P.sb("KtotT", [128, 16, 128], BF16)
        self.FT = P.sb("FT", [128, 2, 16, 2, 128], BF16)
        self.EZ = P.sb("EZ", [128, 2, 16, 2, 128], BF16)
        self.phc = P.sb("phc", [128, 16, 128], F32)
        self.phs = P.sb("phs", [128, 16, 128], F32)
        self.rho = P.sb("rhot", [128, 16, 128], F32)
        self.g1r = P.sb("g1r", [128, 16, 8], F32)
        self.g1i = P.sb("g1i", [128, 16, 8], F32)
        self.hfin = P.sb("hfin", [128, 16, 2], F32)
        self.hnew = P.sb("hnew", [128, 16, 2], F32)
        self.wglu = P.sb("wglu", [128, 2, 256], BF16)
        self.bglu = P.sb("bglu", [128, 2], F32)
        wglu, bglu = self.wglu, self.bglu
        P.dma(lambda e: e.dma_start(out=wglu[:], in_=self.s5_w_glu[l].rearrange("(k p) n -> p k n", p=128)), writes=["wglu"], eng="pool")
        P.dma(lambda e: e.dma_start(out=bglu[:], in_=self.s5_bglu_col[l]), writes=["wglu"])
        P.mark()
        self.Kf32 = P.sb("Kf32", [128, 16, 128], F32)
        self.s5_consts()
        self.s5_setup(l)
        P.op("pool", lambda e: e.memset(self.hnew[:], 0.0), reads=["S5S"], writes=["S5S", "ph", "rhot", "g1", "FT", "hnew", "sel"])
        P.release()
        self.s5_run(LC // 8, self.ucT_d, False, not last, self.catcT_d[512:768, :], True, "c")
        self.s5_run(T // 8, self.uT_d, True, True, self.catT_d[512:768, :], False, "x")
        P.release()

    def dump(self, name, src, shape, dt):
        o = self.dbg_out(name, shape, dt)
        self.P.dma(lambda e: e.dma_start(out=o, in_=src))

    def build(self):
        P = self.P
        self.x_src, self.xc_src = self.x_in, self.ctx_in
        self.logits_x = P.sb("logits_x", [128, T // 128, NEXP], F32)
        self.logits_c = P.sb("logits_c", [128, LC // 128, NEXP], F32)
        P.op("pool", lambda e: e.memset(self.logits_x[:], 0.0), writes=["logits0"])
        P.op("pool", lambda e: e.memset(self.logits_c[:], 0.0), writes=["logits1"])
        for l in range(self.nlayers):
            last = l == DEPTH - 1
            self.phase_mod(l)
            if self.stop == "mod":
                self.dump("modrows", self.modrows[l], [2, 6 * DM], F32)
                break
            P.mark()
            self.load_w_in(l)
            self.build_wext(l, 1)
            self.phase_inproj(l, True, not last)
            self.build_wext(l, 0)
            self.phase_inproj(l, False, True)
            P.release()
            if self.stop == "inproj":
                self.dump("qT", self.qT_d, [512, T], BF16)
                self.dump("kT", self.kT_d, [256, T], BF16)
                self.dump("v", self.v_d, [T, 130], BF16)
                self.dump("uT", self.uT_d, [256, T], BF16)
                self.dump("hT", self.hT_d, [256, T], BF16)
                self.dump("kcT", self.kcT_d, [256, LC], BF16)
                self.dump("ucT", self.ucT_d, [256, LC], BF16)
                break
            if os.environ.get("DBG_SKIPATTN") is None:
                self.phase_attn(l, not last)
            if self.stop == "attn":
                self.dump("catT", self.catT_d, [DM, T], BF16)
                self.dump("catcT", self.catcT_d, [DM, LC], BF16)
                break
            if os.environ.get("DBG_SKIPS5") is None:
                self.phase_s5(l, last)
            if self.stop == "s5":
                self.dump("catT", self.catT_d, [DM, T], BF16)
                self.dump("catcT", self.catcT_d, [DM, LC], BF16)
                break
            if not last:
                self.phase_conv(l, True)
            self.phase_conv(l, False)
            if self.stop == "conv":
                self.dump("catT", self.catT_d, [DM, T], BF16)
                self.dump("catcT", self.catcT_d, [DM, LC], BF16)
                break
            if not last:
                self.phase_outproj(l, 1)
            self.phase_outproj(l, 0)
            if self.stop == "outproj":
                self.dump("catT", self.catT_d, [DM, T], BF16)
                self.dump("catcT", self.catcT_d, [DM, LC], BF16)
                self.dump("xmid", self.xmid_d, [T, DM], F32)
                self.dump("xcmid", self.xcmid_d, [LC, DM], F32)
                self.dump("h2", self.h2_d, [T, DM], BF16)
                lo = self.dbg_out("logits", [128, T // 128, NEXP], F32)
                P.dma(lambda e: e.dma_start(out=lo, in_=self.logits_x[:]), reads=["logits0"])
                break
            P.barrier()
            self.phase_moe(l, not last)
            if self.stop == "moe":
                self.dump("h2", self.h2_d, [T, DM], BF16)
                self.dump("h2c", self.h2c_d, [LC, DM], BF16)
                self.dump("moe", self.moe_d, [T, DM], F32)
                self.dump("moec", self.moec_d, [LC, DM], F32)
                lo = self.dbg_out("logits", [128, T // 128, NEXP], F32)
                P.dma(lambda e: e.dma_start(out=lo, in_=self.logits_x[:]), reads=["logits0"])
                lo2 = self.dbg_out("logitsc", [128, LC // 128, NEXP], F32)
                P.dma(lambda e: e.dma_start(out=lo2, in_=self.logits_c[:]), reads=["logits1"])
                break
            if not last:
                self.phase_ln2(l, 1, self.xc1_d)
            self.phase_ln2(l, 0, self.out if last else self.x1_d)
            self.x_src, self.xc_src = self.x1_d, self.xc1_d
            if self.stop == "ln2":
                self.dump("x1", self.x1_d, [T, DM], F32)
                self.dump("xc1", self.xc1_d, [LC, DM], F32)
                break
        P.barrier()
        P.emit()
        return self.nc


def rope_tables():
    t = np.arange(T)
    pos_row = (t // 64).astype(np.float32)
    pos_col = (t % 64).astype(np.float32)
    inv_freq = (10000.0 ** (-np.arange(16, dtype=np.float32) / 16)).astype(np.float32)
    ang = np.zeros((64, T), np.float32)
    for j in range(64):
        pos = pos_row if j < 32 else pos_col
        ang[j] = pos * inv_freq[j % 16]
    cos = np.cos(ang).astype(np.float32)
    sin = np.sin(ang).astype(np.float32)
    return np.ascontiguousarray(np.concatenate([cos, cos], 0)), np.ascontiguousarray(np.concatenate([sin, sin], 0))


def make_in_maps(inputs, ncores=2):
    f = lambda a: np.ascontiguousarray(np.asarray(a, dtype=np.float32))
    cosT, sinT = rope_tables()
    maps = []
    for b in range(ncores):
        ccol = np.stack([f(inputs["c"])[b].reshape(8, 128).T, f(inputs["c_ctx"]).reshape(8, 128).T], axis=-1)
        m = {"x": f(inputs["x"])[b], "ctx": f(inputs["ctx"])[b], "ccol": np.ascontiguousarray(ccol),
             "w_mod": f(inputs["w_mod"]), "b_mod": f(inputs["b_mod"]), "w_in": f(inputs["w_in"]), "b_in": f(inputs["b_in"]),
             "cosT": cosT, "sinT": sinT, "attn_sink": f(inputs["attn_sink"]),
             "conv_w_col": np.ascontiguousarray(f(inputs["conv_w_dw"])[:, :, 0, :].reshape(DEPTH, 31, 2, 128).transpose(0, 3, 2, 1)),
             "conv_vec_col": np.ascontiguousarray(np.stack([f(inputs[k]).reshape(DEPTH, 2, 128).transpose(0, 2, 1)
                                                            for k in ("conv_b_dw", "conv_ln_g", "conv_ln_b", "conv_b_pw")], axis=-1)),
             "conv_w_pw": f(inputs["conv_w_pw"])}
        def nat(a):
            a = f(a)
            return a
        lam = lambda k: f(inputs[k]).reshape(DEPTH, 2, 8, 2, 64).transpose(0, 3, 4, 1, 2).reshape(DEPTH, 128, 16)
        ldt = np.broadcast_to(f(inputs["s5_log_dt"]).reshape(DEPTH, 2, 8, 2, 1).transpose(0, 3, 4, 1, 2), (DEPTH, 2, 64, 2, 8)).reshape(DEPTH, 128, 16)
        bm_ = lambda k: f(inputs[k]).reshape(DEPTH, 2, 8, 2, 64, 16).transpose(0, 3, 4, 1, 2, 5).reshape(DEPTH, 128, 256)
        cm_ = lambda k: f(inputs[k]).reshape(DEPTH, 2, 8, 2, 16, 64).transpose(0, 3, 5, 1, 2, 4).reshape(DEPTH, 128, 256)
        m["s5nat"] = np.ascontiguousarray(np.concatenate([lam("s5_lam_re"), lam("s5_lam_im"), ldt, bm_("s5_b_re"), bm_("s5_b_im"), cm_("s5_c_re"), cm_("s5_c_im")], axis=2))
        m["s5_dcol"] = np.ascontiguousarray(np.tile(f(inputs["s5_d"]).reshape(DEPTH, 16, 16).transpose(0, 2, 1), (1, 8, 1)))
        m["s5_w_glu"] = f(inputs["s5_w_glu"])
        m["s5_bglu_col"] = np.ascontiguousarray(f(inputs["s5_b_glu"]).reshape(DEPTH, 2, 128).transpose(0, 2, 1))
        for k in ("w_out", "b_out", "ln1_g", "ln1_b", "ln2_g", "ln2_b", "w_router", "exp_w_gate", "exp_w_up", "exp_w_down"):
            m[k] = f(inputs[k])
        maps.append(m)
    return maps


def kernel(**inputs):
    B = Builder()
    nc = B.build()
    maps = make_in_maps(inputs)
    res = run_bass_kernel_spmd(nc, maps, core_ids=[0, 1])
    return np.stack([r["out"] for r in res.results], 0).astype(np.float32)
```

```python
import os
import numpy as np
import concourse.bass as bass
import concourse.mybir as mybir
from concourse.bass_utils import run_bass_kernel_spmd

F32 = mybir.dt.float32
BF16 = mybir.dt.bfloat16
I32 = mybir.dt.int32
AF = mybir.ActivationFunctionType
ALU = mybir.AluOpType
AX = mybir.AxisListType

ENGS = ("pe", "act", "dve", "pool", "sp")
NDMA = 24


class Prog:
    def __init__(self, nc):
        self.nc = nc
        self.q = {e: [] for e in ENGS}
        self.cnt = {e: 0 for e in ENGS}
        self.seen = {e: {} for e in ENGS}
        self.last_w = {}
        self.readers = {}
        self.dma_i = {e: 0 for e in ENGS}
        self.dma_slot_val = {(e, i): 0 for e in ENGS for i in range(NDMA)}
        self.sb_off = 16640
        self.sb_marks = []
        self.sb_max = 0
        self.uid = 0

    def sb(self, name, shape, dtype=F32):
        self.uid += 1
        name = "%s_%d" % (name, self.uid)
        esz = mybir.dt.size(dtype)
        n = 1
        for s in shape[1:]:
            n *= s
        nbytes = (n * esz + 63) // 64 * 64
        t = self.nc.alloc_sbuf_tensor_at(name, list(shape), dtype, offset=self.sb_off)
        self.last_off = self.sb_off
        self.sb_off += nbytes
        self.sb_max = max(self.sb_max, self.sb_off)
        assert self.sb_off <= 229376, (name, self.sb_off)
        return t

    def sb_at(self, name, shape, dtype, offset):
        self.uid += 1
        return self.nc.alloc_sbuf_tensor_at("%s_%d" % (name, self.uid), list(shape), dtype, offset=offset)

    def mark(self):
        self.sb_marks.append(self.sb_off)

    def release(self):
        self.barrier()
        self.sb_off = self.sb_marks.pop()

    def _deps(self, reads, writes):
        deps = {}

        def add(k, v):
            if v > deps.get(k, 0):
                deps[k] = v
        for r in reads:
            w = self.last_w.get(r)
            if w:
                add(*w)
        for r in writes:
            w = self.last_w.get(r)
            if w:
                add(*w)
            for k, v in self.readers.get(r, {}).items():
                add(k, v)
        return deps

    def _commit(self, me, reads, writes):
        k, v = me
        for r in reads:
            self.readers.setdefault(r, {})[k] = v
        for r in writes:
            self.last_w[r] = me
            self.readers[r] = {}

    def op(self, eng, fn, reads=(), writes=()):
        deps = self._deps(reads, writes)
        waits = []
        for k, v in deps.items():
            if eng == "pe" and k == "pe":
                continue
            if self.seen[eng].get(k, 0) >= v:
                continue
            self.seen[eng][k] = v
            waits.append((k, v))
        self.cnt[eng] += 1
        me = (eng, self.cnt[eng])
        self.q[eng].append((waits, fn, eng, 1))
        self._commit(me, reads, writes)
        return me

    def dma(self, fn, reads=(), writes=(), eng="sp"):
        slot = (eng, self.dma_i[eng] % NDMA)
        self.dma_i[eng] += 1
        key = ("dma", slot)
        deps = self._deps(reads, writes)
        prev = self.dma_slot_val[slot]
        if prev:
            deps[key] = max(deps.get(key, 0), prev)
        waits = []
        for k, v in deps.items():
            if self.seen[eng].get(k, 0) >= v:
                continue
            self.seen[eng][k] = v
            waits.append((k, v))
        val = prev + 16
        self.dma_slot_val[slot] = val
        me = (key, val)
        self.q[eng].append((waits, fn, key, 16))
        self._commit(me, reads, writes)
        return me

    def barrier(self):
        targets = [(e, self.cnt[e]) for e in ENGS if self.cnt[e]]
        targets += [(("dma", s), v) for s, v in self.dma_slot_val.items() if v]
        for eng in ENGS:
            waits = []
            for k, v in targets:
                if self.seen[eng].get(k, 0) >= v:
                    continue
                self.seen[eng][k] = v
                waits.append((k, v))
            if waits:
                self.q[eng].append((waits, None, None, 0))

    def emit(self):
        nc = self.nc
        from contextlib import ExitStack
        with ExitStack() as st:
            sems = {}
            for e in ENGS:
                sems[e] = st.enter_context(nc.semaphore("s_" + e))
            for (e, i), v in self.dma_slot_val.items():
                if v:
                    sems[("dma", (e, i))] = st.enter_context(nc.semaphore("s_dma_%s%d" % (e, i)))
            block = st.enter_context(nc.Block())
            handles = {"pe": block.tensor, "act": block.scalar, "dve": block.vector,
                       "pool": block.gpsimd, "sp": block.sync}
            for e in ENGS:
                items = self.q[e]

                def body(h, items=items):
                    for waits, fn, incsem, incv in items:
                        for k, v in waits:
                            h.wait_ge(sems[k], v)
                        if fn is not None:
                            fn(h).then_inc(sems[incsem], incv)
                handles[e](body)


T = 8192
DM = 1024
LC = 256
NEXP = 16
FF = 2048
DEPTH = 2
ALPHA = (2.0 * DEPTH) ** 0.25
EPS = 1e-5
NCH = 19
CH_Q, CH_QR, CH_K, CH_KR, CH_V, CH_U, CH_A, CH_G = 0, 4, 8, 10, 12, 13, 15, 17


class Builder:
    def __init__(self, stop=None, dbg=False, nlayers=DEPTH):
        nc = bass.Bass("TRN2", target_bir_lowering=False)
        self.nc = nc
        self.P = Prog(nc)
        self.stop = stop
        self.dbg = dbg
        self.nlayers = nlayers
        self.dbg_outs = {}

        def di(name, shape, dt=F32):
            return nc.dram_tensor(name, list(shape), dt, kind="ExternalInput").ap()
        self.x_in = di("x", [T, DM])
        self.ctx_in = di("ctx", [LC, DM])
        self.ccol = di("ccol", [128, 8, 2])
        self.w_mod = di("w_mod", [DEPTH, DM, 6 * DM])
        self.b_mod = di("b_mod", [DEPTH, 6 * DM])
        self.w_in = di("w_in", [DEPTH, DM, 1536])
        self.b_in = di("b_in", [DEPTH, 1536])
        self.cosT = di("cosT", [128, T])
        self.sinT = di("sinT", [128, T])
        self.attn_sink = di("attn_sink", [DEPTH, 8])
        self.conv_w_col = di("conv_w_col", [DEPTH, 128, 2, 31])
        self.conv_vec_col = di("conv_vec_col", [DEPTH, 128, 2, 4])
        self.conv_w_pw = di("conv_w_pw", [DEPTH, 256, 256])
        self.w_out = di("w_out", [DEPTH, DM, DM])
        self.b_out = di("b_out", [DEPTH, DM])
        self.ln1_g = di("ln1_g", [DEPTH, DM])
        self.ln1_b = di("ln1_b", [DEPTH, DM])
        self.ln2_g = di("ln2_g", [DEPTH, DM])
        self.ln2_b = di("ln2_b", [DEPTH, DM])
        self.w_router = di("w_router", [DEPTH, DM, NEXP])
        self.s5nat = di("s5nat", [DEPTH, 128, 1072])
        self.s5_dcol = di("s5_dcol", [DEPTH, 128, 16])
        self.s5_w_glu = di("s5_w_glu", [DEPTH, 256, 256])
        self.s5_bglu_col = di("s5_bglu_col", [DEPTH, 128, 2])
        self.w_gate = di("exp_w_gate", [DEPTH, NEXP, DM, FF])
        self.w_up = di("exp_w_up", [DEPTH, NEXP, DM, FF])
        self.w_down = di("exp_w_down", [DEPTH, NEXP, FF, DM])
        self.out = nc.dram_tensor("out", [T, DM], F32, kind="ExternalOutput").ap()

        def dscr(name, shape, dt=F32):
            return nc.dram_tensor(name, list(shape), dt, kind="Internal").ap()
        self.dscr = dscr
        self.modrows = dscr("modrows", [DEPTH, 2, 6 * DM])
        self.qT_d = dscr("qT_d", [512, T], BF16)
        self.kT_d = dscr("kT_d", [256, T], BF16)
        self.v_d = dscr("v_d", [T, 130], BF16)
        self.uT_d = dscr("uT_d", [256, T], BF16)
        self.hT_d = dscr("hT_d", [256, T], BF16)
        self.qcT_d = dscr("qcT_d", [512, LC], BF16)
        self.kcT_d = dscr("kcT_d", [256, LC], BF16)
        self.vc_d = dscr("vc_d", [LC, 130], BF16)
        self.ucT_d = dscr("ucT_d", [256, LC], BF16)
        self.hcT_d = dscr("hcT_d", [256, LC], BF16)
        self.catT_d = dscr("catT_d", [DM, T], BF16)
        self.catcT_d = dscr("catcT_d", [DM, LC], BF16)
        self.xmid_d = dscr("xmid_d", [T, DM], F32)
        self.xcmid_d = dscr("xcmid_d", [LC, DM], F32)
        self.h2_d = dscr("h2_d", [T, DM], BF16)
        self.h2c_d = dscr("h2c_d", [LC, DM], BF16)
        self.moe_d = dscr("moe_d", [T, DM], F32)
        self.moec_d = dscr("moec_d", [LC, DM], F32)
        self.x1_d = dscr("x1_d", [T, DM], F32)
        self.ssh_d = dscr("ssh_d", [2, 8, 2, 128, T // 8], BF16)
        self.xc1_d = dscr("xc1_d", [LC, DM], F32)

        self.ps = [nc.alloc_psum_tensor("ps%d" % i, [128, 512], F32) for i in range(8)]
        self.PS = ["ps%d" % i for i in range(8)]
        self.consts()

    def dbg_out(self, name, shape, dt=F32):
        t = self.nc.dram_tensor("dbg_" + name, list(shape), dt, kind="ExternalOutput").ap()
        self.dbg_outs[name] = t
        return t

    def consts(self):
        P = self.P
        self.identf = P.sb("identf", [128, 128], F32)
        self.ident = P.sb("ident", [128, 128], BF16)
        self.ones_bf = P.sb("ones", [128, 128], BF16)
        P.op("pool", lambda e: e.memset(self.identf[:], 1.0), writes=["identf"])
        P.op("pool", lambda e: e.affine_select(out=self.identf[:], in_=self.identf[:], pattern=[[-1, 128]],
                                               compare_op=ALU.is_equal, fill=0.0, base=0, channel_multiplier=1),
             reads=["identf"], writes=["identf"])
        P.op("dve", lambda e: e.tensor_copy(out=self.ident[:], in_=self.identf[:]), reads=["identf"], writes=["ident"])
        P.op("pool", lambda e: e.memset(self.ones_bf[:], 1.0), writes=["ones"])

    def mm(self, out, lhsT, rhs, start, stop, reads, writes):
        self.P.op("pe", lambda e: e.matmul(out=out, lhsT=lhsT, rhs=rhs, start=start, stop=stop), reads, writes)

    def phase_mod(self, l):
        P = self.P
        P.mark()
        sc = P.sb("sc", [128, 8, 2], F32)
        scb = P.sb("scb", [128, 8, 2], BF16)
        P.dma(lambda e: e.dma_start(out=sc[:], in_=self.ccol), writes=["sc"])
        P.op("act", lambda e: e.activation(out=scb[:], in_=sc[:], func=AF.Silu), reads=["sc"], writes=["scb"])
        bm = P.sb("bm", [2, 6 * DM], F32)
        P.dma(lambda e: e.dma_start(out=bm[:], in_=self.b_mod[l].partition_broadcast(2)), writes=["bm"])
        mods = P.sb("mods", [2, 6 * DM], F32)
        wts = [P.sb("wmt%d" % i, [128, 8, 512], BF16) for i in range(2)]
        for ch in range(12):
            wt = wts[ch % 2]
            wn = "wmt%d" % (ch % 2)
            src = self.w_mod[l][:, ch * 512:(ch + 1) * 512].rearrange("(k p) n -> p k n", p=128)
            P.dma(lambda e, wt=wt, src=src: e.dma_start(out=wt[:], in_=src), writes=[wn], eng="pool")
            pb = ch % 2
            for k in range(8):
                self.mm(self.ps[pb][0:2, :], scb[:, k, :], wt[:, k, :], k == 0, k == 7, [wn, "scb"], [self.PS[pb]])
            P.op("dve", lambda e, ch=ch, pb=pb: e.tensor_tensor(out=mods[:, ch * 512:(ch + 1) * 512], in0=self.ps[pb][0:2, :],
                                                               in1=bm[:, ch * 512:(ch + 1) * 512], op=ALU.add),
                 reads=[self.PS[pb], "bm"], writes=["mods"])
        P.dma(lambda e: e.dma_start(out=self.modrows[l], in_=mods[:]), reads=["mods"], writes=["modrows"])
        P.release()

    def load_w_in(self, l):
        P = self.P
        self.w9 = P.sb("w9", [128, 9, 1536], BF16)
        w9 = self.w9
        P.op("pool", lambda e: e.memset(w9[:, 8, :], 0.0), writes=["w9"])
        for k0 in range(0, 8, 2):
            src = self.w_in[l][k0 * 128:(k0 + 2) * 128, :].rearrange("(k p) n -> p k n", p=128)
            P.dma(lambda e, k0=k0, src=src: e.dma_start(out=w9[:, k0:k0 + 2, :], in_=src), writes=["w9"], eng="pool")
        P.dma(lambda e: e.dma_start(out=w9[0:1, 8, :], in_=self.b_in[l:l + 1, :]), writes=["w9"], eng="pool")
        self.wext = P.sb("wext", [128, 9, NCH * 128], BF16)
        self.bext = P.sb("bext", [128, NCH], F32)

    def build_wext(self, l, var):
        P = self.P
        w9, wext = self.w9, self.wext
        P.mark()
        P.op("act", lambda e: e.activation(out=wext[:, :, 0:512], in_=w9[:, :, 0:512], func=AF.Copy), reads=["w9"], writes=["wext"])
        P.op("dve", lambda e: e.tensor_copy(out=wext[:, :, 12 * 128:19 * 128], in_=w9[:, :, 640:1536]), reads=["w9"], writes=["wext"])
        kd = wext[:, :, 1024:1280].rearrange("p k (h d f) -> p k h d f", h=2, d=2)
        ks = w9[:, :, 512:640].rearrange("p k (h f) -> p k h f", h=2)
        for d in range(2):
            P.op("pool", lambda e, d=d: e.tensor_copy(out=kd[:, :, :, d, :], in_=ks), reads=["w9"], writes=["wext"])
        for (dst0, src0, n) in ((512, 0, 512), (1280, 1024, 256)):
            dv = wext[:, :, dst0:dst0 + n].rearrange("p k (a s f) -> p k a s f", s=2, f=16)
            sv = wext[:, :, src0:src0 + n].rearrange("p k (a s f) -> p k a s f", s=2, f=16)
            P.op("act", lambda e, dv=dv, sv=sv: e.mul(out=dv[:, :, :, 0, :], in_=sv[:, :, :, 1, :], mul=-1.0),
                 reads=["wext"], writes=["wext"])
            P.op("dve", lambda e, dv=dv, sv=sv: e.tensor_copy(out=dv[:, :, :, 1, :], in_=sv[:, :, :, 0, :]),
                 reads=["wext"], writes=["wext"])
        sh = P.sb("sh", [128, 8], F32)
        scl = P.sb("scl", [128, 8], F32)
        P.dma(lambda e: e.dma_start(out=sh[:], in_=self.modrows[l, var, 0:1024].rearrange("(k p) -> p k", p=128), allow_slow_non_contiguous=True),
              reads=["modrows"], writes=["sh"])
        P.dma(lambda e: e.dma_start(out=scl[:], in_=self.modrows[l, var, 1024:2048].rearrange("(k p) -> p k", p=128), allow_slow_non_contiguous=True),
              reads=["modrows"], writes=["scl"])
        sha = P.sb("sha", [128, 9], BF16)
        P.op("pool", lambda e: e.memset(sha[:], 0.0), writes=["sha"])
        P.op("pool", lambda e: e.memset(sha[0:1, 8:9], 1.0), reads=["sha"], writes=["sha"])
        P.op("dve", lambda e: e.tensor_copy(out=sha[:, 0:8], in_=sh[:]), reads=["sh", "sha"], writes=["sha"])
        P.op("dve", lambda e: e.tensor_scalar_add(out=scl[:], in0=scl[:], scalar1=1.0), reads=["scl"], writes=["scl"])
        for c in range(NCH):
            for k in range(9):
                self.mm(self.ps[7][:, c:c + 1], wext[:, k, c * 128:(c + 1) * 128], sha[:, k:k + 1], k == 0, k == 8,
                        ["wext", "sha"], [self.PS[7]])
        P.op("dve", lambda e: e.tensor_copy(out=self.bext[:], in_=self.ps[7][:, 0:NCH]), reads=[self.PS[7]], writes=["bext"])
        for k in range(8):
            if k % 2 == 0:
                P.op("dve", lambda e, k=k: e.tensor_scalar(out=wext[:, k, :], in0=wext[:, k, :], scalar1=scl[:, k:k + 1], scalar2=None, op0=ALU.mult),
                     reads=["wext", "scl", self.PS[7]], writes=["wext"])
            else:
                P.op("act", lambda e, k=k: e.activation(out=wext[:, k, :], in_=wext[:, k, :], func=AF.Copy, scale=scl[:, k:k + 1]),
                     reads=["wext", "scl", self.PS[7]], writes=["wext"])
        P.release()

    def ln_tile_T(self, src_rows, src_res, xnT, xnT_res, col0, bufi):
        P = self.P
        xt, xn, st, mv, rs = self.ln_bufs[bufi]
        tg = "ln%d" % bufi
        P.dma(lambda e: e.dma_start(out=xt[:], in_=src_rows), reads=[src_res], writes=[tg + "xt"])
        for hh in range(2):
            P.op("dve", lambda e, hh=hh: e.bn_stats(out=st[:, hh, :], in_=xt[:, hh * 512:(hh + 1) * 512]), reads=[tg + "xt"], writes=[tg + "st"])
        P.op("dve", lambda e: e.bn_aggr(out=mv[:], in_=st[:].rearrange("p a b -> p (a b)")), reads=[tg + "st"], writes=[tg + "mv"])
        eps_t = self.eps_t
        P.op("act", lambda e: e.activation(out=rs[:], in_=mv[:, 1:2], func=AF.Sqrt, bias=eps_t[:, 0:1], scale=1.0), reads=[tg + "mv", "eps"], writes=[tg + "rs"])
        P.op("dve", lambda e: e.reciprocal(out=rs[:], in_=rs[:]), reads=[tg + "rs"], writes=[tg + "rs"])
        P.op("dve", lambda e: e.tensor_scalar(out=xn[:], in0=xt[:], scalar1=mv[:, 0:1], scalar2=rs[:, 0:1], op0=ALU.subtract, op1=ALU.mult),
             reads=[tg + "xt", tg + "mv", tg + "rs"], writes=[tg + "xn"])
        pb = 6 + bufi
        psT = self.ps[pb][:, :].bitcast(BF16)
        for k in range(8):
            P.op("pe", lambda e, k=k: e.transpose(out=psT[:, k * 128:(k + 1) * 128], in_=xn[:, k * 128:(k + 1) * 128], identity=self.ident[:]),
                 reads=[tg + "xn", "ident"], writes=[self.PS[pb]])
        P.op("act", lambda e: e.activation(out=xnT[:, :, col0:col0 + 128], in_=psT.rearrange("p (k t) -> p k t", k=8), func=AF.Copy),
             reads=[self.PS[pb]], writes=[xnT_res])

    def alloc_ln_bufs(self):
        P = self.P
        self.eps_t = eps_t = P.sb("eps", [128, 1], F32)
        P.op("pool", lambda e: e.memset(eps_t[:], EPS), writes=["eps"])
        self.ln_bufs = []
        for i in range(2):
            self.ln_bufs.append((P.sb("xt%d" % i, [128, DM], F32), P.sb("xn%d" % i, [128, DM], BF16), P.sb("st%d" % i, [128, 2, 6], F32),
                                 P.sb("mv%d" % i, [128, 2], F32), P.sb("rs%d" % i, [128, 1], F32)))

    def inproj_ln(self, src, src_res, t0, N, xnT, xres):
        for i in range(N // 128):
            self.ln_tile_T(src[t0 + i * 128:t0 + (i + 1) * 128, :], src_res, xnT, xres, i * 128, i % 2)

    def inproj_proj(self, l, t0, N, is_ctx, need_full, xnT, xres, blk):
        P = self.P
        nt = N // 128
        wext, bext = self.wext, self.bext
        bank, obc, tfc = self._bank, self._obc, self._tfc

        def proj(c):
            pb = bank[0]
            bank[0] = (pb + 1) % 5
            for k in range(8):
                self.mm(self.ps[pb][:, 0:N], wext[:, k, c * 128:(c + 1) * 128], xnT[:, k, 0:N], k == 0, k == 7, ["wext", xres], [self.PS[pb]])
            return pb

        def nob():
            i = obc[0] % len(self.ev_bf)
            obc[0] += 1
            return self.ev_bf[i], "evbf%d" % i

        def ntf():
            i = tfc[0] % len(self.ev_f)
            tfc[0] += 1
            return self.ev_f[i], "evf%d" % i
        cs, sn = self.cs_ts[blk % 2], self.sn_ts[blk % 2]
        csn, snn = "cs%d" % (blk % 2), "sn%d" % (blk % 2)
        if not is_ctx:
            P.dma(lambda e: e.dma_start(out=cs[:, 0:N], in_=self.cosT[:, t0:t0 + N]), writes=[csn])
            P.dma(lambda e: e.dma_start(out=sn[:, 0:N], in_=self.sinT[:, t0:t0 + N]), writes=[snn])
        qk = []
        if need_full:
            qk += [(c, CH_QR + c, (self.qcT_d if is_ctx else self.qT_d)[c * 128:(c + 1) * 128, t0:t0 + N]) for c in range(4)]
        qk += [(CH_K + c, CH_KR + c, (self.kcT_d if is_ctx else self.kT_d)[c * 128:(c + 1) * 128, t0:t0 + N]) for c in range(2)]
        for j, (c, cr, dst) in enumerate(qk):
            ob, on = nob()
            if is_ctx:
                pb = proj(c)
                P.op("act", lambda e, c=c, ob=ob, pb=pb: e.activation(out=ob[:, 0:N], in_=self.ps[pb][:, 0:N], func=AF.Identity, bias=bext[:, c:c + 1], scale=1.0),
                     reads=[self.PS[pb], "bext"], writes=[on])
            else:
                p0 = proj(c)
                p1 = proj(cr)
                t1, t1n = ntf()
                t2, t2n = ntf()
                P.op("dve", lambda e, c=c, p0=p0, t1=t1: e.scalar_tensor_tensor(out=t1[:, 0:N], in0=self.ps[p0][:, 0:N], scalar=bext[:, c:c + 1], in1=cs[:, 0:N], op0=ALU.add, op1=ALU.mult),
                     reads=[self.PS[p0], "bext", csn], writes=[t1n])
                P.op("dve", lambda e, cr=cr, p1=p1, t2=t2: e.scalar_tensor_tensor(out=t2[:, 0:N], in0=self.ps[p1][:, 0:N], scalar=bext[:, cr:cr + 1], in1=sn[:, 0:N], op0=ALU.add, op1=ALU.mult),
                     reads=[self.PS[p1], "bext", snn], writes=[t2n])
                P.op("pool", lambda e, ob=ob, t1=t1, t2=t2: e.tensor_tensor(out=ob[:, 0:N], in0=t1[:, 0:N], in1=t2[:, 0:N], op=ALU.add), reads=[t1n, t2n], writes=[on])
            P.dma(lambda e, dst=dst, ob=ob: e.dma_start(out=dst, in_=ob[:, 0:N]), reads=[on])
        pv = proj(CH_V)
        vb, vbn = nob()
        P.op("act", lambda e: e.activation(out=vb[:, 0:N], in_=self.ps[pv][:, 0:N], func=AF.Identity, bias=bext[:, CH_V:CH_V + 1], scale=1.0),
             reads=[self.PS[pv], "bext"], writes=[vbn])
        pt = 5
        psT = self.ps[pt][:, :].bitcast(BF16)
        for i in range(nt):
            P.op("pe", lambda e, i=i: e.transpose(out=psT[:, i * 128:(i + 1) * 128], in_=vb[:, i * 128:(i + 1) * 128], identity=self.ident[:]),
                 reads=[vbn, "ident"], writes=[self.PS[pt]])
        va = self.v_augs[blk % 2]
        van = "vaug%d" % (blk % 2)
        P.op("dve", lambda e: e.tensor_copy(out=va[:, 0:nt, :, 0:64], in_=psT[:, 0:nt * 128].rearrange("p (i h f) -> p i h f", i=nt, h=2)),
             reads=[self.PS[pt]], writes=[van])
        vdst = (self.vc_d if is_ctx else self.v_d)[t0:t0 + N, :].rearrange("(i p) f -> p i f", p=128)
        P.dma(lambda e: e.dma_start(out=vdst, in_=va[:, 0:nt, :, :].rearrange("p i h f -> p i (h f)")), reads=[van])
        for c in range(2):
            pu = proj(CH_U + c)
            ob, on = nob()
            P.op("act", lambda e, c=c, ob=ob, pu=pu: e.activation(out=ob[:, 0:N], in_=self.ps[pu][:, 0:N], func=AF.Identity, bias=bext[:, CH_U + c:CH_U + c + 1], scale=1.0),
                 reads=[self.PS[pu], "bext"], writes=[on])
            dst = (self.ucT_d if is_ctx else self.uT_d)[c * 128:(c + 1) * 128, t0:t0 + N]
            P.dma(lambda e, dst=dst, ob=ob: e.dma_start(out=dst, in_=ob[:, 0:N]), reads=[on])
        if need_full:
            for c in range(2):
                pa = proj(CH_A + c)
                pg = proj(CH_G + c)
                sg, sgn = ntf()
                ob, on = nob()
                P.op("act", lambda e, c=c, pg=pg, sg=sg: e.activation(out=sg[:, 0:N], in_=self.ps[pg][:, 0:N], func=AF.Sigmoid, bias=bext[:, CH_G + c:CH_G + c + 1], scale=1.0),
                     reads=[self.PS[pg], "bext"], writes=[sgn])
                P.op("dve", lambda e, c=c, pa=pa, sg=sg, ob=ob: e.scalar_tensor_tensor(out=ob[:, 0:N], in0=self.ps[pa][:, 0:N], scalar=bext[:, CH_A + c:CH_A + c + 1], in1=sg[:, 0:N], op0=ALU.add, op1=ALU.mult),
                     reads=[self.PS[pa], "bext", sgn], writes=[on])
                dst = (self.hcT_d if is_ctx else self.hT_d)[c * 128:(c + 1) * 128, t0:t0 + N]
                P.dma(lambda e, dst=dst, ob=ob: e.dma_start(out=dst, in_=ob[:, 0:N]), reads=[on])

    def phase_inproj(self, l, is_ctx, need_full):
        P = self.P
        P.mark()
        self.alloc_ln_bufs()
        xnTs = [P.sb("xnT%d" % i, [128, 8, 512], BF16) for i in range(2)]
        self.cs_ts = [P.sb("cs%d" % i, [128, 512], F32) for i in range(2)]
        self.sn_ts = [P.sb("sn%d" % i, [128, 512], F32) for i in range(2)]
        self.ev_f = [P.sb("evf%d" % i, [128, 512], F32) for i in range(6)]
        self.ev_bf = [P.sb("evbf%d" % i, [128, 512], BF16) for i in range(8)]
        self.v_augs = [P.sb("vaug%d" % i, [128, 4, 2, 65], BF16) for i in range(2)]
        for i, va in enumerate(self.v_augs):
            P.op("pool", lambda e, va=va: e.memset(va[:], 1.0), writes=["vaug%d" % i])
        self._bank, self._obc, self._tfc = [0], [0], [0]
        if is_ctx:
            self.inproj_ln(self.xc_src, "xc", 0, LC, xnTs[0], "xnT0")
            self.inproj_proj(l, 0, LC, True, need_full, xnTs[0], "xnT0", 0)
        else:
            nb_ = int(os.environ.get('DBG_NBLK', T // 512))
            self.inproj_ln(self.x_src, "xsrc", 0, 512, xnTs[0], "xnT0")
            for b in range(nb_):
                if b + 1 < nb_:
                    self.inproj_ln(self.x_src, "xsrc", (b + 1) * 512, 512, xnTs[(b + 1) % 2], "xnT%d" % ((b + 1) % 2))
                self.inproj_proj(l, b * 512, 512, False, True, xnTs[b % 2], "xnT%d" % (b % 2), b)
        P.release()

    def attn_consts(self, l):
        P = self.P
        mf = P.sb("mf", [128, 128], F32)
        self.mlo = P.sb("mlo", [128, 128], BF16)
        self.mhi = P.sb("mhi", [128, 128], BF16)
        for (m, name, sign) in ((self.mlo, "mlo", 1), (self.mhi, "mhi", -1)):
            P.op("pool", lambda e: e.memset(mf[:], 1.0), writes=["mf"])
            P.op("pool", lambda e, sign=sign: e.affine_select(out=mf[:], in_=mf[:], pattern=[[-sign, 128]], compare_op=ALU.is_ge, fill=0.0,
                                                               base=0, channel_multiplier=sign), reads=["mf"], writes=["mf"])
            P.op("dve", lambda e, m=m: e.tensor_copy(out=m[:], in_=mf[:]), reads=["mf"], writes=[name])
        self.esink = es = P.sb("esink", [128, 8], F32)
        P.dma(lambda e: e.dma_start(out=es[:], in_=self.attn_sink[l].partition_broadcast(128)), writes=["esink"])
        P.op("act", lambda e: e.activation(out=es[:], in_=es[:], func=AF.Exp), reads=["esink"], writes=["esink"])

    def attn_qtile(self, qblk, qres, qc0, keys, stage, stage_res, sti):
        P = self.P
        nk = len(keys)
        if int(os.environ.get("DBG_Q", 99)) < 0:
            return
        at = self.at_tile
        esink = self.esink
        for h in range(2):
            for ki, (kf, kres, vf, vres, mask) in enumerate(keys):
                for hh in range(4):
                    c, s = 2 * h + hh // 2, hh % 2
                    self.mm(self.ps[ki][:, hh * 128:(hh + 1) * 128], kf(h, s), qblk[:, c, qc0:qc0 + 128],
                            True, True, [kres, qres], [self.PS[ki]])
            pT = self.pT
            dq = int(os.environ.get("DBG_Q", 99))
            if dq == 0:
                continue
            for ki, (kf, kres, vf, vres, mask) in enumerate(keys):
                P.op("act", lambda e, ki=ki: e.activation(out=pT[:, ki, :], in_=self.ps[ki][:, :], func=AF.Exp, scale=0.125),
                     reads=[self.PS[ki]], writes=["pT%d" % ki])
                if mask is not None:
                    P.op("pool", lambda e, ki=ki, mask=mask: e.tensor_tensor(out=pT[:, ki, :].rearrange("p (a i) -> p a i", a=4), in0=pT[:, ki, :].rearrange("p (a i) -> p a i", a=4),
                                                                          in1=mask[:, :].unsqueeze(1).to_broadcast([128, 4, 128]), op=ALU.mult),
                         reads=["pT%d" % ki, "mlo", "mhi"], writes=["pT%d" % ki])
            if dq == 1:
                continue
            for hh in range(4):
                for ki, (kf, kres, vf, vres, mask) in enumerate(keys):
                    self.mm(self.ps[5][:, hh * 65:(hh + 1) * 65], pT[:, ki, hh * 128:(hh + 1) * 128], vf(h), ki == 0, ki == nk - 1,
                            ["pT%d" % ki, vres], [self.PS[5]])
            if dq == 2:
                continue
            den = self.den
            ov = self.ps[5][:, 0:260].rearrange("p (a f) -> p a f", a=4)
            P.op("dve", lambda e, h=h: e.tensor_tensor(out=den[:], in0=ov[:, :, 64], in1=esink[:, 4 * h:4 * h + 4], op=ALU.add),
                 reads=[self.PS[5], "esink"], writes=["den"])
            P.op("dve", lambda e: e.reciprocal(out=den[:], in_=den[:]), reads=["den"], writes=["den"])
            P.op("dve", lambda e, h=h: e.tensor_tensor(out=at[:, 256 * h:256 * (h + 1)].rearrange("p (a f) -> p a f", a=4), in0=ov[:, :, 0:64],
                                                      in1=den[:, :].unsqueeze(2).to_broadcast([128, 4, 64]), op=ALU.mult),
                 reads=[self.PS[5], "den"], writes=["at"])
        if int(os.environ.get("DBG_Q", 99)) <= 3:
            return
        psT = self.ps[6][:, :].bitcast(BF16)
        for c in range(4):
            P.op("pe", lambda e, c=c: e.transpose(out=psT[:, c * 128:(c + 1) * 128], in_=at[:, c * 128:(c + 1) * 128], identity=self.ident[:]),
                 reads=["at", "ident"], writes=[self.PS[6]])
        P.op("act", lambda e: e.activation(out=stage[:, :, sti * 128:(sti + 1) * 128], in_=psT[:, 0:512].rearrange("p (c t) -> p c t", c=4), func=AF.Copy),
             reads=[self.PS[6]], writes=[stage_res])

    def phase_attn(self, l, do_ctx):
        P = self.P
        P.mark()
        self.attn_consts(l)
        dbgA = int(os.environ.get("DBG_A", 99))
        if dbgA == 0:
            P.release()
            return
        self.pT = P.sb("pT", [128, 5, 512], BF16)
        self.den = P.sb("den", [128, 4], F32)
        self.at_tile = P.sb("at", [128, 512], BF16)
        kc = P.sb("kc", [128, 2, 2, LC], BF16)
        vc = P.sb("vc", [128, 2, 130], BF16)
        for s_ in range(2):
            P.dma(lambda e, s_=s_: e.dma_start(out=kc[:, s_, :, :], in_=self.kcT_d.rearrange("(h p) t -> p h t", p=128)), writes=["kc"])
            P.op("pool", lambda e, s_=s_: e.memset(kc[64 * (1 - s_):64 * (2 - s_), s_, :, :], 0.0), reads=["kc"], writes=["kc"])
        P.dma(lambda e: e.dma_start(out=vc[:], in_=self.vc_d.rearrange("(i p) f -> p i f", p=128)), writes=["vc"])
        ckeys = [(lambda h, s_, i=i: kc[:, s_, h, i * 128:(i + 1) * 128], "kc", lambda h, i=i: vc[:, i, h * 65:(h + 1) * 65], "vc", None) for i in range(2)]
        stages = [P.sb("ast%d" % i, [128, 4, 512], BF16) for i in range(2)]
        if dbgA == 1:
            P.release()
            return
        if do_ctx:
            qb = P.sb("qcb", [128, 4, LC], BF16)
            P.dma(lambda e, qb=qb: e.dma_start(out=qb[:], in_=self.qcT_d.rearrange("(c p) t -> p c t", p=128)), writes=["qcb"])
            for n in range(2):
                self.attn_qtile(qb, "qcb", n * 128, ckeys, stages[0], "ast0", n)
            P.dma(lambda e: e.dma_start(out=self.catcT_d[0:512, :].rearrange("(c p) t -> p c t", p=128), in_=stages[0][:, :, 0:LC]), reads=["ast0"])
        if dbgA == 2:
            P.release()
            return
        kx = P.sb("kx", [128, 2, 2, T], BF16)
        vx = P.sb("vx", [128, T // 128, 130], BF16)
        for s_ in range(2):
            for h in range(2):
                P.dma(lambda e, h=h, s_=s_: e.dma_start(out=kx[:, s_, h, :], in_=self.kT_d[h * 128:(h + 1) * 128, :]), writes=["kx"])
                P.op("pool", lambda e, h=h, s_=s_: e.memset(kx[64 * (1 - s_):64 * (2 - s_), s_, h, :], 0.0), reads=["kx"], writes=["kx"])
        for i0 in range(0, T // 128, 8):
            P.dma(lambda e, i0=i0: e.dma_start(out=vx[:, i0:i0 + 8, :], in_=self.v_d[i0 * 128:(i0 + 8) * 128, :].rearrange("(i p) f -> p i f", p=128)), writes=["vx"])
        qbs = [P.sb("qb%d" % i, [128, 4, 512], BF16) for i in range(2)]
        for b in range(int(os.environ.get('DBG_NBLK', T // 512))):
            qb, qn = qbs[b % 2], "qb%d" % (b % 2)
            P.dma(lambda e, qb=qb, b=b: e.dma_start(out=qb[:], in_=self.qT_d[:, b * 512:(b + 1) * 512].rearrange("(c p) t -> p c t", p=128)), writes=[qn])
            st, sn = stages[b % 2], "ast%d" % (b % 2)
            for i in range(4):
                n = 4 * b + i
                keys = []
                for (kt, mask) in ((n - 1, self.mlo), (n, None), (n + 1, self.mhi)):
                    if 0 <= kt < T // 128:
                        keys.append((lambda h, s_, kt=kt: kx[:, s_, h, kt * 128:(kt + 1) * 128], "kx", lambda h, kt=kt: vx[:, kt, h * 65:(h + 1) * 65], "vx", mask))
                self.attn_qtile(qb, qn, i * 128, keys + ckeys, st, sn, i)
            P.dma(lambda e, st=st, b=b: e.dma_start(out=self.catT_d[0:512, b * 512:(b + 1) * 512].rearrange("(c p) t -> p c t", p=128), in_=st[:]), reads=[sn])
        P.release()

    def phase_conv(self, l, is_ctx):
        P = self.P
        P.mark()
        N = LC if is_ctx else T
        src = self.hcT_d if is_ctx else self.hT_d
        dstT = self.catcT_d if is_ctx else self.catT_d
        wcol = P.sb("cw", [128, 2, 31], F32)
        bcol = P.sb("cb", [128, 2, 4], F32)
        P.dma(lambda e: e.dma_start(out=wcol[:], in_=self.conv_w_col[l]), writes=["cw"])
        P.dma(lambda e: e.dma_start(out=bcol[:], in_=self.conv_vec_col[l]), writes=["cb"])
        wpw = P.sb("wpw", [128, 2, 256], BF16)
        P.dma(lambda e: e.dma_start(out=wpw[:], in_=self.conv_w_pw[l].rearrange("(k p) n -> p k n", p=128)), writes=["wpw"], eng="pool")
        diag = P.sb("diag", [128, 2, 31, 128], BF16)
        for cc in range(2):
            for k in range(31):
                if k % 2 == 0:
                    P.op("dve", lambda e, cc=cc, k=k: e.tensor_scalar(out=diag[:, cc, k, :], in0=self.identf[:], scalar1=wcol[:, cc, k:k + 1], scalar2=None, op0=ALU.mult),
                         reads=["identf", "cw"], writes=["diag"])
                else:
                    P.op("act", lambda e, cc=cc, k=k: e.activation(out=diag[:, cc, k, :], in_=self.identf[:], func=AF.Copy, scale=wcol[:, cc, k:k + 1]),
                         reads=["identf", "cw"], writes=["diag"])
        avg = P.sb("avg", [128, 128], BF16)
        P.op("pool", lambda e: e.memset(avg[:], 1.0 / 256.0), writes=["avg"])
        hp = P.sb("hp", [128, 2, N + 30], BF16)
        P.op("pool", lambda e: e.memset(hp[:, :, 0:15], 0.0), writes=["hp"])
        P.op("pool", lambda e: e.memset(hp[:, :, N + 15:N + 30], 0.0), reads=["hp"], writes=["hp"])
        for cc in range(2):
            P.dma(lambda e, cc=cc: e.dma_start(out=hp[:, cc, 15:15 + N], in_=src[cc * 128:(cc + 1) * 128, :]), reads=["hp"], writes=["hp"])
        BW = min(512, N)
        hcs = [P.sb("hcv%d" % i, [128, 2, BW], F32) for i in range(2)]
        hcbs = [P.sb("hcb%d" % i, [128, 2, BW], BF16) for i in range(2)]
        sqs = [P.sb("sq%d" % i, [128, 2, BW], BF16) for i in range(2)]
        m2 = P.sb("m2", [128, BW], F32)
        rstd = P.sb("rstd", [128, BW], F32)
        tts = [P.sb("tt%d" % i, [128, BW], F32) for i in range(2)]
        hn = P.sb("hn", [128, 2, BW], BF16)
        ob = [P.sb("cob%d" % i, [128, BW], BF16) for i in range(2)]
        eps_t = P.sb("ceps", [128, 1], F32)
        P.op("pool", lambda e: e.memset(eps_t[:], EPS), writes=["ceps"])
        nb = N // BW
        if not is_ctx:
            nb = int(os.environ.get("DBG_NBLK", nb))

        def SA(b):
            t0 = b * BW
            j = b % 2
            hc, hcb, sq = hcs[j], hcbs[j], sqs[j]
            for cc in range(2):
                pb = 2 * j + cc
                for k in range(31):
                    self.mm(self.ps[pb][:, 0:BW], diag[:, cc, k, :], hp[:, cc, t0 + k:t0 + k + BW], k == 0, k == 30, ["diag", "hp"], [self.PS[pb]])
                P.op("act", lambda e, cc=cc, pb=pb: e.activation(out=hc[:, cc, :], in_=self.ps[pb][:, 0:BW], func=AF.Identity, bias=bcol[:, cc, 0:1], scale=1.0),
                     reads=[self.PS[pb], "cb"], writes=["hcv%d" % j])
                P.op("act", lambda e, cc=cc, pb=pb: e.activation(out=sq[:, cc, :], in_=self.ps[pb][:, 0:BW], func=AF.Square, bias=bcol[:, cc, 0:1], scale=1.0),
                     reads=[self.PS[pb], "cb"], writes=["sq%d" % j])
                P.op("pool", lambda e, cc=cc: e.tensor_copy(out=hcb[:, cc, :], in_=hc[:, cc, :]), reads=["hcv%d" % j], writes=["hcb%d" % j])

        def SB(b):
            t0 = b * BW
            j = b % 2
            hc, hcb, sq = hcs[j], hcbs[j], sqs[j]
            for cc in range(2):
                self.mm(self.ps[4][:, 0:BW], avg[:], hcb[:, cc, :], cc == 0, cc == 1, ["avg", "hcb%d" % j], [self.PS[4]])
            for cc in range(2):
                self.mm(self.ps[5][:, 0:BW], avg[:], sq[:, cc, :], cc == 0, cc == 1, ["avg", "sq%d" % j], [self.PS[5]])
            P.op("act", lambda e: e.activation(out=m2[:], in_=self.ps[4][:, 0:BW], func=AF.Square), reads=[self.PS[4]], writes=["m2"])
            P.op("dve", lambda e: e.tensor_tensor(out=rstd[:], in0=self.ps[5][:, 0:BW], in1=m2[:], op=ALU.subtract), reads=[self.PS[5], "m2"], writes=["rstd"])
            P.op("act", lambda e: e.activation(out=rstd[:], in_=rstd[:], func=AF.Sqrt, bias=eps_t[:, 0:1], scale=1.0), reads=["rstd", "ceps"], writes=["rstd"])
            P.op("dve", lambda e: e.reciprocal(out=rstd[:], in_=rstd[:]), reads=["rstd"], writes=["rstd"])
            for cc in range(2):
                tt = tts[cc]
                P.op("dve", lambda e, cc=cc, tt=tt: e.tensor_tensor(out=tt[:], in0=hc[:, cc, :], in1=self.ps[4][:, 0:BW], op=ALU.subtract), reads=["hcv%d" % j, self.PS[4]], writes=["tt%d" % cc])
                P.op("dve", lambda e, tt=tt: e.tensor_tensor(out=tt[:], in0=tt[:], in1=rstd[:], op=ALU.mult), reads=["tt%d" % cc, "rstd"], writes=["tt%d" % cc])
                P.op("act", lambda e, cc=cc, tt=tt: e.activation(out=hn[:, cc, :], in_=tt[:], func=AF.Silu, bias=bcol[:, cc, 2:3], scale=bcol[:, cc, 1:2]),
                     reads=["tt%d" % cc, "cb"], writes=["hn"])
            for co in range(2):
                for cc in range(2):
                    self.mm(self.ps[6 + co][:, 0:BW], wpw[:, cc, co * 128:(co + 1) * 128], hn[:, cc, :], cc == 0, cc == 1, ["wpw", "hn"], [self.PS[6 + co]])
                o = ob[co]
                P.op("act", lambda e, co=co, o=o: e.activation(out=o[:], in_=self.ps[6 + co][:, 0:BW], func=AF.Identity, bias=bcol[:, co, 3:4], scale=1.0),
                     reads=[self.PS[6 + co], "cb"], writes=["cob%d" % co])
                P.dma(lambda e, co=co, o=o: e.dma_start(out=dstT[768 + co * 128:768 + (co + 1) * 128, t0:t0 + BW], in_=o[:]), reads=["cob%d" % co])
        if nb:
            SA(0)
        for b in range(nb):
            if b + 1 < nb:
                SA(b + 1)
            SB(b)
        P.release()

    def row_tile(self, name, src_row):
        t = self.P.sb(name, [128, DM], F32)
        self.P.dma(lambda e: e.dma_start(out=t[:], in_=src_row.partition_broadcast(128)), reads=["modrows"], writes=[name])
        return t

    def ln_rows(self, xt, xres, outt, ores, tg):
        P = self.P
        st, mv, rs = self.lnr_bufs[tg]
        n = "lnr%d" % tg
        for hh in range(2):
            P.op("dve", lambda e, hh=hh: e.bn_stats(out=st[:, hh, :], in_=xt[:, hh * 512:(hh + 1) * 512]), reads=[xres], writes=[n + "st"])
        P.op("dve", lambda e: e.bn_aggr(out=mv[:], in_=st[:].rearrange("p a b -> p (a b)")), reads=[n + "st"], writes=[n + "mv"])
        eps_t = self.eps2
        P.op("act", lambda e: e.activation(out=rs[:], in_=mv[:, 1:2], func=AF.Sqrt, bias=eps_t[:, 0:1], scale=1.0), reads=[n + "mv", "eps2"], writes=[n + "rs"])
        P.op("dve", lambda e: e.reciprocal(out=rs[:], in_=rs[:]), reads=[n + "rs"], writes=[n + "rs"])
        P.op("dve", lambda e: e.tensor_scalar(out=outt[:], in0=xt[:], scalar1=mv[:, 0:1], scalar2=rs[:, 0:1], op0=ALU.subtract, op1=ALU.mult),
             reads=[xres, n + "mv", n + "rs"], writes=[ores])

    def alloc_lnr(self):
        P = self.P
        self.eps2 = e2 = P.sb("eps2", [128, 1], F32)
        P.op("pool", lambda e: e.memset(e2[:], EPS), writes=["eps2"])
        self.lnr_bufs = [(P.sb("lst%d" % i, [128, 2, 6], F32), P.sb("lmv%d" % i, [128, 2], F32), P.sb("lrs%d" % i, [128, 1], F32)) for i in range(2)]

    def phase_outproj(self, l, var):
        P = self.P
        P.mark()
        N = LC if var else T
        catT = self.catcT_d if var else self.catT_d
        xsrc = self.xc_src if var else self.x_src
        xmid_d = self.xcmid_d if var else self.xmid_d
        h2_d = self.h2c_d if var else self.h2_d
        logits = self.logits_c if var else self.logits_x
        mr = self.modrows[l, var]
        g1 = self.row_tile("g1", mr[2 * DM:3 * DM])
        sc2 = self.row_tile("sc2", mr[4 * DM:5 * DM])
        sh2 = self.row_tile("sh2", mr[3 * DM:4 * DM])
        bo = self.row_tile("bo", self.b_out[l])
        lg = self.row_tile("lg", self.ln1_g[l])
        lb = self.row_tile("lb", self.ln1_b[l])
        P.op("pool", lambda e: e.tensor_tensor(out=bo[:], in0=bo[:], in1=g1[:], op=ALU.mult), reads=["bo", "g1"], writes=["bo"])
        P.op("pool", lambda e: e.tensor_scalar_add(out=sc2[:], in0=sc2[:], scalar1=1.0), reads=["sc2"], writes=["sc2"])
        wo = P.sb("wo", [128, 8, DM], BF16)
        for k0 in range(0, 8, 2):
            P.dma(lambda e, k0=k0: e.dma_start(out=wo[:, k0:k0 + 2, :], in_=self.w_out[l][k0 * 128:(k0 + 2) * 128, :].rearrange("(k p) n -> p k n", p=128)), writes=["wo"], eng="pool")
        for k in range(8):
            eng = ("dve", "pool")[k % 2]
            P.op(eng, lambda e, k=k: e.tensor_tensor(out=wo[:, k, :], in0=wo[:, k, :], in1=g1[:], op=ALU.mult), reads=["wo", "g1"], writes=["wo"])
        wr = P.sb("wr", [128, 8, NEXP], BF16)
        P.dma(lambda e: e.dma_start(out=wr[:], in_=self.w_router[l].rearrange("(k p) n -> p k n", p=128)), writes=["wr"], eng="pool")
        eps_t = P.sb("oeps", [128, 1], F32)
        P.op("pool", lambda e: e.memset(eps_t[:], EPS), writes=["oeps"])
        BW = min(512, N)
        tpb = BW // 128
        D = 6
        cbs = [P.sb("catb%d" % i, [128, 8, BW], BF16) for i in range(3)]
        xts = [P.sb("oxt%d" % i, [128, DM], F32) for i in range(D)]
        rts = [P.sb("ort%d" % i, [128, DM], F32) for i in range(D)]
        xms = [P.sb("oxm%d" % i, [128, DM], F32) for i in range(D)]
        hfs = [P.sb("ohf%d" % i, [128, DM], F32) for i in range(D)]
        hbs = [P.sb("ohb%d" % i, [128, DM], BF16) for i in range(D)]
        sts = [P.sb("ost%d" % i, [128, 2, 2, 6], F32) for i in range(D)]
        mvs = [P.sb("omv%d" % i, [128, 2, 4], F32) for i in range(D)]
        h2Ts = [P.sb("h2T%d" % i, [128, 8, 128], BF16) for i in range(2)]
        nt_ = N // 128
        if not var:
            nt_ = int(os.environ.get("DBG_NBLK", nt_ // tpb)) * tpb

        def ln_stats(src, sres, d, w):
            st, mv = sts[d], mvs[d]
            sn, mn = "ost%d_%d" % (d, w), "omv%d_%d" % (d, w)
            for hh in range(2):
                P.op("dve", lambda e, hh=hh: e.bn_stats(out=st[:, w, hh, :], in_=src[:, hh * 512:(hh + 1) * 512]), reads=[sres], writes=[sn])
            P.op("dve", lambda e: e.bn_aggr(out=mv[:, w, 0:2], in_=st[:, w, :, :].rearrange("p a b -> p (a b)")), reads=[sn], writes=[mn])
            P.op("act", lambda e: e.activation(out=mv[:, w, 2:3], in_=mv[:, w, 1:2], func=AF.Sqrt, bias=eps_t[:, 0:1], scale=1.0), reads=[mn, "oeps"], writes=[mn])
            P.op("dve", lambda e: e.reciprocal(out=mv[:, w, 2:3], in_=mv[:, w, 2:3]), reads=[mn], writes=[mn])
            P.op("dve", lambda e: e.scalar_tensor_tensor(out=mv[:, w, 3:4], in0=mv[:, w, 0:1], scalar=-1.0, in1=mv[:, w, 2:3], op0=ALU.mult, op1=ALU.mult), reads=[mn], writes=[mn])
            return mn

        def S0(n):
            d = n % D
            if n % tpb == 0:
                b_ = n // tpb
                cb = cbs[b_ % 3]
                P.dma(lambda e: e.dma_start(out=cb[:], in_=catT[:, b_ * BW:(b_ + 1) * BW].rearrange("(k p) t -> p k t", p=128)), writes=["catb%d" % (b_ % 3)])
            xt = xts[d]
            P.dma(lambda e: e.dma_start(out=xt[:], in_=xsrc[n * 128:(n + 1) * 128, :]), writes=["oxt%d" % d])

        def S1(n):
            d = n % D
            b_, i = n // tpb, n % tpb
            cb, cn = cbs[b_ % 3], "catb%d" % (b_ % 3)
            xt = xts[d]
            for hf in range(2):
                pb = 2 * (n % 2) + hf
                for k in range(8):
                    self.mm(self.ps[pb][:, :], cb[:, k, i * 128:(i + 1) * 128], wo[:, k, hf * 512:(hf + 1) * 512], k == 0, k == 7, [cn, "wo"], [self.PS[pb]])
            P.op("act", lambda e: e.activation(out=xt[:], in_=xt[:], func=AF.Copy, scale=ALPHA), reads=["oxt%d" % d], writes=["oxt%d" % d])
            P.op("pool", lambda e: e.tensor_tensor(out=xt[:], in0=xt[:], in1=bo[:], op=ALU.add), reads=["oxt%d" % d, "bo"], writes=["oxt%d" % d])

        def S2(n):
            d = n % D
            xt, rt = xts[d], rts[d]
            for hf in range(2):
                pb = 2 * (n % 2) + hf
                P.op("dve", lambda e, hf=hf, pb=pb: e.tensor_tensor(out=rt[:, hf * 512:(hf + 1) * 512], in0=self.ps[pb][:, :], in1=xt[:, hf * 512:(hf + 1) * 512], op=ALU.add),
                     reads=[self.PS[pb], "oxt%d" % d], writes=["ort%d" % d])
            ln_stats(rt, "ort%d" % d, d, 0)

        def S3(n):
            d = n % D
            rt, xm, mv = rts[d], xms[d], mvs[d]
            P.op("act", lambda e: e.activation(out=xm[:], in_=rt[:], func=AF.Identity, scale=mv[:, 0, 2:3], bias=mv[:, 0, 3:4]), reads=["ort%d" % d, "omv%d_0" % d], writes=["oxm%d" % d])
            P.op("dve", lambda e: e.tensor_tensor(out=xm[:], in0=xm[:], in1=lg[:], op=ALU.mult), reads=["oxm%d" % d, "lg"], writes=["oxm%d" % d])
            P.op("pool", lambda e: e.tensor_tensor(out=xm[:], in0=xm[:], in1=lb[:], op=ALU.add), reads=["oxm%d" % d, "lb"], writes=["oxm%d" % d])
            P.dma(lambda e: e.dma_start(out=xmid_d[n * 128:(n + 1) * 128, :], in_=xm[:]), reads=["oxm%d" % d])

        def S4(n):
            d = n % D
            ln_stats(xms[d], "oxm%d" % d, d, 1)

        def S5(n):
            d = n % D
            xm, hf_, hb, mv = xms[d], hfs[d], hbs[d], mvs[d]
            P.op("act", lambda e: e.activation(out=hf_[:], in_=xm[:], func=AF.Identity, scale=mv[:, 1, 2:3], bias=mv[:, 1, 3:4]), reads=["oxm%d" % d, "omv%d_1" % d], writes=["ohf%d" % d])
            P.op("dve", lambda e: e.tensor_tensor(out=hf_[:], in0=hf_[:], in1=sc2[:], op=ALU.mult), reads=["ohf%d" % d, "sc2"], writes=["ohf%d" % d])
            P.op("pool", lambda e: e.tensor_tensor(out=hb[:], in0=hf_[:], in1=sh2[:], op=ALU.add), reads=["ohf%d" % d, "sh2"], writes=["ohb%d" % d])
            P.dma(lambda e: e.dma_start(out=h2_d[n * 128:(n + 1) * 128, :], in_=hb[:]), reads=["ohb%d" % d])

        def S6(n):
            d = n % D
            hb = hbs[d]
            j = n % 2
            h2T = h2Ts[j]
            psT = self.ps[4 + j][:, :].bitcast(BF16)
            for k in range(8):
                P.op("pe", lambda e, k=k: e.transpose(out=psT[:, k * 128:(k + 1) * 128], in_=hb[:, k * 128:(k + 1) * 128], identity=self.ident[:]),
                     reads=["ohb%d" % d, "ident"], writes=[self.PS[4 + j]])
            P.op("act", lambda e: e.activation(out=h2T[:], in_=psT.rearrange("p (k t) -> p k t", k=8), func=AF.Copy), reads=[self.PS[4 + j]], writes=["h2T%d" % j])
            for k in range(8):
                self.mm(self.ps[6 + j][:, 0:NEXP], h2T[:, k, :], wr[:, k, :], k == 0, k == 7, ["h2T%d" % j, "wr"], [self.PS[6 + j]])
            P.op("act", lambda e: e.activation(out=logits[:, n, :], in_=self.ps[6 + j][:, 0:NEXP], func=AF.Copy), reads=[self.PS[6 + j]], writes=["logits%d" % var])
        stages = [S0, S1, S2, S3, S4, S5, S6]
        for step in range(nt_ + len(stages) - 1):
            for si in reversed(range(len(stages))):
                n = step - si
                if 0 <= n < nt_:
                    stages[si](n)
        P.release()

    def phase_moe(self, l, do_ctx):
        P = self.P
        P.mark()
        NT = T // 128
        NTA = NT + 2
        CAPX, CAPC = 2 * T // NEXP, 2 * LC // NEXP
        nexp = int(os.environ.get("DBG_NEXP", NEXP))
        NS = NEXP * NT
        pos = P.sb("pos", [128, NEXP, NT], F32)
        k128 = P.sb("k128", [128, 9], F32)
        affh = P.sb("affh", [128, NTA, NEXP], BF16)
        affl = P.sb("affl", [128, NTA, NEXP], BF16)
        iq = P.sb("iq", [128, 128], F32)
        ip = P.sb("ip", [128, 1], F32)
        tix = P.sb("tix", [128, NT], BF16)
        posc = P.sb("posc", [128, NEXP, 2], F32)
        Bc = P.sb("Bc", [128, NEXP, 2], BF16)
        P.mark()
        aff = P.sb("aff", [128, NTA, NEXP], F32)
        P.op("dve", lambda e: e.tensor_copy(out=aff[:, 0:NT, :], in_=self.logits_x[:]), reads=["logits0"], writes=["aff"])
        P.op("dve", lambda e: e.tensor_copy(out=aff[:, NT:NTA, :], in_=self.logits_c[:]), reads=["logits1", "aff"], writes=["aff"])
        mx = P.sb("mx", [128, NTA], F32)
        P.op("dve", lambda e: e.tensor_reduce(out=mx[:], in_=aff[:], axis=AX.X, op=ALU.max), reads=["aff"], writes=["mx"])
        P.op("dve", lambda e: e.tensor_tensor(out=aff[:], in0=aff[:], in1=mx[:, :].unsqueeze(2).to_broadcast([128, NTA, NEXP]), op=ALU.subtract), reads=["aff", "mx"], writes=["aff"])
        P.op("act", lambda e: e.activation(out=aff[:], in_=aff[:], func=AF.Exp), reads=["aff"], writes=["aff"])
        P.op("dve", lambda e: e.tensor_reduce(out=mx[:], in_=aff[:], axis=AX.X, op=ALU.add), reads=["aff"], writes=["mx"])
        P.op("dve", lambda e: e.reciprocal(out=mx[:], in_=mx[:]), reads=["mx"], writes=["mx"])
        P.op("dve", lambda e: e.tensor_tensor(out=aff[:], in0=aff[:], in1=mx[:, :].unsqueeze(2).to_broadcast([128, NTA, NEXP]), op=ALU.mult), reads=["aff", "mx"], writes=["aff"])
        lo = P.sb("lo", [128, 2, NEXP], F32)
        hi = P.sb("hi", [128, 2, NEXP], F32)
        mid = P.sb("mid", [128, 2, NEXP], F32)
        capt = P.sb("capt", [128, 2, NEXP], F32)
        tmp = P.sb("btmp", [128, 2, NEXP], F32)
        pred = P.sb("pred", [128, 2, NEXP], F32)
        cnt = P.sb("cnt", [128, 2, NEXP], BF16)
        cntf = P.sb("cntf", [128, 2, NEXP], F32)
        cmp_ = P.sb("cmp", [128, NTA, NEXP], BF16)
        P.op("pool", lambda e: e.memset(lo[:], 0.0), writes=["lo"])
        P.op("pool", lambda e: e.memset(hi[:], 1.0), writes=["hi"])
        P.op("pool", lambda e: e.memset(mid[:], 0.5), writes=["mid"])
        P.op("pool", lambda e: e.memset(capt[:, 0, :], float(CAPX)), writes=["capt"])
        P.op("pool", lambda e: e.memset(capt[:, 1, :], float(CAPC)), reads=["capt"], writes=["capt"])
        groups = ((0, 0, NT), (1, NT, NTA))

        def compare(thr, thr_res):
            for (g, a, b) in groups:
                P.op("dve", lambda e, g=g, a=a, b=b: e.tensor_tensor(out=cmp_[:, a:b, :], in0=aff[:, a:b, :], in1=thr[:, g, :].unsqueeze(1).to_broadcast([128, b - a, NEXP]), op=ALU.is_ge),
                     reads=["aff", thr_res], writes=["cmp"])
        for it in range(int(os.environ.get("DBG_NBIS", 34))):
            compare(mid, "mid")
            for (g, a, b) in groups:
                P.op("dve", lambda e, g=g, a=a, b=b: e.tensor_reduce(out=cntf[:, g, :], in_=cmp_[:, a:b, :].rearrange("p t e -> p e t"), axis=AX.X, op=ALU.add),
                     reads=["cmp"], writes=["cntf"])
            P.op("dve", lambda e: e.tensor_copy(out=cnt[:], in_=cntf[:]), reads=["cntf"], writes=["cnt"])
            self.mm(self.ps[0][:, 0:2 * NEXP], self.ones_bf[:], cnt[:].rearrange("p g e -> p (g e)"), True, True, ["cnt", "ones"], [self.PS[0]])
            P.op("dve", lambda e: e.tensor_tensor(out=pred[:].rearrange("p g e -> p (g e)"), in0=self.ps[0][:, 0:2 * NEXP], in1=capt[:].rearrange("p g e -> p (g e)"), op=ALU.is_ge),
                 reads=[self.PS[0], "capt"], writes=["pred"])
            P.op("dve", lambda e: e.tensor_tensor(out=tmp[:], in0=pred[:], in1=mid[:], op=ALU.mult), reads=["pred", "mid"], writes=["btmp"])
            P.op("dve", lambda e: e.tensor_tensor(out=lo[:], in0=lo[:], in1=tmp[:], op=ALU.max), reads=["lo", "btmp"], writes=["lo"])
            P.op("dve", lambda e: e.scalar_tensor_tensor(out=tmp[:], in0=pred[:], scalar=4.0, in1=mid[:], op0=ALU.mult, op1=ALU.add), reads=["pred", "mid", "lo"], writes=["btmp"])
            P.op("dve", lambda e: e.tensor_tensor(out=hi[:], in0=hi[:], in1=tmp[:], op=ALU.min), reads=["hi", "btmp"], writes=["hi"])
            P.op("dve", lambda e: e.tensor_tensor(out=tmp[:], in0=lo[:], in1=hi[:], op=ALU.add), reads=["lo", "hi"], writes=["btmp"])
            P.op("dve", lambda e: e.tensor_scalar(out=mid[:], in0=tmp[:], scalar1=0.5, scalar2=None, op0=ALU.mult), reads=["btmp"], writes=["mid"])
        compare(lo, "lo")
        NS = NEXP * NT
        mA = P.sb("mA", [128, NEXP, NT], F32)
        mB = P.sb("mB", [128, NEXP, NT], F32)
        msk = P.sb("msk", [128, NEXP, NT], F32)
        P.op("dve", lambda e: e.tensor_copy(out=msk[:], in_=cmp_[:, 0:NT, :].rearrange("p t e -> p e t")), reads=["cmp"], writes=["msk"])
        P.op("pool", lambda e: e.tensor_copy(out=mA[:], in_=msk[:]), reads=["msk"], writes=["mA"])
        cur, nxt, cn, nn = mA, mB, "mA", "mB"
        sft = 1
        while sft < NT:
            P.op("dve", lambda e, cur=cur, nxt=nxt, sft=sft: e.tensor_tensor(out=nxt[:, :, sft:], in0=cur[:, :, sft:], in1=cur[:, :, 0:NT - sft], op=ALU.add), reads=[cn], writes=[nn])
            P.op("pool", lambda e, cur=cur, nxt=nxt, sft=sft: e.tensor_copy(out=nxt[:, :, 0:sft], in_=cur[:, :, 0:sft]), reads=[cn], writes=[nn])
            cur, nxt, cn, nn = nxt, cur, nn, cn
            sft *= 2
        inc, incn = cur, cn
        rc = P.sb("rc", [128, NEXP], BF16)
        P.op("dve", lambda e: e.tensor_copy(out=rc[:], in_=inc[:, :, NT - 1]), reads=[incn], writes=["rc"])
        tri = P.sb("tri", [128, 128], BF16)
        trif = P.sb("trif", [128, 128], F32)
        P.op("pool", lambda e: e.memset(trif[:], 1.0), writes=["trif"])
        P.op("pool", lambda e: e.affine_select(out=trif[:], in_=trif[:], pattern=[[1, 128]], compare_op=ALU.is_gt, fill=0.0, base=0, channel_multiplier=-1), reads=["trif"], writes=["trif"])
        P.op("dve", lambda e: e.tensor_copy(out=tri[:], in_=trif[:]), reads=["trif"], writes=["tri"])
        self.mm(self.ps[1][:, 0:NEXP], tri[:], rc[:], True, True, ["tri", "rc"], [self.PS[1]])
        P.op("dve", lambda e: e.tensor_tensor(out=pos[:], in0=inc[:], in1=msk[:], op=ALU.subtract), reads=[incn, "msk"], writes=["pos"])
        rb = P.sb("rb", [128, NEXP], F32)
        P.op("dve", lambda e: e.tensor_copy(out=rb[:], in_=self.ps[1][:, 0:NEXP]), reads=[self.PS[1]], writes=["rb"])
        P.op("dve", lambda e: e.tensor_tensor(out=pos[:], in0=pos[:], in1=rb[:, :].unsqueeze(2).to_broadcast([128, NEXP, NT]), op=ALU.add), reads=["pos", "rb"], writes=["pos"])
        P.op("dve", lambda e: e.scalar_tensor_tensor(out=pos[:], in0=msk[:], scalar=-8192.0, in1=pos[:], op0=ALU.mult, op1=ALU.add), reads=["pos", "msk"], writes=["pos"])
        P.op("dve", lambda e: e.tensor_scalar_add(out=pos[:], in0=pos[:], scalar1=8192.0), reads=["pos"], writes=["pos"])
        P.op("pool", lambda e: e.iota(k128[:], pattern=[[128, 9]], base=0, channel_multiplier=0, allow_small_or_imprecise_dtypes=True), writes=["k128"])
        afft = P.sb("afft", [128, NTA, NEXP], F32)
        P.op("dve", lambda e: e.tensor_copy(out=affh[:], in_=aff[:]), reads=["aff"], writes=["affh"])
        P.op("dve", lambda e: e.tensor_tensor(out=afft[:], in0=aff[:], in1=affh[:], op=ALU.subtract), reads=["aff", "affh"], writes=["afft"])
        P.op("dve", lambda e: e.tensor_copy(out=affl[:], in_=afft[:]), reads=["afft"], writes=["affl"])
        P.op("pool", lambda e: e.iota(iq[:], pattern=[[1, 128]], base=0, channel_multiplier=0, allow_small_or_imprecise_dtypes=True), writes=["iq"])
        P.op("pool", lambda e: e.iota(ip[:], pattern=[[1, 1]], base=0, channel_multiplier=1, allow_small_or_imprecise_dtypes=True), writes=["ip"])
        P.op("pool", lambda e: e.iota(tix[:], pattern=[[1, NT]], base=0, channel_multiplier=0, allow_small_or_imprecise_dtypes=True), writes=["tix"])
        if do_ctx:
            mc = P.sb("mc", [128, NEXP, 2], F32)
            P.op("dve", lambda e: e.tensor_copy(out=mc[:], in_=cmp_[:, NT:NTA, :].rearrange("p t e -> p e t")), reads=["cmp"], writes=["mc"])
            rcc = P.sb("rcc", [128, NEXP], BF16)
            P.op("dve", lambda e: e.tensor_tensor(out=rcc[:], in0=mc[:, :, 0], in1=mc[:, :, 1], op=ALU.add), reads=["mc"], writes=["rcc"])
            self.mm(self.ps[1][:, NEXP:2 * NEXP], tri[:], rcc[:], True, True, ["tri", "rcc"], [self.PS[1]])
            P.op("dve", lambda e: e.tensor_copy(out=posc[:, :, 0], in_=self.ps[1][:, NEXP:2 * NEXP]), reads=[self.PS[1]], writes=["posc"])
            P.op("dve", lambda e: e.tensor_tensor(out=posc[:, :, 1], in0=posc[:, :, 0], in1=mc[:, :, 0], op=ALU.add), reads=["posc", "mc"], writes=["posc"])
            P.op("dve", lambda e: e.scalar_tensor_tensor(out=posc[:], in0=mc[:], scalar=-8192.0, in1=posc[:], op0=ALU.mult, op1=ALU.add), reads=["posc", "mc"], writes=["posc"])
            P.op("dve", lambda e: e.tensor_scalar_add(out=posc[:], in0=posc[:], scalar1=8192.0), reads=["posc"], writes=["posc"])
            P.op("dve", lambda e: e.tensor_single_scalar(out=Bc[:], in_=posc[:], scalar=float(CAPC), op=ALU.is_lt), reads=["posc"], writes=["Bc"])
        P.release()
        P.mark()
        zt = P.sb("zt", [128, 1024], F32)
        P.op("pool", lambda e: e.memset(zt[:], 0.0), writes=["zt"])
        for i in range(T // 128):
            P.dma(lambda e, i=i: e.dma_start(out=self.moe_d[i * 128:(i + 1) * 128, :], in_=zt[:]), reads=["zt"], writes=["moe_z%d" % i])
        if do_ctx:
            for i in range(LC // 128):
                P.dma(lambda e, i=i: e.dma_start(out=self.moec_d[i * 128:(i + 1) * 128, :], in_=zt[:]), reads=["zt"], writes=["moe_zc%d" % i])
        P.release()
        xss = [P.sb("xs%d" % i, [128, 9, DM], BF16) for i in range(2)]
        ge9e = P.sb("ge9e", [128, NT, 9], BF16)
        B8e = P.sb("B8e", [128, NT, 8], BF16)
        hi8e = P.sb("hi8e", [128, NT], F32)
        lo7e = P.sb("lo7e", [128, NT], F32)
        xsT = P.sb("xsT", [128, 8, 1056], BF16)
        hidT = P.sb("hidT", [128, 16, 1056], BF16)
        wds = [P.sb("wd%d" % i, [128, 16, DM], BF16) for i in range(2)]
        wgs = [P.sb("wg%d" % i, [128, 8, 256], BF16) for i in range(2)]
        wus = [P.sb("wu%d" % i, [128, 8, 256], BF16) for i in range(2)]
        Ats = [P.sb("At%d" % i, [128, 128], BF16) for i in range(2)]
        Abs_ = [P.sb("Ab0", [128, 16, 128], BF16)] * 2
        Rs = [P.sb("R%d" % i, [128, NT, 32], BF16) for i in range(2)]
        Rc = P.sb("Rc", [128, 2, 4], BF16)
        idxf = P.sb("idxf", [128, 9], F32)
        idxi = [P.sb("idxi%d" % i, [128, 9], I32) for i in range(2)]
        gts = [P.sb("gts%d" % i, [128, 9], F32) for i in range(2)]
        sg = P.sb("sgm", [128, 1056], F32)
        ysts = [P.sb("yst%d" % i, [128, DM], F32) for i in range(2)]
        NSL = 1056 if do_ctx else 1024
        cwc = [0]
        prev_sc = ["moe_z"]
        def partA(ex):
            pe_ = ex % 2
            R, Rn = Rs[pe_], "R%d" % pe_
            ii, iin = idxi[pe_], "idxi%d" % pe_
            gt, gtn = gts[pe_], "gts%d" % pe_
            xs, xsn = xss[pe_], "xs%d" % pe_
            b8 = B8e[:]
            pe3 = pos[:, ex, :]
            P.op("dve", lambda e, pe3=pe3: e.tensor_tensor(out=ge9e[:], in0=pe3.unsqueeze(2).to_broadcast([128, NT, 9]), in1=k128[:, :].unsqueeze(1).to_broadcast([128, NT, 9]), op=ALU.is_ge), reads=["pos", "k128"], writes=["ge9e"])
            P.op("dve", lambda e: e.tensor_tensor(out=B8e[:], in0=ge9e[:, :, 0:8], in1=ge9e[:, :, 1:9], op=ALU.subtract), reads=["ge9e"], writes=["B8"])
            P.op("dve", lambda e: e.tensor_reduce(out=hi8e[:], in_=ge9e[:, :, 1:9], axis=AX.X, op=ALU.add), reads=["ge9e"], writes=["hi8e"])
            P.op("dve", lambda e, pe3=pe3: e.scalar_tensor_tensor(out=lo7e[:], in0=hi8e[:], scalar=-128.0, in1=pe3, op0=ALU.mult, op1=ALU.add), reads=["hi8e", "pos"], writes=["lo7"])
            P.op("dve", lambda e, R=R, b8=b8: e.tensor_tensor(out=R[:, :, 0:8], in0=b8, in1=tix[:, :].unsqueeze(2).to_broadcast([128, NT, 8]), op=ALU.mult), reads=["B8", "tix"], writes=[Rn])
            P.op("dve", lambda e, R=R, b8=b8: e.tensor_scalar(out=R[:, :, 8:16], in0=b8, scalar1=ip[:, 0:1], scalar2=None, op0=ALU.mult), reads=["B8", "ip"], writes=[Rn])
            P.op("dve", lambda e, R=R, b8=b8, ex=ex: e.tensor_tensor(out=R[:, :, 16:24], in0=b8, in1=affh[:, 0:NT, ex].unsqueeze(2).to_broadcast([128, NT, 8]), op=ALU.mult), reads=["B8", "affh"], writes=[Rn])
            P.op("dve", lambda e, R=R, b8=b8, ex=ex: e.tensor_tensor(out=R[:, :, 24:32], in0=b8, in1=affl[:, 0:NT, ex].unsqueeze(2).to_broadcast([128, NT, 8]), op=ALU.mult), reads=["B8", "affl"], writes=[Rn])
            for t0 in range(0, NT, 16):
                Ab, Abn = Abs_[0], "Ab0"
                P.op("dve", lambda e, Ab=Ab, t0=t0: e.tensor_tensor(out=Ab[:], in0=iq[:, :].unsqueeze(1).to_broadcast([128, 16, 128]),
                                                                   in1=lo7e[:, t0:t0 + 16].unsqueeze(2).to_broadcast([128, 16, 128]), op=ALU.is_equal), reads=["iq", "lo7"], writes=[Abn])
                for t in range(t0, t0 + 16):
                    self.mm(self.ps[2][:, 0:32], Ab[:, t - t0, :], R[:, t, :], t == 0, t == NT - 1, [Abn, Rn], [self.PS[2]])
            if do_ctx:
                P.op("dve", lambda e, ex=ex: e.tensor_scalar(out=Rc[:, :, 0], in0=Bc[:, ex, :], scalar1=float(NT), scalar2=None, op0=ALU.mult) if False else
                     e.tensor_copy(out=Rc[:, :, 0], in_=Bc[:, ex, :]), reads=["Bc"], writes=["Rc"])
                P.op("dve", lambda e, ex=ex: e.tensor_scalar(out=Rc[:, :, 1], in0=Bc[:, ex, :], scalar1=ip[:, 0:1], scalar2=None, op0=ALU.mult), reads=["Bc", "ip", "Rc"], writes=["Rc"])
                P.op("dve", lambda e, ex=ex: e.tensor_tensor(out=Rc[:, :, 2], in0=Bc[:, ex, :], in1=affh[:, NT:NTA, ex], op=ALU.mult), reads=["Bc", "affh", "Rc"], writes=["Rc"])
                P.op("dve", lambda e, ex=ex: e.tensor_tensor(out=Rc[:, :, 3], in0=Bc[:, ex, :], in1=affl[:, NT:NTA, ex], op=ALU.mult), reads=["Bc", "affl", "Rc"], writes=["Rc"])
                P.op("pool", lambda e: e.memset(Rc[:, 0, 0:1], 0.0), reads=["Rc"], writes=["Rc"])
                for t in range(2):
                    At, An = Ats[t % 2], "At%d" % (t % 2)
                    P.op("dve", lambda e, At=At, ex=ex, t=t: e.tensor_scalar(out=At[:], in0=iq[:], scalar1=posc[:, ex, t:t + 1], scalar2=None, op0=ALU.is_equal), reads=["iq", "posc"], writes=[An])
                    self.mm(self.ps[2][:, 32:36], At[:], Rc[:, t, :], t == 0, t == 1, [An, "Rc"], [self.PS[2]])
            P.op("dve", lambda e: e.scalar_tensor_tensor(out=idxf[:, 0:8], in0=self.ps[2][:, 0:8], scalar=128.0, in1=self.ps[2][:, 8:16], op0=ALU.mult, op1=ALU.add) if False else
                 e.tensor_copy(out=idxf[:, 0:8], in_=self.ps[2][:, 8:16]), reads=[self.PS[2]], writes=["idxf"])
            P.op("dve", lambda e: e.scalar_tensor_tensor(out=idxf[:, 0:8], in0=self.ps[2][:, 0:8], scalar=128.0, in1=idxf[:, 0:8], op0=ALU.mult, op1=ALU.add), reads=[self.PS[2], "idxf"], writes=["idxf"])
            P.op("dve", lambda e, gt=gt: e.tensor_copy(out=gt[:, 0:8], in_=self.ps[2][:, 24:32]), reads=[self.PS[2]], writes=[gtn])
            P.op("dve", lambda e, gt=gt: e.tensor_tensor(out=gt[:, 0:8], in0=self.ps[2][:, 16:24], in1=gt[:, 0:8], op=ALU.add), reads=[self.PS[2], gtn], writes=[gtn])
            if do_ctx:
                P.op("dve", lambda e: e.tensor_copy(out=idxf[:, 8:9], in_=self.ps[2][:, 33:34]), reads=[self.PS[2], "idxf"], writes=["idxf"])
                P.op("dve", lambda e: e.scalar_tensor_tensor(out=idxf[:, 8:9], in0=self.ps[2][:, 32:33], scalar=128.0, in1=idxf[:, 8:9], op0=ALU.mult, op1=ALU.add), reads=[self.PS[2], "idxf"], writes=["idxf"])
                P.op("dve", lambda e, gt=gt: e.tensor_copy(out=gt[:, 8:9], in_=self.ps[2][:, 35:36]), reads=[self.PS[2], gtn], writes=[gtn])
                P.op("dve", lambda e, gt=gt: e.tensor_tensor(out=gt[:, 8:9], in0=self.ps[2][:, 34:35], in1=gt[:, 8:9], op=ALU.add), reads=[self.PS[2], gtn], writes=[gtn])
            P.op("dve", lambda e, ii=ii: e.tensor_copy(out=ii[:], in_=idxf[:]), reads=["idxf"], writes=[iin])
            for k in range(8):
                P.dma(lambda e, k=k, ii=ii: e.indirect_dma_start(out=xs[:, k, :], out_offset=None, in_=self.h2_d[:, :],
                                                                in_offset=bass.IndirectOffsetOnAxis(ap=ii[:, k:k + 1], axis=0)),
                      reads=[iin, "h2"], writes=[xsn], eng="pool")
            if do_ctx:
                P.dma(lambda e, ii=ii: e.indirect_dma_start(out=xs[0:CAPC, 8, :], out_offset=None, in_=self.h2c_d[:, :],
                                                           in_offset=bass.IndirectOffsetOnAxis(ap=ii[0:CAPC, 8:9], axis=0)),
                      reads=[iin, "h2"], writes=[xsn], eng="pool")

        def partB(ex):
            pe_ = ex % 2
            ii, iin = idxi[pe_], "idxi%d" % pe_
            gt, gtn = gts[pe_], "gts%d" % pe_
            xs, xsn = xss[pe_], "xs%d" % pe_
            nst = 9 if do_ctx else 8
            for j in range(nst):
                rows = 128 if j < 8 else CAPC
                pb = 6 + j % 2
                psT = self.ps[pb][:, :].bitcast(BF16)
                for dk in range(8):
                    P.op("pe", lambda e, j=j, dk=dk, psT=psT, rows=rows: e.transpose(out=psT[:, dk * 128:dk * 128 + rows], in_=xs[0:rows, j, dk * 128:(dk + 1) * 128], identity=self.ident[0:rows, 0:rows]),
                         reads=[xsn, "ident"], writes=[self.PS[pb]])
                P.op("act", lambda e, j=j, psT=psT, rows=rows: e.activation(out=xsT[:, :, j * 128:j * 128 + rows], in_=psT.rearrange("p (k t) -> p k t", k=8)[:, :, 0:rows], func=AF.Copy),
                     reads=[self.PS[pb]], writes=["xsT"])

        def partC(ex):
            pe_ = ex % 2
            ii, iin = idxi[pe_], "idxi%d" % pe_
            gt, gtn = gts[pe_], "gts%d" % pe_
            xs, xsn = xss[pe_], "xs%d" % pe_
            nst = 9 if do_ctx else 8
            wd_unused = None
            wd, wdn = wds[pe_], "wd%d" % pe_
            for f0 in range(0, 16, 4):
                P.dma(lambda e, f0=f0, wd=wd, ex=ex: e.dma_start(out=wd[:, f0:f0 + 4, :], in_=self.w_down[l, ex, f0 * 128:(f0 + 4) * 128, :].rearrange("(k p) n -> p k n", p=128)), writes=[wdn], eng="pool")
            segs = [(0, 512), (512, 512)] + ([(1024, CAPC)] if do_ctx else [])
            for fc in range(8):
                wg, wgn = wgs[cwc[0] % 2], "wg%d" % (cwc[0] % 2)
                wu, wun = wus[cwc[0] % 2], "wu%d" % (cwc[0] % 2)
                cwc[0] += 1
                P.dma(lambda e, wg=wg, ex=ex, fc=fc: e.dma_start(out=wg[:], in_=self.w_gate[l, ex, :, fc * 256:(fc + 1) * 256].rearrange("(k p) n -> p k n", p=128)), writes=[wgn], eng="pool")
                P.dma(lambda e, wu=wu, ex=ex, fc=fc: e.dma_start(out=wu[:], in_=self.w_up[l, ex, :, fc * 256:(fc + 1) * 256].rearrange("(k p) n -> p k n", p=128)), writes=[wun], eng="pool")
                for fi in range(2):
                    ft = fc * 2 + fi
                    for si, (s0, sn_) in enumerate(segs):
                        for k in range(8):
                            self.mm(self.ps[si][:, 0:sn_], wg[:, k, fi * 128:(fi + 1) * 128], xsT[:, k, s0:s0 + sn_], k == 0, k == 7, [wgn, "xsT"], [self.PS[si]])
                        for k in range(8):
                            self.mm(self.ps[3 + si][:, 0:sn_], wu[:, k, fi * 128:(fi + 1) * 128], xsT[:, k, s0:s0 + sn_], k == 0, k == 7, [wun, "xsT"], [self.PS[3 + si]])
                    for si, (s0, sn_) in enumerate(segs):
                        P.op("act", lambda e, si=si, s0=s0, sn_=sn_: e.activation(out=sg[:, s0:s0 + sn_], in_=self.ps[si][:, 0:sn_], func=AF.Silu), reads=[self.PS[si]], writes=["sgm%d" % si])
                        P.op("dve", lambda e, si=si, s0=s0, sn_=sn_, ft=ft: e.tensor_tensor(out=hidT[:, ft, s0:s0 + sn_], in0=self.ps[3 + si][:, 0:sn_], in1=sg[:, s0:s0 + sn_], op=ALU.mult),
                             reads=[self.PS[3 + si], "sgm%d" % si], writes=["hidT"])
            cur_sc = []
            for j in range(nst):
                rows = 128 if j < 8 else CAPC
                ys, ysn = ysts[j % 2], "yst%d" % (j % 2)
                for hf in range(2):
                    pb = 6 + hf
                    for ft in range(16):
                        self.mm(self.ps[pb][0:rows, :], hidT[:, ft, j * 128:j * 128 + rows], wd[:, ft, hf * 512:(hf + 1) * 512], ft == 0, ft == 15, ["hidT", wdn], [self.PS[pb]])
                    P.op("act", lambda e, ys=ys, hf=hf, pb=pb, rows=rows, gt=gt, j=j: e.activation(out=ys[0:rows, hf * 512:(hf + 1) * 512], in_=self.ps[pb][0:rows, :], func=AF.Copy, scale=gt[0:rows, j:j + 1]),
                         reads=[self.PS[pb], gtn], writes=[ysn])
                dst = self.moe_d if j < 8 else self.moec_d
                scn = "sc%d_%d" % (ex, j)
                P.dma(lambda e, ys=ys, rows=rows, ii=ii, j=j, dst=dst: e.indirect_dma_start(out=dst[:, :], out_offset=bass.IndirectOffsetOnAxis(ap=ii[0:rows, j:j + 1], axis=0),
                                                                                           in_=ys[0:rows, :], in_offset=None, compute_op=ALU.add),
                      reads=[ysn, iin] + list(prev_sc), writes=[scn], eng="pool")
                cur_sc.append(scn)
            prev_sc[:] = cur_sc

        partA(0)
        for ex in range(nexp):
            partB(ex)
            if ex + 1 < nexp:
                partA(ex + 1)
            partC(ex)
        P.release()

    def phase_ln2(self, l, var, dst):
        P = self.P
        P.mark()
        N = LC if var else T
        xmid_d = self.xcmid_d if var else self.xmid_d
        moe_d = self.moec_d if var else self.moe_d
        mr = self.modrows[l, var]
        g2 = self.row_tile("g2", mr[5 * DM:6 * DM])
        lg = self.row_tile("lg2", self.ln2_g[l])
        lb = self.row_tile("lb2", self.ln2_b[l])
        eps_t = P.sb("feps", [128, 1], F32)
        P.op("pool", lambda e: e.memset(eps_t[:], EPS), writes=["feps"])
        D = 8
        xts = [P.sb("fxt%d" % i, [128, DM], F32) for i in range(D)]
        mts = [P.sb("fmt%d" % i, [128, DM], F32) for i in range(D)]
        ots = [P.sb("fot%d" % i, [128, DM], F32) for i in range(D)]
        sts = [P.sb("fst%d" % i, [128, 2, 6], F32) for i in range(D)]
        mvs = [P.sb("fmv%d" % i, [128, 4], F32) for i in range(D)]
        nt = N // 128
        if not var:
            nt = int(os.environ.get("DBG_NBLK", nt // 4)) * 4

        def S0(n):
            d = n % D
            xt, mt = xts[d], mts[d]
            P.dma(lambda e: e.dma_start(out=xt[:], in_=xmid_d[n * 128:(n + 1) * 128, :]), writes=["fxt%d" % d])
            P.dma(lambda e: e.dma_start(out=mt[:], in_=moe_d[n * 128:(n + 1) * 128, :]), writes=["fmt%d" % d], eng="act")

        def S1(n):
            d = n % D
            xt, mt = xts[d], mts[d]
            P.op("pool", lambda e: e.tensor_tensor(out=mt[:], in0=mt[:], in1=g2[:], op=ALU.mult), reads=["fmt%d" % d, "g2"], writes=["fmt%d" % d])
            P.op("act", lambda e: e.activation(out=xt[:], in_=xt[:], func=AF.Copy, scale=ALPHA), reads=["fxt%d" % d], writes=["fxt%d" % d])

        def S2(n):
            d = n % D
            xt, mt, st, mv = xts[d], mts[d], sts[d], mvs[d]
            P.op("dve", lambda e: e.tensor_tensor(out=mt[:], in0=mt[:], in1=xt[:], op=ALU.add), reads=["fmt%d" % d, "fxt%d" % d], writes=["fmt%d" % d])
            sn, mn = "fst%d" % d, "fmv%d" % d
            for hh in range(2):
                P.op("dve", lambda e, hh=hh: e.bn_stats(out=st[:, hh, :], in_=mt[:, hh * 512:(hh + 1) * 512]), reads=["fmt%d" % d], writes=[sn])
            P.op("dve", lambda e: e.bn_aggr(out=mv[:, 0:2], in_=st[:].rearrange("p a b -> p (a b)")), reads=[sn], writes=[mn])
            P.op("act", lambda e: e.activation(out=mv[:, 2:3], in_=mv[:, 1:2], func=AF.Sqrt, bias=eps_t[:, 0:1], scale=1.0), reads=[mn, "feps"], writes=[mn])
            P.op("dve", lambda e: e.reciprocal(out=mv[:, 2:3], in_=mv[:, 2:3]), reads=[mn], writes=[mn])
            P.op("dve", lambda e: e.scalar_tensor_tensor(out=mv[:, 3:4], in0=mv[:, 0:1], scalar=-1.0, in1=mv[:, 2:3], op0=ALU.mult, op1=ALU.mult), reads=[mn], writes=[mn])

        def S3(n):
            d = n % D
            mt, ot, mv = mts[d], ots[d], mvs[d]
            P.op("act", lambda e: e.activation(out=ot[:], in_=mt[:], func=AF.Identity, scale=mv[:, 2:3], bias=mv[:, 3:4]), reads=["fmt%d" % d, "fmv%d" % d], writes=["fot%d" % d])
            P.op("dve", lambda e: e.tensor_tensor(out=ot[:], in0=ot[:], in1=lg[:], op=ALU.mult), reads=["fot%d" % d, "lg2"], writes=["fot%d" % d])

        def S4(n):
            d = n % D
            ot = ots[d]
            P.op("pool", lambda e: e.tensor_tensor(out=ot[:], in0=ot[:], in1=lb[:], op=ALU.add), reads=["fot%d" % d, "lb2"], writes=["fot%d" % d])
            P.dma(lambda e: e.dma_start(out=dst[n * 128:(n + 1) * 128, :], in_=ot[:]), reads=["fot%d" % d])
        stages = [S0, (lambda n: None), S1, S2, S3, S4]
        for step in range(nt + len(stages) - 1):
            for si in reversed(range(len(stages))):
                n = step - si
                if 0 <= n < nt:
                    stages[si](n)
        P.release()

    def s5_consts(self):
        P = self.P
        sel = self.sel
        P.op("pool", lambda e: e.memset(sel[:], 0.0), writes=["sel"])
        for a in range(8):
            for b in range(8):
                eng = ("dve", "pool", "act")[(a * 8 + b) % 3]
                if eng == "act":
                    P.op("act", lambda e, a=a, b=b: e.activation(out=sel[:, a, b, 16 * b:16 * b + 16], in_=self.identf[:, 16 * a:16 * a + 16], func=AF.Copy), reads=["identf", "sel"], writes=["sel"])
                else:
                    P.op(eng, lambda e, a=a, b=b: e.tensor_copy(out=sel[:, a, b, 16 * b:16 * b + 16], in_=self.identf[:, 16 * a:16 * a + 16]), reads=["identf", "sel"], writes=["sel"])
        qi = P.sb("qi", [128, 1], I32)
        P.op("pool", lambda e: e.iota(qi[:], pattern=[[1, 1]], base=0, channel_multiplier=1), writes=["qi"])
        P.op("dve", lambda e: e.tensor_single_scalar(out=qi[:], in_=qi[:], scalar=4, op=ALU.arith_shift_right), reads=["qi"], writes=["qi"])
        qf = P.sb("qf", [128, 1], F32)
        P.op("dve", lambda e: e.tensor_copy(out=qf[:], in_=qi[:]), reads=["qi"], writes=["qf"])
        iv = P.sb("iv", [128, 8, 16], F32)
        P.op("pool", lambda e: e.iota(iv[:], pattern=[[1, 8], [0, 16]], base=0, channel_multiplier=0, allow_small_or_imprecise_dtypes=True), writes=["iv"])
        self.mskf = P.sb("mskf", [128, 128], F32)
        self.mskb = P.sb("mskb", [128, 128], F32)
        P.op("dve", lambda e: e.tensor_scalar(out=self.mskf[:], in0=iv[:].rearrange("p a b -> p (a b)"), scalar1=qf[:, 0:1], scalar2=None, op0=ALU.is_ge), reads=["iv", "qf"], writes=["mskf"])
        P.op("dve", lambda e: e.tensor_scalar(out=self.mskb[:], in0=iv[:].rearrange("p a b -> p (a b)"), scalar1=qf[:, 0:1], scalar2=None, op0=ALU.is_le), reads=["iv", "qf"], writes=["mskb"])

    def s5_setup(self, l):
        P = self.P
        S = ["S5S"]

        def so(eng, fn):
            P.op(eng, fn, reads=S, writes=S)

        def tt(o, a, b, op):
            so("dve", lambda e: e.tensor_tensor(out=o, in0=a, in1=b, op=op))

        def ts(o, a, c1, op0):
            so("dve", lambda e: e.tensor_scalar(out=o, in0=a, scalar1=c1, scalar2=None, op0=op0))

        def cp(o, a):
            so("dve", lambda e: e.tensor_copy(out=o, in_=a))

        def cmul(outr, outi, ar, ai, br, bi, t1, t2):
            tt(t1, ar, br, ALU.mult)
            tt(t2, ai, bi, ALU.mult)
            tt(outr, t1, t2, ALU.subtract)
            tt(t1, ar, bi, ALU.mult)
            tt(t2, ai, br, ALU.mult)
            tt(outi, t1, t2, ALU.add)
        sp = P.sb("s5par", [128, 1072], F32)
        P.dma(lambda e: e.dma_start(out=sp[:], in_=self.s5nat[l]), writes=S)
        lre, lim, ldt = sp[:, 0:16], sp[:, 16:32], sp[:, 32:48]
        Bre = sp[:, 48:304].rearrange("p (s c) -> p s c", s=16)
        Bim = sp[:, 304:560].rearrange("p (s c) -> p s c", s=16)
        Cre = sp[:, 560:816].rearrange("p (s c) -> p s c", s=16)
        Cim = sp[:, 816:1072].rearrange("p (s c) -> p s c", s=16)
        sm = P.sb("s5sm", [128, 16, 16], F32)
        R = lambda i: sm[:, i, :]
        dt, xr, th, cs_, sn_, t1, t2, am1, den, zr, zi = [R(i) for i in range(11)]
        so("act", lambda e: e.activation(out=dt, in_=ldt, func=AF.Exp))
        tt(xr, lre, dt, ALU.mult)
        tt(th, lim, dt, ALU.mult)
        hp_ = P.sb("halfpi", [128, 1], F32)
        so("pool", lambda e: e.memset(hp_[:], float(np.pi / 2)))
        so("act", lambda e: e.activation(out=sn_, in_=th, func=AF.Sin, scale=1.0 / 16))
        so("act", lambda e: e.activation(out=cs_, in_=th, func=AF.Sin, scale=1.0 / 16, bias=hp_[:, 0:1]))
        for _ in range(4):
            tt(t1, cs_, cs_, ALU.mult)
            tt(t2, sn_, sn_, ALU.mult)
            tt(sn_, sn_, cs_, ALU.mult)
            ts(sn_, sn_, 2.0, ALU.mult)
            tt(cs_, t1, t2, ALU.subtract)
        ekr = P.sb("ekr", [128, 16, 9], F32)
        eki = P.sb("eki", [128, 16, 9], F32)
        so("pool", lambda e: e.memset(ekr[:, :, 0], 1.0))
        so("pool", lambda e: e.memset(eki[:, :, 0], 0.0))
        cp(ekr[:, :, 1], cs_)
        cp(eki[:, :, 1], sn_)
        for k in range(1, 8):
            cmul(ekr[:, :, k + 1], eki[:, :, k + 1], ekr[:, :, k], eki[:, :, k], cs_, sn_, t1, t2)
        kv = P.sb("kv", [128, 16], F32)
        so("pool", lambda e: e.iota(kv[:], pattern=[[1, 16]], base=-7, channel_multiplier=0, allow_small_or_imprecise_dtypes=True))
        mag = P.sb("mag", [128, 16, 16], F32)
        apr = P.sb("apr", [128, 16, 16], F32)
        api = P.sb("api", [128, 16, 16], F32)
        tt(mag[:], xr.unsqueeze(2).to_broadcast([128, 16, 16]), kv[:, :].unsqueeze(1).to_broadcast([128, 16, 16]), ALU.mult)
        so("act", lambda e: e.activation(out=mag[:], in_=mag[:], func=AF.Exp))
        tt(apr[:, :, 7:16], mag[:, :, 7:16], ekr[:], ALU.mult)
        tt(api[:, :, 7:16], mag[:, :, 7:16], eki[:], ALU.mult)
        tt(apr[:, :, 0:7], mag[:, :, 0:7], ekr[:, :, 7:0:-1], ALU.mult)
        tt(api[:, :, 0:7], mag[:, :, 0:7], eki[:, :, 7:0:-1], ALU.mult)
        ts(api[:, :, 0:7], api[:, :, 0:7], -1.0, ALU.mult)
        ts(am1, apr[:, :, 8], -1.0, ALU.add)
        tt(t1, lre, lre, ALU.mult)
        tt(t2, lim, lim, ALU.mult)
        tt(den, t1, t2, ALU.add)
        so("dve", lambda e: e.reciprocal(out=den, in_=den))
        tt(t1, am1, lre, ALU.mult)
        tt(t2, api[:, :, 8], lim, ALU.mult)
        tt(zr, t1, t2, ALU.add)
        tt(zr, zr, den, ALU.mult)
        tt(t1, api[:, :, 8], lre, ALU.mult)
        tt(t2, am1, lim, ALU.mult)
        tt(zi, t1, t2, ALU.subtract)
        tt(zi, zi, den, ALU.mult)
        bbr = P.sb("bbr", [128, 16, 16], F32)
        bbi = P.sb("bbi", [128, 16, 16], F32)
        w1 = P.sb("w1", [128, 16, 8, 16], F32)
        w2 = P.sb("w2", [128, 16, 8, 16], F32)
        bc = lambda a: a.unsqueeze(2).to_broadcast([128, 16, 16])
        cmul(bbr[:], bbi[:], bc(zr), bc(zi), Bre, Bim, w1[:, :, 0, :], w2[:, :, 0, :])
        prod_r = P.sb("prodr", [128, 16, 8, 16], F32)
        prod_i = P.sb("prodi", [128, 16, 8, 16], F32)
        xa_r = P.sb("xar", [128, 16, 8, 16], F32)
        xa_i = P.sb("xai", [128, 16, 8, 16], F32)
        zA = P.sb("zA", [128, 128], F32)
        zB = P.sb("zB", [128, 128], F32)
        kacc = P.sb("kacc", [128, 128], F32)
        ktmp = P.sb("ktmp", [128, 128], F32)
        dcol = P.sb("dcol", [128, 16], F32)
        P.dma(lambda e: e.dma_start(out=dcol[:], in_=self.s5_dcol[l]), reads=S, writes=S)
        b4 = lambda a: a.unsqueeze(3).to_broadcast([128, 16, 8, 16])
        c4 = lambda a: a.unsqueeze(2).to_broadcast([128, 16, 8, 16])
        sl_neg, sl_pos, sl_7m, sl_p1, sl_8m = slice(7, None, -1), slice(7, 15), slice(14, 6, -1), slice(8, 16), slice(15, 7, -1)

        def prod(outr, outi, sl, mr, mi):
            cmul(outr, outi, b4(apr[:, :, sl]), b4(api[:, :, sl]), c4(mr), c4(mi), w1[:], w2[:])

        def gap(t, d, g):
            return t[:, d * 8 + g // 2, :, :].rearrange("p a b -> p (a b)")

        def zpad(dst, src, par, scale):
            so("pool", lambda e: e.memset(dst[64 * (1 - par):64 * (2 - par), :], 0.0))
            so("dve", lambda e: e.tensor_scalar(out=dst[64 * par:64 * par + 64, :], in0=src[64 * par:64 * par + 64, :], scalar1=scale, scalar2=None, op0=ALU.mult))
        PS0, PS1 = [self.PS[0]] + S, [self.PS[1]] + S
        for d in range(2):
            prod(prod_r[:], prod_i[:], sl_7m if d == 0 else sl_pos, bbr[:], bbi[:])
            for g in range(16):
                for ri, src in enumerate((prod_r, prod_i)):
                    zpad(zA, gap(src, d, g), g % 2, 1.0)
                    P.op("pe", lambda e: e.transpose(out=self.ps[0][:, 0:128], in_=zA[:], identity=self.identf[:]), reads=PS0, writes=PS0)
                    P.op("act", lambda e, d=d, g=g, ri=ri: e.activation(out=self.FT[:, d, g, ri, :], in_=self.ps[0][:, 0:128], func=AF.Copy), reads=PS0, writes=PS0)
            prod(prod_r[:], prod_i[:], sl_p1 if d == 0 else sl_8m, Cre, Cim)
            for g in range(16):
                for ri, (src, sc_) in enumerate(((prod_r, 1.0), (prod_i, -1.0))):
                    zpad(self.EZ[:, d, g, ri, :], gap(src, d, g), g % 2, sc_)
            prod(xa_r[:], xa_i[:], sl_neg if d == 0 else sl_pos, bbr[:], bbi[:])
            prod(prod_r[:], prod_i[:], sl_pos if d == 0 else sl_neg, Cre, Cim)
            for g in range(16):
                zpad(zA, gap(xa_r, d, g), g % 2, 1.0)
                zpad(zB, gap(xa_i, d, g), g % 2, -1.0)
                P.op("pe", lambda e, d=d, g=g: e.matmul(out=self.ps[1][:, 0:128], lhsT=zA[:], rhs=gap(prod_r, d, g), start=True, stop=False), reads=PS1, writes=PS1)
                P.op("pe", lambda e, d=d, g=g: e.matmul(out=self.ps[1][:, 0:128], lhsT=zB[:], rhs=gap(prod_i, d, g), start=False, stop=True), reads=PS1, writes=PS1)
                msk = self.mskf if d == 0 else self.mskb
                P.op("dve", lambda e, msk=msk: e.tensor_tensor(out=ktmp[:], in0=self.ps[1][:, 0:128], in1=msk[:], op=ALU.mult), reads=PS1 + ["mskf", "mskb"], writes=PS1)
                if d == 0:
                    so("dve", lambda e, g=g: e.scalar_tensor_tensor(out=self.Kf32[:, g, :], in0=self.identf[:], scalar=dcol[:, g:g + 1], in1=ktmp[:], op0=ALU.mult, op1=ALU.add))
                else:
                    tt(kacc[:], ktmp[:], self.Kf32[:, g, :], ALU.add)
                    cp(self.KtotT[:, g, :], kacc[:])
        g1r, g1i, phc, phs = self.g1r, self.g1i, self.phc, self.phs
        cp(g1r[:, :, 0], ekr[:, :, 8])
        cp(g1i[:, :, 0], eki[:, :, 8])
        for m in range(7):
            cmul(g1r[:, :, m + 1], g1i[:, :, m + 1], g1r[:, :, m], g1i[:, :, m], g1r[:, :, m], g1i[:, :, m], t1, t2)
        so("pool", lambda e: e.memset(phc[:, :, 0:1], 1.0))
        so("pool", lambda e: e.memset(phs[:, :, 0:1], 0.0))
        wv1 = w1[:].rearrange("p s a b -> p s (a b)")
        wv2 = w2[:].rearrange("p s a b -> p s (a b)")
        for m in range(7):
            n = 1 << m
            cmul(phc[:, :, n:2 * n], phs[:, :, n:2 * n], phc[:, :, 0:n], phs[:, :, 0:n],
                 g1r[:, :, m:m + 1].to_broadcast([128, 16, n]), g1i[:, :, m:m + 1].to_broadcast([128, 16, n]), wv1[:, :, 0:n], wv2[:, :, 0:n])
        cp(self.rho[:], mag[:, :, 15:16].to_broadcast([128, 16, 128]))

    def s5_run(self, N, uT_src, use_h0, with_output, out_dst, store_final, tag):
        P = self.P
        P.mark()
        TT = 8 * N
        W = min(512, N)
        nh = N // W
        L = min(128, N)
        nseg = N // L
        ut = P.sb("s5ut", [128, 2, TT], BF16)
        ut_off = P.last_off
        U = P.sb("s5U", [128, 16, N], BF16)
        U_off = P.last_off
        for hc in range(2):
            P.dma(lambda e, hc=hc: e.dma_start(out=ut[:, hc, :], in_=uT_src[hc * 128:(hc + 1) * 128, :]), writes=["s5ut"])
        sel = self.sel
        cnt = 0
        for g in range(16):
            uv = ut[:, g // 8, :].rearrange("p (j i) -> p i j", i=8)
            for h in range(nh):
                pb = cnt % 2
                cnt += 1
                for i0 in range(8):
                    self.mm(self.ps[pb][:, 0:W], sel[:, g % 8, i0, :], uv[:, i0, h * W:(h + 1) * W], i0 == 0, i0 == 7, ["sel", "s5ut"], [self.PS[pb]])
                if pb == 0:
                    P.op("act", lambda e, g=g, h=h, pb=pb: e.activation(out=U[:, g, h * W:(h + 1) * W], in_=self.ps[pb][:, 0:W], func=AF.Copy), reads=[self.PS[pb]], writes=["s5U"])
                else:
                    P.op("dve", lambda e, g=g, h=h, pb=pb: e.tensor_copy(out=U[:, g, h * W:(h + 1) * W], in_=self.ps[pb][:, 0:W]), reads=[self.PS[pb]], writes=["s5U"])
        P.barrier()
        P.mark()
        sets = []
        for si_ in range(2):
            bufs = []
            for bi in range(8):
                if si_ == 0:
                    bufs.append(P.sb("s5w%d_%d" % (si_, bi), [128, N], F32))
                else:
                    bufs.append(P.sb_at("s5w%d_%d" % (si_, bi), [128, N], F32, ut_off + bi * N * 4))
            sets.append(bufs)
        shbs = [P.sb("s5shb%d" % i, [128, 2, N], BF16) for i in range(2)]
        inis = [P.sb("s5ini%d" % i, [128, 4], F32) for i in range(2)]
        hfin, g1r, g1i, hnew, rho = self.hfin, self.g1r, self.g1i, self.hnew, self.rho
        FT, phc, phs = self.FT, self.phc, self.phs
        bankc = [0]

        def SA(s_):
            d, gp = s_ // 8, s_ % 8
            k_ = s_ % 2
            Vr, Vi = sets[k_][0], sets[k_][1]
            for ri, V in enumerate((Vr, Vi)):
                vn = "s5V%d_%d" % (ri, k_)
                for h in range(nh):
                    pb = 2 + bankc[0] % 4
                    bankc[0] += 1
                    self.mm(self.ps[pb][:, 0:W], FT[:, d, 2 * gp, ri, :], U[:, 2 * gp, h * W:(h + 1) * W], True, False, ["FT", "s5U"], [self.PS[pb]])
                    self.mm(self.ps[pb][:, 0:W], FT[:, d, 2 * gp + 1, ri, :], U[:, 2 * gp + 1, h * W:(h + 1) * W], False, True, ["FT", "s5U"], [self.PS[pb]])
                    if d == 0:
                        ov = V[:, h * W:(h + 1) * W]
                    else:
                        ov = V[:, N - 1 - h * W:(N - 1 - (h + 1) * W if (h + 1) * W < N else None):-1]
                    P.op("act", lambda e, ov=ov, pb=pb: e.activation(out=ov, in_=self.ps[pb][:, 0:W], func=AF.Copy), reads=[self.PS[pb]], writes=[vn])

        def SB(s_):
            d, gp = s_ // 8, s_ % 8
            k_ = s_ % 2
            Vr, Vi, Wr, Wi, Sr, Si, ta, tc_ = sets[k_]
            shb, ini = shbs[k_], inis[k_]
            nm = lambda x: "s5%s_%d" % (x, k_)
            c3 = phc[:, s_, 0:L].unsqueeze(1).to_broadcast([128, nseg, L])
            s3 = phs[:, s_, 0:L].unsqueeze(1).to_broadcast([128, nseg, L])
            v3 = lambda t: t[:, :].rearrange("p (a b) -> p a b", b=L)
            P.op("dve", lambda e: e.tensor_tensor(out=v3(ta), in0=v3(Vr), in1=c3, op=ALU.mult), reads=[nm("V0"), "ph"], writes=[nm("ta")])
            P.op("dve", lambda e: e.tensor_tensor(out=v3(Wr), in0=v3(Vi), in1=s3, op=ALU.mult), reads=[nm("V1"), "ph"], writes=[nm("Wr")])
            P.op("dve", lambda e: e.tensor_tensor(out=Wr[:], in0=Wr[:], in1=ta[:], op=ALU.add), reads=[nm("Wr"), nm("ta")], writes=[nm("Wr")])
            P.op("pool", lambda e: e.tensor_tensor(out=v3(tc_), in0=v3(Vi), in1=c3, op=ALU.mult), reads=[nm("V1"), "ph"], writes=[nm("tc")])
            P.op("pool", lambda e: e.tensor_tensor(out=v3(Wi), in0=v3(Vr), in1=s3, op=ALU.mult), reads=[nm("V0"), "ph"], writes=[nm("Wi")])
            P.op("pool", lambda e: e.tensor_tensor(out=Wi[:], in0=tc_[:], in1=Wi[:], op=ALU.subtract), reads=[nm("Wi"), nm("tc")], writes=[nm("Wi")])
            for sg_ in range(nseg):
                a, b = sg_ * L, (sg_ + 1) * L
                qr = None
                if sg_ == 0:
                    if use_h0:
                        qr, qi = g1r[:, s_, 0:1], g1i[:, s_, 0:1]
                        lr, li = hfin[:, s_, 0:1], hfin[:, s_, 1:2]
                else:
                    m7 = {128: 7, 32: 5}[L]
                    qr, qi = g1r[:, s_, m7:m7 + 1], g1i[:, s_, m7:m7 + 1]
                    lr, li = Sr[:, a - 1:a], Si[:, a - 1:a]
                rds = [nm("Sr"), nm("Si"), "hfin", "g1", nm("ini")]
                if qr is None:
                    P.op("pool", lambda e: e.memset(ini[:], 0.0), reads=[nm("ini")], writes=[nm("ini")])
                else:
                    P.op("dve", lambda e, li=li, qi=qi: e.tensor_tensor(out=ini[:, 2:3], in0=li, in1=qi, op=ALU.mult), reads=rds, writes=[nm("ini")])
                    P.op("dve", lambda e, lr=lr, qr=qr: e.scalar_tensor_tensor(out=ini[:, 0:1], in0=lr, scalar=qr, in1=ini[:, 2:3], op0=ALU.mult, op1=ALU.subtract), reads=rds, writes=[nm("ini")])
                    P.op("dve", lambda e, li=li, qr=qr: e.tensor_tensor(out=ini[:, 3:4], in0=li, in1=qr, op=ALU.mult), reads=rds, writes=[nm("ini")])
                    P.op("dve", lambda e, lr=lr, qi=qi: e.scalar_tensor_tensor(out=ini[:, 1:2], in0=lr, scalar=qi, in1=ini[:, 3:4], op0=ALU.mult, op1=ALU.add), reads=rds, writes=[nm("ini")])
                P.op("dve", lambda e, a=a, b=b: e.tensor_tensor_scan(out=Sr[:, a:b], data0=rho[:, s_, 0:L], data1=Wr[:, a:b], initial=ini[:, 0:1], op0=ALU.mult, op1=ALU.add),
                     reads=[nm("Wr"), "rhot", nm("ini")], writes=[nm("Sr")])
                P.op("dve", lambda e, a=a, b=b: e.tensor_tensor_scan(out=Si[:, a:b], data0=rho[:, s_, 0:L], data1=Wi[:, a:b], initial=ini[:, 1:2], op0=ALU.mult, op1=ALU.add),
                     reads=[nm("Wi"), "rhot", nm("ini")], writes=[nm("Si")])
            P.op("dve", lambda e: e.tensor_tensor(out=v3(ta), in0=v3(Sr), in1=c3, op=ALU.mult), reads=[nm("Sr"), "ph"], writes=[nm("ta")])
            P.op("dve", lambda e: e.tensor_tensor(out=v3(Vr), in0=v3(Si), in1=s3, op=ALU.mult), reads=[nm("Si"), "ph"], writes=[nm("V0")])
            P.op("dve", lambda e: e.tensor_tensor(out=Wr[:], in0=ta[:], in1=Vr[:], op=ALU.subtract), reads=[nm("ta"), nm("V0")], writes=[nm("Wr")])
            P.op("pool", lambda e: e.tensor_tensor(out=v3(tc_), in0=v3(Si), in1=c3, op=ALU.mult), reads=[nm("Si"), "ph"], writes=[nm("tc")])
            P.op("pool", lambda e: e.tensor_tensor(out=v3(Vi), in0=v3(Sr), in1=s3, op=ALU.mult), reads=[nm("Sr"), "ph"], writes=[nm("V1")])
            P.op("pool", lambda e: e.tensor_tensor(out=Wi[:], in0=tc_[:], in1=Vi[:], op=ALU.add), reads=[nm("tc"), nm("V1")], writes=[nm("Wi")])
            for ri, H in enumerate((Wr, Wi)):
                hn = nm(("Wr", "Wi")[ri])
                if d == 0:
                    P.op("act", lambda e, ri=ri, H=H: e.activation(out=shb[:, ri, 1:N], in_=H[:, 0:N - 1], func=AF.Copy), reads=[hn], writes=[nm("shb")])
                    edge = shb[:, ri, 0:1]
                else:
                    P.op("act", lambda e, ri=ri, H=H: e.activation(out=shb[:, ri, 0:N - 1], in_=H[:, N - 2::-1], func=AF.Copy), reads=[hn], writes=[nm("shb")])
                    edge = shb[:, ri, N - 1:N]
                if use_h0:
                    P.op("dve", lambda e, edge=edge, ri=ri: e.tensor_copy(out=edge, in_=hfin[:, s_, ri:ri + 1]), reads=["hfin", nm("shb")], writes=[nm("shb")])
                else:
                    P.op("pool", lambda e, edge=edge: e.memset(edge, 0.0), reads=[nm("shb")], writes=[nm("shb")])
            if with_output:
                P.dma(lambda e: e.dma_start(out=self.ssh_d[d, gp, :, :, 0:N].rearrange("r p n -> p r n"), in_=shb[:]), reads=[nm("shb")], writes=["ssh_d"])
            if store_final:
                for ri, H in enumerate((Wr, Wi)):
                    P.op("dve", lambda e, ri=ri, H=H: e.tensor_copy(out=hnew[:, s_, ri:ri + 1], in_=H[:, N - 1:N]), reads=[nm(("Wr", "Wi")[ri]), nm("shb")], writes=["hnew"])
        SA(0)
        for s_ in range(16):
            if s_ + 1 < 16:
                SA(s_ + 1)
            SB(s_)
        if store_final:
            P.op("dve", lambda e: e.tensor_copy(out=hfin[:], in_=hnew[:]), reads=["hnew", "s5shb_0", "s5shb_1", "s5ini_0", "s5ini_1"], writes=["hfin"])
        P.release()
        if with_output:
            Yb = P.sb_at("s5Y", [128, 16, N], BF16, ut_off)
            ssbs = [P.sb("s5ssb%d" % i, [128, 2, 2, N], BF16) for i in range(2)]
            for g in range(16):
                gp = g // 2
                ssb, ssn = ssbs[gp % 2], "s5ssb%d" % (gp % 2)
                if g % 2 == 0:
                    for d in range(2):
                        P.dma(lambda e, d=d, gp=gp, ssb=ssb: e.dma_start(out=ssb[:, d, :, :], in_=self.ssh_d[d, gp, :, :, 0:N].rearrange("r p n -> p r n")), reads=["ssh_d"], writes=[ssn])
                for h in range(nh):
                    pb = 4 + (g * nh + h) % 2
                    sl = slice(h * W, (h + 1) * W)
                    self.mm(self.ps[pb][:, 0:W], self.KtotT[:, g, :], U[:, g, sl], True, False, ["S5S", "s5U"], [self.PS[pb]])
                    for d in range(2):
                        for ri in range(2):
                            self.mm(self.ps[pb][:, 0:W], self.EZ[:, d, g, ri, :], ssb[:, d, ri, sl], False, d == 1 and ri == 1, ["S5S", ssn], [self.PS[pb]])
                    if pb == 4:
                        P.op("act", lambda e, g=g, sl=sl, pb=pb: e.activation(out=Yb[:, g, sl], in_=self.ps[pb][:, 0:W], func=AF.Copy), reads=[self.PS[pb]], writes=["s5ut"])
                    else:
                        P.op("dve", lambda e, g=g, sl=sl, pb=pb: e.tensor_copy(out=Yb[:, g, sl], in_=self.ps[pb][:, 0:W]), reads=[self.PS[pb]], writes=["s5ut"])
            if self.dbg and tag == "x" and self.stop == "s5":
                oy = self.dbg_out("Yb", [128, 16, N], BF16)
                P.dma(lambda e: e.dma_start(out=oy, in_=Yb[:]), reads=["s5ut"])
                ou = self.dbg_out("U", [128, 16, N], BF16)
                P.dma(lambda e: e.dma_start(out=ou, in_=U[:]), reads=["s5U"])
                self.dump("ssh", self.ssh_d, [2, 8, 2, 128, T // 8], BF16)
                for nm, t_, shp, dt_ in (("FT", self.FT, [128, 2, 16, 2, 128], BF16), ("EZ", self.EZ, [128, 2, 16, 2, 128], BF16), ("KtotT", self.KtotT, [128, 16, 128], BF16),
                                         ("phc", self.phc, [128, 16, 128], F32), ("phs", self.phs, [128, 16, 128], F32), ("rho", self.rho, [128, 16, 128], F32),
                                         ("g1r", self.g1r, [128, 16, 8], F32), ("g1i", self.g1i, [128, 16, 8], F32), ("hfin", self.hfin, [128, 16, 2], F32)):
                    od = self.dbg_out(nm, shp, dt_)
                    P.dma(lambda e, od=od, t_=t_: e.dma_start(out=od, in_=t_[:]), reads=["S5S", "hfin"])
            zT = P.sb_at("s5zT", [128, 2, TT], BF16, U_off)
            y2 = P.sb("s5y2", [128, W], F32)
            sgm = P.sb("s5sg", [128, W], F32)
            cnt = 0
            for hc in range(2):
                zv = zT[:, hc, :].rearrange("p (j i) -> p i j", i=8)
                for i0 in range(8):
                    for h in range(nh):
                        pb = 6 + cnt % 2
                        cnt += 1
                        sl = slice(h * W, (h + 1) * W)
                        for g8 in range(8):
                            self.mm(self.ps[pb][:, 0:W], sel[:, i0, g8, :], Yb[:, hc * 8 + g8, sl], g8 == 0, g8 == 7, ["sel", "s5ut"], [self.PS[pb]])
                        P.op("act", lambda e, pb=pb: e.activation(out=y2[:], in_=self.ps[pb][:, 0:W], func=AF.Square), reads=[self.PS[pb]], writes=["s5y2"])
                        P.op("dve", lambda e: e.tensor_scalar(out=y2[:], in0=y2[:], scalar1=0.044715, scalar2=1.0, op0=ALU.mult, op1=ALU.add), reads=["s5y2"], writes=["s5y2"])
                        P.op("dve", lambda e, pb=pb: e.tensor_tensor(out=y2[:], in0=self.ps[pb][:, 0:W], in1=y2[:], op=ALU.mult), reads=[self.PS[pb], "s5y2"], writes=["s5y2"])
                        P.op("act", lambda e: e.activation(out=sgm[:], in_=y2[:], func=AF.Sigmoid, scale=1.5957691216057308), reads=["s5y2"], writes=["s5sg"])
                        P.op("dve", lambda e, pb=pb, zv=zv, i0=i0, sl=sl: e.tensor_tensor(out=zv[:, i0, sl], in0=self.ps[pb][:, 0:W], in1=sgm[:], op=ALU.mult), reads=[self.PS[pb], "s5sg"], writes=["s5U"])
            BW = min(512, TT)
            gt = P.sb("s5gt", [128, BW], F32)
            obs = [P.sb("s5ob%d" % i, [128, BW], BF16) for i in range(2)]
            for b in range(TT // BW):
                sl = slice(b * BW, (b + 1) * BW)
                for ho in range(2):
                    pb = 2 + ho
                    for hc in range(2):
                        self.mm(self.ps[pb][:, 0:BW], self.wglu[:, hc, ho * 128:(ho + 1) * 128], zT[:, hc, sl], hc == 0, hc == 1, ["wglu", "s5U"], [self.PS[pb]])
                    P.op("act", lambda e, pb=pb, ho=ho: e.activation(out=gt[:], in_=self.ps[pb][:, 0:BW], func=AF.Sigmoid, bias=self.bglu[:, ho:ho + 1], scale=1.0), reads=[self.PS[pb], "wglu"], writes=["s5gt"])
                    ob = obs[ho]
                    P.op("dve", lambda e, ob=ob, ho=ho, sl=sl: e.tensor_tensor(out=ob[:], in0=zT[:, ho, sl], in1=gt[:], op=ALU.mult), reads=["s5U", "s5gt"], writes=["s5ob%d" % ho])
                    P.dma(lambda e, ob=ob, ho=ho, sl=sl: e.dma_start(out=out_dst[ho * 128:(ho + 1) * 128, sl], in_=ob[:]), reads=["s5ob%d" % ho])
        P.release()

    def phase_s5(self, l, last):
        P = self.P
        P.mark()
        self.sel = P.sb("sel", [128, 8, 8, 128], BF16)
        self.KtotT = P.sb("KtotT", [128, 16, 128], BF16)
        self.FT = P.sb("FT", [128, 2, 16, 2, 128], BF16)
        self.EZ = P.sb("EZ", [128, 2, 16, 2, 128], BF16)
        self.phc = P.sb("phc", [128, 16, 128], F32)
        self.phs = P.sb("phs", [128, 16, 128], F32)
        self.rho = P.sb("rhot", [128, 16, 128], F32)
        self.g1r = P.sb("g1r", [128, 16, 8], F32)
        self.g1i = P.sb("g1i", [128, 16, 8], F32)
        self.hfin = P.sb("hfin", [128, 16, 2], F32)
        self.hnew = P.sb("hnew", [128, 16, 2], F32)
        self.wglu = P.sb("wglu", [128, 2, 256], BF16)
        self.bglu = P.sb("bglu", [128, 2], F32)
        wglu, bglu = self.wglu, self.bglu
        P.dma(lambda e: e.dma_start(out=wglu[:], in_=self.s5_w_glu[l].rearrange("(k p) n -> p k n", p=128)), writes=["wglu"], eng="pool")
        P.dma(lambda e: e.dma_start(out=bglu[:], in_=self.s5_bglu_col[l]), writes=["wglu"])
        P.mark()
        self.Kf32 = P.sb("Kf32", [128, 16, 128], F32)
        self.s5_consts()
        self.s5_setup(l)
        P.op("pool", lambda e: e.memset(self.hnew[:], 0.0), reads=["S5S"], writes=["S5S", "ph", "rhot", "g1", "FT", "hnew", "sel"])
        P.release()
        self.s5_run(LC // 8, self.ucT_d, False, not last, self.catcT_d[512:768, :], True, "c")
        self.s5_run(T // 8, self.uT_d, True, True, self.catT_d[512:768, :], False, "x")
        P.release()

    def dump(self, name, src, shape, dt):
        o = self.dbg_out(name, shape, dt)
        self.P.dma(lambda e: e.dma_start(out=o, in_=src))

    def build(self):
        P = self.P
        self.x_src, self.xc_src = self.x_in, self.ctx_in
        self.logits_x = P.sb("logits_x", [128, T // 128, NEXP], F32)
        self.logits_c = P.sb("logits_c", [128, LC // 128, NEXP], F32)
        P.op("pool", lambda e: e.memset(self.logits_x[:], 0.0), writes=["logits0"])
        P.op("pool", lambda e: e.memset(self.logits_c[:], 0.0), writes=["logits1"])
        for l in range(self.nlayers):
            last = l == DEPTH - 1
            self.phase_mod(l)
            if self.stop == "mod":
                self.dump("modrows", self.modrows[l], [2, 6 * DM], F32)
                break
            P.mark()
            self.load_w_in(l)
            self.build_wext(l, 1)
            self.phase_inproj(l, True, not last)
            self.build_wext(l, 0)
            self.phase_inproj(l, False, True)
            P.release()
            if self.stop == "inproj":
                self.dump("qT", self.qT_d, [512, T], BF16)
                self.dump("kT", self.kT_d, [256, T], BF16)
                self.dump("v", self.v_d, [T, 130], BF16)
                self.dump("uT", self.uT_d, [256, T], BF16)
                self.dump("hT", self.hT_d, [256, T], BF16)
                self.dump("kcT", self.kcT_d, [256, LC], BF16)
                self.dump("ucT", self.ucT_d, [256, LC], BF16)
                break
            if os.environ.get("DBG_SKIPATTN") is None:
                self.phase_attn(l, not last)
            if self.stop == "attn":
                self.dump("catT", self.catT_d, [DM, T], BF16)
                self.dump("catcT", self.catcT_d, [DM, LC], BF16)
                break
            if os.environ.get("DBG_SKIPS5") is None:
                self.phase_s5(l, last)
            if self.stop == "s5":
                self.dump("catT", self.catT_d, [DM, T], BF16)
                self.dump("catcT", self.catcT_d, [DM, LC], BF16)
                break
            if not last:
                self.phase_conv(l, True)
            self.phase_conv(l, False)
            if self.stop == "conv":
                self.dump("catT", self.catT_d, [DM, T], BF16)
                self.dump("catcT", self.catcT_d, [DM, LC], BF16)
                break
            if not last:
                self.phase_outproj(l, 1)
            self.phase_outproj(l, 0)
            if self.stop == "outproj":
                self.dump("catT", self.catT_d, [DM, T], BF16)
                self.dump("catcT", self.catcT_d, [DM, LC], BF16)
                self.dump("xmid", self.xmid_d, [T, DM], F32)
                self.dump("xcmid", self.xcmid_d, [LC, DM], F32)
                self.dump("h2", self.h2_d, [T, DM], BF16)
                lo = self.dbg_out("logits", [128, T // 128, NEXP], F32)
                P.dma(lambda e: e.dma_start(out=lo, in_=self.logits_x[:]), reads=["logits0"])
                break
            P.barrier()
            self.phase_moe(l, not last)
            if self.stop == "moe":
                self.dump("h2", self.h2_d, [T, DM], BF16)
                self.dump("h2c", self.h2c_d, [LC, DM], BF16)
                self.dump("moe", self.moe_d, [T, DM], F32)
                self.dump("moec", self.moec_d, [LC, DM], F32)
                lo = self.dbg_out("logits", [128, T // 128, NEXP], F32)
                P.dma(lambda e: e.dma_start(out=lo, in_=self.logits_x[:]), reads=["logits0"])
                lo2 = self.dbg_out("logitsc", [128, LC // 128, NEXP], F32)
                P.dma(lambda e: e.dma_start(out=lo2, in_=self.logits_c[:]), reads=["logits1"])
                break
            if not last:
                self.phase_ln2(l, 1, self.xc1_d)
            self.phase_ln2(l, 0, self.out if last else self.x1_d)
            self.x_src, self.xc_src = self.x1_d, self.xc1_d
            if self.stop == "ln2":
                self.dump("x1", self.x1_d, [T, DM], F32)
                self.dump("xc1", self.xc1_d, [LC, DM], F32)
                break
        P.barrier()
        P.emit()
        return self.nc


def rope_tables():
    t = np.arange(T)
    pos_row = (t // 64).astype(np.float32)
    pos_col = (t % 64).astype(np.float32)
    inv_freq = (10000.0 ** (-np.arange(16, dtype=np.float32) / 16)).astype(np.float32)
    ang = np.zeros((64, T), np.float32)
    for j in range(64):
        pos = pos_row if j < 32 else pos_col
        ang[j] = pos * inv_freq[j % 16]
    cos = np.cos(ang).astype(np.float32)
    sin = np.sin(ang).astype(np.float32)
    return np.ascontiguousarray(np.concatenate([cos, cos], 0)), np.ascontiguousarray(np.concatenate([sin, sin], 0))


def make_in_maps(inputs, ncores=2):
    f = lambda a: np.ascontiguousarray(np.asarray(a, dtype=np.float32))
    cosT, sinT = rope_tables()
    maps = []
    for b in range(ncores):
        ccol = np.stack([f(inputs["c"])[b].reshape(8, 128).T, f(inputs["c_ctx"]).reshape(8, 128).T], axis=-1)
        m = {"x": f(inputs["x"])[b], "ctx": f(inputs["ctx"])[b], "ccol": np.ascontiguousarray(ccol),
             "w_mod": f(inputs["w_mod"]), "b_mod": f(inputs["b_mod"]), "w_in": f(inputs["w_in"]), "b_in": f(inputs["b_in"]),
             "cosT": cosT, "sinT": sinT, "attn_sink": f(inputs["attn_sink"]),
             "conv_w_col": np.ascontiguousarray(f(inputs["conv_w_dw"])[:, :, 0, :].reshape(DEPTH, 31, 2, 128).transpose(0, 3, 2, 1)),
             "conv_vec_col": np.ascontiguousarray(np.stack([f(inputs[k]).reshape(DEPTH, 2, 128).transpose(0, 2, 1)
                                                            for k in ("conv_b_dw", "conv_ln_g", "conv_ln_b", "conv_b_pw")], axis=-1)),
             "conv_w_pw": f(inputs["conv_w_pw"])}
        def nat(a):
            a = f(a)
            return a
        lam = lambda k: f(inputs[k]).reshape(DEPTH, 2, 8, 2, 64).transpose(0, 3, 4, 1, 2).reshape(DEPTH, 128, 16)
        ldt = np.broadcast_to(f(inputs["s5_log_dt"]).reshape(DEPTH, 2, 8, 2, 1).transpose(0, 3, 4, 1, 2), (DEPTH, 2, 64, 2, 8)).reshape(DEPTH, 128, 16)
        bm_ = lambda k: f(inputs[k]).reshape(DEPTH, 2, 8, 2, 64, 16).transpose(0, 3, 4, 1, 2, 5).reshape(DEPTH, 128, 256)
        cm_ = lambda k: f(inputs[k]).reshape(DEPTH, 2, 8, 2, 16, 64).transpose(0, 3, 5, 1, 2, 4).reshape(DEPTH, 128, 256)
        m["s5nat"] = np.ascontiguousarray(np.concatenate([lam("s5_lam_re"), lam("s5_lam_im"), ldt, bm_("s5_b_re"), bm_("s5_b_im"), cm_("s5_c_re"), cm_("s5_c_im")], axis=2))
        m["s5_dcol"] = np.ascontiguousarray(np.tile(f(inputs["s5_d"]).reshape(DEPTH, 16, 16).transpose(0, 2, 1), (1, 8, 1)))
        m["s5_w_glu"] = f(inputs["s5_w_glu"])
        m["s5_bglu_col"] = np.ascontiguousarray(f(inputs["s5_b_glu"]).reshape(DEPTH, 2, 128).transpose(0, 2, 1))
        for k in ("w_out", "b_out", "ln1_g", "ln1_b", "ln2_g", "ln2_b", "w_router", "exp_w_gate", "exp_w_up", "exp_w_down"):
            m[k] = f(inputs[k])
        maps.append(m)
    return maps


def kernel(**inputs):
    B = Builder()
    nc = B.build()
    maps = make_in_maps(inputs)
    res = run_bass_kernel_spmd(nc, maps, core_ids=[0, 1])
    return np.stack([r["out"] for r in res.results], 0).astype(np.float32)
```
